# Optimizing a Trainium2 kernel written in Bass

```python
import jax, jax.numpy as jnp
from jax import lax
import numpy as np

D_MODEL = 2048
BATCH = 4
SEQ = 4096
DEPTH = 2

CONV_WIDTH = D_MODEL // 2
DW_CONV_SIZE = 31
SG_WIDTH = D_MODEL // 2
SG_CHUNK = 128
SG_GROUPS = 8
SG_GROUP_DIM = SG_WIDTH // SG_GROUPS
N_HEADS = D_MODEL // 128
N_KV_GROUPS = 4
HEADS_PER_GROUP = N_HEADS // N_KV_GROUPS
HEAD_DIM = D_MODEL // 32
CMP_BLOCK = 32
CMP_STRIDE = 16
CMP_HIDDEN = 4 * HEAD_DIM
SLC_BLOCK = 64
SLC_TOP_N = 16
WINDOW = 512
NSA_Q_BLOCK = 64
D_FF = 5632
FFN_CONV_SIZE = 3
A_IN = 2 * CONV_WIDTH
B_IN = 2 * SG_WIDTH
Q_IN = N_HEADS * HEAD_DIM
KV_IN = 6 * N_KV_GROUPS * HEAD_DIM
NSA_GATE_IN = 3 * N_HEADS
MERGE_IN = 3 * D_MODEL
N_IN = A_IN + B_IN + Q_IN + KV_IN + NSA_GATE_IN + MERGE_IN

EPS = 1e-6
NEG = -1e30
FORCE = 1e4

kernel_name = "hybrid_conformer_gmlp_nsa_block"


def rms_norm(x, w):
    x32 = x.astype(jnp.float32)
    y = x32 * lax.rsqrt(jnp.mean(x32 * x32, axis=-1, keepdims=True) + EPS)
    return (y * w.astype(jnp.float32)).astype(x.dtype)


def layer_norm(x, g, b):
    x32 = x.astype(jnp.float32)
    xc = x32 - jnp.mean(x32, axis=-1, keepdims=True)
    var = jnp.mean(xc * xc, axis=-1, keepdims=True)
    return (xc * lax.rsqrt(var + EPS) * g.astype(jnp.float32) + b.astype(jnp.float32)).astype(x.dtype)


def causal_depthwise_conv(x, w, b):
    k = w.shape[0]
    y = lax.conv_general_dilated(
        x, w[:, None, :].astype(x.dtype), window_strides=(1,), padding=[(k - 1, 0)],
        dimension_numbers=("NWC", "WIO", "NWC"), feature_group_count=x.shape[-1])
    return y + b


def masked_softmax(s, mask):
    s = jnp.where(mask, s.astype(jnp.float32), NEG)
    m = jnp.max(s, axis=-1, keepdims=True)
    e = jnp.where(mask, jnp.exp(s - m), 0.0)
    return e / jnp.maximum(jnp.sum(e, axis=-1, keepdims=True), 1e-30)


def _split_points():
    sizes = (A_IN, B_IN, Q_IN, KV_IN, NSA_GATE_IN, MERGE_IN)
    return [int(v) for v in np.cumsum(sizes)[:-1]]


def conformer_conv(glu_in, conv_w, conv_b, ln_g, ln_b, w_out):
    a, g = jnp.split(glu_in, 2, axis=-1)
    h = a * jax.nn.sigmoid(g)
    h = causal_depthwise_conv(h, conv_w, conv_b)
    h = jax.nn.silu(layer_norm(h, ln_g, ln_b))
    return h @ w_out


def chunked_spatial_gating(uv, ln_g, ln_b, sg_w, sg_b, w_out):
    bsz, seq, _ = uv.shape
    z = jax.nn.gelu(uv, approximate=False)
    u, v = jnp.split(z, 2, axis=-1)
    v = layer_norm(v, ln_g, ln_b)
    v = v.reshape(bsz, seq // SG_CHUNK, SG_CHUNK, SG_GROUPS, SG_GROUP_DIM)
    causal = jnp.tril(jnp.ones((SG_CHUNK, SG_CHUNK), dtype=bool))
    w = jnp.where(causal, sg_w, 0.0).astype(v.dtype)
    f = jnp.einsum("gts,bcsgd->bctgd", w, v) + sg_b.T[None, None, :, :, None]
    return (u * f.reshape(bsz, seq, SG_WIDTH)) @ w_out


def compress_blocks(k, pe, w1, w2):
    bsz, seq, g, dh = k.shape
    r = CMP_BLOCK // CMP_STRIDE
    n_piece = seq // CMP_STRIDE
    pieces = k.reshape(bsz, n_piece, CMP_STRIDE, g, dh)
    blocks = jnp.concatenate([pieces[:, i:n_piece - r + 1 + i] for i in range(r)], axis=2)
    blocks = blocks + pe[None, None, :, None, :]
    hid = jax.nn.silu(jnp.einsum("bnlgd,ldh->bngh", blocks, w1))
    return jnp.einsum("bngh,hd->bngd", hid, w2)


def _cmp_to_slc_matrix(n_cmp, n_slc):
    r = CMP_BLOCK // CMP_STRIDE
    a = SLC_BLOCK // CMP_STRIDE
    piece = np.arange(n_cmp)[:, None] + np.arange(r)[None, :]
    rows = np.broadcast_to(np.arange(n_cmp)[:, None], piece.shape)
    m = np.zeros((n_cmp, n_slc), np.float32)
    np.add.at(m, (rows, piece // a), 1.0)
    return jnp.asarray(m)


def native_sparse_attention(q, kv, gate_logits, pe_k, w1_k, w2_k, pe_v, w1_v, w2_v, w_out):
    bsz, seq, _ = q.shape
    g, hpg, dh = N_KV_GROUPS, HEADS_PER_GROUP, HEAD_DIM
    q = q.reshape(bsz, seq, g, hpg, dh) * (dh ** -0.5)
    k_cmp, v_cmp, k_slc, v_slc, k_win, v_win = [
        t.reshape(bsz, seq, g, dh) for t in jnp.split(kv, 6, axis=-1)]
    gates = jax.nn.sigmoid(gate_logits).reshape(bsz, seq, g, hpg, 3)

    kc = compress_blocks(k_cmp, pe_k, w1_k, w2_k)
    vc = compress_blocks(v_cmp, pe_v, w1_v, w2_v)
    n_cmp = kc.shape[1]
    cmp_end = jnp.arange(n_cmp) * CMP_STRIDE + CMP_BLOCK - 1
    n_slc = seq // SLC_BLOCK
    top_n = min(SLC_TOP_N, n_slc)
    slc_map = _cmp_to_slc_matrix(n_cmp, n_slc)
    ks_blk = k_slc.reshape(bsz, n_slc, SLC_BLOCK, g, dh).transpose(0, 3, 1, 2, 4)
    vs_blk = v_slc.reshape(bsz, n_slc, SLC_BLOCK, g, dh).transpose(0, 3, 1, 2, 4)
    kw_pad = jnp.pad(k_win, ((0, 0), (WINDOW, 0), (0, 0), (0, 0)))
    vw_pad = jnp.pad(v_win, ((0, 0), (WINDOW, 0), (0, 0), (0, 0)))
    b_idx = jnp.arange(bsz)[:, None, None, None]
    g_idx = jnp.arange(g)[None, :, None, None]
    blk_j = jnp.arange(n_slc)
    offs = jnp.arange(SLC_BLOCK)

    def block_step(start):
        t = start + jnp.arange(NSA_Q_BLOCK)
        qb = lax.dynamic_slice_in_dim(q, start, NSA_Q_BLOCK, axis=1)
        s = jnp.einsum("bqghd,bngd->bghqn", qb, kc)
        p_cmp = masked_softmax(s, cmp_end[None, :] <= t[:, None])
        o_cmp = jnp.einsum("bghqn,bngd->bqghd", p_cmp.astype(vc.dtype), vc)
        imp = jnp.einsum("bghqn,nj->bgqj", p_cmp, slc_map)
        cur = (t // SLC_BLOCK)[:, None]
        valid = blk_j[None, :] <= cur
        forced = (blk_j[None, :] == 0) | (blk_j[None, :] == cur) | (blk_j[None, :] == cur - 1)
        score = jnp.where(valid, jnp.where(forced, FORCE, imp), NEG)
        _, idx = lax.top_k(score, top_n)
        kg = ks_blk[b_idx, g_idx, idx].reshape(bsz, g, NSA_Q_BLOCK, top_n * SLC_BLOCK, dh)
        vg = vs_blk[b_idx, g_idx, idx].reshape(bsz, g, NSA_Q_BLOCK, top_n * SLC_BLOCK, dh)
        pos = (idx[..., None] * SLC_BLOCK + offs).reshape(bsz, g, NSA_Q_BLOCK, top_n * SLC_BLOCK)
        s = jnp.einsum("bqghd,bgqkd->bghqk", qb, kg)
        p = masked_softmax(s, (pos <= t[:, None])[:, :, None])
        o_slc = jnp.einsum("bghqk,bgqkd->bqghd", p.astype(vg.dtype), vg)
        kw = lax.dynamic_slice_in_dim(kw_pad, start, WINDOW + NSA_Q_BLOCK, axis=1)
        vw = lax.dynamic_slice_in_dim(vw_pad, start, WINDOW + NSA_Q_BLOCK, axis=1)
        kpos = start - WINDOW + jnp.arange(WINDOW + NSA_Q_BLOCK)
        wmask = ((kpos[None, :] <= t[:, None]) & (kpos[None, :] > t[:, None] - WINDOW)
                 & (kpos[None, :] >= 0))
        s = jnp.einsum("bqghd,bkgd->bghqk", qb, kw)
        p = masked_softmax(s, wmask)
        o_win = jnp.einsum("bghqk,bkgd->bqghd", p.astype(vw.dtype), vw)
        gb = lax.dynamic_slice_in_dim(gates, start, NSA_Q_BLOCK, axis=1)
        return gb[..., 0:1] * o_cmp + gb[..., 1:2] * o_slc + gb[..., 2:3] * o_win

    starts = jnp.arange(seq // NSA_Q_BLOCK, dtype=jnp.int32) * NSA_Q_BLOCK
    o = lax.map(block_step, starts)
    o = o.transpose(1, 0, 2, 3, 4, 5).reshape(bsz, seq, N_HEADS * HEAD_DIM)
    return o @ w_out


def setup_inputs(seed: int = 0) -> dict:
    key = jax.random.key(seed)
    ks = jax.random.split(key, 32)
    L, D = DEPTH, D_MODEL

    def nrm(k, shape, scale):
        return jax.random.normal(k, shape, jnp.float32) * scale

    def gain(k, n):
        return 1.0 + nrm(k, (L, n), 0.05)

    return {
        "x": nrm(ks[0], (BATCH, SEQ, D), 1.0),
        "norm_mix_pre": gain(ks[1], D),
        "norm_mix_post": gain(ks[2], D),
        "norm_ffn_pre": gain(ks[3], D),
        "norm_ffn_post": gain(ks[4], D),
        "w_in": nrm(ks[5], (L, D, N_IN), D ** -0.5),
        "conv_a_w": nrm(ks[6], (L, DW_CONV_SIZE, CONV_WIDTH), DW_CONV_SIZE ** -0.5),
        "conv_a_b": nrm(ks[7], (L, CONV_WIDTH), 0.02),
        "ln_a_g": gain(ks[8], CONV_WIDTH),
        "ln_a_b": nrm(ks[9], (L, CONV_WIDTH), 0.02),
        "w_a_out": nrm(ks[10], (L, CONV_WIDTH, D), CONV_WIDTH ** -0.5),
        "ln_b_g": gain(ks[11], SG_WIDTH),
        "ln_b_b": nrm(ks[12], (L, SG_WIDTH), 0.02),
        "sg_w": nrm(ks[13], (L, SG_GROUPS, SG_CHUNK, SG_CHUNK), SG_CHUNK ** -0.5),
        "sg_b": 1.0 + nrm(ks[14], (L, SG_GROUPS, SG_CHUNK), 0.05),
        "w_b_out": nrm(ks[15], (L, SG_WIDTH, D), SG_WIDTH ** -0.5),
        "cmp_pe_k": nrm(ks[16], (L, CMP_BLOCK, HEAD_DIM), 0.1),
        "cmp_w1_k": nrm(ks[17], (L, CMP_BLOCK, HEAD_DIM, CMP_HIDDEN), (CMP_BLOCK * HEAD_DIM) ** -0.5),
        "cmp_w2_k": nrm(ks[18], (L, CMP_HIDDEN, HEAD_DIM), CMP_HIDDEN ** -0.5),
        "cmp_pe_v": nrm(ks[19], (L, CMP_BLOCK, HEAD_DIM), 0.1),
        "cmp_w1_v": nrm(ks[20], (L, CMP_BLOCK, HEAD_DIM, CMP_HIDDEN), (CMP_BLOCK * HEAD_DIM) ** -0.5),
        "cmp_w2_v": nrm(ks[21], (L, CMP_HIDDEN, HEAD_DIM), CMP_HIDDEN ** -0.5),
        "w_c_out": nrm(ks[22], (L, N_HEADS * HEAD_DIM, D), (N_HEADS * HEAD_DIM) ** -0.5),
        "w_o": nrm(ks[23], (L, D, D), D ** -0.5),
        "w_up": nrm(ks[24], (L, D, 2 * D_FF), D ** -0.5),
        "ffn_conv_w": nrm(ks[25], (L, FFN_CONV_SIZE, 2 * D_FF), FFN_CONV_SIZE ** -0.5),
        "ffn_conv_b": nrm(ks[26], (L, 2 * D_FF), 0.02),
        "w_down": nrm(ks[27], (L, D_FF, D), D_FF ** -0.5),
    }


def reference(x, norm_mix_pre, norm_mix_post, norm_ffn_pre, norm_ffn_post, w_in,
              conv_a_w, conv_a_b, ln_a_g, ln_a_b, w_a_out,
              ln_b_g, ln_b_b, sg_w, sg_b, w_b_out,
              cmp_pe_k, cmp_w1_k, cmp_w2_k, cmp_pe_v, cmp_w1_v, cmp_w2_v, w_c_out,
              w_o, w_up, ffn_conv_w, ffn_conv_b, w_down):
    splits = _split_points()
    for l in range(DEPTH):
        h = rms_norm(x, norm_mix_pre[l])
        proj = h @ w_in[l]
        a_in, b_in, q, kv, nsa_g, merge_g = jnp.split(proj, splits, axis=-1)
        y_a = conformer_conv(a_in, conv_a_w[l], conv_a_b[l], ln_a_g[l], ln_a_b[l], w_a_out[l])
        y_b = chunked_spatial_gating(b_in, ln_b_g[l], ln_b_b[l], sg_w[l], sg_b[l], w_b_out[l])
        y_c = native_sparse_attention(q, kv, nsa_g, cmp_pe_k[l], cmp_w1_k[l], cmp_w2_k[l],
                                      cmp_pe_v[l], cmp_w1_v[l], cmp_w2_v[l], w_c_out[l])
        g_a, g_b, g_c = jnp.split(jax.nn.sigmoid(merge_g), 3, axis=-1)
        mixed = (g_a * y_a + g_b * y_b + g_c * y_c) @ w_o[l]
        x = x + rms_norm(mixed, norm_mix_post[l])
        h = rms_norm(x, norm_ffn_pre[l])
        u = causal_depthwise_conv(h @ w_up[l], ffn_conv_w[l], ffn_conv_b[l])
        u_g, u_v = jnp.split(u, 2, axis=-1)
        x = x + rms_norm((jax.nn.silu(u_g) * u_v) @ w_down[l], norm_ffn_post[l])
    return x
```

```python
import numpy as np
from contextlib import ExitStack
import concourse.bass as bass
import concourse.mybir as mybir
from concourse.bass_utils import run_bass_kernel_spmd

F32 = mybir.dt.float32
BF16 = mybir.dt.bfloat16
AF = mybir.ActivationFunctionType
ALU = mybir.AluOpType
AX = mybir.AxisListType

ENGS = ("pe", "act", "dve", "pool", "sp")

D = 2048
SEQ = 4096
PAD = 128
NCOL = PAD + SEQ
NIN = 12848
DFF = 5632
EPS = 1e-6
C_A, C_B, C_Q, C_KV, C_G, C_M = 0, 2048, 4096, 5120, 6656, 6704


class Reg:
    __slots__ = ("name", "w", "r", "dkey", "dcount")

    def __init__(self, name):
        self.name = name
        self.w = None
        self.r = {}
        self.dkey = None
        self.dcount = 0


class _Rec:
    def __init__(self):
        self.call = None

    def __getattr__(self, name):
        def f(*a, **k):
            self.call = (name, a, k)
            return self
        return f


def _record(fn):
    r = _Rec()
    fn(r)
    assert r.call is not None
    return r.call


class Prog:
    def __init__(self, nc):
        self.nc = nc
        self.streams = {e: [] for e in ENGS}
        self.cnt = {e: 0 for e in ENGS}
        self.seen = {e: {} for e in ENGS}
        self.dsems = {}
        self.ndsem = 0
        self.semh = {}
        self.free_dkeys = []
        self.phase_keys = []

    def sb(self, stack, name, shape, dt):
        return stack.enter_context(self.nc.sbuf_tensor(name, list(shape), dt))

    def _waits(self, eng, deps):
        out = []
        seen = self.seen[eng]
        best = {}
        for (k, v) in deps:
            if best.get(k, -1) < v:
                best[k] = v
        for k, v in best.items():
            if k == eng:
                if eng == "pe":
                    continue
                if v <= self.cnt[eng] - 2:
                    continue
            if seen.get(k, -1) >= v:
                continue
            seen[k] = v
            out.append((k, v))
        return out

    def _deps(self, reads, writes):
        deps = []
        for r in reads:
            if r.w is not None:
                deps.append(r.w)
        for w in writes:
            if w.w is not None:
                deps.append(w.w)
            deps.extend(w.r.items())
        return deps

    def op(self, eng, fn, reads=(), writes=()):
        waits = self._waits(eng, self._deps(reads, writes))
        self.cnt[eng] += 1
        me = (eng, self.cnt[eng])
        self.streams[eng].append((waits, _record(fn), (eng, 1)))
        for r in reads:
            r.r[me[0]] = me[1]
        for w in writes:
            w.w = me
            w.r = {}
        return me

    def dma(self, q, fn, sb, load, dram=()):
        if sb.dkey is None:
            if self.free_dkeys:
                sb.dkey = self.free_dkeys.pop()
                sb.dcount = self.dsems[sb.dkey]
            else:
                sb.dkey = "d%d" % self.ndsem
                self.ndsem += 1
            self.phase_keys.append(sb.dkey)
        if load:
            deps = self._deps(dram, [sb])
        else:
            deps = self._deps([sb], dram)
        waits = self._waits(q, deps)
        sb.dcount += 16
        self.dsems[sb.dkey] = sb.dcount
        me = (sb.dkey, sb.dcount)
        self.streams[q].append((waits, _record(fn), (sb.dkey, 16)))
        if load:
            sb.w = me
            sb.r = {}
            for d in dram:
                d.r[me[0]] = me[1]
        else:
            sb.r[me[0]] = me[1]
            for d in dram:
                d.w = me
                d.r = {}
        return me

    def barrier(self):
        allk = [(e, self.cnt[e]) for e in ENGS if self.cnt[e] > 0]
        allk += list(self.dsems.items())
        for e in ENGS:
            waits = self._waits(e, [kv for kv in allk if kv[0] != e])
            if waits:
                self.streams[e].append((waits, None, None))

    def end_phase(self, persistent=False):
        self.barrier()
        if not persistent:
            self.free_dkeys.extend(self.phase_keys)
        self.phase_keys = []

    def emit(self):
        nc = self.nc
        keys = list(ENGS) + list(self.dsems.keys())
        with ExitStack() as st:
            for k in keys:
                self.semh[k] = st.enter_context(nc.semaphore("s_" + k))
            block = st.enter_context(nc.Block())
            semh = self.semh

            def run(e, stream):
                for waits, fn, inc in stream:
                    for (k, v) in waits:
                        e.wait_ge(semh[k], v)
                    if fn is not None:
                        name, a, k = fn
                        ins = getattr(e, name)(*a, **k)
                        ins.then_inc(semh[inc[0]], inc[1])

            @block.tensor
            def _(e):
                run(e, self.streams["pe"])

            @block.scalar
            def _(e):
                run(e, self.streams["act"])

            @block.vector
            def _(e):
                run(e, self.streams["dve"])

            @block.gpsimd
            def _(e):
                run(e, self.streams["pool"])

            @block.sync
            def _(e):
                run(e, self.streams["sp"])


class Builder:
    def __init__(self, debug=False, nlayers=2, stop=None):
        self.debug = debug
        self.stop = stop
        nc = bass.Bass("TRN2", target_bir_lowering=False)
        self.nc = nc
        self.P = Prog(nc)
        self.I = {}
        self.regs = {}
        self.gst = ExitStack()
        self._uid = 0
        self.declare_io()
        self.alloc_consts()

    def R(self, name):
        if name not in self.regs:
            self.regs[name] = Reg(name)
        return self.regs[name]

    def newreg(self, name):
        self._uid += 1
        return Reg("%s_%d" % (name, self._uid))

    def din(self, name, shape, dt=F32):
        t = self.nc.dram_tensor(name, list(shape), dt, kind="ExternalInput").ap()
        self.I[name] = t
        return t

    def dscr(self, name, shape, dt, out=False):
        kind = "ExternalOutput" if (out or self.debug) else "Internal"
        return self.nc.dram_tensor(name, list(shape), dt, kind=kind).ap()

    def tile(self, st, name, shape, dt):
        self._uid += 1
        t = self.P.sb(st, "%s_%d" % (name, self._uid), shape, dt)
        return t, self.newreg(name)

    def dump(self, name, tile_, reg, shape, dt=F32):
        if not self.debug or True:
            return
        d = self.nc.dram_tensor("dbg_" + name, list(shape), dt, kind="ExternalOutput").ap()
        self.P.dma("sp", lambda e: e.dma_start(out=d, in_=tile_[:]), reg, False)

    def declare_io(self):
        L = 2
        self.xT = self.din("xT", [D, NCOL])
        self.w_in = self.din("w_in", [L, D, NIN])
        self.w_a_out = self.din("w_a_out", [L, 1024, D])
        self.w_b_out = self.din("w_b_out", [L, 1024, D])
        self.w_c_out = self.din("w_c_out", [L, 1024, D])
        self.w_o = self.din("w_o", [L, D, D])
        self.w_up = self.din("w_up", [L, D, 2 * DFF])
        self.w_down = self.din("w_down", [L, DFF, D])
        self.normw = self.din("normw", [128, L * 4 * 16])
        self.convaw = self.din("convaw", [128, L * 8 * 31])
        self.avec = self.din("avec", [128, L * 3 * 8])
        self.lnb = self.din("lnb", [128, L * 2 * 1024])
        self.sgw = self.din("sgw", [L, 128, 8 * 128])
        self.sgb = self.din("sgb", [1, L * 1024])
        self.cw1 = self.din("cw1", [L * 2, 64, 32 * 256])
        self.cw2 = self.din("cw2", [L * 2, 256, 64])
        self.cpe = self.din("cpe", [L * 2, 64, 32])
        self.ffw = self.din("ffw", [128, L * 88 * 3])
        self.ffb = self.din("ffb", [128, L * 88])
        self.c_ident = self.din("c_ident", [128, 128])
        self.c_trile = self.din("c_trile", [128, 128])
        self.c_trigt = self.din("c_trigt", [128, 128])
        self.c_dtab = self.din("c_dtab", [128, 128])
        self.c_E = self.din("c_E", [64, 32 * 128])
        self.c_smap = self.din("c_smap", [128, 2 * 64])
        self.m_tok = self.din("m_tok", [128, NCOL])
        self.m_kval = self.din("m_kval", [128, 32])
        self.m_nval = self.din("m_nval", [128, 2])
        self.m_sel = self.din("m_sel", [3, 128, 32 * 64])
        self.XM = self.dscr("XM", [D, NCOL], F32)
        self.X1 = self.dscr("X1", [D, NCOL], F32)
        self.OUT = self.dscr("OUT", [D, 2048], F32, out=True)
        self.GLU = self.dscr("GLU", [1024, NCOL], BF16)
        self.HA = self.dscr("HA", [1024, NCOL], BF16)
        self.HB = self.dscr("HB", [1024, NCOL], BF16)
        self.OC = self.dscr("OC", [1024, NCOL], BF16)
        self.Q = self.dscr("Q", [1024, NCOL], BF16)
        self.GATES = self.dscr("GATES", [NCOL, 48], F32)
        self.KT = [self.dscr("KT%d" % i, [256, NCOL], BF16) for i in range(4)]
        self.VT = [self.dscr("VT%d" % i, [NCOL, 256], BF16) for i in range(2)]

    def alloc_consts(self):
        P, st = self.P, self.gst
        nc = self.nc
        T = lambda n, s, d: self.tile(st, n, s, d)
        self.ident_f, r_if = T("ident_f", [128, 128], F32)
        self.ident_b, r_ib = T("ident_b", [128, 128], BF16)
        self.ones_b, r_ob = T("ones_b", [128, 128], BF16)
        self.trile, r1 = T("trile", [128, 128], F32)
        self.trigt, r2 = T("trigt", [128, 128], F32)
        self.dtab, r3 = T("dtab", [128, 128], F32)
        self.tokv, r4 = T("tokv", [128, NCOL], BF16)
        self.kval, r5 = T("kval", [128, 32], F32)
        self.nval, r6 = T("nval", [128, 2], F32)
        self.normw_s, r7 = T("normw", [128, 128], F32)
        self.epsT, r8 = T("eps", [128, 1], F32)
        self.zero_f, r9 = T("zero_f", [128, 128], F32)
        self.r_const = self.newreg("const")
        rc = self.r_const
        ld = lambda t, src: P.dma("sp", lambda e: e.dma_start(out=t[:], in_=src), rc, True)
        ld(self.ident_f, self.c_ident)
        ld(self.trile, self.c_trile)
        ld(self.trigt, self.c_trigt)
        ld(self.dtab, self.c_dtab)
        ld(self.kval, self.m_kval)
        ld(self.nval, self.m_nval)
        ld(self.normw_s, self.normw)
        P.dma("pool", lambda e: e.dma_start(out=self.ident_b[:], in_=self.c_ident), rc, True)
        P.dma("pool", lambda e: e.dma_start(out=self.tokv[:], in_=self.m_tok), rc, True)
        P.op("dve", lambda e: e.memset(self.ones_b[:], 1.0), [], [rc])
        P.op("dve", lambda e: e.memset(self.epsT[:], EPS), [], [rc])
        P.op("dve", lambda e: e.memset(self.zero_f[:], 0.0), [], [rc])
        for X in (self.XM, self.X1):
            Xv = X.rearrange("(c p) t -> p c t", p=128)
            for c in range(16):
                P.dma("sp", lambda e, c=c, Xv=Xv: e.dma_start(out=Xv[:, c, 0:128], in_=self.zero_f[:]), rc, False)
        self.ps = []
        self.psr = []
        for i in range(8):
            t = st.enter_context(nc.psum_tensor("psb%d" % i, [128, 512], F32))
            self.ps.append(t)
            self.psr.append(self.newreg("ps%d" % i))
        P.end_phase(persistent=True)

    def load_norm(self, st_tiles, X, col0, n, nw_off, dst_fn, dst_reg, bank=7):
        P = self.P
        xt, r_xt, sq, r_sq, rs, r_rs = st_tiles
        Xv = X.rearrange("(c p) t -> p c t", p=128)
        P.dma("sp", lambda e: e.dma_start(out=xt[:, :, 0:n], in_=Xv[:, :, col0:col0 + n]), r_xt, True)
        ps, pr = self.ps[bank], self.psr[bank]
        for c in range(16):
            P.op("act", lambda e, c=c: e.activation(out=sq[c % 2][:, 0:n], in_=xt[:, c, 0:n], func=AF.Square), [r_xt], [r_sq[c % 2]])
            P.op("pe", lambda e, c=c: e.matmul(ps[:, 0:n], lhsT=self.ones_b[:], rhs=sq[c % 2][:, 0:n], start=(c == 0), stop=(c == 15)), [r_sq[c % 2]], [pr])
        P.op("act", lambda e: e.activation(out=rs[:, 0:n], in_=ps[:, 0:n], func=AF.Sqrt, bias=self.epsT[:, 0:1], scale=1.0 / D), [pr], [r_rs])
        P.op("dve", lambda e: e.reciprocal(out=rs[:, 0:n], in_=rs[:, 0:n]), [r_rs], [r_rs])
        P.op("dve", lambda e: e.tensor_tensor(out=rs[:, 0:n], in0=rs[:, 0:n], in1=self.tokv[:, col0:col0 + n], op=ALU.mult), [r_rs], [r_rs])
        for c in range(16):
            P.op("dve", lambda e, c=c: e.scalar_tensor_tensor(out=dst_fn(c), in0=xt[:, c, 0:n], scalar=self.normw_s[:, nw_off + c:nw_off + c + 1],
                                                              in1=rs[:, 0:n], op0=ALU.mult, op1=ALU.mult), [r_xt, r_rs], [dst_reg])

    def norm_tiles(self, st):
        xt, r_xt = self.tile(st, "xt", [128, 16, 256], F32)
        sq0, r0 = self.tile(st, "sq0", [128, 256], BF16)
        sq1, r1 = self.tile(st, "sq1", [128, 256], BF16)
        rs, r_rs = self.tile(st, "rs", [128, 256], F32)
        return (xt, r_xt, [sq0, sq1], [r0, r1], rs, r_rs)

    def fill_hT(self, st_tiles, X, col0, ntok, nw_off, hT, hregs):
        for i, t0 in enumerate(range(0, ntok, 256)):
            n = min(256, ntok - t0)
            self.load_norm(st_tiles, X, col0 + t0, n, nw_off, lambda c, t0=t0, n=n: hT[:, c, t0:t0 + n], hregs[i])

    def wload(self, wbuf, wreg, Wsrc, kc, col0, ncols, off=0, k0=0):
        view = wbuf[:, off:off + kc * ncols].rearrange("p (k n) -> p k n", k=kc)
        src = Wsrc.rearrange("(k p) n -> p k n", p=128)[:, k0:k0 + kc, col0:col0 + ncols]
        self.P.dma("pool", lambda e: e.dma_start(out=view, in_=src), wreg, True)
        return view

    def run_jobs(self, jobs):
        loads = [i for i, (lf, _) in enumerate(jobs) if lf is not None]
        if loads:
            jobs[loads[0]][0](0)
        li = 0
        for i, (lf, cf) in enumerate(jobs):
            if lf is None:
                cf(None)
                continue
            if li + 1 < len(loads):
                jobs[loads[li + 1]][0]((li + 1) % 2)
            cf(li % 2)
            li += 1

    def post_norm(self, st, mixed, r_mixed, ss_bank, n, nw_off, Xin, Xout, cin0, cout0, tl):
        P = self.P
        rs2, r_rs2, xr, r_xr, ot, r_ot, tt, r_tt = tl
        ps, pr = self.ps[ss_bank], self.psr[ss_bank]
        P.op("act", lambda e: e.activation(out=rs2[:, 0:n], in_=ps[:, 0:n], func=AF.Sqrt, bias=self.epsT[:, 0:1], scale=1.0 / D), [pr], [r_rs2])
        P.op("dve", lambda e: e.reciprocal(out=rs2[:, 0:n], in_=rs2[:, 0:n]), [r_rs2], [r_rs2])
        Xi = Xin.rearrange("(c p) t -> p c t", p=128)
        Xo = Xout.rearrange("(c p) t -> p c t", p=128)
        for c in range(16):
            b = c % 2
            P.dma("sp", lambda e, c=c, b=b: e.dma_start(out=xr[b][:, 0:n], in_=Xi[:, c, cin0:cin0 + n]), r_xr[b], True)
            P.op("dve", lambda e, c=c, b=b: e.tensor_tensor(out=tt[b][:, 0:n], in0=mixed[:, c, 0:n], in1=rs2[:, 0:n], op=ALU.mult), [r_mixed, r_rs2], [r_tt[b]])
            P.op("dve", lambda e, c=c, b=b: e.scalar_tensor_tensor(out=ot[b][:, 0:n], in0=tt[b][:, 0:n], scalar=self.normw_s[:, nw_off + c:nw_off + c + 1],
                                                                   in1=xr[b][:, 0:n], op0=ALU.mult, op1=ALU.add), [r_tt[b], r_xr[b]], [r_ot[b]])
            P.dma("sp", lambda e, c=c, b=b: e.dma_start(out=Xo[:, c, cout0:cout0 + n], in_=ot[b][:, 0:n]), r_ot[b], False)

    def post_tiles(self, st):
        rs2, r_rs2 = self.tile(st, "rs2", [128, 512], F32)
        xr = []; r_xr = []; ot = []; r_ot = []; tt = []; r_tt = []
        for b in range(2):
            a, ra = self.tile(st, "xr", [128, 512], F32); xr.append(a); r_xr.append(ra)
            a, ra = self.tile(st, "ot", [128, 512], F32); ot.append(a); r_ot.append(ra)
            a, ra = self.tile(st, "tt", [128, 512], F32); tt.append(a); r_tt.append(ra)
        return (rs2, r_rs2, xr, r_xr, ot, r_ot, tt, r_tt)

    def tok_tiles(self, start, ntok, step=512):
        return [(t0, min(step, start + ntok - t0)) for t0 in range(start, start + ntok, step)]

    def phase_kv(self, l, X):
        P = self.P
        with ExitStack() as st:
            nt = self.norm_tiles(st)
            wkv, r_wkv = self.tile(st, "wkv", [128, 16 * 1536], BF16)
            wv = wkv[:].rearrange("p (k n) -> p k n", k=16)
            Wl = self.w_in[l]
            for j in range(3):
                src = Wl.rearrange("(k p) n -> p k n", p=128)[:, :, C_KV + j * 512:C_KV + (j + 1) * 512]
                P.dma("pool", lambda e, j=j, src=src: e.dma_start(out=wv[:, :, j * 512:(j + 1) * 512], in_=src), r_wkv, True)
            hts = [self.tile(st, "hTt", [128, 16, 256], BF16) for _ in range(2)]
            ksts = [self.tile(st, "kst", [128, 8, 256], BF16) for _ in range(2)]
            vsts = [self.tile(st, "vst", [128, 2, 512], BF16) for _ in range(2)]
            fm_off = [0, 256, 512, 1024]
            tm_off = [768, 1280]
            for ti, t0 in enumerate(range(0, SEQ, 256)):
                hT, r_h = hts[ti % 2]
                kst, r_k = ksts[ti % 2]
                vst, r_v = vsts[ti % 2]
                col0 = PAD + t0
                self.load_norm(nt, X, col0, 256, (l * 4 + 0) * 16, lambda c, hT=hT: hT[:, c, :], r_h)
                for ty in range(4):
                    bk = ty % 4
                    ps, pr = self.ps[bk], self.psr[bk]
                    for half in range(2):
                        off = fm_off[ty] + half * 128
                        for kc in range(16):
                            P.op("pe", lambda e, ps=ps, half=half, off=off, kc=kc, hT=hT: e.matmul(ps[:, half * 256:(half + 1) * 256], lhsT=wv[:, kc, off:off + 128],
                                                                                                  rhs=hT[:, kc, :], start=(kc == 0), stop=(kc == 15)), [r_wkv, r_h], [pr])
                    P.op("act", lambda e, ps=ps, ty=ty, kst=kst: e.activation(out=kst[:, 2 * ty:2 * ty + 2, :], in_=ps[:].rearrange("p (h t) -> p h t", h=2), func=AF.Copy), [pr], [r_k])
                for ty in range(4):
                    dst = self.KT[ty].rearrange("(h p) t -> p h t", p=128)[:, :, col0:col0 + 256]
                    P.dma("sp", lambda e, ty=ty, dst=dst, kst=kst: e.dma_start(out=dst, in_=kst[:, 2 * ty:2 * ty + 2, :]), r_k, False, [self.R("KT")])
                for sub in range(2):
                    ps, pr = self.ps[4 + sub], self.psr[4 + sub]
                    for j in range(2):
                        for kc in range(16):
                            P.op("pe", lambda e, ps=ps, sub=sub, j=j, kc=kc, hT=hT: e.matmul(ps[:, j * 256:(j + 1) * 256], lhsT=hT[:, kc, sub * 128:(sub + 1) * 128],
                                                                                             rhs=wv[:, kc, tm_off[j]:tm_off[j] + 256], start=(kc == 0), stop=(kc == 15)), [r_wkv, r_h], [pr])
                    P.op("dve", lambda e, ps=ps, sub=sub, vst=vst: e.tensor_copy(out=vst[:, sub, :], in_=ps[:]), [pr], [r_v])
                for j in range(2):
                    dst = self.VT[j][col0:col0 + 256, :].rearrange("(s p) c -> p s c", p=128)
                    P.dma("sp", lambda e, j=j, dst=dst, vst=vst: e.dma_start(out=dst, in_=vst[:, :, j * 256:(j + 1) * 256]), r_v, False, [self.R("VT")])
            P.end_phase()

    def phase_m1(self, l, X, r0, r1):
        P = self.P
        T = r1 - r0
        TA = T + 128
        Wl = self.w_in[l]
        with ExitStack() as st:
            nt = self.norm_tiles(st)
            hT, _ = self.tile(st, "hT", [128, 16, TA], BF16)
            hregs = [self.newreg("hT") for _ in range((TA + 255) // 256)]
            wb = [self.tile(st, "wbuf", [128, 8192], BF16) for _ in range(2)]
            wvt, r_wvt = self.tile(st, "wvt", [128, 16 * 1024], BF16)
            ut, r_ut = self.tile(st, "ut", [128, 8, 128], BF16)
            sig = [self.tile(st, "sig", [128, 512], F32) for _ in range(2)]
            gst = [self.tile(st, "gst", [128, 512], BF16) for _ in range(2)]
            qst = [self.tile(st, "qst", [128, 512], BF16) for _ in range(2)]
            vg = [self.tile(st, "vg", [128, 1024], F32)] * 2
            vln = [self.tile(st, "vln", [128, 1024], BF16)] * 2
            bst, r_bst = self.tile(st, "bst", [128, 2, 6], F32)
            mv, r_mv = self.tile(st, "mv", [128, 2], F32)
            wsT, r_ws = self.tile(st, "wsT", [128, 8, 128], BF16)
            wsF, r_wsF = self.tile(st, "wsF", [128, 8, 128], BF16)
            lnbt, r_lnb = self.tile(st, "lnbt", [128, 2, 1024], F32)
            sgbr, r_sgb = self.tile(st, "sgbr", [1, 1024], BF16)
            gall, r_gall = self.tile(st, "gall", [128, T // 128, 48], F32)
            P.dma("sp", lambda e: e.dma_start(out=lnbt[:], in_=self.lnb[:, l * 2048:(l + 1) * 2048].rearrange("p (a c) -> p a c", a=2)), r_lnb, True)
            P.dma("pool", lambda e: e.dma_start(out=sgbr[:], in_=self.sgb[:, l * 1024:(l + 1) * 1024]), r_sgb, True)
            P.dma("pool", lambda e: e.dma_start(out=wsF[:], in_=self.sgw[l].rearrange("s (g t) -> s g t", g=8)), r_wsF, True)
            P.op("dve", lambda e: e.tensor_tensor(out=wsT[:], in0=wsF[:], in1=self.trile[:].unsqueeze(1).to_broadcast([128, 8, 128]), op=ALU.mult), [r_wsF], [r_ws])
            for j in range(2):
                src = Wl.rearrange("(k p) n -> p k n", p=128)[:, :, C_B + 1024 + j * 512:C_B + 1024 + (j + 1) * 512]
                P.dma("pool", lambda e, j=j, src=src: e.dma_start(out=wvt[:].rearrange("p (k n) -> p k n", k=16)[:, :, j * 512:(j + 1) * 512], in_=src), r_wvt, True)
            wv3 = wvt[:].rearrange("p (k n) -> p k n", k=16)
            self.fill_hT(nt, X, PAD + r0 - 128, TA, (l * 4 + 0) * 16, hT, hregs)
            tilesA = self.tok_tiles(0, TA)
            tilesR = self.tok_tiles(128, T)
            jobs = []
            psrot = [0]

            def nextps():
                b = psrot[0] % 4
                psrot[0] += 1
                return self.ps[b], self.psr[b]

            def mm16(ps, n, wview, c0, t0):
                for kc in range(16):
                    P.op("pe", lambda e, kc=kc: e.matmul(ps[:, 0:n], lhsT=wview[:, kc, c0:c0 + 128], rhs=hT[:, kc, t0:t0 + n], start=(kc == 0), stop=(kc == 15)),
                         [wview_reg[0]] + hregs, [ps_reg[0]])

            wview_reg = [None]
            ps_reg = [None]
            for jg in range(4):
                def lf(b, jg=jg):
                    self.wload(wb[b][0], wb[b][1], Wl, 16, C_A + jg * 256, 256, 0)
                    self.wload(wb[b][0], wb[b][1], Wl, 16, C_A + 1024 + jg * 256, 256, 16 * 256)

                def cf(b, jg=jg):
                    wa = wb[b][0][:, 0:4096].rearrange("p (k n) -> p k n", k=16)
                    wg = wb[b][0][:, 4096:8192].rearrange("p (k n) -> p k n", k=16)
                    wview_reg[0] = wb[b][1]
                    k = 0
                    for cc in range(2):
                        ch = jg * 2 + cc
                        for (t0, n) in tilesA:
                            pa, ra = nextps()
                            ps_reg[0] = ra
                            mm16(pa, n, wa, cc * 128, t0)
                            pg, rg = nextps()
                            ps_reg[0] = rg
                            mm16(pg, n, wg, cc * 128, t0)
                            s_, rs_ = sig[k % 2]
                            g_, rg_ = gst[k % 2]
                            k += 1
                            P.op("act", lambda e, pg=pg, s_=s_, n=n: e.activation(out=s_[:, 0:n], in_=pg[:, 0:n], func=AF.Sigmoid), [rg], [rs_])
                            P.op("dve", lambda e, pa=pa, s_=s_, g_=g_, n=n: e.tensor_tensor(out=g_[:, 0:n], in0=pa[:, 0:n], in1=s_[:, 0:n], op=ALU.mult), [ra, rs_], [rg_])
                            c0 = PAD + r0 - 128 + t0
                            P.dma("sp", lambda e, g_=g_, ch=ch, c0=c0, n=n: e.dma_start(out=self.GLU[ch * 128:(ch + 1) * 128, c0:c0 + n], in_=g_[:, 0:n]), rg_, False, [self.R("GLU")])
                jobs.append((lf, cf))
            ku = [0]
            for jg in range(2):
                def lf(b, jg=jg):
                    self.wload(wb[b][0], wb[b][1], Wl, 16, C_B + jg * 512, 512, 0)

                def cf(b, jg=jg):
                    wu = wb[b][0][:].rearrange("p (k n) -> p k n", k=16)
                    wview_reg[0] = wb[b][1]
                    for cc in range(4):
                        ch = jg * 4 + cc
                        for (t0, n) in tilesR:
                            pu, ru = nextps()
                            ps_reg[0] = ru
                            mm16(pu, n, wu, cc * 128, t0)
                            q_, rq_ = qst[ku[0] % 2]
                            ku[0] += 1
                            P.op("act", lambda e, pu=pu, q_=q_, n=n: e.activation(out=q_[:, 0:n], in_=pu[:, 0:n], func=AF.Gelu), [ru], [rq_])
                            c0 = PAD + r0 - 128 + t0
                            P.dma("sp", lambda e, q_=q_, ch=ch, c0=c0, n=n: e.dma_start(out=self.HB[ch * 128:(ch + 1) * 128, c0:c0 + n], in_=q_[:, 0:n]), rq_, False)
                jobs.append((lf, cf))
            for jg in range(2):
                def lf(b, jg=jg):
                    self.wload(wb[b][0], wb[b][1], Wl, 16, C_Q + jg * 512, 512, 0)

                def cf(b, jg=jg):
                    wq = wb[b][0][:].rearrange("p (k n) -> p k n", k=16)
                    wview_reg[0] = wb[b][1]
                    k = 0
                    for cc in range(4):
                        ch = jg * 4 + cc
                        for (t0, n) in tilesR:
                            pq, rq = nextps()
                            ps_reg[0] = rq
                            mm16(pq, n, wq, cc * 128, t0)
                            q_, rq_ = qst[k % 2]
                            k += 1
                            P.op("act", lambda e, pq=pq, q_=q_, n=n: e.activation(out=q_[:, 0:n], in_=pq[:, 0:n], func=AF.Copy, scale=0.125), [rq], [rq_])
                            c0 = PAD + r0 - 128 + t0
                            P.dma("sp", lambda e, q_=q_, ch=ch, c0=c0, n=n: e.dma_start(out=self.Q[ch * 128:(ch + 1) * 128, c0:c0 + n], in_=q_[:, 0:n]), rq_, False, [self.R("Q")])
                jobs.append((lf, cf))
            def lfg(b):
                self.wload(wb[b][0], wb[b][1], Wl, 16, C_G, 48, 0)

            def cfg(b):
                wg = wb[b][0][:, 0:16 * 48].rearrange("p (k n) -> p k n", k=16)
                for i in range(T // 128):
                    pg, rg = nextps()
                    t0 = 128 + i * 128
                    for kc in range(16):
                        P.op("pe", lambda e, kc=kc, pg=pg, t0=t0: e.matmul(pg[:, 0:48], lhsT=hT[:, kc, t0:t0 + 128], rhs=wg[:, kc, :], start=(kc == 0), stop=(kc == 15)),
                             [wb[b][1]] + hregs, [rg])
                    P.op("act", lambda e, pg=pg, i=i: e.activation(out=gall[:, i, :], in_=pg[:, 0:48], func=AF.Sigmoid), [rg], [r_gall])
                dst = self.GATES[PAD + r0:PAD + r1, :].rearrange("(n p) c -> p n c", p=128)
                P.dma("sp", lambda e: e.dma_start(out=dst, in_=gall[:]), r_gall, False, [self.R("GATES")])
            jobs.append((lfg, cfg))
            self.run_jobs(jobs)
            P.barrier()
            HBv = self.HB.rearrange("(c p) t -> p c t", p=128)
            for i in range(T // 128):
                t0 = 128 + i * 128
                cu = PAD + r0 + i * 128
                P.dma("sp", lambda e, cu=cu: e.dma_start(out=ut[:], in_=HBv[:, :, cu:cu + 128]), r_ut, True)
                vg_, rvg = vg[i % 2]
                vl_, rvl = vln[i % 2]
                for half in range(2):
                    pv, rv = self.ps[4 + half], self.psr[4 + half]
                    for kc in range(16):
                        P.op("pe", lambda e, kc=kc, pv=pv, half=half, t0=t0: e.matmul(pv[:], lhsT=hT[:, kc, t0:t0 + 128], rhs=wv3[:, kc, half * 512:(half + 1) * 512],
                                                                                     start=(kc == 0), stop=(kc == 15)), [r_wvt] + hregs, [rv])
                    P.op("act", lambda e, pv=pv, half=half, vg_=vg_: e.activation(out=vg_[:, half * 512:(half + 1) * 512], in_=pv[:], func=AF.Gelu), [rv], [rvg])
                for half in range(2):
                    P.op("dve", lambda e, half=half, vg_=vg_: e.bn_stats(out=bst[:, half, :], in_=vg_[:, half * 512:(half + 1) * 512]), [rvg], [r_bst])
                P.op("dve", lambda e: e.bn_aggr(out=mv[:], in_=bst[:].rearrange("p a s -> p (a s)")), [r_bst], [r_mv])
                P.op("act", lambda e: e.activation(out=mv[:, 1:2], in_=mv[:, 1:2], func=AF.Sqrt, bias=self.epsT[:, 0:1], scale=1.0), [r_mv], [r_mv])
                P.op("dve", lambda e: e.reciprocal(out=mv[:, 1:2], in_=mv[:, 1:2]), [r_mv], [r_mv])
                P.op("dve", lambda e, vg_=vg_: e.tensor_scalar(out=vg_[:], in0=vg_[:], scalar1=mv[:, 0:1], scalar2=mv[:, 1:2], op0=ALU.subtract, op1=ALU.mult), [rvg, r_mv], [rvg])
                P.op("dve", lambda e, vg_=vg_: e.tensor_tensor(out=vg_[:], in0=vg_[:], in1=lnbt[:, 0, :], op=ALU.mult), [rvg, r_lnb], [rvg])
                P.op("dve", lambda e, vg_=vg_, vl_=vl_: e.tensor_tensor(out=vl_[:], in0=vg_[:], in1=lnbt[:, 1, :], op=ALU.add), [rvg, r_lnb], [rvl])
                for hb in range(2):
                    pf, rf = self.ps[6 + hb], self.psr[6 + hb]
                    for gg in range(4):
                        g = hb * 4 + gg
                        P.op("pe", lambda e, pf=pf, gg=gg, g=g, vl_=vl_: e.matmul(pf[:, gg * 128:(gg + 1) * 128], lhsT=vl_[:, g * 128:(g + 1) * 128], rhs=wsT[:, g, :], start=True, stop=False),
                             [rvl, r_ws], [rf])
                        P.op("pe", lambda e, pf=pf, gg=gg, g=g: e.matmul(pf[:, gg * 128:(gg + 1) * 128], lhsT=self.ones_b[0:1, :], rhs=sgbr[0:1, g * 128:(g + 1) * 128], start=False, stop=True),
                             [r_sgb], [rf])
                    P.op("dve", lambda e, pf=pf, hb=hb: e.tensor_tensor(out=ut[:, hb * 4:(hb + 1) * 4, :], in0=ut[:, hb * 4:(hb + 1) * 4, :],
                                                                      in1=pf[:].rearrange("p (g t) -> p g t", g=4), op=ALU.mult), [rf, r_ut], [r_ut])
                P.dma("sp", lambda e, cu=cu: e.dma_start(out=HBv[:, :, cu:cu + 128], in_=ut[:]), r_ut, False)
            P.end_phase()

    def phase_m2(self, l, r0, r1):
        P = self.P
        T = r1 - r0
        TA = T + 128
        with ExitStack() as st:
            gl = [self.tile(st, "gl", [128, TA], BF16) for _ in range(3)]
            acc = [self.tile(st, "acc", [128, T], F32) for _ in range(3)]
            ctmp = [self.tile(st, "ctmp", [128, T], F32) for _ in range(2)]
            cbf, r_cbf = self.tile(st, "cbf", [128, 8, T], BF16)
            cw, r_cw = self.tile(st, "cw", [128, 8, 31], F32)
            av, r_av = self.tile(st, "av", [128, 3, 8], F32)
            sq = [self.tile(st, "sq", [128, 512], BF16) for _ in range(2)]
            mean, r_mean = self.tile(st, "mean", [128, 512], F32)
            msq, r_msq = self.tile(st, "msq", [128, 512], F32)
            rstd, r_rstd = self.tile(st, "rstd", [128, 512], F32)
            t1 = [self.tile(st, "t1", [128, 512], F32) for _ in range(2)]
            hst = [self.tile(st, "hst", [128, 8, 512], BF16) for _ in range(2)]
            P.dma("sp", lambda e: e.dma_start(out=cw[:], in_=self.convaw[:, l * 248:(l + 1) * 248].rearrange("p (c k) -> p c k", c=8)), r_cw, True)
            P.dma("sp", lambda e: e.dma_start(out=av[:], in_=self.avec[:, l * 24:(l + 1) * 24].rearrange("p (a c) -> p a c", a=3)), r_av, True)
            for c in range(8):
                g_, rg = gl[c % 3]
                a_, ra = acc[c % 3]
                c0 = PAD + r0 - 128
                P.dma("sp", lambda e, g_=g_, c=c, c0=c0: e.dma_start(out=g_[:], in_=self.GLU[c * 128:(c + 1) * 128, c0:c0 + TA]), rg, True, [self.R("GLU")])
                if c in (2, 5, 7):
                    for k in range(31):
                        t_, rt_ = ctmp[k % 2]
                        P.op("act", lambda e: e.activation(out=t_[:], in_=g_[:, 98 + k:98 + k + T], func=AF.Copy, scale=cw[:, c, k:k + 1]), [rg, r_cw], [rt_])
                        if k == 0:
                            P.op("pool", lambda e: e.tensor_scalar(out=a_[:], in0=t_[:], scalar1=av[:, 0, c:c + 1], scalar2=None, op0=ALU.add), [rt_, r_av], [ra])
                        else:
                            P.op("pool", lambda e: e.tensor_tensor(out=a_[:], in0=a_[:], in1=t_[:], op=ALU.add), [rt_, ra], [ra])
                else:
                    P.op("dve", lambda e: e.tensor_scalar(out=a_[:], in0=g_[:, 98:98 + T], scalar1=cw[:, c, 0:1], scalar2=av[:, 0, c:c + 1], op0=ALU.mult, op1=ALU.add),
                         [rg, r_cw, r_av], [ra])
                    for k in range(1, 31):
                        P.op("dve", lambda e: e.scalar_tensor_tensor(out=a_[:], in0=g_[:, 98 + k:98 + k + T], scalar=cw[:, c, k:k + 1], in1=a_[:], op0=ALU.mult, op1=ALU.add),
                             [rg, r_cw, ra], [ra])
                P.op("act", lambda e, a_=a_, c=c: e.activation(out=cbf[:, c, :], in_=a_[:], func=AF.Copy), [ra], [r_cbf])
            for ti, (t0, n) in enumerate(self.tok_tiles(0, T)):
                pS, rS = self.ps[0], self.psr[0]
                pQ, rQ = self.ps[1], self.psr[1]
                for c in range(8):
                    s_, rs_ = sq[c % 2]
                    P.op("act", lambda e, s_=s_, c=c, t0=t0, n=n: e.activation(out=s_[:, 0:n], in_=cbf[:, c, t0:t0 + n], func=AF.Square), [r_cbf], [rs_])
                    P.op("pe", lambda e, c=c, t0=t0, n=n: e.matmul(pS[:, 0:n], lhsT=self.ones_b[:], rhs=cbf[:, c, t0:t0 + n], start=(c == 0), stop=(c == 7)), [r_cbf], [rS])
                    P.op("pe", lambda e, s_=s_, c=c, n=n: e.matmul(pQ[:, 0:n], lhsT=self.ones_b[:], rhs=s_[:, 0:n], start=(c == 0), stop=(c == 7)), [rs_], [rQ])
                P.op("dve", lambda e, n=n: e.tensor_scalar(out=mean[:, 0:n], in0=pS[:, 0:n], scalar1=1.0 / 1024, scalar2=None, op0=ALU.mult), [rS], [r_mean])
                P.op("dve", lambda e, n=n: e.tensor_tensor(out=msq[:, 0:n], in0=mean[:, 0:n], in1=mean[:, 0:n], op=ALU.mult), [r_mean], [r_msq])
                P.op("dve", lambda e, n=n: e.scalar_tensor_tensor(out=rstd[:, 0:n], in0=pQ[:, 0:n], scalar=1.0 / 1024, in1=msq[:, 0:n], op0=ALU.mult, op1=ALU.subtract), [rQ, r_msq], [r_rstd])
                P.op("act", lambda e, n=n: e.activation(out=rstd[:, 0:n], in_=rstd[:, 0:n], func=AF.Sqrt, bias=self.epsT[:, 0:1], scale=1.0), [r_rstd], [r_rstd])
                P.op("dve", lambda e, n=n: e.reciprocal(out=rstd[:, 0:n], in_=rstd[:, 0:n]), [r_rstd], [r_rstd])
                h_, rh = hst[ti % 2]
                for c in range(8):
                    t_, rt = t1[c % 2]
                    P.op("dve", lambda e, t_=t_, c=c, t0=t0, n=n: e.tensor_tensor(out=t_[:, 0:n], in0=cbf[:, c, t0:t0 + n], in1=mean[:, 0:n], op=ALU.subtract), [r_cbf, r_mean], [rt])
                    P.op("dve", lambda e, t_=t_, n=n: e.tensor_tensor(out=t_[:, 0:n], in0=t_[:, 0:n], in1=rstd[:, 0:n], op=ALU.mult), [rt, r_rstd], [rt])
                    P.op("act", lambda e, t_=t_, h_=h_, c=c, n=n: e.activation(out=h_[:, c, 0:n], in_=t_[:, 0:n], func=AF.Silu, bias=av[:, 2, c:c + 1], scale=av[:, 1, c:c + 1]), [rt, r_av], [rh])
                dst = self.HA.rearrange("(c p) t -> p c t", p=128)[:, :, PAD + r0 + t0:PAD + r0 + t0 + n]
                P.dma("sp", lambda e, dst=dst, h_=h_, n=n: e.dma_start(out=dst, in_=h_[:, :, 0:n]), rh, False, [self.R("HA")])
            P.end_phase()

    def phase_att(self, l, r0, r1):
        P = self.P
        T = r1 - r0
        nqt = T // 128
        qt0 = r0 // 128
        BIG = 30000.0
        with ExitStack() as st:
            kin = [self.tile(st, "kin", [128, SEQ], BF16) for _ in range(4)]
            vs, r_vs = self.tile(st, "vs", [128, 32, 65], BF16)
            vw, r_vw = self.tile(st, "vw", [128, 32, 65], BF16)
            qT, r_q = self.tile(st, "qT", [128, 4, T], BF16)
            gt, r_gt = self.tile(st, "gt", [128, nqt, 48], F32)
            mtab, r_mt = self.tile(st, "mtab", [128, 3, nqt, 64], F32)
            Et, r_E = self.tile(st, "Et", [128, 32, 128], BF16)
            smap, r_sm = self.tile(st, "smap", [128, 2, 64], BF16)
            w1 = [self.tile(st, "w1", [64, 32, 256], BF16) for _ in range(2)]
            w2 = [self.tile(st, "w2", [128, 2, 64], BF16) for _ in range(2)]
            pe = [self.tile(st, "pe", [64, 32], BF16) for _ in range(2)]
            cb = [self.tile(st, "cb", [128, 2], F32) for _ in range(2)]
            hid, r_hid = self.tile(st, "hid", [128, 2, 256], BF16)
            kcT, r_kc = self.tile(st, "kcT", [128, 256], BF16)
            rv, r_rv = self.tile(st, "rv", [128, 2, 64], BF16)
            pb = [self.tile(st, "pb", [128, 4, 128], BF16) for _ in range(4)]
            cm, r_cm = self.tile(st, "cm", [128, 128], F32)
            cmb = [self.tile(st, "cmb", [128, 4, 128], BF16) for _ in range(4)]
            trib = [self.tile(st, "trib", [128, 4, 128], BF16) for _ in range(2)]
            selb = [self.tile(st, "selb", [128, 4, 128], BF16) for _ in range(2)]
            negb, r_negb = self.tile(st, "negb", [128, 1], F32)
            sm = {}
            for nm, shp in (("den", [128, 4]), ("cg", [128, 4]), ("imp", [128, 64]), ("score", [128, 64]), ("wk", [128, 64]), ("sel", [128, 64]), ("m8", [128, 8]),
                            ("oacc", [128, 4, 64]), ("otmp", [128, 4, 64]), ("den2", [128, 4]), ("cg2", [128, 4])):
                sm[nm] = self.tile(st, nm, shp, F32)
            obf = [self.tile(st, "obf", [128, 256], BF16) for _ in range(2)]
            ost = [self.tile(st, "ost", [128, 2, 128], BF16) for _ in range(2)]
            P.dma("sp", lambda e: e.dma_start(out=gt[:], in_=self.GATES[PAD + r0:PAD + r1, :].rearrange("(n p) c -> p n c", p=128)), r_gt, True)
            for a_ in range(3):
                P.dma("sp", lambda e: e.dma_start(out=mtab[:, a_, :, :], in_=self.m_sel[a_].rearrange("p (q j) -> p q j", j=64)[:, qt0:qt0 + nqt, :]), r_mt, True)
            P.op("dve", lambda e: e.memset(Et[64:128], 0.0), [], [r_E])
            P.op("dve", lambda e: e.memset(qT[64:128], 0.0), [], [r_q])
            for ty in (2, 3):
                P.op("dve", lambda e: e.memset(kin[ty][0][64:128], 0.0), [], [kin[ty][1]])
            for j_ in range(2):
                P.op("dve", lambda e: e.memset(selb[j_][0][64:128], 0.0), [], [selb[j_][1]])
            P.dma("pool", lambda e: e.dma_start(out=Et[0:64], in_=self.c_E.rearrange("j (k c) -> j k c", c=128)), r_E, True)
            P.dma("pool", lambda e: e.dma_start(out=smap[:], in_=self.c_smap.rearrange("p (a j) -> p a j", a=2)), r_sm, True)
            for kv in range(2):
                i_ = l * 2 + kv
                P.dma("pool", lambda e: e.dma_start(out=w1[kv][0][:], in_=self.cw1[i_].rearrange("d (l h) -> d l h", l=32)), w1[kv][1], True)
                P.dma("pool", lambda e: e.dma_start(out=w2[kv][0][:], in_=self.cw2[i_].rearrange("(c p) d -> p c d", p=128)), w2[kv][1], True)
                P.dma("pool", lambda e: e.dma_start(out=pe[kv][0][:], in_=self.cpe[i_]), pe[kv][1], True)
            P.op("dve", lambda e: e.memset(vs[:, :, 64:65], 1.0), [], [r_vs])
            P.op("dve", lambda e: e.memset(vw[:, :, 64:65], 1.0), [], [r_vw])
            P.op("dve", lambda e: e.memset(hid[:], 0.0), [], [r_hid])
            P.op("dve", lambda e: e.memset(kcT[:], 0.0), [], [r_kc])
            P.op("dve", lambda e: e.memset(negb[:], -BIG), [], [r_negb])
            for ti_, tri in enumerate((self.trile, self.trigt)):
                P.op("dve", lambda e: e.tensor_scalar(out=trib[ti_][0][:], in0=tri[:].unsqueeze(1).to_broadcast([128, 4, 128]), scalar1=-1.0, scalar2=BIG, op0=ALU.add, op1=ALU.mult), [], [trib[ti_][1]])
            for kv in range(2):
                ps, pr = self.ps[0], self.psr[0]
                for hc in range(2):
                    for li in range(32):
                        P.op("pe", lambda e: e.matmul(ps[:, hc:hc + 1], lhsT=w1[kv][0][:, li, hc * 128:(hc + 1) * 128], rhs=pe[kv][0][:, li:li + 1],
                                                      start=(li == 0), stop=(li == 31)), [w1[kv][1], pe[kv][1]], [pr])
                P.op("act", lambda e: e.activation(out=cb[kv][0][:], in_=ps[:, 0:2], func=AF.Copy), [pr], [cb[kv][1]])
            PC, PS_, PW, T32, TBF = 3, 4, 5, 6, 7
            psbf = self.ps[TBF][:].bitcast(BF16)
            den, r_den = sm["den"]; cg, r_cg = sm["cg"]; imp, r_imp = sm["imp"]; score, r_sc = sm["score"]
            wk, r_wk = sm["wk"]; sel, r_sel = sm["sel"]; m8, r_m8 = sm["m8"]; oacc, r_oa = sm["oacc"]; otmp, r_ot = sm["otmp"]
            den2, r_den2 = sm["den2"]; cg2, r_cg2 = sm["cg2"]
            cnt = {"s": 0, "p": 0, "cmb": 0, "q": 0}
            for g in range(4):
                for ty in range(4):
                    P.dma("sp", lambda e: e.dma_start(out=kin[ty][0][0:64], in_=self.KT[ty][g * 64:(g + 1) * 64, PAD:PAD + SEQ]), kin[ty][1], True)
                P.dma("sp", lambda e: e.dma_start(out=vs[:, :, 0:64], in_=self.VT[0][PAD:PAD + SEQ, g * 64:(g + 1) * 64].rearrange("(n p) d -> p n d", p=128)), r_vs, True)
                P.dma("sp", lambda e: e.dma_start(out=vw[:, :, 0:64], in_=self.VT[1][PAD:PAD + SEQ, g * 64:(g + 1) * 64].rearrange("(n p) d -> p n d", p=128)), r_vw, True)
                P.op("dve", lambda e: e.tensor_tensor(out=vw[:], in0=vw[:], in1=self.kval[:, 0:32].unsqueeze(2).to_broadcast([128, 32, 65]), op=ALU.mult), [r_vw], [r_vw])
                P.dma("sp", lambda e: e.dma_start(out=qT[0:64], in_=self.Q[g * 256:(g + 1) * 256, PAD + r0:PAD + r1].rearrange("(h d) t -> d h t", d=64)), r_q, True)
                for kv in range(2):
                    src, rsrc = kin[kv]
                    for hc in range(2):
                        ps, pr = self.ps[hc], self.psr[hc]
                        for li in range(32):
                            P.op("pe", lambda e: e.matmul(ps[:, 0:255], lhsT=w1[kv][0][:, li, hc * 128:(hc + 1) * 128], rhs=src[0:64, li:li + 16 * 254 + 1:16],
                                                          start=(li == 0), stop=(li == 31)), [w1[kv][1], rsrc], [pr])
                        P.op("act", lambda e: e.activation(out=hid[:, hc, 0:255], in_=ps[:, 0:255], func=AF.Silu, bias=cb[kv][0][:, hc:hc + 1], scale=1.0), [pr, cb[kv][1]], [r_hid])
                    ps, pr = self.ps[2], self.psr[2]
                    if kv == 0:
                        for hc in range(2):
                            P.op("pe", lambda e: e.matmul(ps[0:64, 0:255], lhsT=w2[0][0][:, hc, :], rhs=hid[:, hc, 0:255], start=(hc == 0), stop=(hc == 1)), [w2[0][1], r_hid], [pr])
                        P.op("act", lambda e: e.activation(out=kcT[0:64, 0:255], in_=ps[0:64, 0:255], func=AF.Copy), [pr], [r_kc])
                    else:
                        for nt_ in range(2):
                            for hc in range(2):
                                P.op("pe", lambda e: e.matmul(ps[:, nt_ * 64:(nt_ + 1) * 64], lhsT=hid[:, hc, nt_ * 128:(nt_ + 1) * 128], rhs=w2[1][0][:, hc, :],
                                                              start=(hc == 0), stop=(hc == 1)), [w2[1][1], r_hid], [pr])
                        P.op("act", lambda e: e.activation(out=rv[:], in_=ps[:, 0:128].rearrange("p (a d) -> p a d", a=2), func=AF.Copy), [pr], [r_rv])
                pending = [None]
                for i in range(nqt):
                    qt = qt0 + i
                    qv = qT[:, :, i * 128:(i + 1) * 128]
                    gsl = gt[:, i, g * 12:(g + 1) * 12].rearrange("p (h b) -> p h b", b=3)
                    pc, rpc = self.ps[PC], self.psr[PC]
                    pso, rpso = self.ps[PS_], self.psr[PS_]
                    pwo, rpwo = self.ps[PW], self.psr[PW]
                    psov = pso[:, 0:260].rearrange("p (h d) -> p h d", h=4)
                    pwov = pwo[:, 0:260].rearrange("p (h d) -> p h d", h=4)
                    sb_, rsb_ = selb[cnt["q"] % 2]
                    ob_, rob_ = obf[cnt["q"] % 2]
                    os_, ros_ = ost[cnt["q"] % 2]
                    cnt["q"] += 1
                    nts = [0] if qt < 16 else [0, 1]
                    steps = []
                    for nt_ in nts:
                        c4, rc4 = cmb[cnt["cmb"] % 4]
                        cnt["cmb"] += 1
                        thr = float(128 * qt - 2048 * nt_ - 31)
                        P.op("dve", lambda e: e.tensor_scalar(out=cm[:], in0=self.dtab[:], scalar1=thr, scalar2=self.nval[:, nt_:nt_ + 1], op0=ALU.is_le, op1=ALU.mult), [], [r_cm])
                        P.op("dve", lambda e: e.tensor_scalar(out=c4[:], in0=cm[:].unsqueeze(1).to_broadcast([128, 4, 128]), scalar1=-1.0, scalar2=BIG, op0=ALU.add, op1=ALU.mult), [r_cm], [rc4])

                        def pv_c(p_, rp_, nt_=nt_):
                            first = (nt_ == nts[0])
                            for h in range(4):
                                P.op("pe", lambda e: e.matmul(pc[:, h * 64:(h + 1) * 64], lhsT=p_[:, h, :], rhs=rv[:, nt_, :], start=(first and h == 0), stop=True, skip_group_check=True), [rp_, r_rv], [rpc])
                                P.op("pe", lambda e: e.matmul(pc[:, 256 + h * 64:256 + (h + 1) * 64], lhsT=p_[:, h, :], rhs=smap[:, nt_, :], start=False, stop=True, skip_group_check=True), [rp_, r_sm], [rpc])
                        steps.append(("c", kcT[:, nt_ * 128:(nt_ + 1) * 128], r_kc, [(self.ident_b[:], c4[:], [rc4])], pv_c))
                    k0 = max(0, qt - 4)
                    for kt in range(k0, qt + 1):
                        biases = []
                        if kt == qt:
                            biases.append((self.ident_b[:], trib[0][0][:], [trib[0][1]]))
                        elif kt == qt - 4:
                            biases.append((self.ident_b[:], trib[1][0][:], [trib[1][1]]))

                        def pv_w(p_, rp_, kt=kt):
                            for h in range(4):
                                P.op("pe", lambda e: e.matmul(pwov[:, h, :], lhsT=p_[:, h, :], rhs=vw[:, kt, :], start=(kt == k0 and h == 0), stop=True, skip_group_check=True), [rp_, r_vw], [rpwo])
                        steps.append(("w", kin[3][0][:, kt * 128:(kt + 1) * 128], kin[3][1], biases, pv_w))
                    n_pre = len(steps)
                    for kt in range(0, qt + 1):
                        biases = [(Et[:, kt, :], sb_[:], [r_E, rsb_])]
                        if kt == qt:
                            biases.append((self.ident_b[:], trib[0][0][:], [trib[0][1]]))

                        def pv_s(p_, rp_, kt=kt):
                            for h in range(4):
                                P.op("pe", lambda e: e.matmul(psov[:, h, :], lhsT=p_[:, h, :], rhs=vs[:, kt, :], start=(kt == 0 and h == 0), stop=True, skip_group_check=True), [rp_, r_vs], [rpso])
                        steps.append(("s", kin[2][0][:, kt * 128:(kt + 1) * 128], kin[2][1], biases, pv_s))
                    N = len(steps)
                    sbank = {}

                    def emit_score(k):
                        kind, lhsT, lreg, biases, _ = steps[k]
                        bk = cnt["s"] % 3
                        cnt["s"] += 1
                        ps, pr = self.ps[bk], self.psr[bk]
                        sbank[k] = (ps, pr)
                        P.op("pe", lambda e: e.matmul(ps[:], lhsT=lhsT, rhs=qv, start=True, stop=(len(biases) == 0)), [lreg, r_q], [pr])
                        for bi, (bl, br, bregs) in enumerate(biases):
                            P.op("pe", lambda e: e.matmul(ps[:], lhsT=bl, rhs=br, start=False, stop=(bi == len(biases) - 1)), bregs, [pr])

                    def post_cmp_dve():
                        pc2 = pc[:, 256:512].rearrange("p (h j) -> p h j", h=4)
                        P.op("dve", lambda e: e.tensor_reduce(out=den[:], in_=pc2, axis=AX.X, op=ALU.add), [rpc], [r_den])
                        P.op("dve", lambda e: e.tensor_scalar(out=den[:], in0=den[:], scalar1=0.5, scalar2=1e-30, op0=ALU.mult, op1=ALU.max), [r_den], [r_den])
                        P.op("dve", lambda e: e.reciprocal(out=den[:], in_=den[:]), [r_den], [r_den])
                        P.op("dve", lambda e: e.tensor_scalar(out=imp[:], in0=pc2[:, 0, :], scalar1=den[:, 0:1], scalar2=None, op0=ALU.mult), [rpc, r_den], [r_imp])
                        for h in range(1, 4):
                            P.op("dve", lambda e: e.scalar_tensor_tensor(out=imp[:], in0=pc2[:, h, :], scalar=den[:, h:h + 1], in1=imp[:], op0=ALU.mult, op1=ALU.add), [rpc, r_den, r_imp], [r_imp])
                        P.op("dve", lambda e: e.tensor_tensor(out=score[:], in0=imp[:], in1=mtab[:, 0, i, :], op=ALU.mult), [r_imp, r_mt], [r_sc])
                        P.op("dve", lambda e: e.tensor_tensor(out=score[:], in0=score[:], in1=mtab[:, 1, i, :], op=ALU.add), [r_sc, r_mt], [r_sc])
                        P.op("dve", lambda e: e.max(out=m8[:], in_=score[:]), [r_sc], [r_m8])
                        P.op("dve", lambda e: e.match_replace(out=wk[:], in_to_replace=m8[:], in_values=score[:], imm_value=-2.0), [r_m8, r_sc], [r_wk])
                        P.op("dve", lambda e: e.max(out=m8[:], in_=wk[:]), [r_wk], [r_m8])
                        P.op("dve", lambda e: e.match_replace(out=wk[:], in_to_replace=m8[:], in_values=wk[:], imm_value=-2.0), [r_m8, r_wk], [r_wk])
                        P.op("dve", lambda e: e.tensor_tensor(out=sel[:], in0=score[:], in1=wk[:], op=ALU.subtract), [r_sc, r_wk], [r_sel])
                        P.op("dve", lambda e: e.scalar_tensor_tensor(out=sel[:], in0=sel[:], scalar=1.0, in1=mtab[:, 2, i, :], op0=ALU.min, op1=ALU.mult), [r_sel, r_mt], [r_sel])
                        P.op("dve", lambda e: e.tensor_tensor(out=cg[:], in0=den[:], in1=gsl[:, :, 0], op=ALU.mult), [r_den, r_gt], [r_cg])
                        P.op("dve", lambda e: e.tensor_tensor(out=oacc[:], in0=pc[:, 0:256].rearrange("p (h d) -> p h d", h=4), in1=cg[:].unsqueeze(2).to_broadcast([128, 4, 64]), op=ALU.mult), [rpc, r_cg], [r_oa])

                    def pre_slc():
                        pt, rpt = self.ps[T32], self.psr[T32]
                        P.op("pe", lambda e: e.transpose(out=pt[0:64, 0:128], in_=sel[:], identity=self.ident_f[:]), [r_sel], [rpt])
                        P.op("act", lambda e: e.activation(out=sb_[0:64], in_=pt[0:64, 0:128].unsqueeze(1).to_broadcast([64, 4, 128]), func=AF.Identity, bias=negb[0:64, 0:1], scale=BIG), [rpt, r_negb], [rsb_])

                    LA = getattr(self, "lookahead", 0)
                    if LA:
                        emit_score(0)
                    for k in range(N):
                        if not LA:
                            if k == n_pre:
                                pre_slc()
                            emit_score(k)
                        elif k + 1 < N:
                            if k + 1 == n_pre:
                                pre_slc()
                            emit_score(k + 1)
                        ps, pr = sbank.pop(k)
                        p_, rp_ = pb[cnt["p"] % 4]
                        cnt["p"] += 1
                        P.op("act", lambda e: e.activation(out=p_[:], in_=ps[:].rearrange("p (h q) -> p h q", h=4), func=AF.Exp), [pr], [rp_])
                        steps[k][4](p_, rp_)
                        if k == len(nts) - 1:
                            post_cmp_dve()
                            if pending[0] is not None:
                                pending[0]()
                                pending[0] = None
                    P.op("dve", lambda e: e.tensor_scalar(out=den2[:], in0=psov[:, :, 64], scalar1=1e-30, scalar2=None, op0=ALU.max), [rpso], [r_den2])
                    P.op("dve", lambda e: e.reciprocal(out=den2[:], in_=den2[:]), [r_den2], [r_den2])
                    P.op("dve", lambda e: e.tensor_tensor(out=cg2[:], in0=den2[:], in1=gsl[:, :, 1], op=ALU.mult), [r_den2, r_gt], [r_cg2])
                    P.op("dve", lambda e: e.tensor_tensor(out=otmp[:], in0=psov[:, :, 0:64], in1=cg2[:].unsqueeze(2).to_broadcast([128, 4, 64]), op=ALU.mult), [rpso, r_cg2], [r_ot])
                    P.op("dve", lambda e: e.tensor_tensor(out=oacc[:], in0=oacc[:], in1=otmp[:], op=ALU.add), [r_oa, r_ot], [r_oa])
                    P.op("dve", lambda e: e.tensor_scalar(out=den2[:], in0=pwov[:, :, 64], scalar1=1e-30, scalar2=None, op0=ALU.max), [rpwo], [r_den2])
                    P.op("dve", lambda e: e.reciprocal(out=den2[:], in_=den2[:]), [r_den2], [r_den2])
                    P.op("dve", lambda e: e.tensor_tensor(out=cg2[:], in0=den2[:], in1=gsl[:, :, 2], op=ALU.mult), [r_den2, r_gt], [r_cg2])
                    P.op("dve", lambda e: e.tensor_tensor(out=otmp[:], in0=pwov[:, :, 0:64], in1=cg2[:].unsqueeze(2).to_broadcast([128, 4, 64]), op=ALU.mult), [rpwo, r_cg2], [r_ot])
                    P.op("dve", lambda e: e.tensor_tensor(out=ob_[:].rearrange("p (h d) -> p h d", h=4), in0=oacc[:], in1=otmp[:], op=ALU.add), [r_oa, r_ot], [rob_])

                    def post_pe(i=i, ob_=ob_, rob_=rob_, os_=os_, ros_=ros_):
                        rptb = self.psr[TBF]
                        for half in range(2):
                            P.op("pe", lambda e: e.transpose(out=psbf[:, half * 128:(half + 1) * 128], in_=ob_[:, half * 128:(half + 1) * 128], identity=self.ident_b[:]), [rob_], [rptb])
                        P.op("act", lambda e: e.activation(out=os_[:], in_=psbf[:, 0:256].rearrange("p (a q) -> p a q", a=2), func=AF.Copy), [rptb], [ros_])
                        c0 = PAD + r0 + i * 128
                        dst = self.OC[g * 256:(g + 1) * 256, c0:c0 + 128].rearrange("(a p) t -> p a t", p=128)
                        P.dma("sp", lambda e: e.dma_start(out=dst, in_=os_[:]), ros_, False)
                    pending[0] = post_pe
                if pending[0] is not None:
                    pending[0]()
                    pending[0] = None
            P.end_phase()

    def phase_merge(self, l, X, Xout, r0, r1):
        P = self.P
        Wl = self.w_in[l]
        with ExitStack() as st:
            nt = self.norm_tiles(st)
            pt = self.post_tiles(st)
            hT, r_h = self.tile(st, "hT", [128, 16, 512], BF16)
            ins = [self.tile(st, "hin", [128, 8, 512], BF16) for _ in range(3)]
            wb = [self.tile(st, "wbuf", [128, 8192], BF16) for _ in range(2)]
            mT, r_m = self.tile(st, "mT", [128, 16, 512], BF16)
            mixed, r_mx = self.tile(st, "mixed", [128, 16, 512], F32)
            sgs = [self.tile(st, "sgs", [128, 4, 512], BF16) for _ in range(2)]
            accm, r_accm = self.tile(st, "accm", [128, 4, 512], F32)
            tmp = [self.tile(st, "tmp", [128, 512], F32) for _ in range(2)]
            sq = [self.tile(st, "sq", [128, 512], BF16) for _ in range(2)]
            srcs = [self.HA, self.HB, self.OC]
            wouts = [self.w_a_out[l], self.w_b_out[l], self.w_c_out[l]]
            hregs = [self.newreg("hT"), self.newreg("hT")]
            sched = []
            pk = [0]

            def nb():
                bk = pk[0] % 6
                pk[0] += 1
                return self.ps[bk], self.psr[bk]
            for (s0, n) in self.tok_tiles(r0, r1 - r0):
                sc_ = {}
                sched.append(sc_)

                def nfn(b, s0=s0, n=n):
                    self.fill_hT(nt, X, PAD + s0, n, (l * 4 + 0) * 16, hT, hregs)
                    for b3 in range(3):
                        P.dma("sp", lambda e: e.dma_start(out=ins[b3][0][:, :, 0:n], in_=srcs[b3].rearrange("(c p) t -> p c t", p=128)[:, :, PAD + s0:PAD + s0 + n]),
                              ins[b3][1], True)
                sc_["N"] = [(None, nfn)]
                jobs = []
                for dg in range(4):
                    for b3 in range(3):
                        def lfg(b, dg=dg, b3=b3):
                            self.wload(wb[b][0], wb[b][1], Wl, 16, C_M + b3 * 2048 + dg * 512, 512, 0)

                        def cfg(b, dg=dg, b3=b3, n=n, hregs=hregs):
                            wbt, rwb = wb[b]
                            wg = wbt[:, 0:8192].rearrange("p (k n) -> p k n", k=16)
                            s_, rs_ = sgs[b3 % 2]
                            for cc in range(4):
                                pg, rg = nb()
                                for kc in range(16):
                                    P.op("pe", lambda e: e.matmul(pg[:, 0:n], lhsT=wg[:, kc, cc * 128:(cc + 1) * 128], rhs=hT[:, kc, 0:n], start=(kc == 0), stop=(kc == 15)), [rwb] + hregs, [rg])
                                P.op("act", lambda e: e.activation(out=s_[:, cc, 0:n], in_=pg[:, 0:n], func=AF.Sigmoid), [rg], [rs_])
                        jobs.append((lfg, cfg))

                        def lfy(b, dg=dg, b3=b3):
                            self.wload(wb[b][0], wb[b][1], wouts[b3], 8, dg * 512, 512, 0)

                        def cfy(b, dg=dg, b3=b3, n=n):
                            wbt, rwb = wb[b]
                            wy = wbt[:, 0:4096].rearrange("p (k n) -> p k n", k=8)
                            s_, rs_ = sgs[b3 % 2]
                            for cc in range(4):
                                py, ry = nb()
                                for kc in range(8):
                                    P.op("pe", lambda e: e.matmul(py[:, 0:n], lhsT=wy[:, kc, cc * 128:(cc + 1) * 128], rhs=ins[b3][0][:, kc, 0:n], start=(kc == 0), stop=(kc == 7)), [rwb, ins[b3][1]], [ry])
                                if b3 == 0:
                                    P.op("dve", lambda e: e.tensor_tensor(out=accm[:, cc, 0:n], in0=py[:, 0:n], in1=s_[:, cc, 0:n], op=ALU.mult), [ry, rs_], [r_accm])
                                else:
                                    t_, rt_ = tmp[cc % 2]
                                    P.op("dve", lambda e: e.tensor_tensor(out=t_[:, 0:n], in0=py[:, 0:n], in1=s_[:, cc, 0:n], op=ALU.mult), [ry, rs_], [rt_])
                                    if b3 == 1:
                                        P.op("dve", lambda e: e.tensor_tensor(out=accm[:, cc, 0:n], in0=accm[:, cc, 0:n], in1=t_[:, 0:n], op=ALU.add), [r_accm, rt_], [r_accm])
                                    else:
                                        P.op("dve", lambda e: e.tensor_tensor(out=mT[:, dg * 4 + cc, 0:n], in0=accm[:, cc, 0:n], in1=t_[:, 0:n], op=ALU.add), [r_accm, rt_], [r_m])
                        jobs.append((lfy, cfy))
                sc_["A"] = jobs
                jobs = []
                for jg in range(4):
                    def lf(b, jg=jg):
                        self.wload(wb[b][0], wb[b][1], self.w_o[l], 16, jg * 512, 512, 0)

                    def cf(b, jg=jg, n=n):
                        wo = wb[b][0][:, 0:8192].rearrange("p (k n) -> p k n", k=16)
                        for cc in range(4):
                            dch = jg * 4 + cc
                            po, ro = self.ps[dch % 4], self.psr[dch % 4]
                            for kc in range(16):
                                P.op("pe", lambda e, kc=kc, po=po, cc=cc: e.matmul(po[:, 0:n], lhsT=wo[:, kc, cc * 128:(cc + 1) * 128], rhs=mT[:, kc, 0:n], start=(kc == 0), stop=(kc == 15)), [wb[b][1], r_m], [ro])
                            s_, rs_ = sq[dch % 2]
                            P.op("act", lambda e, po=po, dch=dch: e.activation(out=mixed[:, dch, 0:n], in_=po[:, 0:n], func=AF.Copy), [ro], [r_mx])
                            P.op("act", lambda e, po=po, s_=s_: e.activation(out=s_[:, 0:n], in_=po[:, 0:n], func=AF.Square), [ro], [rs_])
                            P.op("pe", lambda e, s_=s_, dch=dch: e.matmul(self.ps[6][:, 0:n], lhsT=self.ones_b[:], rhs=s_[:, 0:n], start=(dch == 0), stop=(dch == 15)), [rs_], [self.psr[6]])
                    jobs.append((lf, cf))
                sc_["B"] = jobs
                sc_["P"] = [(None, lambda b, s0=s0, n=n: self.post_norm(st, mixed, r_mx, 6, n, (l * 4 + 1) * 16, X, Xout, PAD + s0, PAD + s0, pt))]
            nt_ = len(sched)
            seq = sched[0]["N"] + sched[0]["A"]
            for t in range(nt_):
                if t + 1 < nt_:
                    seq += sched[t + 1]["N"]
                seq += sched[t]["B"]
                if t + 1 < nt_:
                    seq += sched[t + 1]["A"][:4] + sched[t]["P"] + sched[t + 1]["A"][4:]
                else:
                    seq += sched[t]["P"]
            self.run_jobs(seq)
            P.end_phase()

    def phase_ffn(self, l, X, Xout, f0, f1, out_col0):
        P = self.P
        Wu = self.w_up[l]
        Wd = self.w_down[l]
        with ExitStack() as st:
            nt = self.norm_tiles(st)
            pt = self.post_tiles(st)
            hT, _ = self.tile(st, "hT", [128, 16, 512], BF16)
            wb = [self.tile(st, "wbuf", [128, 8192], BF16) for _ in range(2)]
            act, r_act = self.tile(st, "act", [128, 44, 512], BF16)
            mixed, r_mx = self.tile(st, "mixed", [128, 16, 512], F32)
            pre = [self.tile(st, "pre", [128, 514], F32) for _ in range(4)]
            uu = [self.tile(st, "uu", [128, 512], F32) for _ in range(4)]
            sgf, r_sgf = self.tile(st, "sgf", [128, 4, 512], F32)
            sq = [self.tile(st, "sq", [128, 512], BF16) for _ in range(2)]
            carry, r_carry = self.tile(st, "carry", [128, 88, 2], F32)
            fw, r_fw = self.tile(st, "fw", [128, 88, 3], F32)
            fb, r_fb = self.tile(st, "fb", [128, 88], F32)
            P.dma("sp", lambda e: e.dma_start(out=fw[:], in_=self.ffw[:, l * 264:(l + 1) * 264].rearrange("p (c k) -> p c k", k=3)), r_fw, True)
            P.dma("sp", lambda e: e.dma_start(out=fb[:], in_=self.ffb[:, l * 88:(l + 1) * 88]), r_fb, True)
            tiles = [(f0 - 2, 2)] + self.tok_tiles(f0, f1 - f0)
            pk = [0]
            hregs = [self.newreg("hT"), self.newreg("hT")]
            sched = {}
            for tix, (s0, n) in enumerate(tiles):
                halo = (tix == 0)
                sched[tix] = {}
                sched[tix]["N"] = [(None, lambda b, s0=s0, n=n: self.fill_hT(nt, X, PAD + s0, n, (l * 4 + 2) * 16, hT, hregs))]
                jobs = []
                for grp in range(11):
                    for gv in range(2):
                        def lf(b, grp=grp, gv=gv):
                            self.wload(wb[b][0], wb[b][1], Wu, 16, gv * DFF + grp * 512, 512, 0)

                        def cf(b, grp=grp, gv=gv, n=n, halo=halo, hregs=hregs):
                            wbt, rwb = wb[b]
                            wv_ = wbt[:, 0:8192].rearrange("p (k n) -> p k n", k=16)
                            for cc in range(4):
                                jg = grp * 4 + cc
                                j = jg + 44 * gv
                                bk = pk[0] % 6
                                pk[0] += 1
                                ps, pr = self.ps[bk], self.psr[bk]
                                for kc in range(16):
                                    P.op("pe", lambda e: e.matmul(ps[:, 0:n], lhsT=wv_[:, kc, cc * 128:(cc + 1) * 128], rhs=hT[:, kc, 0:n], start=(kc == 0), stop=(kc == 15)),
                                         [rwb] + hregs, [pr])
                                if halo:
                                    P.op("act", lambda e: e.activation(out=carry[:, j, :], in_=ps[:, 0:2], func=AF.Copy), [pr], [r_carry])
                                    continue
                                p_, rp_ = pre[cc % 4]
                                u_, ru_ = uu[cc % 4]
                                P.op("act", lambda e: e.activation(out=p_[:, 2:2 + n], in_=ps[:, 0:n], func=AF.Copy), [pr], [rp_])
                                P.op("act", lambda e: e.activation(out=p_[:, 0:2], in_=carry[:, j, :], func=AF.Copy), [r_carry], [rp_])
                                P.op("act", lambda e: e.activation(out=carry[:, j, :], in_=p_[:, n:n + 2], func=AF.Copy), [rp_], [r_carry])
                                P.op("dve", lambda e: e.tensor_scalar(out=u_[:, 0:n], in0=p_[:, 2:2 + n], scalar1=fw[:, j, 2:3], scalar2=fb[:, j:j + 1], op0=ALU.mult, op1=ALU.add), [rp_, r_fw, r_fb], [ru_])
                                P.op("dve", lambda e: e.scalar_tensor_tensor(out=u_[:, 0:n], in0=p_[:, 1:1 + n], scalar=fw[:, j, 1:2], in1=u_[:, 0:n], op0=ALU.mult, op1=ALU.add), [rp_, ru_], [ru_])
                                P.op("dve", lambda e: e.scalar_tensor_tensor(out=u_[:, 0:n], in0=p_[:, 0:n], scalar=fw[:, j, 0:1], in1=u_[:, 0:n], op0=ALU.mult, op1=ALU.add), [rp_, ru_], [ru_])
                                if gv == 0:
                                    P.op("act", lambda e: e.activation(out=sgf[:, cc, 0:n], in_=u_[:, 0:n], func=AF.Silu), [ru_], [r_sgf])
                                else:
                                    P.op("dve", lambda e: e.tensor_tensor(out=act[:, jg, 0:n], in0=sgf[:, cc, 0:n], in1=u_[:, 0:n], op=ALU.mult), [r_sgf, ru_], [r_act])
                        jobs.append((lf, cf))
                sched[tix]["U"] = jobs
                jobs = []
                if not halo:
                    kranges = [(0, 16), (16, 16), (32, 12)]
                    for dg in range(4):
                        for kr, (k0_, nk) in enumerate(kranges):
                            def lf(b, dg=dg, k0_=k0_, nk=nk):
                                self.wload(wb[b][0], wb[b][1], Wd, nk, dg * 512, 512, 0, k0=k0_)

                            def cf(b, dg=dg, kr=kr, k0_=k0_, nk=nk, n=n):
                                wd = wb[b][0][:, 0:nk * 512].rearrange("p (k n) -> p k n", k=nk)
                                for cc in range(4):
                                    dch = dg * 4 + cc
                                    po, ro = self.ps[cc], self.psr[cc]
                                    for kc in range(nk):
                                        P.op("pe", lambda e: e.matmul(po[:, 0:n], lhsT=wd[:, kc, cc * 128:(cc + 1) * 128], rhs=act[:, k0_ + kc, 0:n], start=(kr == 0 and kc == 0), stop=(kr == 2 and kc == nk - 1)),
                                             [wb[b][1], r_act], [ro])
                                    if kr == 2:
                                        s_, rs_ = sq[dch % 2]
                                        P.op("act", lambda e: e.activation(out=mixed[:, dch, 0:n], in_=po[:, 0:n], func=AF.Copy), [ro], [r_mx])
                                        P.op("act", lambda e: e.activation(out=s_[:, 0:n], in_=po[:, 0:n], func=AF.Square), [ro], [rs_])
                                        P.op("pe", lambda e: e.matmul(self.ps[6][:, 0:n], lhsT=self.ones_b[:], rhs=s_[:, 0:n], start=(dch == 0), stop=(dch == 15)), [rs_], [self.psr[6]])
                            jobs.append((lf, cf))
                sched[tix]["D"] = jobs
                sched[tix]["P"] = [(None, lambda b, s0=s0, n=n: self.post_norm(st, mixed, r_mx, 6, n, (l * 4 + 3) * 16, X, Xout, PAD + s0, out_col0 + (s0 - f0), pt))]
            nt_ = len(tiles)
            seq = sched[0]["N"] + sched[0]["U"] + sched[1]["N"] + sched[1]["U"]
            for t in range(1, nt_):
                if t + 1 < nt_:
                    seq += sched[t + 1]["N"]
                seq += sched[t]["D"]
                if t + 1 < nt_:
                    seq += sched[t + 1]["U"][:4] + sched[t]["P"] + sched[t + 1]["U"][4:]
                else:
                    seq += sched[t]["P"]
            self.run_jobs(seq)
            P.end_phase()

    def build(self):
        ph = []
        ph.append(lambda: self.phase_kv(0, self.xT))
        for (r0, r1) in ((0, 2048), (2048, 4096)):
            ph.append(lambda r0=r0, r1=r1: self.phase_m1(0, self.xT, r0, r1))
            ph.append(lambda r0=r0, r1=r1: self.phase_m2(0, r0, r1))
            ph.append(lambda r0=r0, r1=r1: self.phase_att(0, r0, r1))
            ph.append(lambda r0=r0, r1=r1: self.phase_merge(0, self.xT, self.XM, r0, r1))
        ph.append(lambda: self.phase_ffn(0, self.XM, self.X1, 0, 4096, PAD))
        ph.append(lambda: self.phase_kv(1, self.X1))
        ph.append(lambda: self.phase_m1(1, self.X1, 1920, 4096))
        ph.append(lambda: self.phase_m2(1, 1920, 4096))
        ph.append(lambda: self.phase_att(1, 1920, 4096))
        ph.append(lambda: self.phase_merge(1, self.X1, self.XM, 1920, 4096))
        ph.append(lambda: self.phase_ffn(1, self.XM, self.OUT, 2048, 4096, 0))
        sel = self.stop if self.stop is not None else range(len(ph))
        for i in sel:
            ph[i]()
        self.P.barrier()
        self.P.emit()
        return self.nc


def _colvec(v, nchunk):
    return np.ascontiguousarray(v.reshape(nchunk, 128).T)


def make_inputs(inp):
    L = 2
    f = lambda a: np.ascontiguousarray(np.asarray(a, dtype=np.float32))
    shared = {}
    for k in ("w_in", "w_a_out", "w_b_out", "w_c_out", "w_o", "w_up", "w_down"):
        shared[k] = f(inp[k])
    nw = np.zeros((128, L * 4 * 16), np.float32)
    for l in range(L):
        for i, k in enumerate(("norm_mix_pre", "norm_mix_post", "norm_ffn_pre", "norm_ffn_post")):
            nw[:, (l * 4 + i) * 16:(l * 4 + i + 1) * 16] = _colvec(f(inp[k])[l], 16)
    shared["normw"] = nw
    caw = np.zeros((128, L * 8 * 31), np.float32)
    av = np.zeros((128, L * 3 * 8), np.float32)
    for l in range(L):
        w = f(inp["conv_a_w"])[l]
        caw[:, l * 248:(l + 1) * 248] = w.T.reshape(8, 128, 31).transpose(1, 0, 2).reshape(128, 248)
        for i, k in enumerate(("conv_a_b", "ln_a_g", "ln_a_b")):
            av[:, (l * 3 + i) * 8:(l * 3 + i + 1) * 8] = _colvec(f(inp[k])[l], 8)
    shared["convaw"] = caw
    shared["avec"] = av
    lnb = np.zeros((128, L * 2 * 1024), np.float32)
    for l in range(L):
        lnb[:, (l * 2) * 1024:(l * 2 + 1) * 1024] = f(inp["ln_b_g"])[l][None, :]
        lnb[:, (l * 2 + 1) * 1024:(l * 2 + 2) * 1024] = f(inp["ln_b_b"])[l][None, :]
    shared["lnb"] = lnb
    shared["sgw"] = np.ascontiguousarray(f(inp["sg_w"]).transpose(0, 3, 1, 2).reshape(L, 128, 1024))
    shared["sgb"] = np.ascontiguousarray(f(inp["sg_b"]).reshape(1, L * 1024))
    cw1 = np.zeros((L * 2, 64, 32 * 256), np.float32)
    cw2 = np.zeros((L * 2, 256, 64), np.float32)
    cpe = np.zeros((L * 2, 64, 32), np.float32)
    for l in range(L):
        for kv, s in enumerate(("k", "v")):
            cw1[l * 2 + kv] = f(inp["cmp_w1_" + s])[l].transpose(1, 0, 2).reshape(64, 32 * 256)
            cw2[l * 2 + kv] = f(inp["cmp_w2_" + s])[l]
            cpe[l * 2 + kv] = f(inp["cmp_pe_" + s])[l].T
    shared["cw1"], shared["cw2"], shared["cpe"] = cw1, cw2, cpe
    ffw = np.zeros((128, L * 88 * 3), np.float32)
    ffb = np.zeros((128, L * 88), np.float32)
    for l in range(L):
        w = f(inp["ffn_conv_w"])[l]
        ffw[:, l * 264:(l + 1) * 264] = w.T.reshape(88, 128, 3).transpose(1, 0, 2).reshape(128, 264)
        ffb[:, l * 88:(l + 1) * 88] = _colvec(f(inp["ffn_conv_b"])[l], 88)
    shared["ffw"], shared["ffb"] = ffw, ffb
    p = np.arange(128)
    shared["c_ident"] = np.eye(128, dtype=np.float32)
    shared["c_trile"] = (p[:, None] <= p[None, :]).astype(np.float32)
    shared["c_trigt"] = (p[:, None] > p[None, :]).astype(np.float32)
    shared["c_dtab"] = (16.0 * p[:, None] - p[None, :]).astype(np.float32)
    k = np.arange(4096)
    shared["c_E"] = (k[None, :] // 64 == np.arange(64)[:, None]).astype(np.float32)
    n = np.arange(256)
    sm = np.zeros((256, 64), np.float32)
    for nn in range(255):
        sm[nn, nn // 4] += 1.0
        sm[nn, (nn + 1) // 4] += 1.0
    shared["c_smap"] = np.ascontiguousarray(sm.reshape(2, 128, 64).transpose(1, 0, 2).reshape(128, 128))
    x = f(inp["x"])
    maps = []
    for b in range(4):
        for s in range(2):
            m = dict(shared)
            xT = np.zeros((D, NCOL), np.float32)
            tok = np.zeros((NCOL,), np.float32)
            if s == 1:
                xT[:, PAD:] = x[b].T
                tok[PAD:] = 1.0
                j0 = 0
            else:
                xT[:, PAD + 2048:] = x[b, :2048].T
                tok[PAD + 2048:] = 1.0
                j0 = 32
            m["xT"] = xT
            m["m_tok"] = np.ascontiguousarray(np.broadcast_to(tok[None, :], (128, NCOL)))
            kval = np.ones((128, 32), np.float32)
            nval = np.ones((128, 2), np.float32)
            if s == 0:
                kval[:, :16] = 0.0
                nval[:, 0] = 0.0
            m["m_kval"], m["m_nval"] = kval, nval
            t = np.arange(4096).reshape(32, 128).T
            cur = t // 64
            j = np.arange(64)[None, None, :]
            valid = (j <= cur[:, :, None]) & (j >= j0)
            forced = ((j == j0) | (j == cur[:, :, None]) | (j == cur[:, :, None] - 1)) & valid
            M1 = (valid & ~forced).astype(np.float32)
            M2 = np.where(forced, 1e4 + j, np.where(valid, 0.0, -1.0)).astype(np.float32)
            M3 = valid.astype(np.float32)
            m["m_sel"] = np.ascontiguousarray(np.stack([M1, M2, M3]).reshape(3, 128, 32 * 64))
            maps.append(m)
    return maps


_CACHE = {}


def kernel(**inputs):
    maps = make_inputs(inputs)
    if "nc" not in _CACHE:
        _CACHE["nc"] = Builder().build()
    nc = _CACHE["nc"]
    res = run_bass_kernel_spmd(nc, maps, core_ids=list(range(8)))
    out = np.zeros((4, SEQ, D), np.float32)
    for b in range(4):
        for s in range(2):
            o = res.results[b * 2 + s]["OUT"]
            out[b, s * 2048:(s + 1) * 2048, :] = o.T
    return out
```

```python
import numpy as np
from contextlib import ExitStack
import concourse.bass as bass
import concourse.mybir as mybir
from concourse.bass_utils import run_bass_kernel_spmd

F32 = mybir.dt.float32
BF16 = mybir.dt.bfloat16
AF = mybir.ActivationFunctionType
ALU = mybir.AluOpType
AX = mybir.AxisListType

ENGS = ("pe", "act", "dve", "pool", "sp")

D = 2048
SEQ = 4096
PAD = 128
NCOL = PAD + SEQ
NIN = 12848
DFF = 5632
EPS = 1e-6
C_A, C_B, C_Q, C_KV, C_G, C_M = 0, 2048, 4096, 5120, 6656, 6704


class Reg:
    __slots__ = ("name", "w", "r", "dkey", "dcount")

    def __init__(self, name):
        self.name = name
        self.w = None
        self.r = {}
        self.dkey = None
        self.dcount = 0


class _Rec:
    def __init__(self):
        self.call = None

    def __getattr__(self, name):
        def f(*a, **k):
            self.call = (name, a, k)
            return self
        return f


def _record(fn):
    r = _Rec()
    fn(r)
    assert r.call is not None
    return r.call


class Prog:
    def __init__(self, nc):
        self.nc = nc
        self.streams = {e: [] for e in ENGS}
        self.cnt = {e: 0 for e in ENGS}
        self.seen = {e: {} for e in ENGS}
        self.dsems = {}
        self.ndsem = 0
        self.semh = {}
        self.free_dkeys = []
        self.phase_keys = []

    def sb(self, stack, name, shape, dt):
        return stack.enter_context(self.nc.sbuf_tensor(name, list(shape), dt))

    def _waits(self, eng, deps):
        out = []
        seen = self.seen[eng]
        best = {}
        for (k, v) in deps:
            if best.get(k, -1) < v:
                best[k] = v
        for k, v in best.items():
            if k == eng:
                if eng == "pe":
                    continue
                if v <= self.cnt[eng] - 2:
                    continue
            if seen.get(k, -1) >= v:
                continue
            seen[k] = v
            out.append((k, v))
        return out

    def _deps(self, reads, writes):
        deps = []
        for r in reads:
            if r.w is not None:
                deps.append(r.w)
        for w in writes:
            if w.w is not None:
                deps.append(w.w)
            deps.extend(w.r.items())
        return deps

    def op(self, eng, fn, reads=(), writes=()):
        waits = self._waits(eng, self._deps(reads, writes))
        self.cnt[eng] += 1
        me = (eng, self.cnt[eng])
        self.streams[eng].append((waits, _record(fn), (eng, 1)))
        for r in reads:
            r.r[me[0]] = me[1]
        for w in writes:
            w.w = me
            w.r = {}
        return me

    def dma(self, q, fn, sb, load, dram=()):
        if sb.dkey is None:
            if self.free_dkeys:
                sb.dkey = self.free_dkeys.pop()
                sb.dcount = self.dsems[sb.dkey]
            else:
                sb.dkey = "d%d" % self.ndsem
                self.ndsem += 1
            self.phase_keys.append(sb.dkey)
        if load:
            deps = self._deps(dram, [sb])
        else:
            deps = self._deps([sb], dram)
        waits = self._waits(q, deps)
        sb.dcount += 16
        self.dsems[sb.dkey] = sb.dcount
        me = (sb.dkey, sb.dcount)
        self.streams[q].append((waits, _record(fn), (sb.dkey, 16)))
        if load:
            sb.w = me
            sb.r = {}
            for d in dram:
                d.r[me[0]] = me[1]
        else:
            sb.r[me[0]] = me[1]
            for d in dram:
                d.w = me
                d.r = {}
        return me

    def barrier(self):
        allk = [(e, self.cnt[e]) for e in ENGS if self.cnt[e] > 0]
        allk += list(self.dsems.items())
        for e in ENGS:
            waits = self._waits(e, [kv for kv in allk if kv[0] != e])
            if waits:
                self.streams[e].append((waits, None, None))

    def end_phase(self, persistent=False):
        self.barrier()
        if not persistent:
            self.free_dkeys.extend(self.phase_keys)
        self.phase_keys = []

    def emit(self):
        nc = self.nc
        keys = list(ENGS) + list(self.dsems.keys())
        with ExitStack() as st:
            for k in keys:
                self.semh[k] = st.enter_context(nc.semaphore("s_" + k))
            block = st.enter_context(nc.Block())
            semh = self.semh

            def run(e, stream):
                for waits, fn, inc in stream:
                    for (k, v) in waits:
                        e.wait_ge(semh[k], v)
                    if fn is not None:
                        name, a, k = fn
                        ins = getattr(e, name)(*a, **k)
                        ins.then_inc(semh[inc[0]], inc[1])

            @block.tensor
            def _(e):
                run(e, self.streams["pe"])

            @block.scalar
            def _(e):
                run(e, self.streams["act"])

            @block.vector
            def _(e):
                run(e, self.streams["dve"])

            @block.gpsimd
            def _(e):
                run(e, self.streams["pool"])

            @block.sync
            def _(e):
                run(e, self.streams["sp"])


class Builder:
    def __init__(self, debug=False, nlayers=2, stop=None):
        self.debug = debug
        self.stop = stop
        nc = bass.Bass("TRN2", target_bir_lowering=False)
        self.nc = nc
        self.P = Prog(nc)
        self.I = {}
        self.regs = {}
        self.gst = ExitStack()
        self._uid = 0
        self.declare_io()
        self.alloc_consts()

    def R(self, name):
        if name not in self.regs:
            self.regs[name] = Reg(name)
        return self.regs[name]

    def newreg(self, name):
        self._uid += 1
        return Reg("%s_%d" % (name, self._uid))

    def din(self, name, shape, dt=F32):
        t = self.nc.dram_tensor(name, list(shape), dt, kind="ExternalInput").ap()
        self.I[name] = t
        return t

    def dscr(self, name, shape, dt, out=False):
        kind = "ExternalOutput" if (out or self.debug) else "Internal"
        return self.nc.dram_tensor(name, list(shape), dt, kind=kind).ap()

    def tile(self, st, name, shape, dt):
        self._uid += 1
        t = self.P.sb(st, "%s_%d" % (name, self._uid), shape, dt)
        return t, self.newreg(name)

    def dump(self, name, tile_, reg, shape, dt=F32):
        if not self.debug or True:
            return
        d = self.nc.dram_tensor("dbg_" + name, list(shape), dt, kind="ExternalOutput").ap()
        self.P.dma("sp", lambda e: e.dma_start(out=d, in_=tile_[:]), reg, False)

    def declare_io(self):
        L = 2
        self.xT = self.din("xT", [D, NCOL])
        self.w_in = self.din("w_in", [L, D, NIN])
        self.w_a_out = self.din("w_a_out", [L, 1024, D])
        self.w_b_out = self.din("w_b_out", [L, 1024, D])
        self.w_c_out = self.din("w_c_out", [L, 1024, D])
        self.w_o = self.din("w_o", [L, D, D])
        self.w_up = self.din("w_up", [L, D, 2 * DFF])
        self.w_down = self.din("w_down", [L, DFF, D])
        self.normw = self.din("normw", [128, L * 4 * 16])
        self.convaw = self.din("convaw", [128, L * 8 * 31])
        self.avec = self.din("avec", [128, L * 3 * 8])
        self.lnb = self.din("lnb", [128, L * 2 * 1024])
        self.sgw = self.din("sgw", [L, 128, 8 * 128])
        self.sgb = self.din("sgb", [1, L * 1024])
        self.cw1 = self.din("cw1", [L * 2, 64, 32 * 256])
        self.cw2 = self.din("cw2", [L * 2, 256, 64])
        self.cpe = self.din("cpe", [L * 2, 64, 32])
        self.ffw = self.din("ffw", [128, L * 88 * 3])
        self.ffb = self.din("ffb", [128, L * 88])
        self.c_ident = self.din("c_ident", [128, 128])
        self.c_trile = self.din("c_trile", [128, 128])
        self.c_trigt = self.din("c_trigt", [128, 128])
        self.c_dtab = self.din("c_dtab", [128, 128])
        self.c_E = self.din("c_E", [64, 32 * 128])
        self.c_smap = self.din("c_smap", [128, 2 * 64])
        self.m_tok = self.din("m_tok", [128, NCOL])
        self.m_kval = self.din("m_kval", [128, 32])
        self.m_nval = self.din("m_nval", [128, 2])
        self.m_sel = self.din("m_sel", [3, 128, 32 * 64])
        self.XM = self.dscr("XM", [D, NCOL], F32)
        self.X1 = self.dscr("X1", [D, NCOL], F32)
        self.OUT = self.dscr("OUT", [D, 2048], F32, out=True)
        self.GLU = self.dscr("GLU", [1024, NCOL], BF16)
        self.HA = self.dscr("HA", [1024, NCOL], BF16)
        self.HB = self.dscr("HB", [1024, NCOL], BF16)
        self.OC = self.dscr("OC", [1024, NCOL], BF16)
        self.Q = self.dscr("Q", [1024, NCOL], BF16)
        self.GATES = self.dscr("GATES", [NCOL, 48], F32)
        self.KT = [self.dscr("KT%d" % i, [256, NCOL], BF16) for i in range(4)]
        self.VT = [self.dscr("VT%d" % i, [NCOL, 256], BF16) for i in range(2)]

    def alloc_consts(self):
        P, st = self.P, self.gst
        nc = self.nc
        T = lambda n, s, d: self.tile(st, n, s, d)
        self.ident_f, r_if = T("ident_f", [128, 128], F32)
        self.ident_b, r_ib = T("ident_b", [128, 128], BF16)
        self.ones_b, r_ob = T("ones_b", [128, 128], BF16)
        self.trile, r1 = T("trile", [128, 128], F32)
        self.trigt, r2 = T("trigt", [128, 128], F32)
        self.dtab, r3 = T("dtab", [128, 128], F32)
        self.tokv, r4 = T("tokv", [128, NCOL], BF16)
        self.kval, r5 = T("kval", [128, 32], F32)
        self.nval, r6 = T("nval", [128, 2], F32)
        self.normw_s, r7 = T("normw", [128, 128], F32)
        self.epsT, r8 = T("eps", [128, 1], F32)
        self.zero_f, r9 = T("zero_f", [128, 128], F32)
        self.r_const = self.newreg("const")
        rc = self.r_const
        ld = lambda t, src: P.dma("sp", lambda e: e.dma_start(out=t[:], in_=src), rc, True)
        ld(self.ident_f, self.c_ident)
        ld(self.trile, self.c_trile)
        ld(self.trigt, self.c_trigt)
        ld(self.dtab, self.c_dtab)
        ld(self.kval, self.m_kval)
        ld(self.nval, self.m_nval)
        ld(self.normw_s, self.normw)
        P.dma("pool", lambda e: e.dma_start(out=self.ident_b[:], in_=self.c_ident), rc, True)
        P.dma("pool", lambda e: e.dma_start(out=self.tokv[:], in_=self.m_tok), rc, True)
        P.op("dve", lambda e: e.memset(self.ones_b[:], 1.0), [], [rc])
        P.op("dve", lambda e: e.memset(self.epsT[:], EPS), [], [rc])
        P.op("dve", lambda e: e.memset(self.zero_f[:], 0.0), [], [rc])
        for X in (self.XM, self.X1):
            Xv = X.rearrange("(c p) t -> p c t", p=128)
            for c in range(16):
                P.dma("sp", lambda e, c=c, Xv=Xv: e.dma_start(out=Xv[:, c, 0:128], in_=self.zero_f[:]), rc, False)
        self.ps = []
        self.psr = []
        for i in range(8):
            t = st.enter_context(nc.psum_tensor("psb%d" % i, [128, 512], F32))
            self.ps.append(t)
            self.psr.append(self.newreg("ps%d" % i))
        P.end_phase(persistent=True)

    def load_norm(self, st_tiles, X, col0, n, nw_off, dst_fn, dst_reg, bank=7):
        P = self.P
        xt, r_xt, sq, r_sq, rs, r_rs = st_tiles
        Xv = X.rearrange("(c p) t -> p c t", p=128)
        P.dma("sp", lambda e: e.dma_start(out=xt[:, :, 0:n], in_=Xv[:, :, col0:col0 + n]), r_xt, True)
        ps, pr = self.ps[bank], self.psr[bank]
        for c in range(16):
            P.op("act", lambda e, c=c: e.activation(out=sq[c % 2][:, 0:n], in_=xt[:, c, 0:n], func=AF.Square), [r_xt], [r_sq[c % 2]])
            P.op("pe", lambda e, c=c: e.matmul(ps[:, 0:n], lhsT=self.ones_b[:], rhs=sq[c % 2][:, 0:n], start=(c == 0), stop=(c == 15)), [r_sq[c % 2]], [pr])
        P.op("act", lambda e: e.activation(out=rs[:, 0:n], in_=ps[:, 0:n], func=AF.Sqrt, bias=self.epsT[:, 0:1], scale=1.0 / D), [pr], [r_rs])
        P.op("dve", lambda e: e.reciprocal(out=rs[:, 0:n], in_=rs[:, 0:n]), [r_rs], [r_rs])
        P.op("dve", lambda e: e.tensor_tensor(out=rs[:, 0:n], in0=rs[:, 0:n], in1=self.tokv[:, col0:col0 + n], op=ALU.mult), [r_rs], [r_rs])
        for c in range(16):
            P.op("dve", lambda e, c=c: e.scalar_tensor_tensor(out=dst_fn(c), in0=xt[:, c, 0:n], scalar=self.normw_s[:, nw_off + c:nw_off + c + 1],
                                                              in1=rs[:, 0:n], op0=ALU.mult, op1=ALU.mult), [r_xt, r_rs], [dst_reg])

    def norm_tiles(self, st):
        xt, r_xt = self.tile(st, "xt", [128, 16, 256], F32)
        sq0, r0 = self.tile(st, "sq0", [128, 256], BF16)
        sq1, r1 = self.tile(st, "sq1", [128, 256], BF16)
        rs, r_rs = self.tile(st, "rs", [128, 256], F32)
        return (xt, r_xt, [sq0, sq1], [r0, r1], rs, r_rs)

    def fill_hT(self, st_tiles, X, col0, ntok, nw_off, hT, hregs):
        for i, t0 in enumerate(range(0, ntok, 256)):
            n = min(256, ntok - t0)
            self.load_norm(st_tiles, X, col0 + t0, n, nw_off, lambda c, t0=t0, n=n: hT[:, c, t0:t0 + n], hregs[i])

    def wload(self, wbuf, wreg, Wsrc, kc, col0, ncols, off=0, k0=0):
        view = wbuf[:, off:off + kc * ncols].rearrange("p (k n) -> p k n", k=kc)
        src = Wsrc.rearrange("(k p) n -> p k n", p=128)[:, k0:k0 + kc, col0:col0 + ncols]
        self.P.dma("pool", lambda e: e.dma_start(out=view, in_=src), wreg, True)
        return view

    def run_jobs(self, jobs):
        loads = [i for i, (lf, _) in enumerate(jobs) if lf is not None]
        if loads:
            jobs[loads[0]][0](0)
        li = 0
        for i, (lf, cf) in enumerate(jobs):
            if lf is None:
                cf(None)
                continue
            if li + 1 < len(loads):
                jobs[loads[li + 1]][0]((li + 1) % 2)
            cf(li % 2)
            li += 1

    def post_norm(self, st, mixed, r_mixed, ss_bank, n, nw_off, Xin, Xout, cin0, cout0, tl):
        P = self.P
        rs2, r_rs2, xr, r_xr, ot, r_ot, tt, r_tt = tl
        ps, pr = self.ps[ss_bank], self.psr[ss_bank]
        P.op("act", lambda e: e.activation(out=rs2[:, 0:n], in_=ps[:, 0:n], func=AF.Sqrt, bias=self.epsT[:, 0:1], scale=1.0 / D), [pr], [r_rs2])
        P.op("dve", lambda e: e.reciprocal(out=rs2[:, 0:n], in_=rs2[:, 0:n]), [r_rs2], [r_rs2])
        Xi = Xin.rearrange("(c p) t -> p c t", p=128)
        Xo = Xout.rearrange("(c p) t -> p c t", p=128)
        for c in range(16):
            b = c % 2
            P.dma("sp", lambda e, c=c, b=b: e.dma_start(out=xr[b][:, 0:n], in_=Xi[:, c, cin0:cin0 + n]), r_xr[b], True)
            P.op("dve", lambda e, c=c, b=b: e.tensor_tensor(out=tt[b][:, 0:n], in0=mixed[:, c, 0:n], in1=rs2[:, 0:n], op=ALU.mult), [r_mixed, r_rs2], [r_tt[b]])
            P.op("dve", lambda e, c=c, b=b: e.scalar_tensor_tensor(out=ot[b][:, 0:n], in0=tt[b][:, 0:n], scalar=self.normw_s[:, nw_off + c:nw_off + c + 1],
                                                                   in1=xr[b][:, 0:n], op0=ALU.mult, op1=ALU.add), [r_tt[b], r_xr[b]], [r_ot[b]])
            P.dma("sp", lambda e, c=c, b=b: e.dma_start(out=Xo[:, c, cout0:cout0 + n], in_=ot[b][:, 0:n]), r_ot[b], False)

    def post_tiles(self, st):
        rs2, r_rs2 = self.tile(st, "rs2", [128, 512], F32)
        xr = []; r_xr = []; ot = []; r_ot = []; tt = []; r_tt = []
        for b in range(2):
            a, ra = self.tile(st, "xr", [128, 512], F32); xr.append(a); r_xr.append(ra)
            a, ra = self.tile(st, "ot", [128, 512], F32); ot.append(a); r_ot.append(ra)
            a, ra = self.tile(st, "tt", [128, 512], F32); tt.append(a); r_tt.append(ra)
        return (rs2, r_rs2, xr, r_xr, ot, r_ot, tt, r_tt)

    def tok_tiles(self, start, ntok, step=512):
        return [(t0, min(step, start + ntok - t0)) for t0 in range(start, start + ntok, step)]

    def phase_kv(self, l, X):
        P = self.P
        with ExitStack() as st:
            nt = self.norm_tiles(st)
            wkv, r_wkv = self.tile(st, "wkv", [128, 16 * 1536], BF16)
            wv = wkv[:].rearrange("p (k n) -> p k n", k=16)
            Wl = self.w_in[l]
            for j in range(3):
                src = Wl.rearrange("(k p) n -> p k n", p=128)[:, :, C_KV + j * 512:C_KV + (j + 1) * 512]
                P.dma("pool", lambda e, j=j, src=src: e.dma_start(out=wv[:, :, j * 512:(j + 1) * 512], in_=src), r_wkv, True)
            hts = [self.tile(st, "hTt", [128, 16, 256], BF16) for _ in range(2)]
            ksts = [self.tile(st, "kst", [128, 8, 256], BF16) for _ in range(2)]
            vsts = [self.tile(st, "vst", [128, 2, 512], BF16) for _ in range(2)]
            fm_off = [0, 256, 512, 1024]
            tm_off = [768, 1280]
            for ti, t0 in enumerate(range(0, SEQ, 256)):
                hT, r_h = hts[ti % 2]
                kst, r_k = ksts[ti % 2]
                vst, r_v = vsts[ti % 2]
                col0 = PAD + t0
                self.load_norm(nt, X, col0, 256, (l * 4 + 0) * 16, lambda c, hT=hT: hT[:, c, :], r_h)
                for ty in range(4):
                    bk = ty % 4
                    ps, pr = self.ps[bk], self.psr[bk]
                    for half in range(2):
                        off = fm_off[ty] + half * 128
                        for kc in range(16):
                            P.op("pe", lambda e, ps=ps, half=half, off=off, kc=kc, hT=hT: e.matmul(ps[:, half * 256:(half + 1) * 256], lhsT=wv[:, kc, off:off + 128],
                                                                                                  rhs=hT[:, kc, :], start=(kc == 0), stop=(kc == 15)), [r_wkv, r_h], [pr])
                    P.op("act", lambda e, ps=ps, ty=ty, kst=kst: e.activation(out=kst[:, 2 * ty:2 * ty + 2, :], in_=ps[:].rearrange("p (h t) -> p h t", h=2), func=AF.Copy), [pr], [r_k])
                for ty in range(4):
                    dst = self.KT[ty].rearrange("(h p) t -> p h t", p=128)[:, :, col0:col0 + 256]
                    P.dma("sp", lambda e, ty=ty, dst=dst, kst=kst: e.dma_start(out=dst, in_=kst[:, 2 * ty:2 * ty + 2, :]), r_k, False, [self.R("KT")])
                for sub in range(2):
                    ps, pr = self.ps[4 + sub], self.psr[4 + sub]
                    for j in range(2):
                        for kc in range(16):
                            P.op("pe", lambda e, ps=ps, sub=sub, j=j, kc=kc, hT=hT: e.matmul(ps[:, j * 256:(j + 1) * 256], lhsT=hT[:, kc, sub * 128:(sub + 1) * 128],
                                                                                             rhs=wv[:, kc, tm_off[j]:tm_off[j] + 256], start=(kc == 0), stop=(kc == 15)), [r_wkv, r_h], [pr])
                    P.op("dve", lambda e, ps=ps, sub=sub, vst=vst: e.tensor_copy(out=vst[:, sub, :], in_=ps[:]), [pr], [r_v])
                for j in range(2):
                    dst = self.VT[j][col0:col0 + 256, :].rearrange("(s p) c -> p s c", p=128)
                    P.dma("sp", lambda e, j=j, dst=dst, vst=vst: e.dma_start(out=dst, in_=vst[:, :, j * 256:(j + 1) * 256]), r_v, False, [self.R("VT")])
            P.end_phase()

    def phase_m1(self, l, X, r0, r1):
        P = self.P
        T = r1 - r0
        TA = T + 128
        Wl = self.w_in[l]
        with ExitStack() as st:
            nt = self.norm_tiles(st)
            hT, _ = self.tile(st, "hT", [128, 16, TA], BF16)
            hregs = [self.newreg("hT") for _ in range((TA + 255) // 256)]
            wb = [self.tile(st, "wbuf", [128, 8192], BF16) for _ in range(2)]
            wvt, r_wvt = self.tile(st, "wvt", [128, 16 * 1024], BF16)
            ut, r_ut = self.tile(st, "ut", [128, 8, 128], BF16)
            sig = [self.tile(st, "sig", [128, 512], F32) for _ in range(2)]
            gst = [self.tile(st, "gst", [128, 512], BF16) for _ in range(2)]
            qst = [self.tile(st, "qst", [128, 512], BF16) for _ in range(2)]
            vg = [self.tile(st, "vg", [128, 1024], F32)] * 2
            vln = [self.tile(st, "vln", [128, 1024], BF16)] * 2
            bst, r_bst = self.tile(st, "bst", [128, 2, 6], F32)
            mv, r_mv = self.tile(st, "mv", [128, 2], F32)
            wsT, r_ws = self.tile(st, "wsT", [128, 8, 128], BF16)
            wsF, r_wsF = self.tile(st, "wsF", [128, 8, 128], BF16)
            lnbt, r_lnb = self.tile(st, "lnbt", [128, 2, 1024], F32)
            sgbr, r_sgb = self.tile(st, "sgbr", [1, 1024], BF16)
            gall, r_gall = self.tile(st, "gall", [128, T // 128, 48], F32)
            P.dma("sp", lambda e: e.dma_start(out=lnbt[:], in_=self.lnb[:, l * 2048:(l + 1) * 2048].rearrange("p (a c) -> p a c", a=2)), r_lnb, True)
            P.dma("pool", lambda e: e.dma_start(out=sgbr[:], in_=self.sgb[:, l * 1024:(l + 1) * 1024]), r_sgb, True)
            P.dma("pool", lambda e: e.dma_start(out=wsF[:], in_=self.sgw[l].rearrange("s (g t) -> s g t", g=8)), r_wsF, True)
            P.op("dve", lambda e: e.tensor_tensor(out=wsT[:], in0=wsF[:], in1=self.trile[:].unsqueeze(1).to_broadcast([128, 8, 128]), op=ALU.mult), [r_wsF], [r_ws])
            for j in range(2):
                src = Wl.rearrange("(k p) n -> p k n", p=128)[:, :, C_B + 1024 + j * 512:C_B + 1024 + (j + 1) * 512]
                P.dma("pool", lambda e, j=j, src=src: e.dma_start(out=wvt[:].rearrange("p (k n) -> p k n", k=16)[:, :, j * 512:(j + 1) * 512], in_=src), r_wvt, True)
            wv3 = wvt[:].rearrange("p (k n) -> p k n", k=16)
            self.fill_hT(nt, X, PAD + r0 - 128, TA, (l * 4 + 0) * 16, hT, hregs)
            tilesA = self.tok_tiles(0, TA)
            tilesR = self.tok_tiles(128, T)
            jobs = []
            psrot = [0]

            def nextps():
                b = psrot[0] % 4
                psrot[0] += 1
                return self.ps[b], self.psr[b]

            def mm16(ps, n, wview, c0, t0):
                for kc in range(16):
                    P.op("pe", lambda e, kc=kc: e.matmul(ps[:, 0:n], lhsT=wview[:, kc, c0:c0 + 128], rhs=hT[:, kc, t0:t0 + n], start=(kc == 0), stop=(kc == 15)),
                         [wview_reg[0]] + hregs, [ps_reg[0]])

            wview_reg = [None]
            ps_reg = [None]
            for jg in range(4):
                def lf(b, jg=jg):
                    self.wload(wb[b][0], wb[b][1], Wl, 16, C_A + jg * 256, 256, 0)
                    self.wload(wb[b][0], wb[b][1], Wl, 16, C_A + 1024 + jg * 256, 256, 16 * 256)

                def cf(b, jg=jg):
                    wa = wb[b][0][:, 0:4096].rearrange("p (k n) -> p k n", k=16)
                    wg = wb[b][0][:, 4096:8192].rearrange("p (k n) -> p k n", k=16)
                    wview_reg[0] = wb[b][1]
                    k = 0
                    for cc in range(2):
                        ch = jg * 2 + cc
                        for (t0, n) in tilesA:
                            pa, ra = nextps()
                            ps_reg[0] = ra
                            mm16(pa, n, wa, cc * 128, t0)
                            pg, rg = nextps()
                            ps_reg[0] = rg
                            mm16(pg, n, wg, cc * 128, t0)
                            s_, rs_ = sig[k % 2]
                            g_, rg_ = gst[k % 2]
                            k += 1
                            P.op("act", lambda e, pg=pg, s_=s_, n=n: e.activation(out=s_[:, 0:n], in_=pg[:, 0:n], func=AF.Sigmoid), [rg], [rs_])
                            P.op("dve", lambda e, pa=pa, s_=s_, g_=g_, n=n: e.tensor_tensor(out=g_[:, 0:n], in0=pa[:, 0:n], in1=s_[:, 0:n], op=ALU.mult), [ra, rs_], [rg_])
                            c0 = PAD + r0 - 128 + t0
                            P.dma("sp", lambda e, g_=g_, ch=ch, c0=c0, n=n: e.dma_start(out=self.GLU[ch * 128:(ch + 1) * 128, c0:c0 + n], in_=g_[:, 0:n]), rg_, False, [self.R("GLU")])
                jobs.append((lf, cf))
            ku = [0]
            for jg in range(2):
                def lf(b, jg=jg):
                    self.wload(wb[b][0], wb[b][1], Wl, 16, C_B + jg * 512, 512, 0)

                def cf(b, jg=jg):
                    wu = wb[b][0][:].rearrange("p (k n) -> p k n", k=16)
                    wview_reg[0] = wb[b][1]
                    for cc in range(4):
                        ch = jg * 4 + cc
                        for (t0, n) in tilesR:
                            pu, ru = nextps()
                            ps_reg[0] = ru
                            mm16(pu, n, wu, cc * 128, t0)
                            q_, rq_ = qst[ku[0] % 2]
                            ku[0] += 1
                            P.op("act", lambda e, pu=pu, q_=q_, n=n: e.activation(out=q_[:, 0:n], in_=pu[:, 0:n], func=AF.Gelu), [ru], [rq_])
                            c0 = PAD + r0 - 128 + t0
                            P.dma("sp", lambda e, q_=q_, ch=ch, c0=c0, n=n: e.dma_start(out=self.HB[ch * 128:(ch + 1) * 128, c0:c0 + n], in_=q_[:, 0:n]), rq_, False)
                jobs.append((lf, cf))
            for jg in range(2):
                def lf(b, jg=jg):
                    self.wload(wb[b][0], wb[b][1], Wl, 16, C_Q + jg * 512, 512, 0)

                def cf(b, jg=jg):
                    wq = wb[b][0][:].rearrange("p (k n) -> p k n", k=16)
                    wview_reg[0] = wb[b][1]
                    k = 0
                    for cc in range(4):
                        ch = jg * 4 + cc
                        for (t0, n) in tilesR:
                            pq, rq = nextps()
                            ps_reg[0] = rq
                            mm16(pq, n, wq, cc * 128, t0)
                            q_, rq_ = qst[k % 2]
                            k += 1
                            P.op("act", lambda e, pq=pq, q_=q_, n=n: e.activation(out=q_[:, 0:n], in_=pq[:, 0:n], func=AF.Copy, scale=0.125), [rq], [rq_])
                            c0 = PAD + r0 - 128 + t0
                            P.dma("sp", lambda e, q_=q_, ch=ch, c0=c0, n=n: e.dma_start(out=self.Q[ch * 128:(ch + 1) * 128, c0:c0 + n], in_=q_[:, 0:n]), rq_, False, [self.R("Q")])
                jobs.append((lf, cf))
            def lfg(b):
                self.wload(wb[b][0], wb[b][1], Wl, 16, C_G, 48, 0)

            def cfg(b):
                wg = wb[b][0][:, 0:16 * 48].rearrange("p (k n) -> p k n", k=16)
                for i in range(T // 128):
                    pg, rg = nextps()
                    t0 = 128 + i * 128
                    for kc in range(16):
                        P.op("pe", lambda e, kc=kc, pg=pg, t0=t0: e.matmul(pg[:, 0:48], lhsT=hT[:, kc, t0:t0 + 128], rhs=wg[:, kc, :], start=(kc == 0), stop=(kc == 15)),
                             [wb[b][1]] + hregs, [rg])
                    P.op("act", lambda e, pg=pg, i=i: e.activation(out=gall[:, i, :], in_=pg[:, 0:48], func=AF.Sigmoid), [rg], [r_gall])
                dst = self.GATES[PAD + r0:PAD + r1, :].rearrange("(n p) c -> p n c", p=128)
                P.dma("sp", lambda e: e.dma_start(out=dst, in_=gall[:]), r_gall, False, [self.R("GATES")])
            jobs.append((lfg, cfg))
            self.run_jobs(jobs)
            P.barrier()
            HBv = self.HB.rearrange("(c p) t -> p c t", p=128)
            for i in range(T // 128):
                t0 = 128 + i * 128
                cu = PAD + r0 + i * 128
                P.dma("sp", lambda e, cu=cu: e.dma_start(out=ut[:], in_=HBv[:, :, cu:cu + 128]), r_ut, True)
                vg_, rvg = vg[i % 2]
                vl_, rvl = vln[i % 2]
                for half in range(2):
                    pv, rv = self.ps[4 + half], self.psr[4 + half]
                    for kc in range(16):
                        P.op("pe", lambda e, kc=kc, pv=pv, half=half, t0=t0: e.matmul(pv[:], lhsT=hT[:, kc, t0:t0 + 128], rhs=wv3[:, kc, half * 512:(half + 1) * 512],
                                                                                     start=(kc == 0), stop=(kc == 15)), [r_wvt] + hregs, [rv])
                    P.op("act", lambda e, pv=pv, half=half, vg_=vg_: e.activation(out=vg_[:, half * 512:(half + 1) * 512], in_=pv[:], func=AF.Gelu), [rv], [rvg])
                for half in range(2):
                    P.op("dve", lambda e, half=half, vg_=vg_: e.bn_stats(out=bst[:, half, :], in_=vg_[:, half * 512:(half + 1) * 512]), [rvg], [r_bst])
                P.op("dve", lambda e: e.bn_aggr(out=mv[:], in_=bst[:].rearrange("p a s -> p (a s)")), [r_bst], [r_mv])
                P.op("act", lambda e: e.activation(out=mv[:, 1:2], in_=mv[:, 1:2], func=AF.Sqrt, bias=self.epsT[:, 0:1], scale=1.0), [r_mv], [r_mv])
                P.op("dve", lambda e: e.reciprocal(out=mv[:, 1:2], in_=mv[:, 1:2]), [r_mv], [r_mv])
                P.op("dve", lambda e, vg_=vg_: e.tensor_scalar(out=vg_[:], in0=vg_[:], scalar1=mv[:, 0:1], scalar2=mv[:, 1:2], op0=ALU.subtract, op1=ALU.mult), [rvg, r_mv], [rvg])
                P.op("dve", lambda e, vg_=vg_: e.tensor_tensor(out=vg_[:], in0=vg_[:], in1=lnbt[:, 0, :], op=ALU.mult), [rvg, r_lnb], [rvg])
                P.op("dve", lambda e, vg_=vg_, vl_=vl_: e.tensor_tensor(out=vl_[:], in0=vg_[:], in1=lnbt[:, 1, :], op=ALU.add), [rvg, r_lnb], [rvl])
                for hb in range(2):
                    pf, rf = self.ps[6 + hb], self.psr[6 + hb]
                    for gg in range(4):
                        g = hb * 4 + gg
                        P.op("pe", lambda e, pf=pf, gg=gg, g=g, vl_=vl_: e.matmul(pf[:, gg * 128:(gg + 1) * 128], lhsT=vl_[:, g * 128:(g + 1) * 128], rhs=wsT[:, g, :], start=True, stop=False),
                             [rvl, r_ws], [rf])
                        P.op("pe", lambda e, pf=pf, gg=gg, g=g: e.matmul(pf[:, gg * 128:(gg + 1) * 128], lhsT=self.ones_b[0:1, :], rhs=sgbr[0:1, g * 128:(g + 1) * 128], start=False, stop=True),
                             [r_sgb], [rf])
                    P.op("dve", lambda e, pf=pf, hb=hb: e.tensor_tensor(out=ut[:, hb * 4:(hb + 1) * 4, :], in0=ut[:, hb * 4:(hb + 1) * 4, :],
                                                                      in1=pf[:].rearrange("p (g t) -> p g t", g=4), op=ALU.mult), [rf, r_ut], [r_ut])
                P.dma("sp", lambda e, cu=cu: e.dma_start(out=HBv[:, :, cu:cu + 128], in_=ut[:]), r_ut, False)
            P.end_phase()

    def phase_m2(self, l, r0, r1):
        P = self.P
        T = r1 - r0
        TA = T + 128
        with ExitStack() as st:
            gl = [self.tile(st, "gl", [128, TA], BF16) for _ in range(3)]
            dgs = [self.tile(st, "dgs", [128, 31, 128], BF16) for _ in range(2)]
            cbf, r_cbf = self.tile(st, "cbf", [128, 8, T], BF16)
            cw, r_cw = self.tile(st, "cw", [128, 8, 31], F32)
            av, r_av = self.tile(st, "av", [128, 3, 8], F32)
            sq = [self.tile(st, "sq", [128, 512], BF16) for _ in range(2)]
            mean, r_mean = self.tile(st, "mean", [128, 512], F32)
            msq, r_msq = self.tile(st, "msq", [128, 512], F32)
            rstd, r_rstd = self.tile(st, "rstd", [128, 512], F32)
            t1 = [self.tile(st, "t1", [128, 512], F32) for _ in range(2)]
            hst = [self.tile(st, "hst", [128, 8, 512], BF16) for _ in range(2)]
            P.dma("sp", lambda e: e.dma_start(out=cw[:], in_=self.convaw[:, l * 248:(l + 1) * 248].rearrange("p (c k) -> p c k", c=8)), r_cw, True)
            P.dma("sp", lambda e: e.dma_start(out=av[:], in_=self.avec[:, l * 24:(l + 1) * 24].rearrange("p (a c) -> p a c", a=3)), r_av, True)
            for c in range(8):
                g_, rg = gl[c % 3]
                d_, rd_ = dgs[c % 2]
                c0 = PAD + r0 - 128
                P.dma("sp", lambda e: e.dma_start(out=g_[:], in_=self.GLU[c * 128:(c + 1) * 128, c0:c0 + TA]), rg, True)
                for k in range(31):
                    P.op("dve", lambda e: e.tensor_scalar(out=d_[:, k, :], in0=self.ident_f[:], scalar1=cw[:, c, k:k + 1], scalar2=None, op0=ALU.mult), [r_cw], [rd_])
                for ti, (t0, n) in enumerate(self.tok_tiles(0, T)):
                    bk = 2 + (c * 8 + ti) % 4
                    ps, pr = self.ps[bk], self.psr[bk]
                    for k in range(31):
                        P.op("pe", lambda e: e.matmul(ps[:, 0:n], lhsT=d_[:, k, :], rhs=g_[:, 98 + k + t0:98 + k + t0 + n], start=(k == 0), stop=(k == 30)), [rd_, rg], [pr])
                    P.op("act", lambda e: e.activation(out=cbf[:, c, t0:t0 + n], in_=ps[:, 0:n], func=AF.Identity, bias=av[:, 0, c:c + 1], scale=1.0), [pr, r_av], [r_cbf])
            for ti, (t0, n) in enumerate(self.tok_tiles(0, T)):
                pS, rS = self.ps[0], self.psr[0]
                pQ, rQ = self.ps[1], self.psr[1]
                for c in range(8):
                    s_, rs_ = sq[c % 2]
                    P.op("act", lambda e, s_=s_, c=c, t0=t0, n=n: e.activation(out=s_[:, 0:n], in_=cbf[:, c, t0:t0 + n], func=AF.Square), [r_cbf], [rs_])
                    P.op("pe", lambda e, c=c, t0=t0, n=n: e.matmul(pS[:, 0:n], lhsT=self.ones_b[:], rhs=cbf[:, c, t0:t0 + n], start=(c == 0), stop=(c == 7)), [r_cbf], [rS])
                    P.op("pe", lambda e, s_=s_, c=c, n=n: e.matmul(pQ[:, 0:n], lhsT=self.ones_b[:], rhs=s_[:, 0:n], start=(c == 0), stop=(c == 7)), [rs_], [rQ])
                P.op("dve", lambda e, n=n: e.tensor_scalar(out=mean[:, 0:n], in0=pS[:, 0:n], scalar1=1.0 / 1024, scalar2=None, op0=ALU.mult), [rS], [r_mean])
                P.op("dve", lambda e, n=n: e.tensor_tensor(out=msq[:, 0:n], in0=mean[:, 0:n], in1=mean[:, 0:n], op=ALU.mult), [r_mean], [r_msq])
                P.op("dve", lambda e, n=n: e.scalar_tensor_tensor(out=rstd[:, 0:n], in0=pQ[:, 0:n], scalar=1.0 / 1024, in1=msq[:, 0:n], op0=ALU.mult, op1=ALU.subtract), [rQ, r_msq], [r_rstd])
                P.op("act", lambda e, n=n: e.activation(out=rstd[:, 0:n], in_=rstd[:, 0:n], func=AF.Sqrt, bias=self.epsT[:, 0:1], scale=1.0), [r_rstd], [r_rstd])
                P.op("dve", lambda e, n=n: e.reciprocal(out=rstd[:, 0:n], in_=rstd[:, 0:n]), [r_rstd], [r_rstd])
                h_, rh = hst[ti % 2]
                for c in range(8):
                    t_, rt = t1[c % 2]
                    P.op("dve", lambda e, t_=t_, c=c, t0=t0, n=n: e.tensor_tensor(out=t_[:, 0:n], in0=cbf[:, c, t0:t0 + n], in1=mean[:, 0:n], op=ALU.subtract), [r_cbf, r_mean], [rt])
                    P.op("dve", lambda e, t_=t_, n=n: e.tensor_tensor(out=t_[:, 0:n], in0=t_[:, 0:n], in1=rstd[:, 0:n], op=ALU.mult), [rt, r_rstd], [rt])
                    P.op("act", lambda e, t_=t_, h_=h_, c=c, n=n: e.activation(out=h_[:, c, 0:n], in_=t_[:, 0:n], func=AF.Silu, bias=av[:, 2, c:c + 1], scale=av[:, 1, c:c + 1]), [rt, r_av], [rh])
                dst = self.HA.rearrange("(c p) t -> p c t", p=128)[:, :, PAD + r0 + t0:PAD + r0 + t0 + n]
                P.dma("sp", lambda e, dst=dst, h_=h_, n=n: e.dma_start(out=dst, in_=h_[:, :, 0:n]), rh, False, [self.R("HA")])
            P.end_phase()

    def phase_att(self, l, r0, r1):
        P = self.P
        T = r1 - r0
        nqt = T // 128
        qt0 = r0 // 128
        BIG = 30000.0
        with ExitStack() as st:
            kin = [self.tile(st, "kin", [128, SEQ], BF16) for _ in range(4)]
            vs, r_vs = self.tile(st, "vs", [128, 32, 65], BF16)
            vw, r_vw = self.tile(st, "vw", [128, 32, 65], BF16)
            qT, r_q = self.tile(st, "qT", [128, 4, T], BF16)
            gt, r_gt = self.tile(st, "gt", [128, nqt, 48], F32)
            mtab, r_mt = self.tile(st, "mtab", [128, 3, nqt, 64], F32)
            Et, r_E = self.tile(st, "Et", [128, 32, 128], BF16)
            smap, r_sm = self.tile(st, "smap", [128, 2, 64], BF16)
            w1 = [self.tile(st, "w1", [64, 32, 256], BF16) for _ in range(2)]
            w2 = [self.tile(st, "w2", [128, 2, 64], BF16) for _ in range(2)]
            pe = [self.tile(st, "pe", [64, 32], BF16) for _ in range(2)]
            cb = [self.tile(st, "cb", [128, 2], F32) for _ in range(2)]
            hid, r_hid = self.tile(st, "hid", [128, 2, 256], BF16)
            kcT, r_kc = self.tile(st, "kcT", [128, 256], BF16)
            rv, r_rv = self.tile(st, "rv", [128, 2, 64], BF16)
            pb = [self.tile(st, "pb", [128, 4, 128], BF16) for _ in range(4)]
            cm, r_cm = self.tile(st, "cm", [128, 128], F32)
            cmb = [self.tile(st, "cmb", [128, 4, 128], BF16) for _ in range(4)]
            trib = [self.tile(st, "trib", [128, 4, 128], BF16) for _ in range(2)]
            selb = [self.tile(st, "selb", [128, 4, 128], BF16) for _ in range(2)]
            negb, r_negb = self.tile(st, "negb", [128, 1], F32)
            sm = {}
            for nm, shp in (("den", [128, 4]), ("cg", [128, 4]), ("imp", [128, 64]), ("score", [128, 64]), ("wk", [128, 64]), ("sel", [128, 64]), ("m8", [128, 8]),
                            ("oacc", [128, 4, 64]), ("otmp", [128, 4, 64]), ("den2", [128, 4]), ("cg2", [128, 4])):
                sm[nm] = self.tile(st, nm, shp, F32)
            obf = [self.tile(st, "obf", [128, 256], BF16) for _ in range(2)]
            ost = [self.tile(st, "ost", [128, 2, 128], BF16) for _ in range(2)]
            P.dma("sp", lambda e: e.dma_start(out=gt[:], in_=self.GATES[PAD + r0:PAD + r1, :].rearrange("(n p) c -> p n c", p=128)), r_gt, True)
            for a_ in range(3):
                P.dma("sp", lambda e: e.dma_start(out=mtab[:, a_, :, :], in_=self.m_sel[a_].rearrange("p (q j) -> p q j", j=64)[:, qt0:qt0 + nqt, :]), r_mt, True)
            P.op("dve", lambda e: e.memset(Et[64:128], 0.0), [], [r_E])
            P.op("dve", lambda e: e.memset(qT[64:128], 0.0), [], [r_q])
            for ty in (2, 3):
                P.op("dve", lambda e: e.memset(kin[ty][0][64:128], 0.0), [], [kin[ty][1]])
            for j_ in range(2):
                P.op("dve", lambda e: e.memset(selb[j_][0][64:128], 0.0), [], [selb[j_][1]])
            P.dma("pool", lambda e: e.dma_start(out=Et[0:64], in_=self.c_E.rearrange("j (k c) -> j k c", c=128)), r_E, True)
            P.dma("pool", lambda e: e.dma_start(out=smap[:], in_=self.c_smap.rearrange("p (a j) -> p a j", a=2)), r_sm, True)
            for kv in range(2):
                i_ = l * 2 + kv
                P.dma("pool", lambda e: e.dma_start(out=w1[kv][0][:], in_=self.cw1[i_].rearrange("d (l h) -> d l h", l=32)), w1[kv][1], True)
                P.dma("pool", lambda e: e.dma_start(out=w2[kv][0][:], in_=self.cw2[i_].rearrange("(c p) d -> p c d", p=128)), w2[kv][1], True)
                P.dma("pool", lambda e: e.dma_start(out=pe[kv][0][:], in_=self.cpe[i_]), pe[kv][1], True)
            P.op("dve", lambda e: e.memset(vs[:, :, 64:65], 1.0), [], [r_vs])
            P.op("dve", lambda e: e.memset(vw[:, :, 64:65], 1.0), [], [r_vw])
            P.op("dve", lambda e: e.memset(hid[:], 0.0), [], [r_hid])
            P.op("dve", lambda e: e.memset(kcT[:], 0.0), [], [r_kc])
            P.op("dve", lambda e: e.memset(negb[:], -BIG), [], [r_negb])
            for ti_, tri in enumerate((self.trile, self.trigt)):
                P.op("dve", lambda e: e.tensor_scalar(out=trib[ti_][0][:], in0=tri[:].unsqueeze(1).to_broadcast([128, 4, 128]), scalar1=-1.0, scalar2=BIG, op0=ALU.add, op1=ALU.mult), [], [trib[ti_][1]])
            for kv in range(2):
                ps, pr = self.ps[0], self.psr[0]
                for hc in range(2):
                    for li in range(32):
                        P.op("pe", lambda e: e.matmul(ps[:, hc:hc + 1], lhsT=w1[kv][0][:, li, hc * 128:(hc + 1) * 128], rhs=pe[kv][0][:, li:li + 1],
                                                      start=(li == 0), stop=(li == 31)), [w1[kv][1], pe[kv][1]], [pr])
                P.op("act", lambda e: e.activation(out=cb[kv][0][:], in_=ps[:, 0:2], func=AF.Copy), [pr], [cb[kv][1]])
            PC, PS_, PW, T32, TBF = 3, 4, 5, 6, 7
            psbf = self.ps[TBF][:].bitcast(BF16)
            den, r_den = sm["den"]; cg, r_cg = sm["cg"]; imp, r_imp = sm["imp"]; score, r_sc = sm["score"]
            wk, r_wk = sm["wk"]; sel, r_sel = sm["sel"]; m8, r_m8 = sm["m8"]; oacc, r_oa = sm["oacc"]; otmp, r_ot = sm["otmp"]
            den2, r_den2 = sm["den2"]; cg2, r_cg2 = sm["cg2"]
            cnt = {"s": 0, "p": 0, "cmb": 0, "q": 0}
            for g in range(4):
                for ty in range(4):
                    P.dma("sp", lambda e: e.dma_start(out=kin[ty][0][0:64], in_=self.KT[ty][g * 64:(g + 1) * 64, PAD:PAD + SEQ]), kin[ty][1], True)
                P.dma("sp", lambda e: e.dma_start(out=vs[:, :, 0:64], in_=self.VT[0][PAD:PAD + SEQ, g * 64:(g + 1) * 64].rearrange("(n p) d -> p n d", p=128)), r_vs, True)
                P.dma("sp", lambda e: e.dma_start(out=vw[:, :, 0:64], in_=self.VT[1][PAD:PAD + SEQ, g * 64:(g + 1) * 64].rearrange("(n p) d -> p n d", p=128)), r_vw, True)
                P.op("dve", lambda e: e.tensor_tensor(out=vw[:], in0=vw[:], in1=self.kval[:, 0:32].unsqueeze(2).to_broadcast([128, 32, 65]), op=ALU.mult), [r_vw], [r_vw])
                P.dma("sp", lambda e: e.dma_start(out=qT[0:64], in_=self.Q[g * 256:(g + 1) * 256, PAD + r0:PAD + r1].rearrange("(h d) t -> d h t", d=64)), r_q, True)
                for kv in range(2):
                    src, rsrc = kin[kv]
                    for hc in range(2):
                        ps, pr = self.ps[hc], self.psr[hc]
                        for li in range(32):
                            P.op("pe", lambda e: e.matmul(ps[:, 0:255], lhsT=w1[kv][0][:, li, hc * 128:(hc + 1) * 128], rhs=src[0:64, li:li + 16 * 254 + 1:16],
                                                          start=(li == 0), stop=(li == 31)), [w1[kv][1], rsrc], [pr])
                        P.op("act", lambda e: e.activation(out=hid[:, hc, 0:255], in_=ps[:, 0:255], func=AF.Silu, bias=cb[kv][0][:, hc:hc + 1], scale=1.0), [pr, cb[kv][1]], [r_hid])
                    ps, pr = self.ps[2], self.psr[2]
                    if kv == 0:
                        for hc in range(2):
                            P.op("pe", lambda e: e.matmul(ps[0:64, 0:255], lhsT=w2[0][0][:, hc, :], rhs=hid[:, hc, 0:255], start=(hc == 0), stop=(hc == 1)), [w2[0][1], r_hid], [pr])
                        P.op("act", lambda e: e.activation(out=kcT[0:64, 0:255], in_=ps[0:64, 0:255], func=AF.Copy), [pr], [r_kc])
                    else:
                        for nt_ in range(2):
                            for hc in range(2):
                                P.op("pe", lambda e: e.matmul(ps[:, nt_ * 64:(nt_ + 1) * 64], lhsT=hid[:, hc, nt_ * 128:(nt_ + 1) * 128], rhs=w2[1][0][:, hc, :],
                                                              start=(hc == 0), stop=(hc == 1)), [w2[1][1], r_hid], [pr])
                        P.op("act", lambda e: e.activation(out=rv[:], in_=ps[:, 0:128].rearrange("p (a d) -> p a d", a=2), func=AF.Copy), [pr], [r_rv])
                pending = [None]
                for i in range(nqt):
                    qt = qt0 + i
                    qv = qT[:, :, i * 128:(i + 1) * 128]
                    gsl = gt[:, i, g * 12:(g + 1) * 12].rearrange("p (h b) -> p h b", b=3)
                    pc, rpc = self.ps[PC], self.psr[PC]
                    pso, rpso = self.ps[PS_], self.psr[PS_]
                    pwo, rpwo = self.ps[PW], self.psr[PW]
                    psov = pso[:, 0:260].rearrange("p (h d) -> p h d", h=4)
                    pwov = pwo[:, 0:260].rearrange("p (h d) -> p h d", h=4)
                    sb_, rsb_ = selb[cnt["q"] % 2]
                    ob_, rob_ = obf[cnt["q"] % 2]
                    os_, ros_ = ost[cnt["q"] % 2]
                    cnt["q"] += 1
                    nts = [0] if qt < 16 else [0, 1]
                    steps = []
                    for nt_ in nts:
                        c4, rc4 = cmb[cnt["cmb"] % 4]
                        cnt["cmb"] += 1
                        thr = float(128 * qt - 2048 * nt_ - 31)
                        P.op("dve", lambda e: e.tensor_scalar(out=cm[:], in0=self.dtab[:], scalar1=thr, scalar2=self.nval[:, nt_:nt_ + 1], op0=ALU.is_le, op1=ALU.mult), [], [r_cm])
                        P.op("dve", lambda e: e.tensor_scalar(out=c4[:], in0=cm[:].unsqueeze(1).to_broadcast([128, 4, 128]), scalar1=-1.0, scalar2=BIG, op0=ALU.add, op1=ALU.mult), [r_cm], [rc4])

                        def pv_c(p_, rp_, nt_=nt_):
                            first = (nt_ == nts[0])
                            for h in range(4):
                                P.op("pe", lambda e: e.matmul(pc[:, h * 64:(h + 1) * 64], lhsT=p_[:, h, :], rhs=rv[:, nt_, :], start=(first and h == 0), stop=True, skip_group_check=True), [rp_, r_rv], [rpc])
                                P.op("pe", lambda e: e.matmul(pc[:, 256 + h * 64:256 + (h + 1) * 64], lhsT=p_[:, h, :], rhs=smap[:, nt_, :], start=False, stop=True, skip_group_check=True), [rp_, r_sm], [rpc])
                        steps.append(("c", kcT[:, nt_ * 128:(nt_ + 1) * 128], r_kc, [(self.ident_b[:], c4[:], [rc4])], pv_c))
                    k0 = max(0, qt - 4)
                    for kt in range(k0, qt + 1):
                        biases = []
                        if kt == qt:
                            biases.append((self.ident_b[:], trib[0][0][:], [trib[0][1]]))
                        elif kt == qt - 4:
                            biases.append((self.ident_b[:], trib[1][0][:], [trib[1][1]]))

                        def pv_w(p_, rp_, kt=kt):
                            for h in range(4):
                                P.op("pe", lambda e: e.matmul(pwov[:, h, :], lhsT=p_[:, h, :], rhs=vw[:, kt, :], start=(kt == k0 and h == 0), stop=True, skip_group_check=True), [rp_, r_vw], [rpwo])
                        steps.append(("w", kin[3][0][:, kt * 128:(kt + 1) * 128], kin[3][1], biases, pv_w))
                    n_pre = len(steps)
                    for kt in range(0, qt + 1):
                        biases = [(Et[:, kt, :], sb_[:], [r_E, rsb_])]
                        if kt == qt:
                            biases.append((self.ident_b[:], trib[0][0][:], [trib[0][1]]))

                        def pv_s(p_, rp_, kt=kt):
                            for h in range(4):
                                P.op("pe", lambda e: e.matmul(psov[:, h, :], lhsT=p_[:, h, :], rhs=vs[:, kt, :], start=(kt == 0 and h == 0), stop=True, skip_group_check=True), [rp_, r_vs], [rpso])
                        steps.append(("s", kin[2][0][:, kt * 128:(kt + 1) * 128], kin[2][1], biases, pv_s))
                    N = len(steps)
                    sbank = {}

                    def emit_score(k):
                        kind, lhsT, lreg, biases, _ = steps[k]
                        bk = cnt["s"] % 3
                        cnt["s"] += 1
                        ps, pr = self.ps[bk], self.psr[bk]
                        sbank[k] = (ps, pr)
                        P.op("pe", lambda e: e.matmul(ps[:], lhsT=lhsT, rhs=qv, start=True, stop=(len(biases) == 0)), [lreg, r_q], [pr])
                        for bi, (bl, br, bregs) in enumerate(biases):
                            P.op("pe", lambda e: e.matmul(ps[:], lhsT=bl, rhs=br, start=False, stop=(bi == len(biases) - 1)), bregs, [pr])

                    def post_cmp_dve():
                        pc2 = pc[:, 256:512].rearrange("p (h j) -> p h j", h=4)
                        P.op("dve", lambda e: e.tensor_reduce(out=den[:], in_=pc2, axis=AX.X, op=ALU.add), [rpc], [r_den])
                        P.op("dve", lambda e: e.tensor_scalar(out=den[:], in0=den[:], scalar1=0.5, scalar2=1e-30, op0=ALU.mult, op1=ALU.max), [r_den], [r_den])
                        P.op("dve", lambda e: e.reciprocal(out=den[:], in_=den[:]), [r_den], [r_den])
                        P.op("dve", lambda e: e.tensor_scalar(out=imp[:], in0=pc2[:, 0, :], scalar1=den[:, 0:1], scalar2=None, op0=ALU.mult), [rpc, r_den], [r_imp])
                        for h in range(1, 4):
                            P.op("dve", lambda e: e.scalar_tensor_tensor(out=imp[:], in0=pc2[:, h, :], scalar=den[:, h:h + 1], in1=imp[:], op0=ALU.mult, op1=ALU.add), [rpc, r_den, r_imp], [r_imp])
                        P.op("dve", lambda e: e.tensor_tensor(out=score[:], in0=imp[:], in1=mtab[:, 0, i, :], op=ALU.mult), [r_imp, r_mt], [r_sc])
                        P.op("dve", lambda e: e.tensor_tensor(out=score[:], in0=score[:], in1=mtab[:, 1, i, :], op=ALU.add), [r_sc, r_mt], [r_sc])
                        P.op("dve", lambda e: e.max(out=m8[:], in_=score[:]), [r_sc], [r_m8])
                        P.op("dve", lambda e: e.match_replace(out=wk[:], in_to_replace=m8[:], in_values=score[:], imm_value=-2.0), [r_m8, r_sc], [r_wk])
                        P.op("dve", lambda e: e.max(out=m8[:], in_=wk[:]), [r_wk], [r_m8])
                        P.op("dve", lambda e: e.match_replace(out=wk[:], in_to_replace=m8[:], in_values=wk[:], imm_value=-2.0), [r_m8, r_wk], [r_wk])
                        P.op("dve", lambda e: e.tensor_tensor(out=sel[:], in0=score[:], in1=wk[:], op=ALU.subtract), [r_sc, r_wk], [r_sel])
                        P.op("dve", lambda e: e.scalar_tensor_tensor(out=sel[:], in0=sel[:], scalar=1.0, in1=mtab[:, 2, i, :], op0=ALU.min, op1=ALU.mult), [r_sel, r_mt], [r_sel])
                        P.op("dve", lambda e: e.tensor_tensor(out=cg[:], in0=den[:], in1=gsl[:, :, 0], op=ALU.mult), [r_den, r_gt], [r_cg])
                        P.op("dve", lambda e: e.tensor_tensor(out=oacc[:], in0=pc[:, 0:256].rearrange("p (h d) -> p h d", h=4), in1=cg[:].unsqueeze(2).to_broadcast([128, 4, 64]), op=ALU.mult), [rpc, r_cg], [r_oa])

                    def pre_slc():
                        pt, rpt = self.ps[T32], self.psr[T32]
                        P.op("pe", lambda e: e.transpose(out=pt[0:64, 0:128], in_=sel[:], identity=self.ident_f[:]), [r_sel], [rpt])
                        P.op("act", lambda e: e.activation(out=sb_[0:64], in_=pt[0:64, 0:128].unsqueeze(1).to_broadcast([64, 4, 128]), func=AF.Identity, bias=negb[0:64, 0:1], scale=BIG), [rpt, r_negb], [rsb_])

                    LA = getattr(self, "lookahead", 0)
                    if LA:
                        emit_score(0)
                    for k in range(N):
                        if not LA:
                            if k == n_pre:
                                pre_slc()
                            emit_score(k)
                        elif k + 1 < N:
                            if k + 1 == n_pre:
                                pre_slc()
                            emit_score(k + 1)
                        ps, pr = sbank.pop(k)
                        p_, rp_ = pb[cnt["p"] % 4]
                        cnt["p"] += 1
                        P.op("act", lambda e: e.activation(out=p_[:], in_=ps[:].rearrange("p (h q) -> p h q", h=4), func=AF.Exp), [pr], [rp_])
                        steps[k][4](p_, rp_)
                        if k == len(nts) - 1:
                            post_cmp_dve()
                            if pending[0] is not None:
                                pending[0]()
                                pending[0] = None
                    P.op("dve", lambda e: e.tensor_scalar(out=den2[:], in0=psov[:, :, 64], scalar1=1e-30, scalar2=None, op0=ALU.max), [rpso], [r_den2])
                    P.op("dve", lambda e: e.reciprocal(out=den2[:], in_=den2[:]), [r_den2], [r_den2])
                    P.op("dve", lambda e: e.tensor_tensor(out=cg2[:], in0=den2[:], in1=gsl[:, :, 1], op=ALU.mult), [r_den2, r_gt], [r_cg2])
                    P.op("dve", lambda e: e.tensor_tensor(out=otmp[:], in0=psov[:, :, 0:64], in1=cg2[:].unsqueeze(2).to_broadcast([128, 4, 64]), op=ALU.mult), [rpso, r_cg2], [r_ot])
                    P.op("dve", lambda e: e.tensor_tensor(out=oacc[:], in0=oacc[:], in1=otmp[:], op=ALU.add), [r_oa, r_ot], [r_oa])
                    P.op("dve", lambda e: e.tensor_scalar(out=den2[:], in0=pwov[:, :, 64], scalar1=1e-30, scalar2=None, op0=ALU.max), [rpwo], [r_den2])
                    P.op("dve", lambda e: e.reciprocal(out=den2[:], in_=den2[:]), [r_den2], [r_den2])
                    P.op("dve", lambda e: e.tensor_tensor(out=cg2[:], in0=den2[:], in1=gsl[:, :, 2], op=ALU.mult), [r_den2, r_gt], [r_cg2])
                    P.op("dve", lambda e: e.tensor_tensor(out=otmp[:], in0=pwov[:, :, 0:64], in1=cg2[:].unsqueeze(2).to_broadcast([128, 4, 64]), op=ALU.mult), [rpwo, r_cg2], [r_ot])
                    P.op("dve", lambda e: e.tensor_tensor(out=ob_[:].rearrange("p (h d) -> p h d", h=4), in0=oacc[:], in1=otmp[:], op=ALU.add), [r_oa, r_ot], [rob_])

                    def post_pe(i=i, ob_=ob_, rob_=rob_, os_=os_, ros_=ros_):
                        rptb = self.psr[TBF]
                        for half in range(2):
                            P.op("pe", lambda e: e.transpose(out=psbf[:, half * 128:(half + 1) * 128], in_=ob_[:, half * 128:(half + 1) * 128], identity=self.ident_b[:]), [rob_], [rptb])
                        P.op("act", lambda e: e.activation(out=os_[:], in_=psbf[:, 0:256].rearrange("p (a q) -> p a q", a=2), func=AF.Copy), [rptb], [ros_])
                        c0 = PAD + r0 + i * 128
                        dst = self.OC[g * 256:(g + 1) * 256, c0:c0 + 128].rearrange("(a p) t -> p a t", p=128)
                        P.dma("sp", lambda e: e.dma_start(out=dst, in_=os_[:]), ros_, False)
                    pending[0] = post_pe
                if pending[0] is not None:
                    pending[0]()
                    pending[0] = None
            P.end_phase()

    def phase_merge(self, l, X, Xout, r0, r1):
        P = self.P
        Wl = self.w_in[l]
        with ExitStack() as st:
            nt = self.norm_tiles(st)
            pt = self.post_tiles(st)
            hT, r_h = self.tile(st, "hT", [128, 16, 512], BF16)
            ins = [self.tile(st, "hin", [128, 8, 512], BF16) for _ in range(3)]
            wb = [self.tile(st, "wbuf", [128, 8192], BF16) for _ in range(2)]
            mT, r_m = self.tile(st, "mT", [128, 16, 512], BF16)
            mixed, r_mx = self.tile(st, "mixed", [128, 16, 512], F32)
            sgs = [self.tile(st, "sgs", [128, 4, 512], BF16) for _ in range(2)]
            accm, r_accm = self.tile(st, "accm", [128, 4, 512], F32)
            tmp = [self.tile(st, "tmp", [128, 512], F32) for _ in range(2)]
            sq = [self.tile(st, "sq", [128, 512], BF16) for _ in range(2)]
            srcs = [self.HA, self.HB, self.OC]
            wouts = [self.w_a_out[l], self.w_b_out[l], self.w_c_out[l]]
            hregs = [self.newreg("hT"), self.newreg("hT")]
            sched = []
            pk = [0]

            def nb():
                bk = pk[0] % 6
                pk[0] += 1
                return self.ps[bk], self.psr[bk]
            for (s0, n) in self.tok_tiles(r0, r1 - r0):
                sc_ = {}
                sched.append(sc_)

                def nfn(b, s0=s0, n=n):
                    self.fill_hT(nt, X, PAD + s0, n, (l * 4 + 0) * 16, hT, hregs)
                    for b3 in range(3):
                        P.dma("sp", lambda e: e.dma_start(out=ins[b3][0][:, :, 0:n], in_=srcs[b3].rearrange("(c p) t -> p c t", p=128)[:, :, PAD + s0:PAD + s0 + n]),
                              ins[b3][1], True)
                sc_["N"] = [(None, nfn)]
                jobs = []
                for dg in range(4):
                    for b3 in range(3):
                        def lfg(b, dg=dg, b3=b3):
                            self.wload(wb[b][0], wb[b][1], Wl, 16, C_M + b3 * 2048 + dg * 512, 512, 0)

                        def cfg(b, dg=dg, b3=b3, n=n, hregs=hregs):
                            wbt, rwb = wb[b]
                            wg = wbt[:, 0:8192].rearrange("p (k n) -> p k n", k=16)
                            s_, rs_ = sgs[b3 % 2]
                            for cc in range(4):
                                pg, rg = nb()
                                for kc in range(16):
                                    P.op("pe", lambda e: e.matmul(pg[:, 0:n], lhsT=wg[:, kc, cc * 128:(cc + 1) * 128], rhs=hT[:, kc, 0:n], start=(kc == 0), stop=(kc == 15)), [rwb] + hregs, [rg])
                                P.op("act", lambda e: e.activation(out=s_[:, cc, 0:n], in_=pg[:, 0:n], func=AF.Sigmoid), [rg], [rs_])
                        jobs.append((lfg, cfg))

                        def lfy(b, dg=dg, b3=b3):
                            self.wload(wb[b][0], wb[b][1], wouts[b3], 8, dg * 512, 512, 0)

                        def cfy(b, dg=dg, b3=b3, n=n):
                            wbt, rwb = wb[b]
                            wy = wbt[:, 0:4096].rearrange("p (k n) -> p k n", k=8)
                            s_, rs_ = sgs[b3 % 2]
                            for cc in range(4):
                                py, ry = nb()
                                for kc in range(8):
                                    P.op("pe", lambda e: e.matmul(py[:, 0:n], lhsT=wy[:, kc, cc * 128:(cc + 1) * 128], rhs=ins[b3][0][:, kc, 0:n], start=(kc == 0), stop=(kc == 7)), [rwb, ins[b3][1]], [ry])
                                if b3 == 0:
                                    P.op("dve", lambda e: e.tensor_tensor(out=accm[:, cc, 0:n], in0=py[:, 0:n], in1=s_[:, cc, 0:n], op=ALU.mult), [ry, rs_], [r_accm])
                                else:
                                    t_, rt_ = tmp[cc % 2]
                                    P.op("dve", lambda e: e.tensor_tensor(out=t_[:, 0:n], in0=py[:, 0:n], in1=s_[:, cc, 0:n], op=ALU.mult), [ry, rs_], [rt_])
                                    if b3 == 1:
                                        P.op("dve", lambda e: e.tensor_tensor(out=accm[:, cc, 0:n], in0=accm[:, cc, 0:n], in1=t_[:, 0:n], op=ALU.add), [r_accm, rt_], [r_accm])
                                    else:
                                        P.op("dve", lambda e: e.tensor_tensor(out=mT[:, dg * 4 + cc, 0:n], in0=accm[:, cc, 0:n], in1=t_[:, 0:n], op=ALU.add), [r_accm, rt_], [r_m])
                        jobs.append((lfy, cfy))
                sc_["A"] = jobs
                jobs = []
                for jg in range(4):
                    def lf(b, jg=jg):
                        self.wload(wb[b][0], wb[b][1], self.w_o[l], 16, jg * 512, 512, 0)

                    def cf(b, jg=jg, n=n):
                        wo = wb[b][0][:, 0:8192].rearrange("p (k n) -> p k n", k=16)
                        for cc in range(4):
                            dch = jg * 4 + cc
                            po, ro = self.ps[dch % 4], self.psr[dch % 4]
                            for kc in range(16):
                                P.op("pe", lambda e, kc=kc, po=po, cc=cc: e.matmul(po[:, 0:n], lhsT=wo[:, kc, cc * 128:(cc + 1) * 128], rhs=mT[:, kc, 0:n], start=(kc == 0), stop=(kc == 15)), [wb[b][1], r_m], [ro])
                            s_, rs_ = sq[dch % 2]
                            P.op("act", lambda e, po=po, dch=dch: e.activation(out=mixed[:, dch, 0:n], in_=po[:, 0:n], func=AF.Copy), [ro], [r_mx])
                            P.op("act", lambda e, po=po, s_=s_: e.activation(out=s_[:, 0:n], in_=po[:, 0:n], func=AF.Square), [ro], [rs_])
                            P.op("pe", lambda e, s_=s_, dch=dch: e.matmul(self.ps[6][:, 0:n], lhsT=self.ones_b[:], rhs=s_[:, 0:n], start=(dch == 0), stop=(dch == 15)), [rs_], [self.psr[6]])
                    jobs.append((lf, cf))
                sc_["B"] = jobs
                sc_["P"] = [(None, lambda b, s0=s0, n=n: self.post_norm(st, mixed, r_mx, 6, n, (l * 4 + 1) * 16, X, Xout, PAD + s0, PAD + s0, pt))]
            nt_ = len(sched)
            seq = sched[0]["N"] + sched[0]["A"]
            for t in range(nt_):
                if t + 1 < nt_:
                    seq += sched[t + 1]["N"]
                seq += sched[t]["B"]
                if t + 1 < nt_:
                    seq += sched[t + 1]["A"][:4] + sched[t]["P"] + sched[t + 1]["A"][4:]
                else:
                    seq += sched[t]["P"]
            self.run_jobs(seq)
            P.end_phase()

    def phase_ffn(self, l, X, Xout, f0, f1, out_col0):
        P = self.P
        Wu = self.w_up[l]
        Wd = self.w_down[l]
        with ExitStack() as st:
            nt = self.norm_tiles(st)
            pt = self.post_tiles(st)
            hT, _ = self.tile(st, "hT", [128, 16, 512], BF16)
            wb = [self.tile(st, "wbuf", [128, 8192], BF16) for _ in range(2)]
            act, r_act = self.tile(st, "act", [128, 44, 512], BF16)
            mixed, r_mx = self.tile(st, "mixed", [128, 16, 512], F32)
            pre = [self.tile(st, "pre", [128, 514], F32) for _ in range(4)]
            uu = [self.tile(st, "uu", [128, 512], F32) for _ in range(4)]
            sgf, r_sgf = self.tile(st, "sgf", [128, 4, 512], F32)
            sq = [self.tile(st, "sq", [128, 512], BF16) for _ in range(2)]
            carry, r_carry = self.tile(st, "carry", [128, 88, 2], F32)
            fw, r_fw = self.tile(st, "fw", [128, 88, 3], F32)
            fb, r_fb = self.tile(st, "fb", [128, 88], F32)
            P.dma("sp", lambda e: e.dma_start(out=fw[:], in_=self.ffw[:, l * 264:(l + 1) * 264].rearrange("p (c k) -> p c k", k=3)), r_fw, True)
            P.dma("sp", lambda e: e.dma_start(out=fb[:], in_=self.ffb[:, l * 88:(l + 1) * 88]), r_fb, True)
            tiles = [(f0 - 2, 2)] + self.tok_tiles(f0, f1 - f0)
            pk = [0]
            hregs = [self.newreg("hT"), self.newreg("hT")]
            sched = {}
            for tix, (s0, n) in enumerate(tiles):
                halo = (tix == 0)
                sched[tix] = {}
                sched[tix]["N"] = [(None, lambda b, s0=s0, n=n: self.fill_hT(nt, X, PAD + s0, n, (l * 4 + 2) * 16, hT, hregs))]
                jobs = []
                for grp in range(11):
                    for gv in range(2):
                        def lf(b, grp=grp, gv=gv):
                            self.wload(wb[b][0], wb[b][1], Wu, 16, gv * DFF + grp * 512, 512, 0)

                        def cf(b, grp=grp, gv=gv, n=n, halo=halo, hregs=hregs):
                            wbt, rwb = wb[b]
                            wv_ = wbt[:, 0:8192].rearrange("p (k n) -> p k n", k=16)
                            for cc in range(4):
                                jg = grp * 4 + cc
                                j = jg + 44 * gv
                                bk = pk[0] % 6
                                pk[0] += 1
                                ps, pr = self.ps[bk], self.psr[bk]
                                for kc in range(16):
                                    P.op("pe", lambda e: e.matmul(ps[:, 0:n], lhsT=wv_[:, kc, cc * 128:(cc + 1) * 128], rhs=hT[:, kc, 0:n], start=(kc == 0), stop=(kc == 15)),
                                         [rwb] + hregs, [pr])
                                if halo:
                                    P.op("act", lambda e: e.activation(out=carry[:, j, :], in_=ps[:, 0:2], func=AF.Copy), [pr], [r_carry])
                                    continue
                                p_, rp_ = pre[cc % 4]
                                u_, ru_ = uu[cc % 4]
                                P.op("act", lambda e: e.activation(out=p_[:, 2:2 + n], in_=ps[:, 0:n], func=AF.Copy), [pr], [rp_])
                                P.op("act", lambda e: e.activation(out=p_[:, 0:2], in_=carry[:, j, :], func=AF.Copy), [r_carry], [rp_])
                                P.op("act", lambda e: e.activation(out=carry[:, j, :], in_=p_[:, n:n + 2], func=AF.Copy), [rp_], [r_carry])
                                P.op("dve", lambda e: e.tensor_scalar(out=u_[:, 0:n], in0=p_[:, 2:2 + n], scalar1=fw[:, j, 2:3], scalar2=fb[:, j:j + 1], op0=ALU.mult, op1=ALU.add), [rp_, r_fw, r_fb], [ru_])
                                P.op("dve", lambda e: e.scalar_tensor_tensor(out=u_[:, 0:n], in0=p_[:, 1:1 + n], scalar=fw[:, j, 1:2], in1=u_[:, 0:n], op0=ALU.mult, op1=ALU.add), [rp_, ru_], [ru_])
                                P.op("dve", lambda e: e.scalar_tensor_tensor(out=u_[:, 0:n], in0=p_[:, 0:n], scalar=fw[:, j, 0:1], in1=u_[:, 0:n], op0=ALU.mult, op1=ALU.add), [rp_, ru_], [ru_])
                                if gv == 0:
                                    P.op("act", lambda e: e.activation(out=sgf[:, cc, 0:n], in_=u_[:, 0:n], func=AF.Silu), [ru_], [r_sgf])
                                else:
                                    P.op("dve", lambda e: e.tensor_tensor(out=act[:, jg, 0:n], in0=sgf[:, cc, 0:n], in1=u_[:, 0:n], op=ALU.mult), [r_sgf, ru_], [r_act])
                        jobs.append((lf, cf))
                sched[tix]["U"] = jobs
                jobs = []
                if not halo:
                    kranges = [(0, 16), (16, 16), (32, 12)]
                    for dg in range(4):
                        for kr, (k0_, nk) in enumerate(kranges):
                            def lf(b, dg=dg, k0_=k0_, nk=nk):
                                self.wload(wb[b][0], wb[b][1], Wd, nk, dg * 512, 512, 0, k0=k0_)

                            def cf(b, dg=dg, kr=kr, k0_=k0_, nk=nk, n=n):
                                wd = wb[b][0][:, 0:nk * 512].rearrange("p (k n) -> p k n", k=nk)
                                for cc in range(4):
                                    dch = dg * 4 + cc
                                    po, ro = self.ps[cc], self.psr[cc]
                                    for kc in range(nk):
                                        P.op("pe", lambda e: e.matmul(po[:, 0:n], lhsT=wd[:, kc, cc * 128:(cc + 1) * 128], rhs=act[:, k0_ + kc, 0:n], start=(kr == 0 and kc == 0), stop=(kr == 2 and kc == nk - 1)),
                                             [wb[b][1], r_act], [ro])
                                    if kr == 2:
                                        s_, rs_ = sq[dch % 2]
                                        P.op("act", lambda e: e.activation(out=mixed[:, dch, 0:n], in_=po[:, 0:n], func=AF.Copy), [ro], [r_mx])
                                        P.op("act", lambda e: e.activation(out=s_[:, 0:n], in_=po[:, 0:n], func=AF.Square), [ro], [rs_])
                                        P.op("pe", lambda e: e.matmul(self.ps[6][:, 0:n], lhsT=self.ones_b[:], rhs=s_[:, 0:n], start=(dch == 0), stop=(dch == 15)), [rs_], [self.psr[6]])
                            jobs.append((lf, cf))
                sched[tix]["D"] = jobs
                sched[tix]["P"] = [(None, lambda b, s0=s0, n=n: self.post_norm(st, mixed, r_mx, 6, n, (l * 4 + 3) * 16, X, Xout, PAD + s0, out_col0 + (s0 - f0), pt))]
            nt_ = len(tiles)
            seq = sched[0]["N"] + sched[0]["U"] + sched[1]["N"] + sched[1]["U"]
            for t in range(1, nt_):
                if t + 1 < nt_:
                    seq += sched[t + 1]["N"]
                seq += sched[t]["D"]
                if t + 1 < nt_:
                    seq += sched[t + 1]["U"][:4] + sched[t]["P"] + sched[t + 1]["U"][4:]
                else:
                    seq += sched[t]["P"]
            self.run_jobs(seq)
            P.end_phase()

    def build(self):
        ph = []
        ph.append(lambda: self.phase_kv(0, self.xT))
        for (r0, r1) in ((0, 2048), (2048, 4096)):
            ph.append(lambda r0=r0, r1=r1: self.phase_m1(0, self.xT, r0, r1))
            ph.append(lambda r0=r0, r1=r1: self.phase_m2(0, r0, r1))
            ph.append(lambda r0=r0, r1=r1: self.phase_att(0, r0, r1))
            ph.append(lambda r0=r0, r1=r1: self.phase_merge(0, self.xT, self.XM, r0, r1))
        ph.append(lambda: self.phase_ffn(0, self.XM, self.X1, 0, 4096, PAD))
        ph.append(lambda: self.phase_kv(1, self.X1))
        ph.append(lambda: self.phase_m1(1, self.X1, 1920, 4096))
        ph.append(lambda: self.phase_m2(1, 1920, 4096))
        ph.append(lambda: self.phase_att(1, 1920, 4096))
        ph.append(lambda: self.phase_merge(1, self.X1, self.XM, 1920, 4096))
        ph.append(lambda: self.phase_ffn(1, self.XM, self.OUT, 2048, 4096, 0))
        sel = self.stop if self.stop is not None else range(len(ph))
        for i in sel:
            ph[i]()
        self.P.barrier()
        self.P.emit()
        return self.nc


def _colvec(v, nchunk):
    return np.ascontiguousarray(v.reshape(nchunk, 128).T)


def make_inputs(inp):
    L = 2
    f = lambda a: np.ascontiguousarray(np.asarray(a, dtype=np.float32))
    shared = {}
    for k in ("w_in", "w_a_out", "w_b_out", "w_c_out", "w_o", "w_up", "w_down"):
        shared[k] = f(inp[k])
    nw = np.zeros((128, L * 4 * 16), np.float32)
    for l in range(L):
        for i, k in enumerate(("norm_mix_pre", "norm_mix_post", "norm_ffn_pre", "norm_ffn_post")):
            nw[:, (l * 4 + i) * 16:(l * 4 + i + 1) * 16] = _colvec(f(inp[k])[l], 16)
    shared["normw"] = nw
    caw = np.zeros((128, L * 8 * 31), np.float32)
    av = np.zeros((128, L * 3 * 8), np.float32)
    for l in range(L):
        w = f(inp["conv_a_w"])[l]
        caw[:, l * 248:(l + 1) * 248] = w.T.reshape(8, 128, 31).transpose(1, 0, 2).reshape(128, 248)
        for i, k in enumerate(("conv_a_b", "ln_a_g", "ln_a_b")):
            av[:, (l * 3 + i) * 8:(l * 3 + i + 1) * 8] = _colvec(f(inp[k])[l], 8)
    shared["convaw"] = caw
    shared["avec"] = av
    lnb = np.zeros((128, L * 2 * 1024), np.float32)
    for l in range(L):
        lnb[:, (l * 2) * 1024:(l * 2 + 1) * 1024] = f(inp["ln_b_g"])[l][None, :]
        lnb[:, (l * 2 + 1) * 1024:(l * 2 + 2) * 1024] = f(inp["ln_b_b"])[l][None, :]
    shared["lnb"] = lnb
    shared["sgw"] = np.ascontiguousarray(f(inp["sg_w"]).transpose(0, 3, 1, 2).reshape(L, 128, 1024))
    shared["sgb"] = np.ascontiguousarray(f(inp["sg_b"]).reshape(1, L * 1024))
    cw1 = np.zeros((L * 2, 64, 32 * 256), np.float32)
    cw2 = np.zeros((L * 2, 256, 64), np.float32)
    cpe = np.zeros((L * 2, 64, 32), np.float32)
    for l in range(L):
        for kv, s in enumerate(("k", "v")):
            cw1[l * 2 + kv] = f(inp["cmp_w1_" + s])[l].transpose(1, 0, 2).reshape(64, 32 * 256)
            cw2[l * 2 + kv] = f(inp["cmp_w2_" + s])[l]
            cpe[l * 2 + kv] = f(inp["cmp_pe_" + s])[l].T
    shared["cw1"], shared["cw2"], shared["cpe"] = cw1, cw2, cpe
    ffw = np.zeros((128, L * 88 * 3), np.float32)
    ffb = np.zeros((128, L * 88), np.float32)
    for l in range(L):
        w = f(inp["ffn_conv_w"])[l]
        ffw[:, l * 264:(l + 1) * 264] = w.T.reshape(88, 128, 3).transpose(1, 0, 2).reshape(128, 264)
        ffb[:, l * 88:(l + 1) * 88] = _colvec(f(inp["ffn_conv_b"])[l], 88)
    shared["ffw"], shared["ffb"] = ffw, ffb
    p = np.arange(128)
    shared["c_ident"] = np.eye(128, dtype=np.float32)
    shared["c_trile"] = (p[:, None] <= p[None, :]).astype(np.float32)
    shared["c_trigt"] = (p[:, None] > p[None, :]).astype(np.float32)
    shared["c_dtab"] = (16.0 * p[:, None] - p[None, :]).astype(np.float32)
    k = np.arange(4096)
    shared["c_E"] = (k[None, :] // 64 == np.arange(64)[:, None]).astype(np.float32)
    n = np.arange(256)
    sm = np.zeros((256, 64), np.float32)
    for nn in range(255):
        sm[nn, nn // 4] += 1.0
        sm[nn, (nn + 1) // 4] += 1.0
    shared["c_smap"] = np.ascontiguousarray(sm.reshape(2, 128, 64).transpose(1, 0, 2).reshape(128, 128))
    x = f(inp["x"])
    maps = []
    for b in range(4):
        for s in range(2):
            m = dict(shared)
            xT = np.zeros((D, NCOL), np.float32)
            tok = np.zeros((NCOL,), np.float32)
            if s == 1:
                xT[:, PAD:] = x[b].T
                tok[PAD:] = 1.0
                j0 = 0
            else:
                xT[:, PAD + 2048:] = x[b, :2048].T
                tok[PAD + 2048:] = 1.0
                j0 = 32
            m["xT"] = xT
            m["m_tok"] = np.ascontiguousarray(np.broadcast_to(tok[None, :], (128, NCOL)))
            kval = np.ones((128, 32), np.float32)
            nval = np.ones((128, 2), np.float32)
            if s == 0:
                kval[:, :16] = 0.0
                nval[:, 0] = 0.0
            m["m_kval"], m["m_nval"] = kval, nval
            t = np.arange(4096).reshape(32, 128).T
            cur = t // 64
            j = np.arange(64)[None, None, :]
            valid = (j <= cur[:, :, None]) & (j >= j0)
            forced = ((j == j0) | (j == cur[:, :, None]) | (j == cur[:, :, None] - 1)) & valid
            M1 = (valid & ~forced).astype(np.float32)
            M2 = np.where(forced, 1e4 + j, np.where(valid, 0.0, -1.0)).astype(np.float32)
            M3 = valid.astype(np.float32)
            m["m_sel"] = np.ascontiguousarray(np.stack([M1, M2, M3]).reshape(3, 128, 32 * 64))
            maps.append(m)
    return maps


_CACHE = {}


def kernel(**inputs):
    maps = make_inputs(inputs)
    if "nc" not in _CACHE:
        _CACHE["nc"] = Builder().build()
    nc = _CACHE["nc"]
    res = run_bass_kernel_spmd(nc, maps, core_ids=list(range(8)))
    out = np.zeros((4, SEQ, D), np.float32)
    for b in range(4):
        for s in range(2):
            o = res.results[b * 2 + s]["OUT"]
            out[b, s * 2048:(s + 1) * 2048, :] = o.T
    return out
```

```python
import numpy as np
from contextlib import ExitStack
import concourse.bass as bass
import concourse.mybir as mybir
from concourse.bass_utils import run_bass_kernel_spmd

F32 = mybir.dt.float32
BF16 = mybir.dt.bfloat16
AF = mybir.ActivationFunctionType
ALU = mybir.AluOpType
AX = mybir.AxisListType

ENGS = ("pe", "act", "dve", "pool", "sp")

D = 2048
SEQ = 4096
PAD = 128
NCOL = PAD + SEQ
NIN = 12848
DFF = 5632
EPS = 1e-6
C_A, C_B, C_Q, C_KV, C_G, C_M = 0, 2048, 4096, 5120, 6656, 6704


class Reg:
    __slots__ = ("name", "w", "r", "dkey", "dcount")

    def __init__(self, name):
        self.name = name
        self.w = None
        self.r = {}
        self.dkey = None
        self.dcount = 0


class _Rec:
    def __init__(self):
        self.call = None

    def __getattr__(self, name):
        def f(*a, **k):
            self.call = (name, a, k)
            return self
        return f


def _record(fn):
    r = _Rec()
    fn(r)
    assert r.call is not None
    return r.call


class Prog:
    def __init__(self, nc):
        self.nc = nc
        self.streams = {e: [] for e in ENGS}
        self.cnt = {e: 0 for e in ENGS}
        self.seen = {e: {} for e in ENGS}
        self.dsems = {}
        self.ndsem = 0
        self.semh = {}
        self.free_dkeys = []
        self.phase_keys = []

    def sb(self, stack, name, shape, dt):
        return stack.enter_context(self.nc.sbuf_tensor(name, list(shape), dt))

    def _waits(self, eng, deps):
        out = []
        seen = self.seen[eng]
        best = {}
        for (k, v) in deps:
            if best.get(k, -1) < v:
                best[k] = v
        for k, v in best.items():
            if k == eng:
                if eng == "pe":
                    continue
                if v <= self.cnt[eng] - 2:
                    continue
            if seen.get(k, -1) >= v:
                continue
            seen[k] = v
            out.append((k, v))
        return out

    def _deps(self, reads, writes):
        deps = []
        for r in reads:
            if r.w is not None:
                deps.append(r.w)
        for w in writes:
            if w.w is not None:
                deps.append(w.w)
            deps.extend(w.r.items())
        return deps

    def op(self, eng, fn, reads=(), writes=()):
        waits = self._waits(eng, self._deps(reads, writes))
        self.cnt[eng] += 1
        me = (eng, self.cnt[eng])
        self.streams[eng].append((waits, _record(fn), (eng, 1)))
        for r in reads:
            r.r[me[0]] = me[1]
        for w in writes:
            w.w = me
            w.r = {}
        return me

    def dma(self, q, fn, sb, load, dram=()):
        if sb.dkey is None:
            if self.free_dkeys:
                sb.dkey = self.free_dkeys.pop()
                sb.dcount = self.dsems[sb.dkey]
            else:
                sb.dkey = "d%d" % self.ndsem
                self.ndsem += 1
            self.phase_keys.append(sb.dkey)
        if load:
            deps = self._deps(dram, [sb])
        else:
            deps = self._deps([sb], dram)
        waits = self._waits(q, deps)
        sb.dcount += 16
        self.dsems[sb.dkey] = sb.dcount
        me = (sb.dkey, sb.dcount)
        self.streams[q].append((waits, _record(fn), (sb.dkey, 16)))
        if load:
            sb.w = me
            sb.r = {}
            for d in dram:
                d.r[me[0]] = me[1]
        else:
            sb.r[me[0]] = me[1]
            for d in dram:
                d.w = me
                d.r = {}
        return me

    def barrier(self):
        allk = [(e, self.cnt[e]) for e in ENGS if self.cnt[e] > 0]
        allk += list(self.dsems.items())
        for e in ENGS:
            waits = self._waits(e, [kv for kv in allk if kv[0] != e])
            if waits:
                self.streams[e].append((waits, None, None))

    def end_phase(self, persistent=False):
        self.barrier()
        if not persistent:
            self.free_dkeys.extend(self.phase_keys)
        self.phase_keys = []

    def emit(self):
        nc = self.nc
        keys = list(ENGS) + list(self.dsems.keys())
        with ExitStack() as st:
            for k in keys:
                self.semh[k] = st.enter_context(nc.semaphore("s_" + k))
            block = st.enter_context(nc.Block())
            semh = self.semh

            def run(e, stream):
                for waits, fn, inc in stream:
                    for (k, v) in waits:
                        e.wait_ge(semh[k], v)
                    if fn is not None:
                        name, a, k = fn
                        ins = getattr(e, name)(*a, **k)
                        ins.then_inc(semh[inc[0]], inc[1])

            @block.tensor
            def _(e):
                run(e, self.streams["pe"])

            @block.scalar
            def _(e):
                run(e, self.streams["act"])

            @block.vector
            def _(e):
                run(e, self.streams["dve"])

            @block.gpsimd
            def _(e):
                run(e, self.streams["pool"])

            @block.sync
            def _(e):
                run(e, self.streams["sp"])


class Builder:
    def __init__(self, debug=False, nlayers=2, stop=None):
        self.debug = debug
        self.stop = stop
        nc = bass.Bass("TRN2", target_bir_lowering=False)
        self.nc = nc
        self.P = Prog(nc)
        self.I = {}
        self.regs = {}
        self.gst = ExitStack()
        self._uid = 0
        self.declare_io()
        self.alloc_consts()

    def R(self, name):
        if name not in self.regs:
            self.regs[name] = Reg(name)
        return self.regs[name]

    def newreg(self, name):
        self._uid += 1
        return Reg("%s_%d" % (name, self._uid))

    def din(self, name, shape, dt=F32):
        t = self.nc.dram_tensor(name, list(shape), dt, kind="ExternalInput").ap()
        self.I[name] = t
        return t

    def dscr(self, name, shape, dt, out=False):
        kind = "ExternalOutput" if (out or self.debug) else "Internal"
        return self.nc.dram_tensor(name, list(shape), dt, kind=kind).ap()

    def tile(self, st, name, shape, dt):
        self._uid += 1
        t = self.P.sb(st, "%s_%d" % (name, self._uid), shape, dt)
        return t, self.newreg(name)

    def dump(self, name, tile_, reg, shape, dt=F32):
        if not self.debug or True:
            return
        d = self.nc.dram_tensor("dbg_" + name, list(shape), dt, kind="ExternalOutput").ap()
        self.P.dma("sp", lambda e: e.dma_start(out=d, in_=tile_[:]), reg, False)

    def declare_io(self):
        L = 2
        self.xT = self.din("xT", [D, NCOL])
        self.w_in = self.din("w_in", [L, D, NIN])
        self.w_a_out = self.din("w_a_out", [L, 1024, D])
        self.w_b_out = self.din("w_b_out", [L, 1024, D])
        self.w_c_out = self.din("w_c_out", [L, 1024, D])
        self.w_o = self.din("w_o", [L, D, D])
        self.w_up = self.din("w_up", [L, D, 2 * DFF])
        self.w_down = self.din("w_down", [L, DFF, D])
        self.normw = self.din("normw", [128, L * 4 * 16])
        self.convaw = self.din("convaw", [128, L * 8 * 31])
        self.avec = self.din("avec", [128, L * 3 * 8])
        self.lnb = self.din("lnb", [128, L * 2 * 1024])
        self.sgw = self.din("sgw", [L, 128, 8 * 128])
        self.sgb = self.din("sgb", [1, L * 1024])
        self.cw1 = self.din("cw1", [L * 2, 64, 32 * 256])
        self.cw2 = self.din("cw2", [L * 2, 256, 64])
        self.cpe = self.din("cpe", [L * 2, 64, 32])
        self.ffw = self.din("ffw", [128, L * 88 * 3])
        self.ffb = self.din("ffb", [128, L * 88])
        self.c_ident = self.din("c_ident", [128, 128])
        self.c_trile = self.din("c_trile", [128, 128])
        self.c_trigt = self.din("c_trigt", [128, 128])
        self.c_dtab = self.din("c_dtab", [128, 128])
        self.c_E = self.din("c_E", [64, 32 * 128])
        self.c_smap = self.din("c_smap", [128, 2 * 64])
        self.m_tok = self.din("m_tok", [128, NCOL])
        self.m_kval = self.din("m_kval", [128, 32])
        self.m_nval = self.din("m_nval", [128, 2])
        self.m_sel = self.din("m_sel", [3, 128, 32 * 64])
        self.XM = self.dscr("XM", [D, NCOL], F32)
        self.X1 = self.dscr("X1", [D, NCOL], F32)
        self.OUT = self.dscr("OUT", [D, 2048], F32, out=True)
        self.GLU = self.dscr("GLU", [1024, NCOL], BF16)
        self.HA = self.dscr("HA", [1024, NCOL], BF16)
        self.HB = self.dscr("HB", [1024, NCOL], BF16)
        self.OC = self.dscr("OC", [1024, NCOL], BF16)
        self.Q = self.dscr("Q", [1024, NCOL], BF16)
        self.GATES = self.dscr("GATES", [NCOL, 48], F32)
        self.KT = [self.dscr("KT%d" % i, [256, NCOL], BF16) for i in range(4)]
        self.VT = [self.dscr("VT%d" % i, [NCOL, 256], BF16) for i in range(2)]

    def alloc_consts(self):
        P, st = self.P, self.gst
        nc = self.nc
        T = lambda n, s, d: self.tile(st, n, s, d)
        self.ident_f, r_if = T("ident_f", [128, 128], F32)
        self.ident_b, r_ib = T("ident_b", [128, 128], BF16)
        self.ones_b, r_ob = T("ones_b", [128, 128], BF16)
        self.trile, r1 = T("trile", [128, 128], F32)
        self.trigt, r2 = T("trigt", [128, 128], F32)
        self.dtab, r3 = T("dtab", [128, 128], F32)
        self.tokv, r4 = T("tokv", [128, NCOL], BF16)
        self.kval, r5 = T("kval", [128, 32], F32)
        self.nval, r6 = T("nval", [128, 2], F32)
        self.normw_s, r7 = T("normw", [128, 128], F32)
        self.epsT, r8 = T("eps", [128, 1], F32)
        self.zero_f, r9 = T("zero_f", [128, 128], F32)
        self.r_const = self.newreg("const")
        rc = self.r_const
        ld = lambda t, src: P.dma("sp", lambda e: e.dma_start(out=t[:], in_=src), rc, True)
        ld(self.ident_f, self.c_ident)
        ld(self.trile, self.c_trile)
        ld(self.trigt, self.c_trigt)
        ld(self.dtab, self.c_dtab)
        ld(self.kval, self.m_kval)
        ld(self.nval, self.m_nval)
        ld(self.normw_s, self.normw)
        P.dma("pool", lambda e: e.dma_start(out=self.ident_b[:], in_=self.c_ident), rc, True)
        P.dma("pool", lambda e: e.dma_start(out=self.tokv[:], in_=self.m_tok), rc, True)
        P.op("dve", lambda e: e.memset(self.ones_b[:], 1.0), [], [rc])
        P.op("dve", lambda e: e.memset(self.epsT[:], EPS), [], [rc])
        P.op("dve", lambda e: e.memset(self.zero_f[:], 0.0), [], [rc])
        for X in (self.XM, self.X1):
            Xv = X.rearrange("(c p) t -> p c t", p=128)
            for c in range(16):
                P.dma("sp", lambda e, c=c, Xv=Xv: e.dma_start(out=Xv[:, c, 0:128], in_=self.zero_f[:]), rc, False)
        self.ps = []
        self.psr = []
        for i in range(8):
            t = st.enter_context(nc.psum_tensor("psb%d" % i, [128, 512], F32))
            self.ps.append(t)
            self.psr.append(self.newreg("ps%d" % i))
        P.end_phase(persistent=True)

    def load_norm(self, st_tiles, X, col0, n, nw_off, dst_fn, dst_reg, bank=7):
        P = self.P
        xt, r_xt, sq, r_sq, rs, r_rs = st_tiles
        Xv = X.rearrange("(c p) t -> p c t", p=128)
        P.dma("sp", lambda e: e.dma_start(out=xt[:, :, 0:n], in_=Xv[:, :, col0:col0 + n]), r_xt, True)
        ps, pr = self.ps[bank], self.psr[bank]
        for c in range(16):
            P.op("act", lambda e, c=c: e.activation(out=sq[c % 2][:, 0:n], in_=xt[:, c, 0:n], func=AF.Square), [r_xt], [r_sq[c % 2]])
            P.op("pe", lambda e, c=c: e.matmul(ps[:, 0:n], lhsT=self.ones_b[:], rhs=sq[c % 2][:, 0:n], start=(c == 0), stop=(c == 15)), [r_sq[c % 2]], [pr])
        P.op("act", lambda e: e.activation(out=rs[:, 0:n], in_=ps[:, 0:n], func=AF.Sqrt, bias=self.epsT[:, 0:1], scale=1.0 / D), [pr], [r_rs])
        P.op("dve", lambda e: e.reciprocal(out=rs[:, 0:n], in_=rs[:, 0:n]), [r_rs], [r_rs])
        P.op("dve", lambda e: e.tensor_tensor(out=rs[:, 0:n], in0=rs[:, 0:n], in1=self.tokv[:, col0:col0 + n], op=ALU.mult), [r_rs], [r_rs])
        for c in range(16):
            P.op("dve", lambda e, c=c: e.scalar_tensor_tensor(out=dst_fn(c), in0=xt[:, c, 0:n], scalar=self.normw_s[:, nw_off + c:nw_off + c + 1],
                                                              in1=rs[:, 0:n], op0=ALU.mult, op1=ALU.mult), [r_xt, r_rs], [dst_reg])

    def norm_tiles(self, st):
        xt, r_xt = self.tile(st, "xt", [128, 16, 256], F32)
        sq0, r0 = self.tile(st, "sq0", [128, 256], BF16)
        sq1, r1 = self.tile(st, "sq1", [128, 256], BF16)
        rs, r_rs = self.tile(st, "rs", [128, 256], F32)
        return (xt, r_xt, [sq0, sq1], [r0, r1], rs, r_rs)

    def fill_hT(self, st_tiles, X, col0, ntok, nw_off, hT, hregs):
        for i, t0 in enumerate(range(0, ntok, 256)):
            n = min(256, ntok - t0)
            self.load_norm(st_tiles, X, col0 + t0, n, nw_off, lambda c, t0=t0, n=n: hT[:, c, t0:t0 + n], hregs[i])

    def wload(self, wbuf, wreg, Wsrc, kc, col0, ncols, off=0, k0=0):
        view = wbuf[:, off:off + kc * ncols].rearrange("p (k n) -> p k n", k=kc)
        src = Wsrc.rearrange("(k p) n -> p k n", p=128)[:, k0:k0 + kc, col0:col0 + ncols]
        self.P.dma("pool", lambda e: e.dma_start(out=view, in_=src), wreg, True)
        return view

    def run_jobs(self, jobs):
        loads = [i for i, (lf, _) in enumerate(jobs) if lf is not None]
        if loads:
            jobs[loads[0]][0](0)
        li = 0
        for i, (lf, cf) in enumerate(jobs):
            if lf is None:
                cf(None)
                continue
            if li + 1 < len(loads):
                jobs[loads[li + 1]][0]((li + 1) % 2)
            cf(li % 2)
            li += 1

    def post_norm(self, st, mixed, r_mixed, ss_bank, n, nw_off, Xin, Xout, cin0, cout0, tl):
        P = self.P
        rs2, r_rs2, xr, r_xr, ot, r_ot, tt, r_tt = tl
        ps, pr = self.ps[ss_bank], self.psr[ss_bank]
        P.op("act", lambda e: e.activation(out=rs2[:, 0:n], in_=ps[:, 0:n], func=AF.Sqrt, bias=self.epsT[:, 0:1], scale=1.0 / D), [pr], [r_rs2])
        P.op("dve", lambda e: e.reciprocal(out=rs2[:, 0:n], in_=rs2[:, 0:n]), [r_rs2], [r_rs2])
        Xi = Xin.rearrange("(c p) t -> p c t", p=128)
        Xo = Xout.rearrange("(c p) t -> p c t", p=128)
        for c in range(16):
            b = c % 2
            P.dma("sp", lambda e, c=c, b=b: e.dma_start(out=xr[b][:, 0:n], in_=Xi[:, c, cin0:cin0 + n]), r_xr[b], True)
            P.op("dve", lambda e, c=c, b=b: e.tensor_tensor(out=tt[b][:, 0:n], in0=mixed[:, c, 0:n], in1=rs2[:, 0:n], op=ALU.mult), [r_mixed, r_rs2], [r_tt[b]])
            P.op("dve", lambda e, c=c, b=b: e.scalar_tensor_tensor(out=ot[b][:, 0:n], in0=tt[b][:, 0:n], scalar=self.normw_s[:, nw_off + c:nw_off + c + 1],
                                                                   in1=xr[b][:, 0:n], op0=ALU.mult, op1=ALU.add), [r_tt[b], r_xr[b]], [r_ot[b]])
            P.dma("sp", lambda e, c=c, b=b: e.dma_start(out=Xo[:, c, cout0:cout0 + n], in_=ot[b][:, 0:n]), r_ot[b], False)

    def post_tiles(self, st):
        rs2, r_rs2 = self.tile(st, "rs2", [128, 512], F32)
        xr = []; r_xr = []; ot = []; r_ot = []; tt = []; r_tt = []
        for b in range(2):
            a, ra = self.tile(st, "xr", [128, 512], F32); xr.append(a); r_xr.append(ra)
            a, ra = self.tile(st, "ot", [128, 512], F32); ot.append(a); r_ot.append(ra)
            a, ra = self.tile(st, "tt", [128, 512], F32); tt.append(a); r_tt.append(ra)
        return (rs2, r_rs2, xr, r_xr, ot, r_ot, tt, r_tt)

    def tok_tiles(self, start, ntok, step=512):
        return [(t0, min(step, start + ntok - t0)) for t0 in range(start, start + ntok, step)]

    def phase_kv(self, l, X):
        P = self.P
        with ExitStack() as st:
            nt = self.norm_tiles(st)
            wkv, r_wkv = self.tile(st, "wkv", [128, 16 * 1536], BF16)
            wv = wkv[:].rearrange("p (k n) -> p k n", k=16)
            Wl = self.w_in[l]
            for j in range(3):
                src = Wl.rearrange("(k p) n -> p k n", p=128)[:, :, C_KV + j * 512:C_KV + (j + 1) * 512]
                P.dma("pool", lambda e, j=j, src=src: e.dma_start(out=wv[:, :, j * 512:(j + 1) * 512], in_=src), r_wkv, True)
            hts = [self.tile(st, "hTt", [128, 16, 256], BF16) for _ in range(2)]
            ksts = [self.tile(st, "kst", [128, 8, 256], BF16) for _ in range(2)]
            vsts = [self.tile(st, "vst", [128, 2, 512], BF16) for _ in range(2)]
            fm_off = [0, 256, 512, 1024]
            tm_off = [768, 1280]
            for ti, t0 in enumerate(range(0, SEQ, 256)):
                hT, r_h = hts[ti % 2]
                kst, r_k = ksts[ti % 2]
                vst, r_v = vsts[ti % 2]
                col0 = PAD + t0
                self.load_norm(nt, X, col0, 256, (l * 4 + 0) * 16, lambda c, hT=hT: hT[:, c, :], r_h)
                for ty in range(4):
                    bk = ty % 4
                    ps, pr = self.ps[bk], self.psr[bk]
                    for half in range(2):
                        off = fm_off[ty] + half * 128
                        for kc in range(16):
                            P.op("pe", lambda e, ps=ps, half=half, off=off, kc=kc, hT=hT: e.matmul(ps[:, half * 256:(half + 1) * 256], lhsT=wv[:, kc, off:off + 128],
                                                                                                  rhs=hT[:, kc, :], start=(kc == 0), stop=(kc == 15)), [r_wkv, r_h], [pr])
                    P.op("act", lambda e, ps=ps, ty=ty, kst=kst: e.activation(out=kst[:, 2 * ty:2 * ty + 2, :], in_=ps[:].rearrange("p (h t) -> p h t", h=2), func=AF.Copy), [pr], [r_k])
                for ty in range(4):
                    dst = self.KT[ty].rearrange("(h p) t -> p h t", p=128)[:, :, col0:col0 + 256]
                    P.dma("sp", lambda e, ty=ty, dst=dst, kst=kst: e.dma_start(out=dst, in_=kst[:, 2 * ty:2 * ty + 2, :]), r_k, False, [self.R("KT")])
                for sub in range(2):
                    ps, pr = self.ps[4 + sub], self.psr[4 + sub]
                    for j in range(2):
                        for kc in range(16):
                            P.op("pe", lambda e, ps=ps, sub=sub, j=j, kc=kc, hT=hT: e.matmul(ps[:, j * 256:(j + 1) * 256], lhsT=hT[:, kc, sub * 128:(sub + 1) * 128],
                                                                                             rhs=wv[:, kc, tm_off[j]:tm_off[j] + 256], start=(kc == 0), stop=(kc == 15)), [r_wkv, r_h], [pr])
                    P.op("dve", lambda e, ps=ps, sub=sub, vst=vst: e.tensor_copy(out=vst[:, sub, :], in_=ps[:]), [pr], [r_v])
                for j in range(2):
                    dst = self.VT[j][col0:col0 + 256, :].rearrange("(s p) c -> p s c", p=128)
                    P.dma("sp", lambda e, j=j, dst=dst, vst=vst: e.dma_start(out=dst, in_=vst[:, :, j * 256:(j + 1) * 256]), r_v, False, [self.R("VT")])
            P.end_phase()

    def phase_m1(self, l, X, r0, r1):
        P = self.P
        T = r1 - r0
        TA = T + 128
        Wl = self.w_in[l]
        with ExitStack() as st:
            nt = self.norm_tiles(st)
            hT, _ = self.tile(st, "hT", [128, 16, TA], BF16)
            hregs = [self.newreg("hT") for _ in range((TA + 255) // 256)]
            wb = [self.tile(st, "wbuf", [128, 8192], BF16) for _ in range(2)]
            wvt, r_wvt = self.tile(st, "wvt", [128, 16 * 1024], BF16)
            ut, r_ut = self.tile(st, "ut", [128, 8, 128], BF16)
            sig = [self.tile(st, "sig", [128, 512], F32) for _ in range(2)]
            gst = [self.tile(st, "gst", [128, 512], BF16) for _ in range(2)]
            qst = [self.tile(st, "qst", [128, 512], BF16) for _ in range(2)]
            vg = [self.tile(st, "vg", [128, 1024], F32)] * 2
            vln = [self.tile(st, "vln", [128, 1024], BF16)] * 2
            bst, r_bst = self.tile(st, "bst", [128, 2, 6], F32)
            mv, r_mv = self.tile(st, "mv", [128, 2], F32)
            wsT, r_ws = self.tile(st, "wsT", [128, 8, 128], BF16)
            wsF, r_wsF = self.tile(st, "wsF", [128, 8, 128], BF16)
            lnbt, r_lnb = self.tile(st, "lnbt", [128, 2, 1024], F32)
            sgbr, r_sgb = self.tile(st, "sgbr", [1, 1024], BF16)
            gall, r_gall = self.tile(st, "gall", [128, T // 128, 48], F32)
            P.dma("sp", lambda e: e.dma_start(out=lnbt[:], in_=self.lnb[:, l * 2048:(l + 1) * 2048].rearrange("p (a c) -> p a c", a=2)), r_lnb, True)
            P.dma("pool", lambda e: e.dma_start(out=sgbr[:], in_=self.sgb[:, l * 1024:(l + 1) * 1024]), r_sgb, True)
            P.dma("pool", lambda e: e.dma_start(out=wsF[:], in_=self.sgw[l].rearrange("s (g t) -> s g t", g=8)), r_wsF, True)
            P.op("dve", lambda e: e.tensor_tensor(out=wsT[:], in0=wsF[:], in1=self.trile[:].unsqueeze(1).to_broadcast([128, 8, 128]), op=ALU.mult), [r_wsF], [r_ws])
            for j in range(2):
                src = Wl.rearrange("(k p) n -> p k n", p=128)[:, :, C_B + 1024 + j * 512:C_B + 1024 + (j + 1) * 512]
                P.dma("pool", lambda e, j=j, src=src: e.dma_start(out=wvt[:].rearrange("p (k n) -> p k n", k=16)[:, :, j * 512:(j + 1) * 512], in_=src), r_wvt, True)
            wv3 = wvt[:].rearrange("p (k n) -> p k n", k=16)
            self.fill_hT(nt, X, PAD + r0 - 128, TA, (l * 4 + 0) * 16, hT, hregs)
            tilesA = self.tok_tiles(0, TA)
            tilesR = self.tok_tiles(128, T)
            jobs = []
            psrot = [0]

            def nextps():
                b = psrot[0] % 4
                psrot[0] += 1
                return self.ps[b], self.psr[b]

            def mm16(ps, n, wview, c0, t0):
                for kc in range(16):
                    P.op("pe", lambda e, kc=kc: e.matmul(ps[:, 0:n], lhsT=wview[:, kc, c0:c0 + 128], rhs=hT[:, kc, t0:t0 + n], start=(kc == 0), stop=(kc == 15)),
                         [wview_reg[0]] + hregs, [ps_reg[0]])

            wview_reg = [None]
            ps_reg = [None]
            for jg in range(4):
                def lf(b, jg=jg):
                    self.wload(wb[b][0], wb[b][1], Wl, 16, C_A + jg * 256, 256, 0)
                    self.wload(wb[b][0], wb[b][1], Wl, 16, C_A + 1024 + jg * 256, 256, 16 * 256)

                def cf(b, jg=jg):
                    wa = wb[b][0][:, 0:4096].rearrange("p (k n) -> p k n", k=16)
                    wg = wb[b][0][:, 4096:8192].rearrange("p (k n) -> p k n", k=16)
                    wview_reg[0] = wb[b][1]
                    k = 0
                    for cc in range(2):
                        ch = jg * 2 + cc
                        for (t0, n) in tilesA:
                            pa, ra = nextps()
                            ps_reg[0] = ra
                            mm16(pa, n, wa, cc * 128, t0)
                            pg, rg = nextps()
                            ps_reg[0] = rg
                            mm16(pg, n, wg, cc * 128, t0)
                            s_, rs_ = sig[k % 2]
                            g_, rg_ = gst[k % 2]
                            k += 1
                            P.op("act", lambda e, pg=pg, s_=s_, n=n: e.activation(out=s_[:, 0:n], in_=pg[:, 0:n], func=AF.Sigmoid), [rg], [rs_])
                            P.op("dve", lambda e, pa=pa, s_=s_, g_=g_, n=n: e.tensor_tensor(out=g_[:, 0:n], in0=pa[:, 0:n], in1=s_[:, 0:n], op=ALU.mult), [ra, rs_], [rg_])
                            c0 = PAD + r0 - 128 + t0
                            P.dma("sp", lambda e, g_=g_, ch=ch, c0=c0, n=n: e.dma_start(out=self.GLU[ch * 128:(ch + 1) * 128, c0:c0 + n], in_=g_[:, 0:n]), rg_, False, [self.R("GLU")])
                jobs.append((lf, cf))
            ku = [0]
            for jg in range(2):
                def lf(b, jg=jg):
                    self.wload(wb[b][0], wb[b][1], Wl, 16, C_B + jg * 512, 512, 0)

                def cf(b, jg=jg):
                    wu = wb[b][0][:].rearrange("p (k n) -> p k n", k=16)
                    wview_reg[0] = wb[b][1]
                    for cc in range(4):
                        ch = jg * 4 + cc
                        for (t0, n) in tilesR:
                            pu, ru = nextps()
                            ps_reg[0] = ru
                            mm16(pu, n, wu, cc * 128, t0)
                            q_, rq_ = qst[ku[0] % 2]
                            ku[0] += 1
                            P.op("act", lambda e, pu=pu, q_=q_, n=n: e.activation(out=q_[:, 0:n], in_=pu[:, 0:n], func=AF.Gelu), [ru], [rq_])
                            c0 = PAD + r0 - 128 + t0
                            P.dma("sp", lambda e, q_=q_, ch=ch, c0=c0, n=n: e.dma_start(out=self.HB[ch * 128:(ch + 1) * 128, c0:c0 + n], in_=q_[:, 0:n]), rq_, False)
                jobs.append((lf, cf))
            for jg in range(2):
                def lf(b, jg=jg):
                    self.wload(wb[b][0], wb[b][1], Wl, 16, C_Q + jg * 512, 512, 0)

                def cf(b, jg=jg):
                    wq = wb[b][0][:].rearrange("p (k n) -> p k n", k=16)
                    wview_reg[0] = wb[b][1]
                    k = 0
                    for cc in range(4):
                        ch = jg * 4 + cc
                        for (t0, n) in tilesR:
                            pq, rq = nextps()
                            ps_reg[0] = rq
                            mm16(pq, n, wq, cc * 128, t0)
                            q_, rq_ = qst[k % 2]
                            k += 1
                            P.op("act", lambda e, pq=pq, q_=q_, n=n: e.activation(out=q_[:, 0:n], in_=pq[:, 0:n], func=AF.Copy, scale=0.125), [rq], [rq_])
                            c0 = PAD + r0 - 128 + t0
                            P.dma("sp", lambda e, q_=q_, ch=ch, c0=c0, n=n: e.dma_start(out=self.Q[ch * 128:(ch + 1) * 128, c0:c0 + n], in_=q_[:, 0:n]), rq_, False, [self.R("Q")])
                jobs.append((lf, cf))
            def lfg(b):
                self.wload(wb[b][0], wb[b][1], Wl, 16, C_G, 48, 0)

            def cfg(b):
                wg = wb[b][0][:, 0:16 * 48].rearrange("p (k n) -> p k n", k=16)
                for i in range(T // 128):
                    pg, rg = nextps()
                    t0 = 128 + i * 128
                    for kc in range(16):
                        P.op("pe", lambda e, kc=kc, pg=pg, t0=t0: e.matmul(pg[:, 0:48], lhsT=hT[:, kc, t0:t0 + 128], rhs=wg[:, kc, :], start=(kc == 0), stop=(kc == 15)),
                             [wb[b][1]] + hregs, [rg])
                    P.op("act", lambda e, pg=pg, i=i: e.activation(out=gall[:, i, :], in_=pg[:, 0:48], func=AF.Sigmoid), [rg], [r_gall])
                dst = self.GATES[PAD + r0:PAD + r1, :].rearrange("(n p) c -> p n c", p=128)
                P.dma("sp", lambda e: e.dma_start(out=dst, in_=gall[:]), r_gall, False, [self.R("GATES")])
            jobs.append((lfg, cfg))
            self.run_jobs(jobs)
            P.barrier()
            HBv = self.HB.rearrange("(c p) t -> p c t", p=128)
            for i in range(T // 128):
                t0 = 128 + i * 128
                cu = PAD + r0 + i * 128
                P.dma("sp", lambda e, cu=cu: e.dma_start(out=ut[:], in_=HBv[:, :, cu:cu + 128]), r_ut, True)
                vg_, rvg = vg[i % 2]
                vl_, rvl = vln[i % 2]
                for half in range(2):
                    pv, rv = self.ps[4 + half], self.psr[4 + half]
                    for kc in range(16):
                        P.op("pe", lambda e, kc=kc, pv=pv, half=half, t0=t0: e.matmul(pv[:], lhsT=hT[:, kc, t0:t0 + 128], rhs=wv3[:, kc, half * 512:(half + 1) * 512],
                                                                                     start=(kc == 0), stop=(kc == 15)), [r_wvt] + hregs, [rv])
                    P.op("act", lambda e, pv=pv, half=half, vg_=vg_: e.activation(out=vg_[:, half * 512:(half + 1) * 512], in_=pv[:], func=AF.Gelu), [rv], [rvg])
                for half in range(2):
                    P.op("dve", lambda e, half=half, vg_=vg_: e.bn_stats(out=bst[:, half, :], in_=vg_[:, half * 512:(half + 1) * 512]), [rvg], [r_bst])
                P.op("dve", lambda e: e.bn_aggr(out=mv[:], in_=bst[:].rearrange("p a s -> p (a s)")), [r_bst], [r_mv])
                P.op("act", lambda e: e.activation(out=mv[:, 1:2], in_=mv[:, 1:2], func=AF.Sqrt, bias=self.epsT[:, 0:1], scale=1.0), [r_mv], [r_mv])
                P.op("dve", lambda e: e.reciprocal(out=mv[:, 1:2], in_=mv[:, 1:2]), [r_mv], [r_mv])
                P.op("dve", lambda e, vg_=vg_: e.tensor_scalar(out=vg_[:], in0=vg_[:], scalar1=mv[:, 0:1], scalar2=mv[:, 1:2], op0=ALU.subtract, op1=ALU.mult), [rvg, r_mv], [rvg])
                P.op("dve", lambda e, vg_=vg_: e.tensor_tensor(out=vg_[:], in0=vg_[:], in1=lnbt[:, 0, :], op=ALU.mult), [rvg, r_lnb], [rvg])
                P.op("dve", lambda e, vg_=vg_, vl_=vl_: e.tensor_tensor(out=vl_[:], in0=vg_[:], in1=lnbt[:, 1, :], op=ALU.add), [rvg, r_lnb], [rvl])
                for hb in range(2):
                    pf, rf = self.ps[6 + hb], self.psr[6 + hb]
                    for gg in range(4):
                        g = hb * 4 + gg
                        P.op("pe", lambda e, pf=pf, gg=gg, g=g, vl_=vl_: e.matmul(pf[:, gg * 128:(gg + 1) * 128], lhsT=vl_[:, g * 128:(g + 1) * 128], rhs=wsT[:, g, :], start=True, stop=False),
                             [rvl, r_ws], [rf])
                        P.op("pe", lambda e, pf=pf, gg=gg, g=g: e.matmul(pf[:, gg * 128:(gg + 1) * 128], lhsT=self.ones_b[0:1, :], rhs=sgbr[0:1, g * 128:(g + 1) * 128], start=False, stop=True),
                             [r_sgb], [rf])
                    P.op("dve", lambda e, pf=pf, hb=hb: e.tensor_tensor(out=ut[:, hb * 4:(hb + 1) * 4, :], in0=ut[:, hb * 4:(hb + 1) * 4, :],
                                                                      in1=pf[:].rearrange("p (g t) -> p g t", g=4), op=ALU.mult), [rf, r_ut], [r_ut])
                P.dma("sp", lambda e, cu=cu: e.dma_start(out=HBv[:, :, cu:cu + 128], in_=ut[:]), r_ut, False)
            P.end_phase()

    def phase_m2(self, l, r0, r1):
        P = self.P
        T = r1 - r0
        TA = T + 128
        with ExitStack() as st:
            gl = [self.tile(st, "gl", [128, TA], BF16) for _ in range(3)]
            dgs = [self.tile(st, "dgs", [128, 31, 128], BF16) for _ in range(2)]
            cbf, r_cbf = self.tile(st, "cbf", [128, 8, T], BF16)
            cw, r_cw = self.tile(st, "cw", [128, 8, 31], F32)
            av, r_av = self.tile(st, "av", [128, 3, 8], F32)
            sq = [self.tile(st, "sq", [128, 512], BF16) for _ in range(2)]
            mean, r_mean = self.tile(st, "mean", [128, 512], F32)
            msq, r_msq = self.tile(st, "msq", [128, 512], F32)
            rstd, r_rstd = self.tile(st, "rstd", [128, 512], F32)
            t1 = [self.tile(st, "t1", [128, 512], F32) for _ in range(2)]
            hst = [self.tile(st, "hst", [128, 8, 512], BF16) for _ in range(2)]
            P.dma("sp", lambda e: e.dma_start(out=cw[:], in_=self.convaw[:, l * 248:(l + 1) * 248].rearrange("p (c k) -> p c k", c=8)), r_cw, True)
            P.dma("sp", lambda e: e.dma_start(out=av[:], in_=self.avec[:, l * 24:(l + 1) * 24].rearrange("p (a c) -> p a c", a=3)), r_av, True)
            for c in range(8):
                g_, rg = gl[c % 3]
                d_, rd_ = dgs[c % 2]
                c0 = PAD + r0 - 128
                P.dma("sp", lambda e: e.dma_start(out=g_[:], in_=self.GLU[c * 128:(c + 1) * 128, c0:c0 + TA]), rg, True)
                for k in range(31):
                    P.op("dve", lambda e: e.tensor_scalar(out=d_[:, k, :], in0=self.ident_f[:], scalar1=cw[:, c, k:k + 1], scalar2=None, op0=ALU.mult), [r_cw], [rd_])
                for ti, (t0, n) in enumerate(self.tok_tiles(0, T)):
                    bk = 2 + (c * 8 + ti) % 4
                    ps, pr = self.ps[bk], self.psr[bk]
                    for k in range(31):
                        P.op("pe", lambda e: e.matmul(ps[:, 0:n], lhsT=d_[:, k, :], rhs=g_[:, 98 + k + t0:98 + k + t0 + n], start=(k == 0), stop=(k == 30)), [rd_, rg], [pr])
                    P.op("act", lambda e: e.activation(out=cbf[:, c, t0:t0 + n], in_=ps[:, 0:n], func=AF.Identity, bias=av[:, 0, c:c + 1], scale=1.0), [pr, r_av], [r_cbf])
            for ti, (t0, n) in enumerate(self.tok_tiles(0, T)):
                pS, rS = self.ps[0], self.psr[0]
                pQ, rQ = self.ps[1], self.psr[1]
                for c in range(8):
                    s_, rs_ = sq[c % 2]
                    P.op("act", lambda e, s_=s_, c=c, t0=t0, n=n: e.activation(out=s_[:, 0:n], in_=cbf[:, c, t0:t0 + n], func=AF.Square), [r_cbf], [rs_])
                    P.op("pe", lambda e, c=c, t0=t0, n=n: e.matmul(pS[:, 0:n], lhsT=self.ones_b[:], rhs=cbf[:, c, t0:t0 + n], start=(c == 0), stop=(c == 7)), [r_cbf], [rS])
                    P.op("pe", lambda e, s_=s_, c=c, n=n: e.matmul(pQ[:, 0:n], lhsT=self.ones_b[:], rhs=s_[:, 0:n], start=(c == 0), stop=(c == 7)), [rs_], [rQ])
                P.op("dve", lambda e, n=n: e.tensor_scalar(out=mean[:, 0:n], in0=pS[:, 0:n], scalar1=1.0 / 1024, scalar2=None, op0=ALU.mult), [rS], [r_mean])
                P.op("dve", lambda e, n=n: e.tensor_tensor(out=msq[:, 0:n], in0=mean[:, 0:n], in1=mean[:, 0:n], op=ALU.mult), [r_mean], [r_msq])
                P.op("dve", lambda e, n=n: e.scalar_tensor_tensor(out=rstd[:, 0:n], in0=pQ[:, 0:n], scalar=1.0 / 1024, in1=msq[:, 0:n], op0=ALU.mult, op1=ALU.subtract), [rQ, r_msq], [r_rstd])
                P.op("act", lambda e, n=n: e.activation(out=rstd[:, 0:n], in_=rstd[:, 0:n], func=AF.Sqrt, bias=self.epsT[:, 0:1], scale=1.0), [r_rstd], [r_rstd])
                P.op("dve", lambda e, n=n: e.reciprocal(out=rstd[:, 0:n], in_=rstd[:, 0:n]), [r_rstd], [r_rstd])
                h_, rh = hst[ti % 2]
                for c in range(8):
                    t_, rt = t1[c % 2]
                    P.op("dve", lambda e, t_=t_, c=c, t0=t0, n=n: e.tensor_tensor(out=t_[:, 0:n], in0=cbf[:, c, t0:t0 + n], in1=mean[:, 0:n], op=ALU.subtract), [r_cbf, r_mean], [rt])
                    P.op("dve", lambda e, t_=t_, n=n: e.tensor_tensor(out=t_[:, 0:n], in0=t_[:, 0:n], in1=rstd[:, 0:n], op=ALU.mult), [rt, r_rstd], [rt])
                    P.op("act", lambda e, t_=t_, h_=h_, c=c, n=n: e.activation(out=h_[:, c, 0:n], in_=t_[:, 0:n], func=AF.Silu, bias=av[:, 2, c:c + 1], scale=av[:, 1, c:c + 1]), [rt, r_av], [rh])
                dst = self.HA.rearrange("(c p) t -> p c t", p=128)[:, :, PAD + r0 + t0:PAD + r0 + t0 + n]
                P.dma("sp", lambda e, dst=dst, h_=h_, n=n: e.dma_start(out=dst, in_=h_[:, :, 0:n]), rh, False, [self.R("HA")])
            P.end_phase()

    def phase_att(self, l, r0, r1):
        P = self.P
        T = r1 - r0
        nqt = T // 128
        qt0 = r0 // 128
        BIG = 30000.0
        with ExitStack() as st:
            kin = [self.tile(st, "kin", [128, SEQ], BF16) for _ in range(4)]
            vs, r_vs = self.tile(st, "vs", [128, 32, 65], BF16)
            vw, r_vw = self.tile(st, "vw", [128, 32, 65], BF16)
            qT, r_q = self.tile(st, "qT", [128, 4, T], BF16)
            gt, r_gt = self.tile(st, "gt", [128, nqt, 48], F32)
            mtab, r_mt = self.tile(st, "mtab", [128, 3, nqt, 64], F32)
            Et, r_E = self.tile(st, "Et", [128, 32, 128], BF16)
            smap, r_sm = self.tile(st, "smap", [128, 2, 64], BF16)
            w1 = [self.tile(st, "w1", [128, 32, 256], BF16) for _ in range(2)]
            w2 = [self.tile(st, "w2", [128, 2, 64], BF16) for _ in range(2)]
            pe = [self.tile(st, "pe", [128, 32], BF16) for _ in range(2)]
            cb = [self.tile(st, "cb", [128, 2], F32) for _ in range(2)]
            hid, r_hid = self.tile(st, "hid", [128, 2, 256], BF16)
            kcT, r_kc = self.tile(st, "kcT", [128, 256], BF16)
            rv, r_rv = self.tile(st, "rv", [128, 2, 64], BF16)
            pb = [self.tile(st, "pb", [128, 4, 128], BF16) for _ in range(4)]
            cm, r_cm = self.tile(st, "cm", [128, 128], F32)
            cmb = [self.tile(st, "cmb", [128, 4, 128], BF16) for _ in range(4)]
            trib = [self.tile(st, "trib", [128, 4, 128], BF16) for _ in range(2)]
            selb = [self.tile(st, "selb", [128, 4, 128], BF16) for _ in range(2)]
            negb, r_negb = self.tile(st, "negb", [128, 1], F32)
            sm = {}
            for nm, shp in (("den", [128, 4]), ("cg", [128, 4]), ("imp", [128, 64]), ("score", [128, 64]), ("wk", [128, 64]), ("sel", [128, 64]), ("m8", [128, 8]),
                            ("oacc", [128, 4, 64]), ("otmp", [128, 4, 64]), ("den2", [128, 4]), ("cg2", [128, 4])):
                sm[nm] = self.tile(st, nm, shp, F32)
            obf = [self.tile(st, "obf", [128, 256], BF16) for _ in range(2)]
            ost = [self.tile(st, "ost", [128, 2, 128], BF16) for _ in range(2)]
            P.dma("sp", lambda e: e.dma_start(out=gt[:], in_=self.GATES[PAD + r0:PAD + r1, :].rearrange("(n p) c -> p n c", p=128)), r_gt, True)
            for a_ in range(3):
                P.dma("sp", lambda e: e.dma_start(out=mtab[:, a_, :, :], in_=self.m_sel[a_].rearrange("p (q j) -> p q j", j=64)[:, qt0:qt0 + nqt, :]), r_mt, True)
            P.op("dve", lambda e: e.memset(Et[64:128], 0.0), [], [r_E])
            P.op("dve", lambda e: e.memset(qT[64:128], 0.0), [], [r_q])
            for ty in range(4):
                P.op("dve", lambda e: e.memset(kin[ty][0][64:128], 0.0), [], [kin[ty][1]])
            for kv in range(2):
                P.op("dve", lambda e: e.memset(w1[kv][0][64:128], 0.0), [], [w1[kv][1]])
                P.op("dve", lambda e: e.memset(pe[kv][0][64:128], 0.0), [], [pe[kv][1]])
            for j_ in range(2):
                P.op("dve", lambda e: e.memset(selb[j_][0][64:128], 0.0), [], [selb[j_][1]])
            P.dma("pool", lambda e: e.dma_start(out=Et[0:64], in_=self.c_E.rearrange("j (k c) -> j k c", c=128)), r_E, True)
            P.dma("pool", lambda e: e.dma_start(out=smap[:], in_=self.c_smap.rearrange("p (a j) -> p a j", a=2)), r_sm, True)
            for kv in range(2):
                i_ = l * 2 + kv
                P.dma("pool", lambda e: e.dma_start(out=w1[kv][0][0:64], in_=self.cw1[i_].rearrange("d (l h) -> d l h", l=32)), w1[kv][1], True)
                P.dma("pool", lambda e: e.dma_start(out=w2[kv][0][:], in_=self.cw2[i_].rearrange("(c p) d -> p c d", p=128)), w2[kv][1], True)
                P.dma("pool", lambda e: e.dma_start(out=pe[kv][0][0:64], in_=self.cpe[i_]), pe[kv][1], True)
            P.op("dve", lambda e: e.memset(vs[:, :, 64:65], 1.0), [], [r_vs])
            P.op("dve", lambda e: e.memset(vw[:, :, 64:65], 1.0), [], [r_vw])
            P.op("dve", lambda e: e.memset(hid[:], 0.0), [], [r_hid])
            P.op("dve", lambda e: e.memset(kcT[:], 0.0), [], [r_kc])
            P.op("dve", lambda e: e.memset(negb[:], -BIG), [], [r_negb])
            for ti_, tri in enumerate((self.trile, self.trigt)):
                P.op("dve", lambda e: e.tensor_scalar(out=trib[ti_][0][:], in0=tri[:].unsqueeze(1).to_broadcast([128, 4, 128]), scalar1=-1.0, scalar2=BIG, op0=ALU.add, op1=ALU.mult), [], [trib[ti_][1]])
            for kv in range(2):
                ps, pr = self.ps[0], self.psr[0]
                for hc in range(2):
                    for li in range(32):
                        P.op("pe", lambda e: e.matmul(ps[:, hc:hc + 1], lhsT=w1[kv][0][:, li, hc * 128:(hc + 1) * 128], rhs=pe[kv][0][:, li:li + 1],
                                                      start=(li == 0), stop=(li == 31)), [w1[kv][1], pe[kv][1]], [pr])
                P.op("act", lambda e: e.activation(out=cb[kv][0][:], in_=ps[:, 0:2], func=AF.Copy), [pr], [cb[kv][1]])
            PC, PS_, PW, T32, TBF = 3, 4, 5, 6, 7
            psbf = self.ps[TBF][:].bitcast(BF16)
            den, r_den = sm["den"]; cg, r_cg = sm["cg"]; imp, r_imp = sm["imp"]; score, r_sc = sm["score"]
            wk, r_wk = sm["wk"]; sel, r_sel = sm["sel"]; m8, r_m8 = sm["m8"]; oacc, r_oa = sm["oacc"]; otmp, r_ot = sm["otmp"]
            den2, r_den2 = sm["den2"]; cg2, r_cg2 = sm["cg2"]
            cnt = {"s": 0, "p": 0, "cmb": 0, "q": 0}
            for g in range(4):
                for ty in range(4):
                    P.dma("sp", lambda e: e.dma_start(out=kin[ty][0][0:64], in_=self.KT[ty][g * 64:(g + 1) * 64, PAD:PAD + SEQ]), kin[ty][1], True)
                P.dma("sp", lambda e: e.dma_start(out=vs[:, :, 0:64], in_=self.VT[0][PAD:PAD + SEQ, g * 64:(g + 1) * 64].rearrange("(n p) d -> p n d", p=128)), r_vs, True)
                P.dma("sp", lambda e: e.dma_start(out=vw[:, :, 0:64], in_=self.VT[1][PAD:PAD + SEQ, g * 64:(g + 1) * 64].rearrange("(n p) d -> p n d", p=128)), r_vw, True)
                P.op("dve", lambda e: e.tensor_tensor(out=vw[:], in0=vw[:], in1=self.kval[:, 0:32].unsqueeze(2).to_broadcast([128, 32, 65]), op=ALU.mult), [r_vw], [r_vw])
                P.dma("sp", lambda e: e.dma_start(out=qT[0:64], in_=self.Q[g * 256:(g + 1) * 256, PAD + r0:PAD + r1].rearrange("(h d) t -> d h t", d=64)), r_q, True)
                for kv in range(2):
                    src, rsrc = kin[kv]
                    for hc in range(2):
                        ps, pr = self.ps[hc], self.psr[hc]
                        for li in range(32):
                            P.op("pe", lambda e: e.matmul(ps[:, 0:255], lhsT=w1[kv][0][:, li, hc * 128:(hc + 1) * 128], rhs=src[:, li:li + 16 * 254 + 1:16],
                                                          start=(li == 0), stop=(li == 31)), [w1[kv][1], rsrc], [pr])
                        P.op("act", lambda e: e.activation(out=hid[:, hc, 0:255], in_=ps[:, 0:255], func=AF.Silu, bias=cb[kv][0][:, hc:hc + 1], scale=1.0), [pr, cb[kv][1]], [r_hid])
                    ps, pr = self.ps[2], self.psr[2]
                    if kv == 0:
                        for hc in range(2):
                            P.op("pe", lambda e: e.matmul(ps[0:64, 0:255], lhsT=w2[0][0][:, hc, :], rhs=hid[:, hc, 0:255], start=(hc == 0), stop=(hc == 1)), [w2[0][1], r_hid], [pr])
                        P.op("act", lambda e: e.activation(out=kcT[0:64, 0:255], in_=ps[0:64, 0:255], func=AF.Copy), [pr], [r_kc])
                    else:
                        for nt_ in range(2):
                            for hc in range(2):
                                P.op("pe", lambda e: e.matmul(ps[:, nt_ * 64:(nt_ + 1) * 64], lhsT=hid[:, hc, nt_ * 128:(nt_ + 1) * 128], rhs=w2[1][0][:, hc, :],
                                                              start=(hc == 0), stop=(hc == 1)), [w2[1][1], r_hid], [pr])
                        P.op("act", lambda e: e.activation(out=rv[:], in_=ps[:, 0:128].rearrange("p (a d) -> p a d", a=2), func=AF.Copy), [pr], [r_rv])
                pending = [None]
                for i in range(nqt):
                    qt = qt0 + i
                    qv = qT[:, :, i * 128:(i + 1) * 128]
                    gsl = gt[:, i, g * 12:(g + 1) * 12].rearrange("p (h b) -> p h b", b=3)
                    pc, rpc = self.ps[PC], self.psr[PC]
                    pso, rpso = self.ps[PS_], self.psr[PS_]
                    pwo, rpwo = self.ps[PW], self.psr[PW]
                    psov = pso[:, 0:260].rearrange("p (h d) -> p h d", h=4)
                    pwov = pwo[:, 0:260].rearrange("p (h d) -> p h d", h=4)
                    sb_, rsb_ = selb[cnt["q"] % 2]
                    ob_, rob_ = obf[cnt["q"] % 2]
                    os_, ros_ = ost[cnt["q"] % 2]
                    cnt["q"] += 1
                    nts = [0] if qt < 16 else [0, 1]
                    steps = []
                    for nt_ in nts:
                        c4, rc4 = cmb[cnt["cmb"] % 4]
                        cnt["cmb"] += 1
                        thr = float(128 * qt - 2048 * nt_ - 31)
                        P.op("dve", lambda e: e.tensor_scalar(out=cm[:], in0=self.dtab[:], scalar1=thr, scalar2=self.nval[:, nt_:nt_ + 1], op0=ALU.is_le, op1=ALU.mult), [], [r_cm])
                        P.op("dve", lambda e: e.tensor_scalar(out=c4[:], in0=cm[:].unsqueeze(1).to_broadcast([128, 4, 128]), scalar1=-1.0, scalar2=BIG, op0=ALU.add, op1=ALU.mult), [r_cm], [rc4])

                        def pv_c(p_, rp_, nt_=nt_):
                            first = (nt_ == nts[0])
                            for h in range(4):
                                P.op("pe", lambda e: e.matmul(pc[:, h * 64:(h + 1) * 64], lhsT=p_[:, h, :], rhs=rv[:, nt_, :], start=(first and h == 0), stop=True, skip_group_check=True), [rp_, r_rv], [rpc])
                                P.op("pe", lambda e: e.matmul(pc[:, 256 + h * 64:256 + (h + 1) * 64], lhsT=p_[:, h, :], rhs=smap[:, nt_, :], start=False, stop=True, skip_group_check=True), [rp_, r_sm], [rpc])
                        steps.append(("c", kcT[:, nt_ * 128:(nt_ + 1) * 128], r_kc, [(self.ident_b[:], c4[:], [rc4])], pv_c))
                    k0 = max(0, qt - 4)
                    for kt in range(k0, qt + 1):
                        biases = []
                        if kt == qt:
                            biases.append((self.ident_b[:], trib[0][0][:], [trib[0][1]]))
                        elif kt == qt - 4:
                            biases.append((self.ident_b[:], trib[1][0][:], [trib[1][1]]))

                        def pv_w(p_, rp_, kt=kt):
                            for h in range(4):
                                P.op("pe", lambda e: e.matmul(pwov[:, h, :], lhsT=p_[:, h, :], rhs=vw[:, kt, :], start=(kt == k0 and h == 0), stop=True, skip_group_check=True), [rp_, r_vw], [rpwo])
                        steps.append(("w", kin[3][0][:, kt * 128:(kt + 1) * 128], kin[3][1], biases, pv_w))
                    n_pre = len(steps)
                    for kt in range(0, qt + 1):
                        biases = [(Et[:, kt, :], sb_[:], [r_E, rsb_])]
                        if kt == qt:
                            biases.append((self.ident_b[:], trib[0][0][:], [trib[0][1]]))

                        def pv_s(p_, rp_, kt=kt):
                            for h in range(4):
                                P.op("pe", lambda e: e.matmul(psov[:, h, :], lhsT=p_[:, h, :], rhs=vs[:, kt, :], start=(kt == 0 and h == 0), stop=True, skip_group_check=True), [rp_, r_vs], [rpso])
                        steps.append(("s", kin[2][0][:, kt * 128:(kt + 1) * 128], kin[2][1], biases, pv_s))
                    N = len(steps)
                    sbank = {}

                    def emit_score(k):
                        kind, lhsT, lreg, biases, _ = steps[k]
                        bk = cnt["s"] % 3
                        cnt["s"] += 1
                        ps, pr = self.ps[bk], self.psr[bk]
                        sbank[k] = (ps, pr)
                        P.op("pe", lambda e: e.matmul(ps[:], lhsT=lhsT, rhs=qv, start=True, stop=(len(biases) == 0)), [lreg, r_q], [pr])
                        for bi, (bl, br, bregs) in enumerate(biases):
                            P.op("pe", lambda e: e.matmul(ps[:], lhsT=bl, rhs=br, start=False, stop=(bi == len(biases) - 1)), bregs, [pr])

                    def post_cmp_dve():
                        pc2 = pc[:, 256:512].rearrange("p (h j) -> p h j", h=4)
                        P.op("dve", lambda e: e.tensor_reduce(out=den[:], in_=pc2, axis=AX.X, op=ALU.add), [rpc], [r_den])
                        P.op("dve", lambda e: e.tensor_scalar(out=den[:], in0=den[:], scalar1=0.5, scalar2=1e-30, op0=ALU.mult, op1=ALU.max), [r_den], [r_den])
                        P.op("dve", lambda e: e.reciprocal(out=den[:], in_=den[:]), [r_den], [r_den])
                        P.op("dve", lambda e: e.tensor_scalar(out=imp[:], in0=pc2[:, 0, :], scalar1=den[:, 0:1], scalar2=None, op0=ALU.mult), [rpc, r_den], [r_imp])
                        for h in range(1, 4):
                            P.op("dve", lambda e: e.scalar_tensor_tensor(out=imp[:], in0=pc2[:, h, :], scalar=den[:, h:h + 1], in1=imp[:], op0=ALU.mult, op1=ALU.add), [rpc, r_den, r_imp], [r_imp])
                        P.op("dve", lambda e: e.tensor_tensor(out=score[:], in0=imp[:], in1=mtab[:, 0, i, :], op=ALU.mult), [r_imp, r_mt], [r_sc])
                        P.op("dve", lambda e: e.tensor_tensor(out=score[:], in0=score[:], in1=mtab[:, 1, i, :], op=ALU.add), [r_sc, r_mt], [r_sc])
                        P.op("dve", lambda e: e.max(out=m8[:], in_=score[:]), [r_sc], [r_m8])
                        P.op("dve", lambda e: e.match_replace(out=wk[:], in_to_replace=m8[:], in_values=score[:], imm_value=-2.0), [r_m8, r_sc], [r_wk])
                        P.op("dve", lambda e: e.max(out=m8[:], in_=wk[:]), [r_wk], [r_m8])
                        P.op("dve", lambda e: e.match_replace(out=wk[:], in_to_replace=m8[:], in_values=wk[:], imm_value=-2.0), [r_m8, r_wk], [r_wk])
                        P.op("dve", lambda e: e.tensor_tensor(out=sel[:], in0=score[:], in1=wk[:], op=ALU.subtract), [r_sc, r_wk], [r_sel])
                        P.op("dve", lambda e: e.scalar_tensor_tensor(out=sel[:], in0=sel[:], scalar=1.0, in1=mtab[:, 2, i, :], op0=ALU.min, op1=ALU.mult), [r_sel, r_mt], [r_sel])
                        P.op("dve", lambda e: e.tensor_tensor(out=cg[:], in0=den[:], in1=gsl[:, :, 0], op=ALU.mult), [r_den, r_gt], [r_cg])
                        P.op("dve", lambda e: e.tensor_tensor(out=oacc[:], in0=pc[:, 0:256].rearrange("p (h d) -> p h d", h=4), in1=cg[:].unsqueeze(2).to_broadcast([128, 4, 64]), op=ALU.mult), [rpc, r_cg], [r_oa])

                    def pre_slc():
                        pt, rpt = self.ps[T32], self.psr[T32]
                        P.op("pe", lambda e: e.transpose(out=pt[0:64, 0:128], in_=sel[:], identity=self.ident_f[:]), [r_sel], [rpt])
                        P.op("act", lambda e: e.activation(out=sb_[0:64], in_=pt[0:64, 0:128].unsqueeze(1).to_broadcast([64, 4, 128]), func=AF.Identity, bias=negb[0:64, 0:1], scale=BIG), [rpt, r_negb], [rsb_])

                    LA = getattr(self, "lookahead", 1)
                    if LA:
                        emit_score(0)
                    for k in range(N):
                        if not LA:
                            if k == n_pre:
                                pre_slc()
                            emit_score(k)
                        elif k + 1 < N:
                            if k + 1 == n_pre:
                                pre_slc()
                            emit_score(k + 1)
                        ps, pr = sbank.pop(k)
                        p_, rp_ = pb[cnt["p"] % 4]
                        cnt["p"] += 1
                        P.op("act", lambda e: e.activation(out=p_[:], in_=ps[:].rearrange("p (h q) -> p h q", h=4), func=AF.Exp), [pr], [rp_])
                        steps[k][4](p_, rp_)
                        if k == len(nts) - 1:
                            post_cmp_dve()
                            if pending[0] is not None:
                                pending[0]()
                                pending[0] = None
                    P.op("dve", lambda e: e.tensor_scalar(out=den2[:], in0=psov[:, :, 64], scalar1=1e-30, scalar2=None, op0=ALU.max), [rpso], [r_den2])
                    P.op("dve", lambda e: e.reciprocal(out=den2[:], in_=den2[:]), [r_den2], [r_den2])
                    P.op("dve", lambda e: e.tensor_tensor(out=cg2[:], in0=den2[:], in1=gsl[:, :, 1], op=ALU.mult), [r_den2, r_gt], [r_cg2])
                    P.op("dve", lambda e: e.tensor_tensor(out=otmp[:], in0=psov[:, :, 0:64], in1=cg2[:].unsqueeze(2).to_broadcast([128, 4, 64]), op=ALU.mult), [rpso, r_cg2], [r_ot])
                    P.op("dve", lambda e: e.tensor_tensor(out=oacc[:], in0=oacc[:], in1=otmp[:], op=ALU.add), [r_oa, r_ot], [r_oa])
                    P.op("dve", lambda e: e.tensor_scalar(out=den2[:], in0=pwov[:, :, 64], scalar1=1e-30, scalar2=None, op0=ALU.max), [rpwo], [r_den2])
                    P.op("dve", lambda e: e.reciprocal(out=den2[:], in_=den2[:]), [r_den2], [r_den2])
                    P.op("dve", lambda e: e.tensor_tensor(out=cg2[:], in0=den2[:], in1=gsl[:, :, 2], op=ALU.mult), [r_den2, r_gt], [r_cg2])
                    P.op("dve", lambda e: e.tensor_tensor(out=otmp[:], in0=pwov[:, :, 0:64], in1=cg2[:].unsqueeze(2).to_broadcast([128, 4, 64]), op=ALU.mult), [rpwo, r_cg2], [r_ot])
                    P.op("dve", lambda e: e.tensor_tensor(out=ob_[:].rearrange("p (h d) -> p h d", h=4), in0=oacc[:], in1=otmp[:], op=ALU.add), [r_oa, r_ot], [rob_])

                    def post_pe(i=i, ob_=ob_, rob_=rob_, os_=os_, ros_=ros_):
                        rptb = self.psr[TBF]
                        for half in range(2):
                            P.op("pe", lambda e: e.transpose(out=psbf[:, half * 128:(half + 1) * 128], in_=ob_[:, half * 128:(half + 1) * 128], identity=self.ident_b[:]), [rob_], [rptb])
                        P.op("act", lambda e: e.activation(out=os_[:], in_=psbf[:, 0:256].rearrange("p (a q) -> p a q", a=2), func=AF.Copy), [rptb], [ros_])
                        c0 = PAD + r0 + i * 128
                        dst = self.OC[g * 256:(g + 1) * 256, c0:c0 + 128].rearrange("(a p) t -> p a t", p=128)
                        P.dma("sp", lambda e: e.dma_start(out=dst, in_=os_[:]), ros_, False)
                    pending[0] = post_pe
                if pending[0] is not None:
                    pending[0]()
                    pending[0] = None
            P.end_phase()

    def phase_merge(self, l, X, Xout, r0, r1):
        P = self.P
        Wl = self.w_in[l]
        with ExitStack() as st:
            nt = self.norm_tiles(st)
            pt = self.post_tiles(st)
            hT, r_h = self.tile(st, "hT", [128, 16, 512], BF16)
            ins = [self.tile(st, "hin", [128, 8, 512], BF16) for _ in range(3)]
            wb = [self.tile(st, "wbuf", [128, 8192], BF16) for _ in range(2)]
            mT, r_m = self.tile(st, "mT", [128, 16, 512], BF16)
            mixed, r_mx = self.tile(st, "mixed", [128, 16, 512], F32)
            sgs = [self.tile(st, "sgs", [128, 4, 512], BF16) for _ in range(2)]
            accm, r_accm = self.tile(st, "accm", [128, 4, 512], F32)
            tmp = [self.tile(st, "tmp", [128, 512], F32) for _ in range(2)]
            sq = [self.tile(st, "sq", [128, 512], BF16) for _ in range(2)]
            srcs = [self.HA, self.HB, self.OC]
            wouts = [self.w_a_out[l], self.w_b_out[l], self.w_c_out[l]]
            hregs = [self.newreg("hT"), self.newreg("hT")]
            sched = []
            pk = [0]

            def nb():
                bk = pk[0] % 6
                pk[0] += 1
                return self.ps[bk], self.psr[bk]
            for (s0, n) in self.tok_tiles(r0, r1 - r0):
                sc_ = {}
                sched.append(sc_)

                def nfn(b, s0=s0, n=n):
                    self.fill_hT(nt, X, PAD + s0, n, (l * 4 + 0) * 16, hT, hregs)
                    for b3 in range(3):
                        P.dma("sp", lambda e: e.dma_start(out=ins[b3][0][:, :, 0:n], in_=srcs[b3].rearrange("(c p) t -> p c t", p=128)[:, :, PAD + s0:PAD + s0 + n]),
                              ins[b3][1], True)
                sc_["N"] = [(None, nfn)]
                jobs = []
                for dg in range(4):
                    for b3 in range(3):
                        def lfg(b, dg=dg, b3=b3):
                            self.wload(wb[b][0], wb[b][1], Wl, 16, C_M + b3 * 2048 + dg * 512, 512, 0)

                        def cfg(b, dg=dg, b3=b3, n=n, hregs=hregs):
                            wbt, rwb = wb[b]
                            wg = wbt[:, 0:8192].rearrange("p (k n) -> p k n", k=16)
                            s_, rs_ = sgs[b3 % 2]
                            for cc in range(4):
                                pg, rg = nb()
                                for kc in range(16):
                                    P.op("pe", lambda e: e.matmul(pg[:, 0:n], lhsT=wg[:, kc, cc * 128:(cc + 1) * 128], rhs=hT[:, kc, 0:n], start=(kc == 0), stop=(kc == 15)), [rwb] + hregs, [rg])
                                P.op("act", lambda e: e.activation(out=s_[:, cc, 0:n], in_=pg[:, 0:n], func=AF.Sigmoid), [rg], [rs_])
                        jobs.append((lfg, cfg))

                        def lfy(b, dg=dg, b3=b3):
                            self.wload(wb[b][0], wb[b][1], wouts[b3], 8, dg * 512, 512, 0)

                        def cfy(b, dg=dg, b3=b3, n=n):
                            wbt, rwb = wb[b]
                            wy = wbt[:, 0:4096].rearrange("p (k n) -> p k n", k=8)
                            s_, rs_ = sgs[b3 % 2]
                            for cc in range(4):
                                py, ry = nb()
                                for kc in range(8):
                                    P.op("pe", lambda e: e.matmul(py[:, 0:n], lhsT=wy[:, kc, cc * 128:(cc + 1) * 128], rhs=ins[b3][0][:, kc, 0:n], start=(kc == 0), stop=(kc == 7)), [rwb, ins[b3][1]], [ry])
                                if b3 == 0:
                                    P.op("dve", lambda e: e.tensor_tensor(out=accm[:, cc, 0:n], in0=py[:, 0:n], in1=s_[:, cc, 0:n], op=ALU.mult), [ry, rs_], [r_accm])
                                else:
                                    t_, rt_ = tmp[cc % 2]
                                    P.op("dve", lambda e: e.tensor_tensor(out=t_[:, 0:n], in0=py[:, 0:n], in1=s_[:, cc, 0:n], op=ALU.mult), [ry, rs_], [rt_])
                                    if b3 == 1:
                                        P.op("dve", lambda e: e.tensor_tensor(out=accm[:, cc, 0:n], in0=accm[:, cc, 0:n], in1=t_[:, 0:n], op=ALU.add), [r_accm, rt_], [r_accm])
                                    else:
                                        P.op("dve", lambda e: e.tensor_tensor(out=mT[:, dg * 4 + cc, 0:n], in0=accm[:, cc, 0:n], in1=t_[:, 0:n], op=ALU.add), [r_accm, rt_], [r_m])
                        jobs.append((lfy, cfy))
                sc_["A"] = jobs
                jobs = []
                for jg in range(4):
                    def lf(b, jg=jg):
                        self.wload(wb[b][0], wb[b][1], self.w_o[l], 16, jg * 512, 512, 0)

                    def cf(b, jg=jg, n=n):
                        wo = wb[b][0][:, 0:8192].rearrange("p (k n) -> p k n", k=16)
                        for cc in range(4):
                            dch = jg * 4 + cc
                            po, ro = self.ps[dch % 4], self.psr[dch % 4]
                            for kc in range(16):
                                P.op("pe", lambda e, kc=kc, po=po, cc=cc: e.matmul(po[:, 0:n], lhsT=wo[:, kc, cc * 128:(cc + 1) * 128], rhs=mT[:, kc, 0:n], start=(kc == 0), stop=(kc == 15)), [wb[b][1], r_m], [ro])
                            s_, rs_ = sq[dch % 2]
                            P.op("act", lambda e, po=po, dch=dch: e.activation(out=mixed[:, dch, 0:n], in_=po[:, 0:n], func=AF.Copy), [ro], [r_mx])
                            P.op("act", lambda e, po=po, s_=s_: e.activation(out=s_[:, 0:n], in_=po[:, 0:n], func=AF.Square), [ro], [rs_])
                            P.op("pe", lambda e, s_=s_, dch=dch: e.matmul(self.ps[6][:, 0:n], lhsT=self.ones_b[:], rhs=s_[:, 0:n], start=(dch == 0), stop=(dch == 15)), [rs_], [self.psr[6]])
                    jobs.append((lf, cf))
                sc_["B"] = jobs
                sc_["P"] = [(None, lambda b, s0=s0, n=n: self.post_norm(st, mixed, r_mx, 6, n, (l * 4 + 1) * 16, X, Xout, PAD + s0, PAD + s0, pt))]
            nt_ = len(sched)
            seq = sched[0]["N"] + sched[0]["A"]
            for t in range(nt_):
                if t + 1 < nt_:
                    seq += sched[t + 1]["N"]
                seq += sched[t]["B"]
                if t + 1 < nt_:
                    seq += sched[t + 1]["A"][:4] + sched[t]["P"] + sched[t + 1]["A"][4:]
                else:
                    seq += sched[t]["P"]
            self.run_jobs(seq)
            P.end_phase()

    def phase_ffn(self, l, X, Xout, f0, f1, out_col0):
        P = self.P
        Wu = self.w_up[l]
        Wd = self.w_down[l]
        with ExitStack() as st:
            nt = self.norm_tiles(st)
            pt = self.post_tiles(st)
            hT, _ = self.tile(st, "hT", [128, 16, 512], BF16)
            wb = [self.tile(st, "wbuf", [128, 8192], BF16) for _ in range(2)]
            act, r_act = self.tile(st, "act", [128, 44, 512], BF16)
            mixed, r_mx = self.tile(st, "mixed", [128, 16, 512], F32)
            pre = [self.tile(st, "pre", [128, 514], F32) for _ in range(4)]
            uu = [self.tile(st, "uu", [128, 512], F32) for _ in range(4)]
            sgf, r_sgf = self.tile(st, "sgf", [128, 4, 512], F32)
            sq = [self.tile(st, "sq", [128, 512], BF16) for _ in range(2)]
            carry, r_carry = self.tile(st, "carry", [128, 88, 2], F32)
            fw, r_fw = self.tile(st, "fw", [128, 88, 3], F32)
            fb, r_fb = self.tile(st, "fb", [128, 88], F32)
            P.dma("sp", lambda e: e.dma_start(out=fw[:], in_=self.ffw[:, l * 264:(l + 1) * 264].rearrange("p (c k) -> p c k", k=3)), r_fw, True)
            P.dma("sp", lambda e: e.dma_start(out=fb[:], in_=self.ffb[:, l * 88:(l + 1) * 88]), r_fb, True)
            tiles = [(f0 - 2, 2)] + self.tok_tiles(f0, f1 - f0)
            pk = [0]
            hregs = [self.newreg("hT"), self.newreg("hT")]
            sched = {}
            for tix, (s0, n) in enumerate(tiles):
                halo = (tix == 0)
                sched[tix] = {}
                sched[tix]["N"] = [(None, lambda b, s0=s0, n=n: self.fill_hT(nt, X, PAD + s0, n, (l * 4 + 2) * 16, hT, hregs))]
                jobs = []
                for grp in range(11):
                    for gv in range(2):
                        def lf(b, grp=grp, gv=gv):
                            self.wload(wb[b][0], wb[b][1], Wu, 16, gv * DFF + grp * 512, 512, 0)

                        def cf(b, grp=grp, gv=gv, n=n, halo=halo, hregs=hregs):
                            wbt, rwb = wb[b]
                            wv_ = wbt[:, 0:8192].rearrange("p (k n) -> p k n", k=16)
                            for cc in range(4):
                                jg = grp * 4 + cc
                                j = jg + 44 * gv
                                bk = pk[0] % 6
                                pk[0] += 1
                                ps, pr = self.ps[bk], self.psr[bk]
                                for kc in range(16):
                                    P.op("pe", lambda e: e.matmul(ps[:, 0:n], lhsT=wv_[:, kc, cc * 128:(cc + 1) * 128], rhs=hT[:, kc, 0:n], start=(kc == 0), stop=(kc == 15)),
                                         [rwb] + hregs, [pr])
                                if halo:
                                    P.op("act", lambda e: e.activation(out=carry[:, j, :], in_=ps[:, 0:2], func=AF.Copy), [pr], [r_carry])
                                    continue
                                p_, rp_ = pre[cc % 4]
                                u_, ru_ = uu[cc % 4]
                                P.op("act", lambda e: e.activation(out=p_[:, 2:2 + n], in_=ps[:, 0:n], func=AF.Copy), [pr], [rp_])
                                P.op("act", lambda e: e.activation(out=p_[:, 0:2], in_=carry[:, j, :], func=AF.Copy), [r_carry], [rp_])
                                P.op("act", lambda e: e.activation(out=carry[:, j, :], in_=p_[:, n:n + 2], func=AF.Copy), [rp_], [r_carry])
                                P.op("dve", lambda e: e.tensor_scalar(out=u_[:, 0:n], in0=p_[:, 2:2 + n], scalar1=fw[:, j, 2:3], scalar2=fb[:, j:j + 1], op0=ALU.mult, op1=ALU.add), [rp_, r_fw, r_fb], [ru_])
                                P.op("dve", lambda e: e.scalar_tensor_tensor(out=u_[:, 0:n], in0=p_[:, 1:1 + n], scalar=fw[:, j, 1:2], in1=u_[:, 0:n], op0=ALU.mult, op1=ALU.add), [rp_, ru_], [ru_])
                                P.op("dve", lambda e: e.scalar_tensor_tensor(out=u_[:, 0:n], in0=p_[:, 0:n], scalar=fw[:, j, 0:1], in1=u_[:, 0:n], op0=ALU.mult, op1=ALU.add), [rp_, ru_], [ru_])
                                if gv == 0:
                                    P.op("act", lambda e: e.activation(out=sgf[:, cc, 0:n], in_=u_[:, 0:n], func=AF.Silu), [ru_], [r_sgf])
                                else:
                                    P.op("dve", lambda e: e.tensor_tensor(out=act[:, jg, 0:n], in0=sgf[:, cc, 0:n], in1=u_[:, 0:n], op=ALU.mult), [r_sgf, ru_], [r_act])
                        jobs.append((lf, cf))
                sched[tix]["U"] = jobs
                jobs = []
                if not halo:
                    kranges = [(0, 16), (16, 16), (32, 12)]
                    for dg in range(4):
                        for kr, (k0_, nk) in enumerate(kranges):
                            def lf(b, dg=dg, k0_=k0_, nk=nk):
                                self.wload(wb[b][0], wb[b][1], Wd, nk, dg * 512, 512, 0, k0=k0_)

                            def cf(b, dg=dg, kr=kr, k0_=k0_, nk=nk, n=n):
                                wd = wb[b][0][:, 0:nk * 512].rearrange("p (k n) -> p k n", k=nk)
                                for cc in range(4):
                                    dch = dg * 4 + cc
                                    po, ro = self.ps[cc], self.psr[cc]
                                    for kc in range(nk):
                                        P.op("pe", lambda e: e.matmul(po[:, 0:n], lhsT=wd[:, kc, cc * 128:(cc + 1) * 128], rhs=act[:, k0_ + kc, 0:n], start=(kr == 0 and kc == 0), stop=(kr == 2 and kc == nk - 1)),
                                             [wb[b][1], r_act], [ro])
                                    if kr == 2:
                                        s_, rs_ = sq[dch % 2]
                                        P.op("act", lambda e: e.activation(out=mixed[:, dch, 0:n], in_=po[:, 0:n], func=AF.Copy), [ro], [r_mx])
                                        P.op("act", lambda e: e.activation(out=s_[:, 0:n], in_=po[:, 0:n], func=AF.Square), [ro], [rs_])
                                        P.op("pe", lambda e: e.matmul(self.ps[6][:, 0:n], lhsT=self.ones_b[:], rhs=s_[:, 0:n], start=(dch == 0), stop=(dch == 15)), [rs_], [self.psr[6]])
                            jobs.append((lf, cf))
                sched[tix]["D"] = jobs
                sched[tix]["P"] = [(None, lambda b, s0=s0, n=n: self.post_norm(st, mixed, r_mx, 6, n, (l * 4 + 3) * 16, X, Xout, PAD + s0, out_col0 + (s0 - f0), pt))]
            nt_ = len(tiles)
            seq = sched[0]["N"] + sched[0]["U"] + sched[1]["N"] + sched[1]["U"]
            for t in range(1, nt_):
                if t + 1 < nt_:
                    seq += sched[t + 1]["N"]
                seq += sched[t]["D"]
                if t + 1 < nt_:
                    seq += sched[t + 1]["U"][:4] + sched[t]["P"] + sched[t + 1]["U"][4:]
                else:
                    seq += sched[t]["P"]
            self.run_jobs(seq)
            P.end_phase()

    def build(self):
        ph = []
        ph.append(lambda: self.phase_kv(0, self.xT))
        for (r0, r1) in ((0, 2048), (2048, 4096)):
            ph.append(lambda r0=r0, r1=r1: self.phase_m1(0, self.xT, r0, r1))
            ph.append(lambda r0=r0, r1=r1: self.phase_m2(0, r0, r1))
            ph.append(lambda r0=r0, r1=r1: self.phase_att(0, r0, r1))
            ph.append(lambda r0=r0, r1=r1: self.phase_merge(0, self.xT, self.XM, r0, r1))
        ph.append(lambda: self.phase_ffn(0, self.XM, self.X1, 0, 4096, PAD))
        ph.append(lambda: self.phase_kv(1, self.X1))
        ph.append(lambda: self.phase_m1(1, self.X1, 1920, 4096))
        ph.append(lambda: self.phase_m2(1, 1920, 4096))
        ph.append(lambda: self.phase_att(1, 1920, 4096))
        ph.append(lambda: self.phase_merge(1, self.X1, self.XM, 1920, 4096))
        ph.append(lambda: self.phase_ffn(1, self.XM, self.OUT, 2048, 4096, 0))
        sel = self.stop if self.stop is not None else range(len(ph))
        for i in sel:
            ph[i]()
        self.P.barrier()
        self.P.emit()
        return self.nc


def _colvec(v, nchunk):
    return np.ascontiguousarray(v.reshape(nchunk, 128).T)


def make_inputs(inp):
    L = 2
    f = lambda a: np.ascontiguousarray(np.asarray(a, dtype=np.float32))
    shared = {}
    for k in ("w_in", "w_a_out", "w_b_out", "w_c_out", "w_o", "w_up", "w_down"):
        shared[k] = f(inp[k])
    nw = np.zeros((128, L * 4 * 16), np.float32)
    for l in range(L):
        for i, k in enumerate(("norm_mix_pre", "norm_mix_post", "norm_ffn_pre", "norm_ffn_post")):
            nw[:, (l * 4 + i) * 16:(l * 4 + i + 1) * 16] = _colvec(f(inp[k])[l], 16)
    shared["normw"] = nw
    caw = np.zeros((128, L * 8 * 31), np.float32)
    av = np.zeros((128, L * 3 * 8), np.float32)
    for l in range(L):
        w = f(inp["conv_a_w"])[l]
        caw[:, l * 248:(l + 1) * 248] = w.T.reshape(8, 128, 31).transpose(1, 0, 2).reshape(128, 248)
        for i, k in enumerate(("conv_a_b", "ln_a_g", "ln_a_b")):
            av[:, (l * 3 + i) * 8:(l * 3 + i + 1) * 8] = _colvec(f(inp[k])[l], 8)
    shared["convaw"] = caw
    shared["avec"] = av
    lnb = np.zeros((128, L * 2 * 1024), np.float32)
    for l in range(L):
        lnb[:, (l * 2) * 1024:(l * 2 + 1) * 1024] = f(inp["ln_b_g"])[l][None, :]
        lnb[:, (l * 2 + 1) * 1024:(l * 2 + 2) * 1024] = f(inp["ln_b_b"])[l][None, :]
    shared["lnb"] = lnb
    shared["sgw"] = np.ascontiguousarray(f(inp["sg_w"]).transpose(0, 3, 1, 2).reshape(L, 128, 1024))
    shared["sgb"] = np.ascontiguousarray(f(inp["sg_b"]).reshape(1, L * 1024))
    cw1 = np.zeros((L * 2, 64, 32 * 256), np.float32)
    cw2 = np.zeros((L * 2, 256, 64), np.float32)
    cpe = np.zeros((L * 2, 64, 32), np.float32)
    for l in range(L):
        for kv, s in enumerate(("k", "v")):
            cw1[l * 2 + kv] = f(inp["cmp_w1_" + s])[l].transpose(1, 0, 2).reshape(64, 32 * 256)
            cw2[l * 2 + kv] = f(inp["cmp_w2_" + s])[l]
            cpe[l * 2 + kv] = f(inp["cmp_pe_" + s])[l].T
    shared["cw1"], shared["cw2"], shared["cpe"] = cw1, cw2, cpe
    ffw = np.zeros((128, L * 88 * 3), np.float32)
    ffb = np.zeros((128, L * 88), np.float32)
    for l in range(L):
        w = f(inp["ffn_conv_w"])[l]
        ffw[:, l * 264:(l + 1) * 264] = w.T.reshape(88, 128, 3).transpose(1, 0, 2).reshape(128, 264)
        ffb[:, l * 88:(l + 1) * 88] = _colvec(f(inp["ffn_conv_b"])[l], 88)
    shared["ffw"], shared["ffb"] = ffw, ffb
    p = np.arange(128)
    shared["c_ident"] = np.eye(128, dtype=np.float32)
    shared["c_trile"] = (p[:, None] <= p[None, :]).astype(np.float32)
    shared["c_trigt"] = (p[:, None] > p[None, :]).astype(np.float32)
    shared["c_dtab"] = (16.0 * p[:, None] - p[None, :]).astype(np.float32)
    k = np.arange(4096)
    shared["c_E"] = (k[None, :] // 64 == np.arange(64)[:, None]).astype(np.float32)
    n = np.arange(256)
    sm = np.zeros((256, 64), np.float32)
    for nn in range(255):
        sm[nn, nn // 4] += 1.0
        sm[nn, (nn + 1) // 4] += 1.0
    shared["c_smap"] = np.ascontiguousarray(sm.reshape(2, 128, 64).transpose(1, 0, 2).reshape(128, 128))
    x = f(inp["x"])
    maps = []
    for b in range(4):
        for s in range(2):
            m = dict(shared)
            xT = np.zeros((D, NCOL), np.float32)
            tok = np.zeros((NCOL,), np.float32)
            if s == 1:
                xT[:, PAD:] = x[b].T
                tok[PAD:] = 1.0
                j0 = 0
            else:
                xT[:, PAD + 2048:] = x[b, :2048].T
                tok[PAD + 2048:] = 1.0
                j0 = 32
            m["xT"] = xT
            m["m_tok"] = np.ascontiguousarray(np.broadcast_to(tok[None, :], (128, NCOL)))
            kval = np.ones((128, 32), np.float32)
            nval = np.ones((128, 2), np.float32)
            if s == 0:
                kval[:, :16] = 0.0
                nval[:, 0] = 0.0
            m["m_kval"], m["m_nval"] = kval, nval
            t = np.arange(4096).reshape(32, 128).T
            cur = t // 64
            j = np.arange(64)[None, None, :]
            valid = (j <= cur[:, :, None]) & (j >= j0)
            forced = ((j == j0) | (j == cur[:, :, None]) | (j == cur[:, :, None] - 1)) & valid
            M1 = (valid & ~forced).astype(np.float32)
            M2 = np.where(forced, 1e4 + j, np.where(valid, 0.0, -1.0)).astype(np.float32)
            M3 = valid.astype(np.float32)
            m["m_sel"] = np.ascontiguousarray(np.stack([M1, M2, M3]).reshape(3, 128, 32 * 64))
            maps.append(m)
    return maps


_CACHE = {}


def kernel(**inputs):
    maps = make_inputs(inputs)
    if "nc" not in _CACHE:
        _CACHE["nc"] = Builder().build()
    nc = _CACHE["nc"]
    res = run_bass_kernel_spmd(nc, maps, core_ids=list(range(8)))
    out = np.zeros((4, SEQ, D), np.float32)
    for b in range(4):
        for s in range(2):
            o = res.results[b * 2 + s]["OUT"]
            out[b, s * 2048:(s + 1) * 2048, :] = o.T
    return out
```

```python
import numpy as np
from contextlib import ExitStack
import concourse.bass as bass
import concourse.mybir as mybir
from concourse.bass_utils import run_bass_kernel_spmd

F32 = mybir.dt.float32
BF16 = mybir.dt.bfloat16
AF = mybir.ActivationFunctionType
ALU = mybir.AluOpType
AX = mybir.AxisListType

ENGS = ("pe", "act", "dve", "pool", "sp")

D = 2048
SEQ = 4096
PAD = 128
NCOL = PAD + SEQ
NIN = 12848
DFF = 5632
EPS = 1e-6
C_A, C_B, C_Q, C_KV, C_G, C_M = 0, 2048, 4096, 5120, 6656, 6704


class Reg:
    __slots__ = ("name", "w", "r", "dkey", "dcount")

    def __init__(self, name):
        self.name = name
        self.w = None
        self.r = {}
        self.dkey = None
        self.dcount = 0


class _Rec:
    def __init__(self):
        self.call = None

    def __getattr__(self, name):
        def f(*a, **k):
            self.call = (name, a, k)
            return self
        return f


def _record(fn):
    r = _Rec()
    fn(r)
    assert r.call is not None
    return r.call


class Prog:
    def __init__(self, nc):
        self.nc = nc
        self.streams = {e: [] for e in ENGS}
        self.cnt = {e: 0 for e in ENGS}
        self.seen = {e: {} for e in ENGS}
        self.dsems = {}
        self.ndsem = 0
        self.semh = {}
        self.free_dkeys = []
        self.phase_keys = []

    def sb(self, stack, name, shape, dt):
        return stack.enter_context(self.nc.sbuf_tensor(name, list(shape), dt))

    def _waits(self, eng, deps):
        out = []
        seen = self.seen[eng]
        best = {}
        for (k, v) in deps:
            if best.get(k, -1) < v:
                best[k] = v
        for k, v in best.items():
            if k == eng:
                if eng == "pe":
                    continue
                if v <= self.cnt[eng] - 2:
                    continue
            if seen.get(k, -1) >= v:
                continue
            seen[k] = v
            out.append((k, v))
        return out

    def _deps(self, reads, writes):
        deps = []
        for r in reads:
            if r.w is not None:
                deps.append(r.w)
        for w in writes:
            if w.w is not None:
                deps.append(w.w)
            deps.extend(w.r.items())
        return deps

    def op(self, eng, fn, reads=(), writes=()):
        waits = self._waits(eng, self._deps(reads, writes))
        self.cnt[eng] += 1
        me = (eng, self.cnt[eng])
        self.streams[eng].append((waits, _record(fn), (eng, 1)))
        for r in reads:
            r.r[me[0]] = me[1]
        for w in writes:
            w.w = me
            w.r = {}
        return me

    def dma(self, q, fn, sb, load, dram=()):
        if sb.dkey is None:
            if self.free_dkeys:
                sb.dkey = self.free_dkeys.pop()
                sb.dcount = self.dsems[sb.dkey]
            else:
                sb.dkey = "d%d" % self.ndsem
                self.ndsem += 1
            self.phase_keys.append(sb.dkey)
        if load:
            deps = self._deps(dram, [sb])
        else:
            deps = self._deps([sb], dram)
        waits = self._waits(q, deps)
        sb.dcount += 16
        self.dsems[sb.dkey] = sb.dcount
        me = (sb.dkey, sb.dcount)
        self.streams[q].append((waits, _record(fn), (sb.dkey, 16)))
        if load:
            sb.w = me
            sb.r = {}
            for d in dram:
                d.r[me[0]] = me[1]
        else:
            sb.r[me[0]] = me[1]
            for d in dram:
                d.w = me
                d.r = {}
        return me

    def barrier(self):
        allk = [(e, self.cnt[e]) for e in ENGS if self.cnt[e] > 0]
        allk += list(self.dsems.items())
        for e in ENGS:
            waits = self._waits(e, [kv for kv in allk if kv[0] != e])
            if waits:
                self.streams[e].append((waits, None, None))

    def end_phase(self, persistent=False):
        self.barrier()
        if not persistent:
            self.free_dkeys.extend(self.phase_keys)
        self.phase_keys = []

    def emit(self):
        nc = self.nc
        keys = list(ENGS) + list(self.dsems.keys())
        with ExitStack() as st:
            for k in keys:
                self.semh[k] = st.enter_context(nc.semaphore("s_" + k))
            block = st.enter_context(nc.Block())
            semh = self.semh

            def run(e, stream):
                for waits, fn, inc in stream:
                    for (k, v) in waits:
                        e.wait_ge(semh[k], v)
                    if fn is not None:
                        name, a, k = fn
                        ins = getattr(e, name)(*a, **k)
                        ins.then_inc(semh[inc[0]], inc[1])

            @block.tensor
            def _(e):
                run(e, self.streams["pe"])

            @block.scalar
            def _(e):
                run(e, self.streams["act"])

            @block.vector
            def _(e):
                run(e, self.streams["dve"])

            @block.gpsimd
            def _(e):
                run(e, self.streams["pool"])

            @block.sync
            def _(e):
                run(e, self.streams["sp"])


class Builder:
    def __init__(self, debug=False, nlayers=2, stop=None):
        self.debug = debug
        self.stop = stop
        nc = bass.Bass("TRN2", target_bir_lowering=False)
        self.nc = nc
        self.P = Prog(nc)
        self.I = {}
        self.regs = {}
        self.gst = ExitStack()
        self._uid = 0
        self.declare_io()
        self.alloc_consts()

    def R(self, name):
        if name not in self.regs:
            self.regs[name] = Reg(name)
        return self.regs[name]

    def newreg(self, name):
        self._uid += 1
        return Reg("%s_%d" % (name, self._uid))

    def din(self, name, shape, dt=F32):
        t = self.nc.dram_tensor(name, list(shape), dt, kind="ExternalInput").ap()
        self.I[name] = t
        return t

    def dscr(self, name, shape, dt, out=False):
        kind = "ExternalOutput" if (out or self.debug) else "Internal"
        return self.nc.dram_tensor(name, list(shape), dt, kind=kind).ap()

    def tile(self, st, name, shape, dt):
        self._uid += 1
        t = self.P.sb(st, "%s_%d" % (name, self._uid), shape, dt)
        return t, self.newreg(name)

    def dump(self, name, tile_, reg, shape, dt=F32):
        if not self.debug or True:
            return
        d = self.nc.dram_tensor("dbg_" + name, list(shape), dt, kind="ExternalOutput").ap()
        self.P.dma("sp", lambda e: e.dma_start(out=d, in_=tile_[:]), reg, False)

    def declare_io(self):
        L = 2
        self.xT = self.din("xT", [D, NCOL])
        self.w_in = self.din("w_in", [L, D, NIN])
        self.w_a_out = self.din("w_a_out", [L, 1024, D])
        self.w_b_out = self.din("w_b_out", [L, 1024, D])
        self.w_c_out = self.din("w_c_out", [L, 1024, D])
        self.w_o = self.din("w_o", [L, D, D])
        self.w_up = self.din("w_up", [L, D, 2 * DFF])
        self.w_down = self.din("w_down", [L, DFF, D])
        self.normw = self.din("normw", [128, L * 4 * 16])
        self.convaw = self.din("convaw", [128, L * 8 * 31])
        self.avec = self.din("avec", [128, L * 3 * 8])
        self.lnb = self.din("lnb", [128, L * 2 * 1024])
        self.sgw = self.din("sgw", [L, 128, 8 * 128])
        self.sgb = self.din("sgb", [1, L * 1024])
        self.cw1 = self.din("cw1", [L * 2, 64, 32 * 256])
        self.cw2 = self.din("cw2", [L * 2, 256, 64])
        self.cpe = self.din("cpe", [L * 2, 64, 32])
        self.ffw = self.din("ffw", [128, L * 88 * 3])
        self.ffb = self.din("ffb", [128, L * 88])
        self.c_ident = self.din("c_ident", [128, 128])
        self.c_trile = self.din("c_trile", [128, 128])
        self.c_trigt = self.din("c_trigt", [128, 128])
        self.c_dtab = self.din("c_dtab", [128, 128])
        self.c_E = self.din("c_E", [64, 32 * 128])
        self.c_smap = self.din("c_smap", [128, 2 * 64])
        self.m_tok = self.din("m_tok", [128, NCOL])
        self.m_kval = self.din("m_kval", [128, 32])
        self.m_nval = self.din("m_nval", [128, 2])
        self.m_sel = self.din("m_sel", [3, 128, 32 * 64])
        self.XM = self.dscr("XM", [D, NCOL], F32)
        self.X1 = self.dscr("X1", [D, NCOL], F32)
        self.OUT = self.dscr("OUT", [D, 2048], F32, out=True)
        self.GLU = self.dscr("GLU", [1024, NCOL], BF16)
        self.HA = self.dscr("HA", [1024, NCOL], BF16)
        self.HB = self.dscr("HB", [1024, NCOL], BF16)
        self.OC = self.dscr("OC", [1024, NCOL], BF16)
        self.Q = self.dscr("Q", [1024, NCOL], BF16)
        self.GATES = self.dscr("GATES", [NCOL, 48], F32)
        self.KT = [self.dscr("KT%d" % i, [256, NCOL], BF16) for i in range(4)]
        self.VT = [self.dscr("VT%d" % i, [NCOL, 256], BF16) for i in range(2)]

    def alloc_consts(self):
        P, st = self.P, self.gst
        nc = self.nc
        T = lambda n, s, d: self.tile(st, n, s, d)
        self.ident_f, r_if = T("ident_f", [128, 128], F32)
        self.ident_b, r_ib = T("ident_b", [128, 128], BF16)
        self.ones_b, r_ob = T("ones_b", [128, 128], BF16)
        self.trile, r1 = T("trile", [128, 128], F32)
        self.trigt, r2 = T("trigt", [128, 128], F32)
        self.dtab, r3 = T("dtab", [128, 128], F32)
        self.tokv, r4 = T("tokv", [128, NCOL], BF16)
        self.kval, r5 = T("kval", [128, 32], F32)
        self.nval, r6 = T("nval", [128, 2], F32)
        self.normw_s, r7 = T("normw", [128, 128], F32)
        self.epsT, r8 = T("eps", [128, 1], F32)
        self.zero_f, r9 = T("zero_f", [128, 128], F32)
        self.r_const = self.newreg("const")
        rc = self.r_const
        ld = lambda t, src: P.dma("sp", lambda e: e.dma_start(out=t[:], in_=src), rc, True)
        ld(self.ident_f, self.c_ident)
        ld(self.trile, self.c_trile)
        ld(self.trigt, self.c_trigt)
        ld(self.dtab, self.c_dtab)
        ld(self.kval, self.m_kval)
        ld(self.nval, self.m_nval)
        ld(self.normw_s, self.normw)
        P.dma("pool", lambda e: e.dma_start(out=self.ident_b[:], in_=self.c_ident), rc, True)
        P.dma("pool", lambda e: e.dma_start(out=self.tokv[:], in_=self.m_tok), rc, True)
        P.op("dve", lambda e: e.memset(self.ones_b[:], 1.0), [], [rc])
        P.op("dve", lambda e: e.memset(self.epsT[:], EPS), [], [rc])
        P.op("dve", lambda e: e.memset(self.zero_f[:], 0.0), [], [rc])
        for X in (self.XM, self.X1):
            Xv = X.rearrange("(c p) t -> p c t", p=128)
            for c in range(16):
                P.dma("sp", lambda e, c=c, Xv=Xv: e.dma_start(out=Xv[:, c, 0:128], in_=self.zero_f[:]), rc, False)
        self.ps = []
        self.psr = []
        for i in range(8):
            t = st.enter_context(nc.psum_tensor("psb%d" % i, [128, 512], F32))
            self.ps.append(t)
            self.psr.append(self.newreg("ps%d" % i))
        P.end_phase(persistent=True)

    def load_norm(self, st_tiles, X, col0, n, nw_off, dst_fn, dst_reg, bank=7):
        P = self.P
        xt, r_xt, sq, r_sq, rs, r_rs = st_tiles
        Xv = X.rearrange("(c p) t -> p c t", p=128)
        P.dma("sp", lambda e: e.dma_start(out=xt[:, :, 0:n], in_=Xv[:, :, col0:col0 + n]), r_xt, True)
        ps, pr = self.ps[bank], self.psr[bank]
        for c in range(16):
            P.op("act", lambda e, c=c: e.activation(out=sq[c % 2][:, 0:n], in_=xt[:, c, 0:n], func=AF.Square), [r_xt], [r_sq[c % 2]])
            P.op("pe", lambda e, c=c: e.matmul(ps[:, 0:n], lhsT=self.ones_b[:], rhs=sq[c % 2][:, 0:n], start=(c == 0), stop=(c == 15)), [r_sq[c % 2]], [pr])
        P.op("act", lambda e: e.activation(out=rs[:, 0:n], in_=ps[:, 0:n], func=AF.Sqrt, bias=self.epsT[:, 0:1], scale=1.0 / D), [pr], [r_rs])
        P.op("dve", lambda e: e.reciprocal(out=rs[:, 0:n], in_=rs[:, 0:n]), [r_rs], [r_rs])
        P.op("dve", lambda e: e.tensor_tensor(out=rs[:, 0:n], in0=rs[:, 0:n], in1=self.tokv[:, col0:col0 + n], op=ALU.mult), [r_rs], [r_rs])
        for c in range(16):
            P.op("dve", lambda e, c=c: e.scalar_tensor_tensor(out=dst_fn(c), in0=xt[:, c, 0:n], scalar=self.normw_s[:, nw_off + c:nw_off + c + 1],
                                                              in1=rs[:, 0:n], op0=ALU.mult, op1=ALU.mult), [r_xt, r_rs], [dst_reg])

    def norm_tiles(self, st):
        xt, r_xt = self.tile(st, "xt", [128, 16, 256], F32)
        sq0, r0 = self.tile(st, "sq0", [128, 256], BF16)
        sq1, r1 = self.tile(st, "sq1", [128, 256], BF16)
        rs, r_rs = self.tile(st, "rs", [128, 256], F32)
        return (xt, r_xt, [sq0, sq1], [r0, r1], rs, r_rs)

    def fill_hT(self, st_tiles, X, col0, ntok, nw_off, hT, hregs):
        for i, t0 in enumerate(range(0, ntok, 256)):
            n = min(256, ntok - t0)
            self.load_norm(st_tiles, X, col0 + t0, n, nw_off, lambda c, t0=t0, n=n: hT[:, c, t0:t0 + n], hregs[i])

    def wload(self, wbuf, wreg, Wsrc, kc, col0, ncols, off=0, k0=0):
        view = wbuf[:, off:off + kc * ncols].rearrange("p (k n) -> p k n", k=kc)
        src = Wsrc.rearrange("(k p) n -> p k n", p=128)[:, k0:k0 + kc, col0:col0 + ncols]
        self.P.dma("pool", lambda e: e.dma_start(out=view, in_=src), wreg, True)
        return view

    def run_jobs(self, jobs):
        loads = [i for i, (lf, _) in enumerate(jobs) if lf is not None]
        if loads:
            jobs[loads[0]][0](0)
        li = 0
        for i, (lf, cf) in enumerate(jobs):
            if lf is None:
                cf(None)
                continue
            if li + 1 < len(loads):
                jobs[loads[li + 1]][0]((li + 1) % 2)
            cf(li % 2)
            li += 1

    def post_norm(self, st, mixed, r_mixed, ss_bank, n, nw_off, Xin, Xout, cin0, cout0, tl):
        P = self.P
        rs2, r_rs2, xr, r_xr, ot, r_ot, tt, r_tt = tl
        ps, pr = self.ps[ss_bank], self.psr[ss_bank]
        P.op("act", lambda e: e.activation(out=rs2[:, 0:n], in_=ps[:, 0:n], func=AF.Sqrt, bias=self.epsT[:, 0:1], scale=1.0 / D), [pr], [r_rs2])
        P.op("dve", lambda e: e.reciprocal(out=rs2[:, 0:n], in_=rs2[:, 0:n]), [r_rs2], [r_rs2])
        Xi = Xin.rearrange("(c p) t -> p c t", p=128)
        Xo = Xout.rearrange("(c p) t -> p c t", p=128)
        for c in range(16):
            b = c % 2
            P.dma("sp", lambda e, c=c, b=b: e.dma_start(out=xr[b][:, 0:n], in_=Xi[:, c, cin0:cin0 + n]), r_xr[b], True)
            P.op("dve", lambda e, c=c, b=b: e.tensor_tensor(out=tt[b][:, 0:n], in0=mixed[:, c, 0:n], in1=rs2[:, 0:n], op=ALU.mult), [r_mixed, r_rs2], [r_tt[b]])
            P.op("dve", lambda e, c=c, b=b: e.scalar_tensor_tensor(out=ot[b][:, 0:n], in0=tt[b][:, 0:n], scalar=self.normw_s[:, nw_off + c:nw_off + c + 1],
                                                                   in1=xr[b][:, 0:n], op0=ALU.mult, op1=ALU.add), [r_tt[b], r_xr[b]], [r_ot[b]])
            P.dma("sp", lambda e, c=c, b=b: e.dma_start(out=Xo[:, c, cout0:cout0 + n], in_=ot[b][:, 0:n]), r_ot[b], False)

    def post_tiles(self, st):
        rs2, r_rs2 = self.tile(st, "rs2", [128, 512], F32)
        xr = []; r_xr = []; ot = []; r_ot = []; tt = []; r_tt = []
        for b in range(2):
            a, ra = self.tile(st, "xr", [128, 512], F32); xr.append(a); r_xr.append(ra)
            a, ra = self.tile(st, "ot", [128, 512], F32); ot.append(a); r_ot.append(ra)
            a, ra = self.tile(st, "tt", [128, 512], F32); tt.append(a); r_tt.append(ra)
        return (rs2, r_rs2, xr, r_xr, ot, r_ot, tt, r_tt)

    def tok_tiles(self, start, ntok, step=512):
        return [(t0, min(step, start + ntok - t0)) for t0 in range(start, start + ntok, step)]

    def phase_kv(self, l, X):
        P = self.P
        with ExitStack() as st:
            nt = self.norm_tiles(st)
            wkv, r_wkv = self.tile(st, "wkv", [128, 16 * 1536], BF16)
            wv = wkv[:].rearrange("p (k n) -> p k n", k=16)
            Wl = self.w_in[l]
            for j in range(3):
                src = Wl.rearrange("(k p) n -> p k n", p=128)[:, :, C_KV + j * 512:C_KV + (j + 1) * 512]
                P.dma("pool", lambda e, j=j, src=src: e.dma_start(out=wv[:, :, j * 512:(j + 1) * 512], in_=src), r_wkv, True)
            hts = [self.tile(st, "hTt", [128, 16, 256], BF16) for _ in range(2)]
            ksts = [self.tile(st, "kst", [128, 8, 256], BF16) for _ in range(2)]
            vsts = [self.tile(st, "vst", [128, 2, 512], BF16) for _ in range(2)]
            fm_off = [0, 256, 512, 1024]
            tm_off = [768, 1280]
            for ti, t0 in enumerate(range(0, SEQ, 256)):
                hT, r_h = hts[ti % 2]
                kst, r_k = ksts[ti % 2]
                vst, r_v = vsts[ti % 2]
                col0 = PAD + t0
                self.load_norm(nt, X, col0, 256, (l * 4 + 0) * 16, lambda c, hT=hT: hT[:, c, :], r_h)
                for ty in range(4):
                    bk = ty % 4
                    ps, pr = self.ps[bk], self.psr[bk]
                    for half in range(2):
                        off = fm_off[ty] + half * 128
                        for kc in range(16):
                            P.op("pe", lambda e, ps=ps, half=half, off=off, kc=kc, hT=hT: e.matmul(ps[:, half * 256:(half + 1) * 256], lhsT=wv[:, kc, off:off + 128],
                                                                                                  rhs=hT[:, kc, :], start=(kc == 0), stop=(kc == 15)), [r_wkv, r_h], [pr])
                    P.op("act", lambda e, ps=ps, ty=ty, kst=kst: e.activation(out=kst[:, 2 * ty:2 * ty + 2, :], in_=ps[:].rearrange("p (h t) -> p h t", h=2), func=AF.Copy), [pr], [r_k])
                for ty in range(4):
                    dst = self.KT[ty].rearrange("(h p) t -> p h t", p=128)[:, :, col0:col0 + 256]
                    P.dma("sp", lambda e, ty=ty, dst=dst, kst=kst: e.dma_start(out=dst, in_=kst[:, 2 * ty:2 * ty + 2, :]), r_k, False, [self.R("KT")])
                for sub in range(2):
                    ps, pr = self.ps[4 + sub], self.psr[4 + sub]
                    for j in range(2):
                        for kc in range(16):
                            P.op("pe", lambda e, ps=ps, sub=sub, j=j, kc=kc, hT=hT: e.matmul(ps[:, j * 256:(j + 1) * 256], lhsT=hT[:, kc, sub * 128:(sub + 1) * 128],
                                                                                             rhs=wv[:, kc, tm_off[j]:tm_off[j] + 256], start=(kc == 0), stop=(kc == 15)), [r_wkv, r_h], [pr])
                    P.op("dve", lambda e, ps=ps, sub=sub, vst=vst: e.tensor_copy(out=vst[:, sub, :], in_=ps[:]), [pr], [r_v])
                for j in range(2):
                    dst = self.VT[j][col0:col0 + 256, :].rearrange("(s p) c -> p s c", p=128)
                    P.dma("sp", lambda e, j=j, dst=dst, vst=vst: e.dma_start(out=dst, in_=vst[:, :, j * 256:(j + 1) * 256]), r_v, False, [self.R("VT")])
            P.end_phase()

    def phase_m1(self, l, X, r0, r1):
        P = self.P
        T = r1 - r0
        TA = T + 128
        Wl = self.w_in[l]
        with ExitStack() as st:
            nt = self.norm_tiles(st)
            hT, _ = self.tile(st, "hT", [128, 16, TA], BF16)
            hregs = [self.newreg("hT") for _ in range((TA + 255) // 256)]
            wb = [self.tile(st, "wbuf", [128, 8192], BF16) for _ in range(2)]
            wvt, r_wvt = self.tile(st, "wvt", [128, 16 * 1024], BF16)
            ut, r_ut = self.tile(st, "ut", [128, 8, 128], BF16)
            sig = [self.tile(st, "sig", [128, 512], F32) for _ in range(2)]
            gst = [self.tile(st, "gst", [128, 512], BF16) for _ in range(2)]
            qst = [self.tile(st, "qst", [128, 512], BF16) for _ in range(2)]
            vg = [self.tile(st, "vg", [128, 1024], F32)] * 2
            vln = [self.tile(st, "vln", [128, 1024], BF16)] * 2
            bst, r_bst = self.tile(st, "bst", [128, 2, 6], F32)
            mv, r_mv = self.tile(st, "mv", [128, 2], F32)
            wsT, r_ws = self.tile(st, "wsT", [128, 8, 128], BF16)
            wsF, r_wsF = self.tile(st, "wsF", [128, 8, 128], BF16)
            lnbt, r_lnb = self.tile(st, "lnbt", [128, 2, 1024], F32)
            sgbr, r_sgb = self.tile(st, "sgbr", [1, 1024], BF16)
            gall, r_gall = self.tile(st, "gall", [128, T // 128, 48], F32)
            P.dma("sp", lambda e: e.dma_start(out=lnbt[:], in_=self.lnb[:, l * 2048:(l + 1) * 2048].rearrange("p (a c) -> p a c", a=2)), r_lnb, True)
            P.dma("pool", lambda e: e.dma_start(out=sgbr[:], in_=self.sgb[:, l * 1024:(l + 1) * 1024]), r_sgb, True)
            P.dma("pool", lambda e: e.dma_start(out=wsF[:], in_=self.sgw[l].rearrange("s (g t) -> s g t", g=8)), r_wsF, True)
            P.op("dve", lambda e: e.tensor_tensor(out=wsT[:], in0=wsF[:], in1=self.trile[:].unsqueeze(1).to_broadcast([128, 8, 128]), op=ALU.mult), [r_wsF], [r_ws])
            for j in range(2):
                src = Wl.rearrange("(k p) n -> p k n", p=128)[:, :, C_B + 1024 + j * 512:C_B + 1024 + (j + 1) * 512]
                P.dma("pool", lambda e, j=j, src=src: e.dma_start(out=wvt[:].rearrange("p (k n) -> p k n", k=16)[:, :, j * 512:(j + 1) * 512], in_=src), r_wvt, True)
            wv3 = wvt[:].rearrange("p (k n) -> p k n", k=16)
            self.fill_hT(nt, X, PAD + r0 - 128, TA, (l * 4 + 0) * 16, hT, hregs)
            tilesA = self.tok_tiles(0, TA)
            tilesR = self.tok_tiles(128, T)
            jobs = []
            psrot = [0]

            def nextps():
                b = psrot[0] % 4
                psrot[0] += 1
                return self.ps[b], self.psr[b]

            def mm16(ps, n, wview, c0, t0):
                for kc in range(16):
                    P.op("pe", lambda e, kc=kc: e.matmul(ps[:, 0:n], lhsT=wview[:, kc, c0:c0 + 128], rhs=hT[:, kc, t0:t0 + n], start=(kc == 0), stop=(kc == 15)),
                         [wview_reg[0]] + hregs, [ps_reg[0]])

            wview_reg = [None]
            ps_reg = [None]
            for jg in range(4):
                def lf(b, jg=jg):
                    self.wload(wb[b][0], wb[b][1], Wl, 16, C_A + jg * 256, 256, 0)
                    self.wload(wb[b][0], wb[b][1], Wl, 16, C_A + 1024 + jg * 256, 256, 16 * 256)

                def cf(b, jg=jg):
                    wa = wb[b][0][:, 0:4096].rearrange("p (k n) -> p k n", k=16)
                    wg = wb[b][0][:, 4096:8192].rearrange("p (k n) -> p k n", k=16)
                    wview_reg[0] = wb[b][1]
                    k = 0
                    for cc in range(2):
                        ch = jg * 2 + cc
                        for (t0, n) in tilesA:
                            pa, ra = nextps()
                            ps_reg[0] = ra
                            mm16(pa, n, wa, cc * 128, t0)
                            pg, rg = nextps()
                            ps_reg[0] = rg
                            mm16(pg, n, wg, cc * 128, t0)
                            s_, rs_ = sig[k % 2]
                            g_, rg_ = gst[k % 2]
                            k += 1
                            P.op("act", lambda e, pg=pg, s_=s_, n=n: e.activation(out=s_[:, 0:n], in_=pg[:, 0:n], func=AF.Sigmoid), [rg], [rs_])
                            P.op("dve", lambda e, pa=pa, s_=s_, g_=g_, n=n: e.tensor_tensor(out=g_[:, 0:n], in0=pa[:, 0:n], in1=s_[:, 0:n], op=ALU.mult), [ra, rs_], [rg_])
                            c0 = PAD + r0 - 128 + t0
                            P.dma("sp", lambda e, g_=g_, ch=ch, c0=c0, n=n: e.dma_start(out=self.GLU[ch * 128:(ch + 1) * 128, c0:c0 + n], in_=g_[:, 0:n]), rg_, False, [self.R("GLU")])
                jobs.append((lf, cf))
            ku = [0]
            for jg in range(2):
                def lf(b, jg=jg):
                    self.wload(wb[b][0], wb[b][1], Wl, 16, C_B + jg * 512, 512, 0)

                def cf(b, jg=jg):
                    wu = wb[b][0][:].rearrange("p (k n) -> p k n", k=16)
                    wview_reg[0] = wb[b][1]
                    for cc in range(4):
                        ch = jg * 4 + cc
                        for (t0, n) in tilesR:
                            pu, ru = nextps()
                            ps_reg[0] = ru
                            mm16(pu, n, wu, cc * 128, t0)
                            q_, rq_ = qst[ku[0] % 2]
                            ku[0] += 1
                            P.op("act", lambda e, pu=pu, q_=q_, n=n: e.activation(out=q_[:, 0:n], in_=pu[:, 0:n], func=AF.Gelu), [ru], [rq_])
                            c0 = PAD + r0 - 128 + t0
                            P.dma("sp", lambda e, q_=q_, ch=ch, c0=c0, n=n: e.dma_start(out=self.HB[ch * 128:(ch + 1) * 128, c0:c0 + n], in_=q_[:, 0:n]), rq_, False)
                jobs.append((lf, cf))
            for jg in range(2):
                def lf(b, jg=jg):
                    self.wload(wb[b][0], wb[b][1], Wl, 16, C_Q + jg * 512, 512, 0)

                def cf(b, jg=jg):
                    wq = wb[b][0][:].rearrange("p (k n) -> p k n", k=16)
                    wview_reg[0] = wb[b][1]
                    k = 0
                    for cc in range(4):
                        ch = jg * 4 + cc
                        for (t0, n) in tilesR:
                            pq, rq = nextps()
                            ps_reg[0] = rq
                            mm16(pq, n, wq, cc * 128, t0)
                            q_, rq_ = qst[k % 2]
                            k += 1
                            P.op("act", lambda e, pq=pq, q_=q_, n=n: e.activation(out=q_[:, 0:n], in_=pq[:, 0:n], func=AF.Copy, scale=0.125), [rq], [rq_])
                            c0 = PAD + r0 - 128 + t0
                            P.dma("sp", lambda e, q_=q_, ch=ch, c0=c0, n=n: e.dma_start(out=self.Q[ch * 128:(ch + 1) * 128, c0:c0 + n], in_=q_[:, 0:n]), rq_, False, [self.R("Q")])
                jobs.append((lf, cf))
            def lfg(b):
                self.wload(wb[b][0], wb[b][1], Wl, 16, C_G, 48, 0)

            def cfg(b):
                wg = wb[b][0][:, 0:16 * 48].rearrange("p (k n) -> p k n", k=16)
                for i in range(T // 128):
                    pg, rg = nextps()
                    t0 = 128 + i * 128
                    for kc in range(16):
                        P.op("pe", lambda e, kc=kc, pg=pg, t0=t0: e.matmul(pg[:, 0:48], lhsT=hT[:, kc, t0:t0 + 128], rhs=wg[:, kc, :], start=(kc == 0), stop=(kc == 15)),
                             [wb[b][1]] + hregs, [rg])
                    P.op("act", lambda e, pg=pg, i=i: e.activation(out=gall[:, i, :], in_=pg[:, 0:48], func=AF.Sigmoid), [rg], [r_gall])
                dst = self.GATES[PAD + r0:PAD + r1, :].rearrange("(n p) c -> p n c", p=128)
                P.dma("sp", lambda e: e.dma_start(out=dst, in_=gall[:]), r_gall, False, [self.R("GATES")])
            jobs.append((lfg, cfg))
            self.run_jobs(jobs)
            P.barrier()
            HBv = self.HB.rearrange("(c p) t -> p c t", p=128)
            for i in range(T // 128):
                t0 = 128 + i * 128
                cu = PAD + r0 + i * 128
                P.dma("sp", lambda e, cu=cu: e.dma_start(out=ut[:], in_=HBv[:, :, cu:cu + 128]), r_ut, True)
                vg_, rvg = vg[i % 2]
                vl_, rvl = vln[i % 2]
                for half in range(2):
                    pv, rv = self.ps[4 + half], self.psr[4 + half]
                    for kc in range(16):
                        P.op("pe", lambda e, kc=kc, pv=pv, half=half, t0=t0: e.matmul(pv[:], lhsT=hT[:, kc, t0:t0 + 128], rhs=wv3[:, kc, half * 512:(half + 1) * 512],
                                                                                     start=(kc == 0), stop=(kc == 15)), [r_wvt] + hregs, [rv])
                    P.op("act", lambda e, pv=pv, half=half, vg_=vg_: e.activation(out=vg_[:, half * 512:(half + 1) * 512], in_=pv[:], func=AF.Gelu), [rv], [rvg])
                for half in range(2):
                    P.op("dve", lambda e, half=half, vg_=vg_: e.bn_stats(out=bst[:, half, :], in_=vg_[:, half * 512:(half + 1) * 512]), [rvg], [r_bst])
                P.op("dve", lambda e: e.bn_aggr(out=mv[:], in_=bst[:].rearrange("p a s -> p (a s)")), [r_bst], [r_mv])
                P.op("act", lambda e: e.activation(out=mv[:, 1:2], in_=mv[:, 1:2], func=AF.Sqrt, bias=self.epsT[:, 0:1], scale=1.0), [r_mv], [r_mv])
                P.op("dve", lambda e: e.reciprocal(out=mv[:, 1:2], in_=mv[:, 1:2]), [r_mv], [r_mv])
                P.op("dve", lambda e, vg_=vg_: e.tensor_scalar(out=vg_[:], in0=vg_[:], scalar1=mv[:, 0:1], scalar2=mv[:, 1:2], op0=ALU.subtract, op1=ALU.mult), [rvg, r_mv], [rvg])
                P.op("dve", lambda e, vg_=vg_: e.tensor_tensor(out=vg_[:], in0=vg_[:], in1=lnbt[:, 0, :], op=ALU.mult), [rvg, r_lnb], [rvg])
                P.op("dve", lambda e, vg_=vg_, vl_=vl_: e.tensor_tensor(out=vl_[:], in0=vg_[:], in1=lnbt[:, 1, :], op=ALU.add), [rvg, r_lnb], [rvl])
                for hb in range(2):
                    pf, rf = self.ps[6 + hb], self.psr[6 + hb]
                    for gg in range(4):
                        g = hb * 4 + gg
                        P.op("pe", lambda e, pf=pf, gg=gg, g=g, vl_=vl_: e.matmul(pf[:, gg * 128:(gg + 1) * 128], lhsT=vl_[:, g * 128:(g + 1) * 128], rhs=wsT[:, g, :], start=True, stop=False),
                             [rvl, r_ws], [rf])
                        P.op("pe", lambda e, pf=pf, gg=gg, g=g: e.matmul(pf[:, gg * 128:(gg + 1) * 128], lhsT=self.ones_b[0:1, :], rhs=sgbr[0:1, g * 128:(g + 1) * 128], start=False, stop=True),
                             [r_sgb], [rf])
                    P.op("dve", lambda e, pf=pf, hb=hb: e.tensor_tensor(out=ut[:, hb * 4:(hb + 1) * 4, :], in0=ut[:, hb * 4:(hb + 1) * 4, :],
                                                                      in1=pf[:].rearrange("p (g t) -> p g t", g=4), op=ALU.mult), [rf, r_ut], [r_ut])
                P.dma("sp", lambda e, cu=cu: e.dma_start(out=HBv[:, :, cu:cu + 128], in_=ut[:]), r_ut, False)
            P.end_phase()

    def phase_m2(self, l, r0, r1):
        P = self.P
        T = r1 - r0
        TA = T + 128
        with ExitStack() as st:
            gl = [self.tile(st, "gl", [128, TA], BF16) for _ in range(3)]
            dgs = [self.tile(st, "dgs", [128, 31, 128], BF16) for _ in range(2)]
            cbf, r_cbf = self.tile(st, "cbf", [128, 8, T], BF16)
            cw, r_cw = self.tile(st, "cw", [128, 8, 31], F32)
            av, r_av = self.tile(st, "av", [128, 3, 8], F32)
            sq = [self.tile(st, "sq", [128, 512], BF16) for _ in range(2)]
            mean, r_mean = self.tile(st, "mean", [128, 512], F32)
            msq, r_msq = self.tile(st, "msq", [128, 512], F32)
            rstd, r_rstd = self.tile(st, "rstd", [128, 512], F32)
            t1 = [self.tile(st, "t1", [128, 512], F32) for _ in range(2)]
            hst = [self.tile(st, "hst", [128, 8, 512], BF16) for _ in range(2)]
            P.dma("sp", lambda e: e.dma_start(out=cw[:], in_=self.convaw[:, l * 248:(l + 1) * 248].rearrange("p (c k) -> p c k", c=8)), r_cw, True)
            P.dma("sp", lambda e: e.dma_start(out=av[:], in_=self.avec[:, l * 24:(l + 1) * 24].rearrange("p (a c) -> p a c", a=3)), r_av, True)
            for c in range(8):
                g_, rg = gl[c % 3]
                d_, rd_ = dgs[c % 2]
                c0 = PAD + r0 - 128
                P.dma("sp", lambda e: e.dma_start(out=g_[:], in_=self.GLU[c * 128:(c + 1) * 128, c0:c0 + TA]), rg, True)
                for k in range(31):
                    P.op("dve", lambda e: e.tensor_scalar(out=d_[:, k, :], in0=self.ident_f[:], scalar1=cw[:, c, k:k + 1], scalar2=None, op0=ALU.mult), [r_cw], [rd_])
                for ti, (t0, n) in enumerate(self.tok_tiles(0, T)):
                    bk = 2 + (c * 8 + ti) % 4
                    ps, pr = self.ps[bk], self.psr[bk]
                    for k in range(31):
                        P.op("pe", lambda e: e.matmul(ps[:, 0:n], lhsT=d_[:, k, :], rhs=g_[:, 98 + k + t0:98 + k + t0 + n], start=(k == 0), stop=(k == 30)), [rd_, rg], [pr])
                    P.op("act", lambda e: e.activation(out=cbf[:, c, t0:t0 + n], in_=ps[:, 0:n], func=AF.Identity, bias=av[:, 0, c:c + 1], scale=1.0), [pr, r_av], [r_cbf])
            for ti, (t0, n) in enumerate(self.tok_tiles(0, T)):
                pS, rS = self.ps[0], self.psr[0]
                pQ, rQ = self.ps[1], self.psr[1]
                for c in range(8):
                    s_, rs_ = sq[c % 2]
                    P.op("act", lambda e, s_=s_, c=c, t0=t0, n=n: e.activation(out=s_[:, 0:n], in_=cbf[:, c, t0:t0 + n], func=AF.Square), [r_cbf], [rs_])
                    P.op("pe", lambda e, c=c, t0=t0, n=n: e.matmul(pS[:, 0:n], lhsT=self.ones_b[:], rhs=cbf[:, c, t0:t0 + n], start=(c == 0), stop=(c == 7)), [r_cbf], [rS])
                    P.op("pe", lambda e, s_=s_, c=c, n=n: e.matmul(pQ[:, 0:n], lhsT=self.ones_b[:], rhs=s_[:, 0:n], start=(c == 0), stop=(c == 7)), [rs_], [rQ])
                P.op("dve", lambda e, n=n: e.tensor_scalar(out=mean[:, 0:n], in0=pS[:, 0:n], scalar1=1.0 / 1024, scalar2=None, op0=ALU.mult), [rS], [r_mean])
                P.op("dve", lambda e, n=n: e.tensor_tensor(out=msq[:, 0:n], in0=mean[:, 0:n], in1=mean[:, 0:n], op=ALU.mult), [r_mean], [r_msq])
                P.op("dve", lambda e, n=n: e.scalar_tensor_tensor(out=rstd[:, 0:n], in0=pQ[:, 0:n], scalar=1.0 / 1024, in1=msq[:, 0:n], op0=ALU.mult, op1=ALU.subtract), [rQ, r_msq], [r_rstd])
                P.op("act", lambda e, n=n: e.activation(out=rstd[:, 0:n], in_=rstd[:, 0:n], func=AF.Sqrt, bias=self.epsT[:, 0:1], scale=1.0), [r_rstd], [r_rstd])
                P.op("dve", lambda e, n=n: e.reciprocal(out=rstd[:, 0:n], in_=rstd[:, 0:n]), [r_rstd], [r_rstd])
                h_, rh = hst[ti % 2]
                for c in range(8):
                    t_, rt = t1[c % 2]
                    P.op("dve", lambda e, t_=t_, c=c, t0=t0, n=n: e.tensor_tensor(out=t_[:, 0:n], in0=cbf[:, c, t0:t0 + n], in1=mean[:, 0:n], op=ALU.subtract), [r_cbf, r_mean], [rt])
                    P.op("dve", lambda e, t_=t_, n=n: e.tensor_tensor(out=t_[:, 0:n], in0=t_[:, 0:n], in1=rstd[:, 0:n], op=ALU.mult), [rt, r_rstd], [rt])
                    P.op("act", lambda e, t_=t_, h_=h_, c=c, n=n: e.activation(out=h_[:, c, 0:n], in_=t_[:, 0:n], func=AF.Silu, bias=av[:, 2, c:c + 1], scale=av[:, 1, c:c + 1]), [rt, r_av], [rh])
                dst = self.HA.rearrange("(c p) t -> p c t", p=128)[:, :, PAD + r0 + t0:PAD + r0 + t0 + n]
                P.dma("sp", lambda e, dst=dst, h_=h_, n=n: e.dma_start(out=dst, in_=h_[:, :, 0:n]), rh, False, [self.R("HA")])
            P.end_phase()

    def phase_att(self, l, r0, r1):
        P = self.P
        T = r1 - r0
        nqt = T // 128
        qt0 = r0 // 128
        BIG = 30000.0
        with ExitStack() as st:
            kin = [self.tile(st, "kin", [128, SEQ], BF16) for _ in range(4)]
            vs, r_vs = self.tile(st, "vs", [128, 32, 65], BF16)
            vw, r_vw = self.tile(st, "vw", [128, 32, 65], BF16)
            qT, r_q = self.tile(st, "qT", [128, 4, T], BF16)
            gt, r_gt = self.tile(st, "gt", [128, nqt, 48], F32)
            mtab, r_mt = self.tile(st, "mtab", [128, 3, nqt, 64], F32)
            Et, r_E = self.tile(st, "Et", [128, 32, 128], BF16)
            smap, r_sm = self.tile(st, "smap", [128, 2, 64], BF16)
            w1 = [self.tile(st, "w1", [128, 32, 256], BF16) for _ in range(2)]
            w2 = [self.tile(st, "w2", [128, 2, 64], BF16) for _ in range(2)]
            pe = [self.tile(st, "pe", [128, 32], BF16) for _ in range(2)]
            cb = [self.tile(st, "cb", [128, 2], F32) for _ in range(2)]
            hid, r_hid = self.tile(st, "hid", [128, 2, 256], BF16)
            kcT, r_kc = self.tile(st, "kcT", [128, 256], BF16)
            rv, r_rv = self.tile(st, "rv", [128, 2, 64], BF16)
            pb = [self.tile(st, "pb", [128, 4, 128], BF16) for _ in range(4)]
            cm, r_cm = self.tile(st, "cm", [128, 128], F32)
            cmb = [self.tile(st, "cmb", [128, 4, 128], BF16) for _ in range(4)]
            trib = [self.tile(st, "trib", [128, 4, 128], BF16) for _ in range(2)]
            selb = [self.tile(st, "selb", [128, 4, 128], BF16) for _ in range(2)]
            negb, r_negb = self.tile(st, "negb", [128, 1], F32)
            sm = {}
            for nm, shp in (("den", [128, 4]), ("cg", [128, 4]), ("imp", [128, 64]), ("score", [128, 64]), ("wk", [128, 64]), ("selw", [128, 128]), ("m8", [128, 8]),
                            ("oacc", [128, 4, 64]), ("otmp", [128, 4, 64]), ("den2", [128, 4]), ("cg2", [128, 4])):
                sm[nm] = self.tile(st, nm, shp, F32)
            obf = [self.tile(st, "obf", [128, 256], BF16) for _ in range(2)]
            ost = [self.tile(st, "ost", [128, 2, 128], BF16) for _ in range(2)]
            P.dma("sp", lambda e: e.dma_start(out=gt[:], in_=self.GATES[PAD + r0:PAD + r1, :].rearrange("(n p) c -> p n c", p=128)), r_gt, True)
            for a_ in range(3):
                P.dma("sp", lambda e: e.dma_start(out=mtab[:, a_, :, :], in_=self.m_sel[a_].rearrange("p (q j) -> p q j", j=64)[:, qt0:qt0 + nqt, :]), r_mt, True)
            P.op("dve", lambda e: e.memset(sm["selw"][0][:], 0.0), [], [sm["selw"][1]])
            P.op("dve", lambda e: e.memset(qT[64:128], 0.0), [], [r_q])
            for ty in (0, 1, 3):
                P.op("dve", lambda e: e.memset(kin[ty][0][64:128], 0.0), [], [kin[ty][1]])
            P.dma("pool", lambda e: e.dma_start(out=kin[2][0][64:128], in_=self.c_E), kin[2][1], True)
            for kv in range(2):
                P.op("dve", lambda e: e.memset(w1[kv][0][64:128], 0.0), [], [w1[kv][1]])
                P.op("dve", lambda e: e.memset(pe[kv][0][64:128], 0.0), [], [pe[kv][1]])


            P.dma("pool", lambda e: e.dma_start(out=smap[:], in_=self.c_smap.rearrange("p (a j) -> p a j", a=2)), r_sm, True)
            for kv in range(2):
                i_ = l * 2 + kv
                P.dma("pool", lambda e: e.dma_start(out=w1[kv][0][0:64], in_=self.cw1[i_].rearrange("d (l h) -> d l h", l=32)), w1[kv][1], True)
                P.dma("pool", lambda e: e.dma_start(out=w2[kv][0][:], in_=self.cw2[i_].rearrange("(c p) d -> p c d", p=128)), w2[kv][1], True)
                P.dma("pool", lambda e: e.dma_start(out=pe[kv][0][0:64], in_=self.cpe[i_]), pe[kv][1], True)
            P.op("dve", lambda e: e.memset(vs[:, :, 64:65], 1.0), [], [r_vs])
            P.op("dve", lambda e: e.memset(vw[:, :, 64:65], 1.0), [], [r_vw])
            P.op("dve", lambda e: e.memset(hid[:], 0.0), [], [r_hid])
            P.op("dve", lambda e: e.memset(kcT[:], 0.0), [], [r_kc])
            P.op("dve", lambda e: e.memset(negb[:], -BIG), [], [r_negb])
            for ti_, tri in enumerate((self.trile, self.trigt)):
                P.op("dve", lambda e: e.tensor_scalar(out=trib[ti_][0][:], in0=tri[:].unsqueeze(1).to_broadcast([128, 4, 128]), scalar1=-1.0, scalar2=BIG, op0=ALU.add, op1=ALU.mult), [], [trib[ti_][1]])
            for kv in range(2):
                ps, pr = self.ps[0], self.psr[0]
                for hc in range(2):
                    for li in range(32):
                        P.op("pe", lambda e: e.matmul(ps[:, hc:hc + 1], lhsT=w1[kv][0][:, li, hc * 128:(hc + 1) * 128], rhs=pe[kv][0][:, li:li + 1],
                                                      start=(li == 0), stop=(li == 31)), [w1[kv][1], pe[kv][1]], [pr])
                P.op("act", lambda e: e.activation(out=cb[kv][0][:], in_=ps[:, 0:2], func=AF.Copy), [pr], [cb[kv][1]])
            PC, PS_, PW, T32, TBF = 3, 4, 5, 6, 7
            psbf = self.ps[TBF][:].bitcast(BF16)
            den, r_den = sm["den"]; cg, r_cg = sm["cg"]; imp, r_imp = sm["imp"]; score, r_sc = sm["score"]
            wk, r_wk = sm["wk"]; selw, r_sel = sm["selw"]; sel = selw[:, 64:128]; m8, r_m8 = sm["m8"]; oacc, r_oa = sm["oacc"]; otmp, r_ot = sm["otmp"]
            den2, r_den2 = sm["den2"]; cg2, r_cg2 = sm["cg2"]
            cnt = {"s": 0, "p": 0, "cmb": 0, "q": 0}
            for g in range(4):
                for ty in range(4):
                    P.dma("sp", lambda e: e.dma_start(out=kin[ty][0][0:64], in_=self.KT[ty][g * 64:(g + 1) * 64, PAD:PAD + SEQ]), kin[ty][1], True)
                P.dma("sp", lambda e: e.dma_start(out=vs[:, :, 0:64], in_=self.VT[0][PAD:PAD + SEQ, g * 64:(g + 1) * 64].rearrange("(n p) d -> p n d", p=128)), r_vs, True)
                P.dma("sp", lambda e: e.dma_start(out=vw[:, :, 0:64], in_=self.VT[1][PAD:PAD + SEQ, g * 64:(g + 1) * 64].rearrange("(n p) d -> p n d", p=128)), r_vw, True)
                P.op("dve", lambda e: e.tensor_tensor(out=vw[:], in0=vw[:], in1=self.kval[:, 0:32].unsqueeze(2).to_broadcast([128, 32, 65]), op=ALU.mult), [r_vw], [r_vw])
                P.dma("sp", lambda e: e.dma_start(out=qT[0:64], in_=self.Q[g * 256:(g + 1) * 256, PAD + r0:PAD + r1].rearrange("(h d) t -> d h t", d=64)), r_q, True)
                for kv in range(2):
                    src, rsrc = kin[kv]
                    for hc in range(2):
                        ps, pr = self.ps[hc], self.psr[hc]
                        for li in range(32):
                            P.op("pe", lambda e: e.matmul(ps[:, 0:255], lhsT=w1[kv][0][:, li, hc * 128:(hc + 1) * 128], rhs=src[:, li:li + 16 * 254 + 1:16],
                                                          start=(li == 0), stop=(li == 31)), [w1[kv][1], rsrc], [pr])
                        P.op("act", lambda e: e.activation(out=hid[:, hc, 0:255], in_=ps[:, 0:255], func=AF.Silu, bias=cb[kv][0][:, hc:hc + 1], scale=1.0), [pr, cb[kv][1]], [r_hid])
                    ps, pr = self.ps[2], self.psr[2]
                    if kv == 0:
                        for hc in range(2):
                            P.op("pe", lambda e: e.matmul(ps[0:64, 0:255], lhsT=w2[0][0][:, hc, :], rhs=hid[:, hc, 0:255], start=(hc == 0), stop=(hc == 1)), [w2[0][1], r_hid], [pr])
                        P.op("act", lambda e: e.activation(out=kcT[0:64, 0:255], in_=ps[0:64, 0:255], func=AF.Copy), [pr], [r_kc])
                    else:
                        for nt_ in range(2):
                            for hc in range(2):
                                P.op("pe", lambda e: e.matmul(ps[:, nt_ * 64:(nt_ + 1) * 64], lhsT=hid[:, hc, nt_ * 128:(nt_ + 1) * 128], rhs=w2[1][0][:, hc, :],
                                                              start=(hc == 0), stop=(hc == 1)), [w2[1][1], r_hid], [pr])
                        P.op("act", lambda e: e.activation(out=rv[:], in_=ps[:, 0:128].rearrange("p (a d) -> p a d", a=2), func=AF.Copy), [pr], [r_rv])
                pending = [None]
                for i in range(nqt):
                    qt = qt0 + i
                    qv = qT[:, :, i * 128:(i + 1) * 128]
                    gsl = gt[:, i, g * 12:(g + 1) * 12].rearrange("p (h b) -> p h b", b=3)
                    pc, rpc = self.ps[PC], self.psr[PC]
                    pso, rpso = self.ps[PS_], self.psr[PS_]
                    pwo, rpwo = self.ps[PW], self.psr[PW]
                    psov = pso[:, 0:260].rearrange("p (h d) -> p h d", h=4)
                    pwov = pwo[:, 0:260].rearrange("p (h d) -> p h d", h=4)
                    sb_, rsb_ = selb[cnt["q"] % 2]
                    ob_, rob_ = obf[cnt["q"] % 2]
                    os_, ros_ = ost[cnt["q"] % 2]
                    cnt["q"] += 1
                    nts = [0] if qt < 16 else [0, 1]
                    steps = []
                    for nt_ in nts:
                        c4, rc4 = cmb[cnt["cmb"] % 4]
                        cnt["cmb"] += 1
                        thr = float(128 * qt - 2048 * nt_ - 31)
                        P.op("dve", lambda e: e.tensor_scalar(out=cm[:], in0=self.dtab[:], scalar1=thr, scalar2=self.nval[:, nt_:nt_ + 1], op0=ALU.is_le, op1=ALU.mult), [], [r_cm])
                        P.op("dve", lambda e: e.tensor_scalar(out=c4[:], in0=cm[:].unsqueeze(1).to_broadcast([128, 4, 128]), scalar1=-1.0, scalar2=BIG, op0=ALU.add, op1=ALU.mult), [r_cm], [rc4])

                        def pv_c(p_, rp_, nt_=nt_):
                            first = (nt_ == nts[0])
                            for h in range(4):
                                P.op("pe", lambda e: e.matmul(pc[:, h * 64:(h + 1) * 64], lhsT=p_[:, h, :], rhs=rv[:, nt_, :], start=(first and h == 0), stop=True, skip_group_check=True), [rp_, r_rv], [rpc])
                                P.op("pe", lambda e: e.matmul(pc[:, 256 + h * 64:256 + (h + 1) * 64], lhsT=p_[:, h, :], rhs=smap[:, nt_, :], start=False, stop=True, skip_group_check=True), [rp_, r_sm], [rpc])
                        steps.append(("c", kcT[:, nt_ * 128:(nt_ + 1) * 128], r_kc, [(self.ident_b[:], c4[:], [rc4])], pv_c))
                    k0 = max(0, qt - 4)
                    for kt in range(k0, qt + 1):
                        biases = []
                        if kt == qt:
                            biases.append((self.ident_b[:], trib[0][0][:], [trib[0][1]]))
                        elif kt == qt - 4:
                            biases.append((self.ident_b[:], trib[1][0][:], [trib[1][1]]))

                        def pv_w(p_, rp_, kt=kt):
                            for h in range(4):
                                P.op("pe", lambda e: e.matmul(pwov[:, h, :], lhsT=p_[:, h, :], rhs=vw[:, kt, :], start=(kt == k0 and h == 0), stop=True, skip_group_check=True), [rp_, r_vw], [rpwo])
                        steps.append(("w", kin[3][0][:, kt * 128:(kt + 1) * 128], kin[3][1], biases, pv_w))
                    n_pre = len(steps)
                    for kt in range(0, qt + 1):
                        biases = []
                        if kt == qt:
                            biases.append((self.ident_b[:], trib[0][0][:], [trib[0][1]]))

                        def pv_s(p_, rp_, kt=kt):
                            for h in range(4):
                                P.op("pe", lambda e: e.matmul(psov[:, h, :], lhsT=p_[:, h, :], rhs=vs[:, kt, :], start=(kt == 0 and h == 0), stop=True, skip_group_check=True), [rp_, r_vs], [rpso])
                        steps.append(("s", kin[2][0][:, kt * 128:(kt + 1) * 128], kin[2][1], biases, pv_s, sb_[:], rsb_))
                    N = len(steps)
                    sbank = {}

                    def emit_score(k):
                        kind, lhsT, lreg, biases = steps[k][0:4]
                        rhs_, rreg_ = (steps[k][5], steps[k][6]) if len(steps[k]) > 5 else (qv, r_q)
                        bk = cnt["s"] % 3
                        cnt["s"] += 1
                        ps, pr = self.ps[bk], self.psr[bk]
                        sbank[k] = (ps, pr)
                        P.op("pe", lambda e: e.matmul(ps[:], lhsT=lhsT, rhs=rhs_, start=True, stop=(len(biases) == 0)), [lreg, rreg_], [pr])
                        for bi, (bl, br, bregs) in enumerate(biases):
                            P.op("pe", lambda e: e.matmul(ps[:], lhsT=bl, rhs=br, start=False, stop=(bi == len(biases) - 1)), bregs, [pr])

                    def post_cmp_dve():
                        pc2 = pc[:, 256:512].rearrange("p (h j) -> p h j", h=4)
                        P.op("dve", lambda e: e.tensor_reduce(out=den[:], in_=pc2, axis=AX.X, op=ALU.add), [rpc], [r_den])
                        P.op("dve", lambda e: e.tensor_scalar(out=den[:], in0=den[:], scalar1=0.5, scalar2=1e-30, op0=ALU.mult, op1=ALU.max), [r_den], [r_den])
                        P.op("dve", lambda e: e.reciprocal(out=den[:], in_=den[:]), [r_den], [r_den])
                        P.op("dve", lambda e: e.tensor_scalar(out=imp[:], in0=pc2[:, 0, :], scalar1=den[:, 0:1], scalar2=None, op0=ALU.mult), [rpc, r_den], [r_imp])
                        for h in range(1, 4):
                            P.op("dve", lambda e: e.scalar_tensor_tensor(out=imp[:], in0=pc2[:, h, :], scalar=den[:, h:h + 1], in1=imp[:], op0=ALU.mult, op1=ALU.add), [rpc, r_den, r_imp], [r_imp])
                        P.op("dve", lambda e: e.tensor_tensor(out=score[:], in0=imp[:], in1=mtab[:, 0, i, :], op=ALU.mult), [r_imp, r_mt], [r_sc])
                        P.op("dve", lambda e: e.tensor_tensor(out=score[:], in0=score[:], in1=mtab[:, 1, i, :], op=ALU.add), [r_sc, r_mt], [r_sc])
                        P.op("dve", lambda e: e.max(out=m8[:], in_=score[:]), [r_sc], [r_m8])
                        P.op("dve", lambda e: e.match_replace(out=wk[:], in_to_replace=m8[:], in_values=score[:], imm_value=-2.0), [r_m8, r_sc], [r_wk])
                        P.op("dve", lambda e: e.max(out=m8[:], in_=wk[:]), [r_wk], [r_m8])
                        P.op("dve", lambda e: e.match_replace(out=wk[:], in_to_replace=m8[:], in_values=wk[:], imm_value=-2.0), [r_m8, r_wk], [r_wk])
                        P.op("dve", lambda e: e.tensor_tensor(out=sel, in0=score[:], in1=wk[:], op=ALU.subtract), [r_sc, r_wk], [r_sel])
                        P.op("dve", lambda e: e.scalar_tensor_tensor(out=sel, in0=sel, scalar=1.0, in1=mtab[:, 2, i, :], op0=ALU.min, op1=ALU.mult), [r_sel, r_mt], [r_sel])
                        P.op("dve", lambda e: e.tensor_tensor(out=cg[:], in0=den[:], in1=gsl[:, :, 0], op=ALU.mult), [r_den, r_gt], [r_cg])
                        P.op("dve", lambda e: e.tensor_tensor(out=oacc[:], in0=pc[:, 0:256].rearrange("p (h d) -> p h d", h=4), in1=cg[:].unsqueeze(2).to_broadcast([128, 4, 64]), op=ALU.mult), [rpc, r_cg], [r_oa])

                    def pre_slc():
                        pt, rpt = self.ps[T32], self.psr[T32]
                        P.op("pe", lambda e: e.transpose(out=pt[:, 0:128], in_=selw[:], identity=self.ident_f[:]), [r_sel], [rpt])
                        P.op("act", lambda e: e.activation(out=sb_[64:128], in_=pt[64:128, 0:128].unsqueeze(1).to_broadcast([64, 4, 128]), func=AF.Identity, bias=negb[64:128, 0:1], scale=BIG), [rpt, r_negb], [rsb_])
                        P.op("dve", lambda e: e.tensor_copy(out=sb_[0:64], in_=qT[0:64, :, i * 128:(i + 1) * 128]), [r_q], [rsb_])

                    LA = getattr(self, "lookahead", 1)
                    if LA:
                        emit_score(0)
                    for k in range(N):
                        if not LA:
                            if k == n_pre:
                                pre_slc()
                            emit_score(k)
                        elif k + 1 < N:
                            if k + 1 == n_pre:
                                pre_slc()
                            emit_score(k + 1)
                        ps, pr = sbank.pop(k)
                        p_, rp_ = pb[cnt["p"] % 4]
                        cnt["p"] += 1
                        P.op("act", lambda e: e.activation(out=p_[:], in_=ps[:].rearrange("p (h q) -> p h q", h=4), func=AF.Exp), [pr], [rp_])
                        steps[k][4](p_, rp_)
                        if k == len(nts) - 1:
                            post_cmp_dve()
                            if pending[0] is not None:
                                pending[0]()
                                pending[0] = None
                    P.op("dve", lambda e: e.tensor_scalar(out=den2[:], in0=psov[:, :, 64], scalar1=1e-30, scalar2=None, op0=ALU.max), [rpso], [r_den2])
                    P.op("dve", lambda e: e.reciprocal(out=den2[:], in_=den2[:]), [r_den2], [r_den2])
                    P.op("dve", lambda e: e.tensor_tensor(out=cg2[:], in0=den2[:], in1=gsl[:, :, 1], op=ALU.mult), [r_den2, r_gt], [r_cg2])
                    P.op("dve", lambda e: e.tensor_tensor(out=otmp[:], in0=psov[:, :, 0:64], in1=cg2[:].unsqueeze(2).to_broadcast([128, 4, 64]), op=ALU.mult), [rpso, r_cg2], [r_ot])
                    P.op("dve", lambda e: e.tensor_tensor(out=oacc[:], in0=oacc[:], in1=otmp[:], op=ALU.add), [r_oa, r_ot], [r_oa])
                    P.op("dve", lambda e: e.tensor_scalar(out=den2[:], in0=pwov[:, :, 64], scalar1=1e-30, scalar2=None, op0=ALU.max), [rpwo], [r_den2])
                    P.op("dve", lambda e: e.reciprocal(out=den2[:], in_=den2[:]), [r_den2], [r_den2])
                    P.op("dve", lambda e: e.tensor_tensor(out=cg2[:], in0=den2[:], in1=gsl[:, :, 2], op=ALU.mult), [r_den2, r_gt], [r_cg2])
                    P.op("dve", lambda e: e.tensor_tensor(out=otmp[:], in0=pwov[:, :, 0:64], in1=cg2[:].unsqueeze(2).to_broadcast([128, 4, 64]), op=ALU.mult), [rpwo, r_cg2], [r_ot])
                    P.op("dve", lambda e: e.tensor_tensor(out=ob_[:].rearrange("p (h d) -> p h d", h=4), in0=oacc[:], in1=otmp[:], op=ALU.add), [r_oa, r_ot], [rob_])

                    def post_pe(i=i, ob_=ob_, rob_=rob_, os_=os_, ros_=ros_):
                        rptb = self.psr[TBF]
                        for half in range(2):
                            P.op("pe", lambda e: e.transpose(out=psbf[:, half * 128:(half + 1) * 128], in_=ob_[:, half * 128:(half + 1) * 128], identity=self.ident_b[:]), [rob_], [rptb])
                        P.op("act", lambda e: e.activation(out=os_[:], in_=psbf[:, 0:256].rearrange("p (a q) -> p a q", a=2), func=AF.Copy), [rptb], [ros_])
                        c0 = PAD + r0 + i * 128
                        dst = self.OC[g * 256:(g + 1) * 256, c0:c0 + 128].rearrange("(a p) t -> p a t", p=128)
                        P.dma("sp", lambda e: e.dma_start(out=dst, in_=os_[:]), ros_, False)
                    pending[0] = post_pe
                if pending[0] is not None:
                    pending[0]()
                    pending[0] = None
            P.end_phase()

    def phase_merge(self, l, X, Xout, r0, r1):
        P = self.P
        Wl = self.w_in[l]
        with ExitStack() as st:
            nt = self.norm_tiles(st)
            pt = self.post_tiles(st)
            hT, r_h = self.tile(st, "hT", [128, 16, 512], BF16)
            ins = [self.tile(st, "hin", [128, 8, 512], BF16) for _ in range(3)]
            wb = [self.tile(st, "wbuf", [128, 8192], BF16) for _ in range(2)]
            mT, r_m = self.tile(st, "mT", [128, 16, 512], BF16)
            mixed, r_mx = self.tile(st, "mixed", [128, 16, 512], F32)
            sgs = [self.tile(st, "sgs", [128, 4, 512], BF16) for _ in range(2)]
            accm, r_accm = self.tile(st, "accm", [128, 4, 512], F32)
            tmp = [self.tile(st, "tmp", [128, 512], F32) for _ in range(2)]
            sq = [self.tile(st, "sq", [128, 512], BF16) for _ in range(2)]
            srcs = [self.HA, self.HB, self.OC]
            wouts = [self.w_a_out[l], self.w_b_out[l], self.w_c_out[l]]
            hregs = [self.newreg("hT"), self.newreg("hT")]
            sched = []
            pk = [0]

            def nb():
                bk = pk[0] % 6
                pk[0] += 1
                return self.ps[bk], self.psr[bk]
            for (s0, n) in self.tok_tiles(r0, r1 - r0):
                sc_ = {}
                sched.append(sc_)

                def nfn(b, s0=s0, n=n):
                    self.fill_hT(nt, X, PAD + s0, n, (l * 4 + 0) * 16, hT, hregs)
                    for b3 in range(3):
                        P.dma("sp", lambda e: e.dma_start(out=ins[b3][0][:, :, 0:n], in_=srcs[b3].rearrange("(c p) t -> p c t", p=128)[:, :, PAD + s0:PAD + s0 + n]),
                              ins[b3][1], True)
                sc_["N"] = [(None, nfn)]
                jobs = []
                for dg in range(4):
                    for b3 in range(3):
                        def lfg(b, dg=dg, b3=b3):
                            self.wload(wb[b][0], wb[b][1], Wl, 16, C_M + b3 * 2048 + dg * 512, 512, 0)

                        def cfg(b, dg=dg, b3=b3, n=n, hregs=hregs):
                            wbt, rwb = wb[b]
                            wg = wbt[:, 0:8192].rearrange("p (k n) -> p k n", k=16)
                            s_, rs_ = sgs[b3 % 2]
                            for cc in range(4):
                                pg, rg = nb()
                                for kc in range(16):
                                    P.op("pe", lambda e: e.matmul(pg[:, 0:n], lhsT=wg[:, kc, cc * 128:(cc + 1) * 128], rhs=hT[:, kc, 0:n], start=(kc == 0), stop=(kc == 15)), [rwb] + hregs, [rg])
                                P.op("act", lambda e: e.activation(out=s_[:, cc, 0:n], in_=pg[:, 0:n], func=AF.Sigmoid), [rg], [rs_])
                        jobs.append((lfg, cfg))

                        def lfy(b, dg=dg, b3=b3):
                            self.wload(wb[b][0], wb[b][1], wouts[b3], 8, dg * 512, 512, 0)

                        def cfy(b, dg=dg, b3=b3, n=n):
                            wbt, rwb = wb[b]
                            wy = wbt[:, 0:4096].rearrange("p (k n) -> p k n", k=8)
                            s_, rs_ = sgs[b3 % 2]
                            for cc in range(4):
                                py, ry = nb()
                                for kc in range(8):
                                    P.op("pe", lambda e: e.matmul(py[:, 0:n], lhsT=wy[:, kc, cc * 128:(cc + 1) * 128], rhs=ins[b3][0][:, kc, 0:n], start=(kc == 0), stop=(kc == 7)), [rwb, ins[b3][1]], [ry])
                                if b3 == 0:
                                    P.op("dve", lambda e: e.tensor_tensor(out=accm[:, cc, 0:n], in0=py[:, 0:n], in1=s_[:, cc, 0:n], op=ALU.mult), [ry, rs_], [r_accm])
                                else:
                                    t_, rt_ = tmp[cc % 2]
                                    P.op("dve", lambda e: e.tensor_tensor(out=t_[:, 0:n], in0=py[:, 0:n], in1=s_[:, cc, 0:n], op=ALU.mult), [ry, rs_], [rt_])
                                    if b3 == 1:
                                        P.op("dve", lambda e: e.tensor_tensor(out=accm[:, cc, 0:n], in0=accm[:, cc, 0:n], in1=t_[:, 0:n], op=ALU.add), [r_accm, rt_], [r_accm])
                                    else:
                                        P.op("dve", lambda e: e.tensor_tensor(out=mT[:, dg * 4 + cc, 0:n], in0=accm[:, cc, 0:n], in1=t_[:, 0:n], op=ALU.add), [r_accm, rt_], [r_m])
                        jobs.append((lfy, cfy))
                sc_["A"] = jobs
                jobs = []
                for jg in range(4):
                    def lf(b, jg=jg):
                        self.wload(wb[b][0], wb[b][1], self.w_o[l], 16, jg * 512, 512, 0)

                    def cf(b, jg=jg, n=n):
                        wo = wb[b][0][:, 0:8192].rearrange("p (k n) -> p k n", k=16)
                        for cc in range(4):
                            dch = jg * 4 + cc
                            po, ro = self.ps[dch % 4], self.psr[dch % 4]
                            for kc in range(16):
                                P.op("pe", lambda e, kc=kc, po=po, cc=cc: e.matmul(po[:, 0:n], lhsT=wo[:, kc, cc * 128:(cc + 1) * 128], rhs=mT[:, kc, 0:n], start=(kc == 0), stop=(kc == 15)), [wb[b][1], r_m], [ro])
                            s_, rs_ = sq[dch % 2]
                            P.op("act", lambda e, po=po, dch=dch: e.activation(out=mixed[:, dch, 0:n], in_=po[:, 0:n], func=AF.Copy), [ro], [r_mx])
                            P.op("act", lambda e, po=po, s_=s_: e.activation(out=s_[:, 0:n], in_=po[:, 0:n], func=AF.Square), [ro], [rs_])
                            P.op("pe", lambda e, s_=s_, dch=dch: e.matmul(self.ps[6][:, 0:n], lhsT=self.ones_b[:], rhs=s_[:, 0:n], start=(dch == 0), stop=(dch == 15)), [rs_], [self.psr[6]])
                    jobs.append((lf, cf))
                sc_["B"] = jobs
                sc_["P"] = [(None, lambda b, s0=s0, n=n: self.post_norm(st, mixed, r_mx, 6, n, (l * 4 + 1) * 16, X, Xout, PAD + s0, PAD + s0, pt))]
            nt_ = len(sched)
            seq = sched[0]["N"] + sched[0]["A"]
            for t in range(nt_):
                if t + 1 < nt_:
                    seq += sched[t + 1]["N"]
                seq += sched[t]["B"]
                if t + 1 < nt_:
                    seq += sched[t + 1]["A"][:4] + sched[t]["P"] + sched[t + 1]["A"][4:]
                else:
                    seq += sched[t]["P"]
            self.run_jobs(seq)
            P.end_phase()

    def phase_ffn(self, l, X, Xout, f0, f1, out_col0):
        P = self.P
        Wu = self.w_up[l]
        Wd = self.w_down[l]
        with ExitStack() as st:
            nt = self.norm_tiles(st)
            pt = self.post_tiles(st)
            hT, _ = self.tile(st, "hT", [128, 16, 512], BF16)
            wb = [self.tile(st, "wbuf", [128, 8192], BF16) for _ in range(2)]
            act, r_act = self.tile(st, "act", [128, 44, 512], BF16)
            mixed, r_mx = self.tile(st, "mixed", [128, 16, 512], F32)
            pre = [self.tile(st, "pre", [128, 514], F32) for _ in range(4)]
            uu = [self.tile(st, "uu", [128, 512], F32) for _ in range(4)]
            sgf, r_sgf = self.tile(st, "sgf", [128, 4, 512], F32)
            sq = [self.tile(st, "sq", [128, 512], BF16) for _ in range(2)]
            carry, r_carry = self.tile(st, "carry", [128, 88, 2], F32)
            fw, r_fw = self.tile(st, "fw", [128, 88, 3], F32)
            fb, r_fb = self.tile(st, "fb", [128, 88], F32)
            P.dma("sp", lambda e: e.dma_start(out=fw[:], in_=self.ffw[:, l * 264:(l + 1) * 264].rearrange("p (c k) -> p c k", k=3)), r_fw, True)
            P.dma("sp", lambda e: e.dma_start(out=fb[:], in_=self.ffb[:, l * 88:(l + 1) * 88]), r_fb, True)
            tiles = [(f0 - 2, 2)] + self.tok_tiles(f0, f1 - f0)
            pk = [0]
            hregs = [self.newreg("hT"), self.newreg("hT")]
            sched = {}
            for tix, (s0, n) in enumerate(tiles):
                halo = (tix == 0)
                sched[tix] = {}
                sched[tix]["N"] = [(None, lambda b, s0=s0, n=n: self.fill_hT(nt, X, PAD + s0, n, (l * 4 + 2) * 16, hT, hregs))]
                jobs = []
                for grp in range(11):
                    for gv in range(2):
                        def lf(b, grp=grp, gv=gv):
                            self.wload(wb[b][0], wb[b][1], Wu, 16, gv * DFF + grp * 512, 512, 0)

                        def cf(b, grp=grp, gv=gv, n=n, halo=halo, hregs=hregs):
                            wbt, rwb = wb[b]
                            wv_ = wbt[:, 0:8192].rearrange("p (k n) -> p k n", k=16)
                            for cc in range(4):
                                jg = grp * 4 + cc
                                j = jg + 44 * gv
                                bk = pk[0] % 6
                                pk[0] += 1
                                ps, pr = self.ps[bk], self.psr[bk]
                                for kc in range(16):
                                    P.op("pe", lambda e: e.matmul(ps[:, 0:n], lhsT=wv_[:, kc, cc * 128:(cc + 1) * 128], rhs=hT[:, kc, 0:n], start=(kc == 0), stop=(kc == 15)),
                                         [rwb] + hregs, [pr])
                                if halo:
                                    P.op("act", lambda e: e.activation(out=carry[:, j, :], in_=ps[:, 0:2], func=AF.Copy), [pr], [r_carry])
                                    continue
                                p_, rp_ = pre[cc % 4]
                                u_, ru_ = uu[cc % 4]
                                P.op("act", lambda e: e.activation(out=p_[:, 2:2 + n], in_=ps[:, 0:n], func=AF.Copy), [pr], [rp_])
                                P.op("act", lambda e: e.activation(out=p_[:, 0:2], in_=carry[:, j, :], func=AF.Copy), [r_carry], [rp_])
                                P.op("act", lambda e: e.activation(out=carry[:, j, :], in_=p_[:, n:n + 2], func=AF.Copy), [rp_], [r_carry])
                                P.op("dve", lambda e: e.tensor_scalar(out=u_[:, 0:n], in0=p_[:, 2:2 + n], scalar1=fw[:, j, 2:3], scalar2=fb[:, j:j + 1], op0=ALU.mult, op1=ALU.add), [rp_, r_fw, r_fb], [ru_])
                                P.op("dve", lambda e: e.scalar_tensor_tensor(out=u_[:, 0:n], in0=p_[:, 1:1 + n], scalar=fw[:, j, 1:2], in1=u_[:, 0:n], op0=ALU.mult, op1=ALU.add), [rp_, ru_], [ru_])
                                P.op("dve", lambda e: e.scalar_tensor_tensor(out=u_[:, 0:n], in0=p_[:, 0:n], scalar=fw[:, j, 0:1], in1=u_[:, 0:n], op0=ALU.mult, op1=ALU.add), [rp_, ru_], [ru_])
                                if gv == 0:
                                    P.op("act", lambda e: e.activation(out=sgf[:, cc, 0:n], in_=u_[:, 0:n], func=AF.Silu), [ru_], [r_sgf])
                                else:
                                    P.op("dve", lambda e: e.tensor_tensor(out=act[:, jg, 0:n], in0=sgf[:, cc, 0:n], in1=u_[:, 0:n], op=ALU.mult), [r_sgf, ru_], [r_act])
                        jobs.append((lf, cf))
                sched[tix]["U"] = jobs
                jobs = []
                if not halo:
                    kranges = [(0, 16), (16, 16), (32, 12)]
                    for dg in range(4):
                        for kr, (k0_, nk) in enumerate(kranges):
                            def lf(b, dg=dg, k0_=k0_, nk=nk):
                                self.wload(wb[b][0], wb[b][1], Wd, nk, dg * 512, 512, 0, k0=k0_)

                            def cf(b, dg=dg, kr=kr, k0_=k0_, nk=nk, n=n):
                                wd = wb[b][0][:, 0:nk * 512].rearrange("p (k n) -> p k n", k=nk)
                                for cc in range(4):
                                    dch = dg * 4 + cc
                                    po, ro = self.ps[cc], self.psr[cc]
                                    for kc in range(nk):
                                        P.op("pe", lambda e: e.matmul(po[:, 0:n], lhsT=wd[:, kc, cc * 128:(cc + 1) * 128], rhs=act[:, k0_ + kc, 0:n], start=(kr == 0 and kc == 0), stop=(kr == 2 and kc == nk - 1)),
                                             [wb[b][1], r_act], [ro])
                                    if kr == 2:
                                        s_, rs_ = sq[dch % 2]
                                        P.op("act", lambda e: e.activation(out=mixed[:, dch, 0:n], in_=po[:, 0:n], func=AF.Copy), [ro], [r_mx])
                                        P.op("act", lambda e: e.activation(out=s_[:, 0:n], in_=po[:, 0:n], func=AF.Square), [ro], [rs_])
                                        P.op("pe", lambda e: e.matmul(self.ps[6][:, 0:n], lhsT=self.ones_b[:], rhs=s_[:, 0:n], start=(dch == 0), stop=(dch == 15)), [rs_], [self.psr[6]])
                            jobs.append((lf, cf))
                sched[tix]["D"] = jobs
                sched[tix]["P"] = [(None, lambda b, s0=s0, n=n: self.post_norm(st, mixed, r_mx, 6, n, (l * 4 + 3) * 16, X, Xout, PAD + s0, out_col0 + (s0 - f0), pt))]
            nt_ = len(tiles)
            seq = sched[0]["N"] + sched[0]["U"] + sched[1]["N"] + sched[1]["U"]
            for t in range(1, nt_):
                if t + 1 < nt_:
                    seq += sched[t + 1]["N"]
                seq += sched[t]["D"]
                if t + 1 < nt_:
                    seq += sched[t + 1]["U"][:4] + sched[t]["P"] + sched[t + 1]["U"][4:]
                else:
                    seq += sched[t]["P"]
            self.run_jobs(seq)
            P.end_phase()

    def build(self):
        ph = []
        ph.append(lambda: self.phase_kv(0, self.xT))
        for (r0, r1) in ((0, 2048), (2048, 4096)):
            ph.append(lambda r0=r0, r1=r1: self.phase_m1(0, self.xT, r0, r1))
            ph.append(lambda r0=r0, r1=r1: self.phase_m2(0, r0, r1))
            ph.append(lambda r0=r0, r1=r1: self.phase_att(0, r0, r1))
            ph.append(lambda r0=r0, r1=r1: self.phase_merge(0, self.xT, self.XM, r0, r1))
        ph.append(lambda: self.phase_ffn(0, self.XM, self.X1, 0, 4096, PAD))
        ph.append(lambda: self.phase_kv(1, self.X1))
        ph.append(lambda: self.phase_m1(1, self.X1, 1920, 4096))
        ph.append(lambda: self.phase_m2(1, 1920, 4096))
        ph.append(lambda: self.phase_att(1, 1920, 4096))
        ph.append(lambda: self.phase_merge(1, self.X1, self.XM, 1920, 4096))
        ph.append(lambda: self.phase_ffn(1, self.XM, self.OUT, 2048, 4096, 0))
        sel = self.stop if self.stop is not None else range(len(ph))
        for i in sel:
            ph[i]()
        self.P.barrier()
        self.P.emit()
        return self.nc


def _colvec(v, nchunk):
    return np.ascontiguousarray(v.reshape(nchunk, 128).T)


def make_inputs(inp):
    L = 2
    f = lambda a: np.ascontiguousarray(np.asarray(a, dtype=np.float32))
    shared = {}
    for k in ("w_in", "w_a_out", "w_b_out", "w_c_out", "w_o", "w_up", "w_down"):
        shared[k] = f(inp[k])
    nw = np.zeros((128, L * 4 * 16), np.float32)
    for l in range(L):
        for i, k in enumerate(("norm_mix_pre", "norm_mix_post", "norm_ffn_pre", "norm_ffn_post")):
            nw[:, (l * 4 + i) * 16:(l * 4 + i + 1) * 16] = _colvec(f(inp[k])[l], 16)
    shared["normw"] = nw
    caw = np.zeros((128, L * 8 * 31), np.float32)
    av = np.zeros((128, L * 3 * 8), np.float32)
    for l in range(L):
        w = f(inp["conv_a_w"])[l]
        caw[:, l * 248:(l + 1) * 248] = w.T.reshape(8, 128, 31).transpose(1, 0, 2).reshape(128, 248)
        for i, k in enumerate(("conv_a_b", "ln_a_g", "ln_a_b")):
            av[:, (l * 3 + i) * 8:(l * 3 + i + 1) * 8] = _colvec(f(inp[k])[l], 8)
    shared["convaw"] = caw
    shared["avec"] = av
    lnb = np.zeros((128, L * 2 * 1024), np.float32)
    for l in range(L):
        lnb[:, (l * 2) * 1024:(l * 2 + 1) * 1024] = f(inp["ln_b_g"])[l][None, :]
        lnb[:, (l * 2 + 1) * 1024:(l * 2 + 2) * 1024] = f(inp["ln_b_b"])[l][None, :]
    shared["lnb"] = lnb
    shared["sgw"] = np.ascontiguousarray(f(inp["sg_w"]).transpose(0, 3, 1, 2).reshape(L, 128, 1024))
    shared["sgb"] = np.ascontiguousarray(f(inp["sg_b"]).reshape(1, L * 1024))
    cw1 = np.zeros((L * 2, 64, 32 * 256), np.float32)
    cw2 = np.zeros((L * 2, 256, 64), np.float32)
    cpe = np.zeros((L * 2, 64, 32), np.float32)
    for l in range(L):
        for kv, s in enumerate(("k", "v")):
            cw1[l * 2 + kv] = f(inp["cmp_w1_" + s])[l].transpose(1, 0, 2).reshape(64, 32 * 256)
            cw2[l * 2 + kv] = f(inp["cmp_w2_" + s])[l]
            cpe[l * 2 + kv] = f(inp["cmp_pe_" + s])[l].T
    shared["cw1"], shared["cw2"], shared["cpe"] = cw1, cw2, cpe
    ffw = np.zeros((128, L * 88 * 3), np.float32)
    ffb = np.zeros((128, L * 88), np.float32)
    for l in range(L):
        w = f(inp["ffn_conv_w"])[l]
        ffw[:, l * 264:(l + 1) * 264] = w.T.reshape(88, 128, 3).transpose(1, 0, 2).reshape(128, 264)
        ffb[:, l * 88:(l + 1) * 88] = _colvec(f(inp["ffn_conv_b"])[l], 88)
    shared["ffw"], shared["ffb"] = ffw, ffb
    p = np.arange(128)
    shared["c_ident"] = np.eye(128, dtype=np.float32)
    shared["c_trile"] = (p[:, None] <= p[None, :]).astype(np.float32)
    shared["c_trigt"] = (p[:, None] > p[None, :]).astype(np.float32)
    shared["c_dtab"] = (16.0 * p[:, None] - p[None, :]).astype(np.float32)
    k = np.arange(4096)
    shared["c_E"] = (k[None, :] // 64 == np.arange(64)[:, None]).astype(np.float32)
    n = np.arange(256)
    sm = np.zeros((256, 64), np.float32)
    for nn in range(255):
        sm[nn, nn // 4] += 1.0
        sm[nn, (nn + 1) // 4] += 1.0
    shared["c_smap"] = np.ascontiguousarray(sm.reshape(2, 128, 64).transpose(1, 0, 2).reshape(128, 128))
    x = f(inp["x"])
    maps = []
    for b in range(4):
        for s in range(2):
            m = dict(shared)
            xT = np.zeros((D, NCOL), np.float32)
            tok = np.zeros((NCOL,), np.float32)
            if s == 1:
                xT[:, PAD:] = x[b].T
                tok[PAD:] = 1.0
                j0 = 0
            else:
                xT[:, PAD + 2048:] = x[b, :2048].T
                tok[PAD + 2048:] = 1.0
                j0 = 32
            m["xT"] = xT
            m["m_tok"] = np.ascontiguousarray(np.broadcast_to(tok[None, :], (128, NCOL)))
            kval = np.ones((128, 32), np.float32)
            nval = np.ones((128, 2), np.float32)
            if s == 0:
                kval[:, :16] = 0.0
                nval[:, 0] = 0.0
            m["m_kval"], m["m_nval"] = kval, nval
            t = np.arange(4096).reshape(32, 128).T
            cur = t // 64
            j = np.arange(64)[None, None, :]
            valid = (j <= cur[:, :, None]) & (j >= j0)
            forced = ((j == j0) | (j == cur[:, :, None]) | (j == cur[:, :, None] - 1)) & valid
            M1 = (valid & ~forced).astype(np.float32)
            M2 = np.where(forced, 1e4 + j, np.where(valid, 0.0, -1.0)).astype(np.float32)
            M3 = valid.astype(np.float32)
            m["m_sel"] = np.ascontiguousarray(np.stack([M1, M2, M3]).reshape(3, 128, 32 * 64))
            maps.append(m)
    return maps


_CACHE = {}


def kernel(**inputs):
    maps = make_inputs(inputs)
    if "nc" not in _CACHE:
        _CACHE["nc"] = Builder().build()
    nc = _CACHE["nc"]
    res = run_bass_kernel_spmd(nc, maps, core_ids=list(range(8)))
    out = np.zeros((4, SEQ, D), np.float32)
    for b in range(4):
        for s in range(2):
            o = res.results[b * 2 + s]["OUT"]
            out[b, s * 2048:(s + 1) * 2048, :] = o.T
    return out
```

```python
import numpy as np
from contextlib import ExitStack
import concourse.bass as bass
import concourse.mybir as mybir
from concourse.bass_utils import run_bass_kernel_spmd

F32 = mybir.dt.float32
BF16 = mybir.dt.bfloat16
AF = mybir.ActivationFunctionType
ALU = mybir.AluOpType
AX = mybir.AxisListType

ENGS = ("pe", "act", "dve", "pool", "sp")

D = 2048
SEQ = 4096
PAD = 128
NCOL = PAD + SEQ
NIN = 12848
DFF = 5632
EPS = 1e-6
C_A, C_B, C_Q, C_KV, C_G, C_M = 0, 2048, 4096, 5120, 6656, 6704


class Reg:
    __slots__ = ("name", "w", "r", "dkey", "dcount")

    def __init__(self, name):
        self.name = name
        self.w = None
        self.r = {}
        self.dkey = None
        self.dcount = 0


class _Rec:
    def __init__(self):
        self.call = None

    def __getattr__(self, name):
        def f(*a, **k):
            self.call = (name, a, k)
            return self
        return f


def _record(fn):
    r = _Rec()
    fn(r)
    assert r.call is not None
    return r.call


class Prog:
    def __init__(self, nc):
        self.nc = nc
        self.streams = {e: [] for e in ENGS}
        self.cnt = {e: 0 for e in ENGS}
        self.seen = {e: {} for e in ENGS}
        self.dsems = {}
        self.ndsem = 0
        self.semh = {}
        self.free_dkeys = []
        self.phase_keys = []

    def sb(self, stack, name, shape, dt):
        return stack.enter_context(self.nc.sbuf_tensor(name, list(shape), dt))

    def _waits(self, eng, deps):
        out = []
        seen = self.seen[eng]
        best = {}
        for (k, v) in deps:
            if best.get(k, -1) < v:
                best[k] = v
        for k, v in best.items():
            if k == eng:
                if eng == "pe":
                    continue
                if v <= self.cnt[eng] - 2:
                    continue
            if seen.get(k, -1) >= v:
                continue
            seen[k] = v
            out.append((k, v))
        return out

    def _deps(self, reads, writes):
        deps = []
        for r in reads:
            if r.w is not None:
                deps.append(r.w)
        for w in writes:
            if w.w is not None:
                deps.append(w.w)
            deps.extend(w.r.items())
        return deps

    def op(self, eng, fn, reads=(), writes=()):
        waits = self._waits(eng, self._deps(reads, writes))
        self.cnt[eng] += 1
        me = (eng, self.cnt[eng])
        self.streams[eng].append((waits, _record(fn), (eng, 1)))
        for r in reads:
            r.r[me[0]] = me[1]
        for w in writes:
            w.w = me
            w.r = {}
        return me

    def dma(self, q, fn, sb, load, dram=()):
        if sb.dkey is None:
            if self.free_dkeys:
                sb.dkey = self.free_dkeys.pop()
                sb.dcount = self.dsems[sb.dkey]
            else:
                sb.dkey = "d%d" % self.ndsem
                self.ndsem += 1
            self.phase_keys.append(sb.dkey)
        if load:
            deps = self._deps(dram, [sb])
        else:
            deps = self._deps([sb], dram)
        waits = self._waits(q, deps)
        sb.dcount += 16
        self.dsems[sb.dkey] = sb.dcount
        me = (sb.dkey, sb.dcount)
        self.streams[q].append((waits, _record(fn), (sb.dkey, 16)))
        if load:
            sb.w = me
            sb.r = {}
            for d in dram:
                d.r[me[0]] = me[1]
        else:
            sb.r[me[0]] = me[1]
            for d in dram:
                d.w = me
                d.r = {}
        return me

    def barrier(self):
        allk = [(e, self.cnt[e]) for e in ENGS if self.cnt[e] > 0]
        allk += list(self.dsems.items())
        for e in ENGS:
            waits = self._waits(e, [kv for kv in allk if kv[0] != e])
            if waits:
                self.streams[e].append((waits, None, None))

    def end_phase(self, persistent=False):
        self.barrier()
        if not persistent:
            self.free_dkeys.extend(self.phase_keys)
        self.phase_keys = []

    def emit(self):
        nc = self.nc
        keys = list(ENGS) + list(self.dsems.keys())
        with ExitStack() as st:
            for k in keys:
                self.semh[k] = st.enter_context(nc.semaphore("s_" + k))
            block = st.enter_context(nc.Block())
            semh = self.semh

            def run(e, stream):
                for waits, fn, inc in stream:
                    for (k, v) in waits:
                        e.wait_ge(semh[k], v)
                    if fn is not None:
                        name, a, k = fn
                        ins = getattr(e, name)(*a, **k)
                        ins.then_inc(semh[inc[0]], inc[1])

            @block.tensor
            def _(e):
                run(e, self.streams["pe"])

            @block.scalar
            def _(e):
                run(e, self.streams["act"])

            @block.vector
            def _(e):
                run(e, self.streams["dve"])

            @block.gpsimd
            def _(e):
                run(e, self.streams["pool"])

            @block.sync
            def _(e):
                run(e, self.streams["sp"])


class Builder:
    def __init__(self, debug=False, nlayers=2, stop=None):
        self.debug = debug
        self.stop = stop
        nc = bass.Bass("TRN2", target_bir_lowering=False)
        self.nc = nc
        self.P = Prog(nc)
        self.I = {}
        self.regs = {}
        self.gst = ExitStack()
        self._uid = 0
        self.declare_io()
        self.alloc_consts()

    def R(self, name):
        if name not in self.regs:
            self.regs[name] = Reg(name)
        return self.regs[name]

    def newreg(self, name):
        self._uid += 1
        return Reg("%s_%d" % (name, self._uid))

    def din(self, name, shape, dt=F32):
        t = self.nc.dram_tensor(name, list(shape), dt, kind="ExternalInput").ap()
        self.I[name] = t
        return t

    def dscr(self, name, shape, dt, out=False):
        kind = "ExternalOutput" if (out or self.debug) else "Internal"
        return self.nc.dram_tensor(name, list(shape), dt, kind=kind).ap()

    def tile(self, st, name, shape, dt):
        self._uid += 1
        t = self.P.sb(st, "%s_%d" % (name, self._uid), shape, dt)
        return t, self.newreg(name)

    def dump(self, name, tile_, reg, shape, dt=F32):
        if not self.debug or True:
            return
        d = self.nc.dram_tensor("dbg_" + name, list(shape), dt, kind="ExternalOutput").ap()
        self.P.dma("sp", lambda e: e.dma_start(out=d, in_=tile_[:]), reg, False)

    def declare_io(self):
        L = 2
        self.xT = self.din("xT", [D, NCOL])
        self.w_in = self.din("w_in", [L, D, NIN])
        self.w_a_out = self.din("w_a_out", [L, 1024, D])
        self.w_b_out = self.din("w_b_out", [L, 1024, D])
        self.w_c_out = self.din("w_c_out", [L, 1024, D])
        self.w_o = self.din("w_o", [L, D, D])
        self.w_up = self.din("w_up", [L, D, 2 * DFF])
        self.w_down = self.din("w_down", [L, DFF, D])
        self.normw = self.din("normw", [128, L * 4 * 16])
        self.convaw = self.din("convaw", [128, L * 8 * 31])
        self.avec = self.din("avec", [128, L * 3 * 8])
        self.lnb = self.din("lnb", [128, L * 2 * 1024])
        self.sgw = self.din("sgw", [L, 128, 8 * 128])
        self.sgb = self.din("sgb", [1, L * 1024])
        self.cw1 = self.din("cw1", [L * 2, 64, 32 * 256])
        self.cw2 = self.din("cw2", [L * 2, 256, 64])
        self.cpe = self.din("cpe", [L * 2, 64, 32])
        self.ffw = self.din("ffw", [128, L * 88 * 3])
        self.ffb = self.din("ffb", [128, L * 88])
        self.c_ident = self.din("c_ident", [128, 128])
        self.c_trile = self.din("c_trile", [128, 128])
        self.c_trigt = self.din("c_trigt", [128, 128])
        self.c_dtab = self.din("c_dtab", [128, 128])
        self.c_E = self.din("c_E", [64, 32 * 128])
        self.c_smap = self.din("c_smap", [128, 2 * 64])
        self.m_tok = self.din("m_tok", [128, NCOL])
        self.m_kval = self.din("m_kval", [128, 32])
        self.m_nval = self.din("m_nval", [128, 2])
        self.m_sel = self.din("m_sel", [3, 128, 32 * 64])
        self.XM = self.dscr("XM", [D, NCOL], F32)
        self.X1 = self.dscr("X1", [D, NCOL], F32)
        self.OUT = self.dscr("OUT", [D, 2048], F32, out=True)
        self.GLU = self.dscr("GLU", [1024, NCOL], BF16)
        self.HA = self.dscr("HA", [1024, NCOL], BF16)
        self.HB = self.dscr("HB", [1024, NCOL], BF16)
        self.OC = self.dscr("OC", [1024, NCOL], BF16)
        self.Q = self.dscr("Q", [1024, NCOL], BF16)
        self.GATES = self.dscr("GATES", [NCOL, 48], F32)
        self.KT = [self.dscr("KT%d" % i, [256, NCOL], BF16) for i in range(4)]
        self.VT = [self.dscr("VT%d" % i, [NCOL, 256], BF16) for i in range(2)]

    def alloc_consts(self):
        P, st = self.P, self.gst
        nc = self.nc
        T = lambda n, s, d: self.tile(st, n, s, d)
        self.ident_f, r_if = T("ident_f", [128, 128], F32)
        self.ident_b, r_ib = T("ident_b", [128, 128], BF16)
        self.ones_b, r_ob = T("ones_b", [128, 128], BF16)
        self.trile, r1 = T("trile", [128, 128], F32)
        self.trigt, r2 = T("trigt", [128, 128], F32)
        self.dtab, r3 = T("dtab", [128, 128], F32)
        self.tokv, r4 = T("tokv", [128, NCOL], BF16)
        self.kval, r5 = T("kval", [128, 32], F32)
        self.nval, r6 = T("nval", [128, 2], F32)
        self.normw_s, r7 = T("normw", [128, 128], F32)
        self.epsT, r8 = T("eps", [128, 1], F32)
        self.zero_f, r9 = T("zero_f", [128, 128], F32)
        self.r_const = self.newreg("const")
        rc = self.r_const
        ld = lambda t, src: P.dma("sp", lambda e: e.dma_start(out=t[:], in_=src), rc, True)
        ld(self.ident_f, self.c_ident)
        ld(self.trile, self.c_trile)
        ld(self.trigt, self.c_trigt)
        ld(self.dtab, self.c_dtab)
        ld(self.kval, self.m_kval)
        ld(self.nval, self.m_nval)
        ld(self.normw_s, self.normw)
        P.dma("pool", lambda e: e.dma_start(out=self.ident_b[:], in_=self.c_ident), rc, True)
        P.dma("pool", lambda e: e.dma_start(out=self.tokv[:], in_=self.m_tok), rc, True)
        P.op("dve", lambda e: e.memset(self.ones_b[:], 1.0), [], [rc])
        P.op("dve", lambda e: e.memset(self.epsT[:], EPS), [], [rc])
        P.op("dve", lambda e: e.memset(self.zero_f[:], 0.0), [], [rc])
        for X in (self.XM, self.X1):
            Xv = X.rearrange("(c p) t -> p c t", p=128)
            for c in range(16):
                P.dma("sp", lambda e, c=c, Xv=Xv: e.dma_start(out=Xv[:, c, 0:128], in_=self.zero_f[:]), rc, False)
        self.ps = []
        self.psr = []
        for i in range(8):
            t = st.enter_context(nc.psum_tensor("psb%d" % i, [128, 512], F32))
            self.ps.append(t)
            self.psr.append(self.newreg("ps%d" % i))
        P.end_phase(persistent=True)

    def load_norm(self, st_tiles, X, col0, n, nw_off, dst_fn, dst_reg, bank=7):
        P = self.P
        xt, r_xt, sq, r_sq, rs, r_rs = st_tiles
        Xv = X.rearrange("(c p) t -> p c t", p=128)
        P.dma("sp", lambda e: e.dma_start(out=xt[:, :, 0:n], in_=Xv[:, :, col0:col0 + n]), r_xt, True)
        ps, pr = self.ps[bank], self.psr[bank]
        for c in range(16):
            P.op("act", lambda e, c=c: e.activation(out=sq[c % 2][:, 0:n], in_=xt[:, c, 0:n], func=AF.Square), [r_xt], [r_sq[c % 2]])
            P.op("pe", lambda e, c=c: e.matmul(ps[:, 0:n], lhsT=self.ones_b[:], rhs=sq[c % 2][:, 0:n], start=(c == 0), stop=(c == 15)), [r_sq[c % 2]], [pr])
        P.op("act", lambda e: e.activation(out=rs[:, 0:n], in_=ps[:, 0:n], func=AF.Sqrt, bias=self.epsT[:, 0:1], scale=1.0 / D), [pr], [r_rs])
        P.op("dve", lambda e: e.reciprocal(out=rs[:, 0:n], in_=rs[:, 0:n]), [r_rs], [r_rs])
        P.op("dve", lambda e: e.tensor_tensor(out=rs[:, 0:n], in0=rs[:, 0:n], in1=self.tokv[:, col0:col0 + n], op=ALU.mult), [r_rs], [r_rs])
        for c in range(16):
            P.op("dve", lambda e, c=c: e.scalar_tensor_tensor(out=dst_fn(c), in0=xt[:, c, 0:n], scalar=self.normw_s[:, nw_off + c:nw_off + c + 1],
                                                              in1=rs[:, 0:n], op0=ALU.mult, op1=ALU.mult), [r_xt, r_rs], [dst_reg])

    def norm_tiles(self, st):
        xt, r_xt = self.tile(st, "xt", [128, 16, 256], F32)
        sq0, r0 = self.tile(st, "sq0", [128, 256], BF16)
        sq1, r1 = self.tile(st, "sq1", [128, 256], BF16)
        rs, r_rs = self.tile(st, "rs", [128, 256], F32)
        return (xt, r_xt, [sq0, sq1], [r0, r1], rs, r_rs)

    def fill_hT(self, st_tiles, X, col0, ntok, nw_off, hT, hregs):
        for i, t0 in enumerate(range(0, ntok, 256)):
            n = min(256, ntok - t0)
            self.load_norm(st_tiles, X, col0 + t0, n, nw_off, lambda c, t0=t0, n=n: hT[:, c, t0:t0 + n], hregs[i])

    def wload(self, wbuf, wreg, Wsrc, kc, col0, ncols, off=0, k0=0):
        view = wbuf[:, off:off + kc * ncols].rearrange("p (k n) -> p k n", k=kc)
        src = Wsrc.rearrange("(k p) n -> p k n", p=128)[:, k0:k0 + kc, col0:col0 + ncols]
        self.P.dma("pool", lambda e: e.dma_start(out=view, in_=src), wreg, True)
        return view

    def run_jobs(self, jobs, nbuf=2):
        loads = [i for i, (lf, _) in enumerate(jobs) if lf is not None]
        for j in range(min(nbuf - 1, len(loads))):
            jobs[loads[j]][0](j % nbuf)
        li = 0
        for i, (lf, cf) in enumerate(jobs):
            if lf is None:
                cf(None)
                continue
            nx = li + nbuf - 1
            if nx < len(loads):
                jobs[loads[nx]][0](nx % nbuf)
            cf(li % nbuf)
            li += 1

    def post_norm(self, st, mixed, r_mixed, ss_bank, n, nw_off, Xin, Xout, cin0, cout0, tl):
        P = self.P
        rs2, r_rs2, xr, r_xr, ot, r_ot, tt, r_tt = tl
        ps, pr = self.ps[ss_bank], self.psr[ss_bank]
        P.op("act", lambda e: e.activation(out=rs2[:, 0:n], in_=ps[:, 0:n], func=AF.Sqrt, bias=self.epsT[:, 0:1], scale=1.0 / D), [pr], [r_rs2])
        P.op("dve", lambda e: e.reciprocal(out=rs2[:, 0:n], in_=rs2[:, 0:n]), [r_rs2], [r_rs2])
        Xi = Xin.rearrange("(c p) t -> p c t", p=128)
        Xo = Xout.rearrange("(c p) t -> p c t", p=128)
        for c in range(16):
            b = c % 2
            P.dma("sp", lambda e, c=c, b=b: e.dma_start(out=xr[b][:, 0:n], in_=Xi[:, c, cin0:cin0 + n]), r_xr[b], True)
            P.op("dve", lambda e, c=c, b=b: e.tensor_tensor(out=tt[b][:, 0:n], in0=mixed[:, c, 0:n], in1=rs2[:, 0:n], op=ALU.mult), [r_mixed, r_rs2], [r_tt[b]])
            P.op("dve", lambda e, c=c, b=b: e.scalar_tensor_tensor(out=ot[b][:, 0:n], in0=tt[b][:, 0:n], scalar=self.normw_s[:, nw_off + c:nw_off + c + 1],
                                                                   in1=xr[b][:, 0:n], op0=ALU.mult, op1=ALU.add), [r_tt[b], r_xr[b]], [r_ot[b]])
            P.dma("sp", lambda e, c=c, b=b: e.dma_start(out=Xo[:, c, cout0:cout0 + n], in_=ot[b][:, 0:n]), r_ot[b], False)

    def post_tiles(self, st):
        rs2, r_rs2 = self.tile(st, "rs2", [128, 512], F32)
        xr = []; r_xr = []; ot = []; r_ot = []; tt = []; r_tt = []
        for b in range(2):
            a, ra = self.tile(st, "xr", [128, 512], F32); xr.append(a); r_xr.append(ra)
            a, ra = self.tile(st, "ot", [128, 512], F32); ot.append(a); r_ot.append(ra)
            a, ra = self.tile(st, "tt", [128, 512], F32); tt.append(a); r_tt.append(ra)
        return (rs2, r_rs2, xr, r_xr, ot, r_ot, tt, r_tt)

    def tok_tiles(self, start, ntok, step=512):
        return [(t0, min(step, start + ntok - t0)) for t0 in range(start, start + ntok, step)]

    def phase_kv(self, l, X):
        P = self.P
        with ExitStack() as st:
            nt = self.norm_tiles(st)
            wkv, r_wkv = self.tile(st, "wkv", [128, 16 * 1536], BF16)
            wv = wkv[:].rearrange("p (k n) -> p k n", k=16)
            Wl = self.w_in[l]
            for j in range(3):
                src = Wl.rearrange("(k p) n -> p k n", p=128)[:, :, C_KV + j * 512:C_KV + (j + 1) * 512]
                P.dma("pool", lambda e, j=j, src=src: e.dma_start(out=wv[:, :, j * 512:(j + 1) * 512], in_=src), r_wkv, True)
            hts = [self.tile(st, "hTt", [128, 16, 256], BF16) for _ in range(2)]
            ksts = [self.tile(st, "kst", [128, 8, 256], BF16) for _ in range(2)]
            vsts = [self.tile(st, "vst", [128, 2, 512], BF16) for _ in range(2)]
            fm_off = [0, 256, 512, 1024]
            tm_off = [768, 1280]
            for ti, t0 in enumerate(range(0, SEQ, 256)):
                hT, r_h = hts[ti % 2]
                kst, r_k = ksts[ti % 2]
                vst, r_v = vsts[ti % 2]
                col0 = PAD + t0
                self.load_norm(nt, X, col0, 256, (l * 4 + 0) * 16, lambda c, hT=hT: hT[:, c, :], r_h)
                for ty in range(4):
                    bk = ty % 4
                    ps, pr = self.ps[bk], self.psr[bk]
                    for half in range(2):
                        off = fm_off[ty] + half * 128
                        for kc in range(16):
                            P.op("pe", lambda e, ps=ps, half=half, off=off, kc=kc, hT=hT: e.matmul(ps[:, half * 256:(half + 1) * 256], lhsT=wv[:, kc, off:off + 128],
                                                                                                  rhs=hT[:, kc, :], start=(kc == 0), stop=(kc == 15)), [r_wkv, r_h], [pr])
                    P.op("act", lambda e, ps=ps, ty=ty, kst=kst: e.activation(out=kst[:, 2 * ty:2 * ty + 2, :], in_=ps[:].rearrange("p (h t) -> p h t", h=2), func=AF.Copy), [pr], [r_k])
                for ty in range(4):
                    dst = self.KT[ty].rearrange("(h p) t -> p h t", p=128)[:, :, col0:col0 + 256]
                    P.dma("sp", lambda e, ty=ty, dst=dst, kst=kst: e.dma_start(out=dst, in_=kst[:, 2 * ty:2 * ty + 2, :]), r_k, False, [self.R("KT")])
                for sub in range(2):
                    ps, pr = self.ps[4 + sub], self.psr[4 + sub]
                    for j in range(2):
                        for kc in range(16):
                            P.op("pe", lambda e, ps=ps, sub=sub, j=j, kc=kc, hT=hT: e.matmul(ps[:, j * 256:(j + 1) * 256], lhsT=hT[:, kc, sub * 128:(sub + 1) * 128],
                                                                                             rhs=wv[:, kc, tm_off[j]:tm_off[j] + 256], start=(kc == 0), stop=(kc == 15)), [r_wkv, r_h], [pr])
                    P.op("dve", lambda e, ps=ps, sub=sub, vst=vst: e.tensor_copy(out=vst[:, sub, :], in_=ps[:]), [pr], [r_v])
                for j in range(2):
                    dst = self.VT[j][col0:col0 + 256, :].rearrange("(s p) c -> p s c", p=128)
                    P.dma("sp", lambda e, j=j, dst=dst, vst=vst: e.dma_start(out=dst, in_=vst[:, :, j * 256:(j + 1) * 256]), r_v, False, [self.R("VT")])
            P.end_phase()

    def phase_m1(self, l, X, r0, r1):
        P = self.P
        T = r1 - r0
        TA = T + 128
        Wl = self.w_in[l]
        with ExitStack() as st:
            nt = self.norm_tiles(st)
            hT, _ = self.tile(st, "hT", [128, 16, TA], BF16)
            hregs = [self.newreg("hT") for _ in range((TA + 255) // 256)]
            wb = [self.tile(st, "wbuf", [128, 8192], BF16) for _ in range(2)]
            wvt, r_wvt = self.tile(st, "wvt", [128, 16 * 1024], BF16)
            ut, r_ut = self.tile(st, "ut", [128, 8, 128], BF16)
            sig = [self.tile(st, "sig", [128, 512], F32) for _ in range(2)]
            gst = [self.tile(st, "gst", [128, 512], BF16) for _ in range(2)]
            qst = [self.tile(st, "qst", [128, 512], BF16) for _ in range(2)]
            vg = [self.tile(st, "vg", [128, 1024], F32)] * 2
            vln = [self.tile(st, "vln", [128, 1024], BF16)] * 2
            bst, r_bst = self.tile(st, "bst", [128, 2, 6], F32)
            mv, r_mv = self.tile(st, "mv", [128, 2], F32)
            wsT, r_ws = self.tile(st, "wsT", [128, 8, 128], BF16)
            wsF, r_wsF = self.tile(st, "wsF", [128, 8, 128], BF16)
            lnbt, r_lnb = self.tile(st, "lnbt", [128, 2, 1024], F32)
            sgbr, r_sgb = self.tile(st, "sgbr", [1, 1024], BF16)
            gall, r_gall = self.tile(st, "gall", [128, T // 128, 48], F32)
            P.dma("sp", lambda e: e.dma_start(out=lnbt[:], in_=self.lnb[:, l * 2048:(l + 1) * 2048].rearrange("p (a c) -> p a c", a=2)), r_lnb, True)
            P.dma("pool", lambda e: e.dma_start(out=sgbr[:], in_=self.sgb[:, l * 1024:(l + 1) * 1024]), r_sgb, True)
            P.dma("pool", lambda e: e.dma_start(out=wsF[:], in_=self.sgw[l].rearrange("s (g t) -> s g t", g=8)), r_wsF, True)
            P.op("dve", lambda e: e.tensor_tensor(out=wsT[:], in0=wsF[:], in1=self.trile[:].unsqueeze(1).to_broadcast([128, 8, 128]), op=ALU.mult), [r_wsF], [r_ws])
            for j in range(2):
                src = Wl.rearrange("(k p) n -> p k n", p=128)[:, :, C_B + 1024 + j * 512:C_B + 1024 + (j + 1) * 512]
                P.dma("pool", lambda e, j=j, src=src: e.dma_start(out=wvt[:].rearrange("p (k n) -> p k n", k=16)[:, :, j * 512:(j + 1) * 512], in_=src), r_wvt, True)
            wv3 = wvt[:].rearrange("p (k n) -> p k n", k=16)
            self.fill_hT(nt, X, PAD + r0 - 128, TA, (l * 4 + 0) * 16, hT, hregs)
            tilesA = self.tok_tiles(0, TA)
            tilesR = self.tok_tiles(128, T)
            jobs = []
            psrot = [0]

            def nextps():
                b = psrot[0] % 4
                psrot[0] += 1
                return self.ps[b], self.psr[b]

            def mm16(ps, n, wview, c0, t0):
                for kc in range(16):
                    P.op("pe", lambda e, kc=kc: e.matmul(ps[:, 0:n], lhsT=wview[:, kc, c0:c0 + 128], rhs=hT[:, kc, t0:t0 + n], start=(kc == 0), stop=(kc == 15)),
                         [wview_reg[0]] + hregs, [ps_reg[0]])

            wview_reg = [None]
            ps_reg = [None]
            for jg in range(4):
                def lf(b, jg=jg):
                    self.wload(wb[b][0], wb[b][1], Wl, 16, C_A + jg * 256, 256, 0)
                    self.wload(wb[b][0], wb[b][1], Wl, 16, C_A + 1024 + jg * 256, 256, 16 * 256)

                def cf(b, jg=jg):
                    wa = wb[b][0][:, 0:4096].rearrange("p (k n) -> p k n", k=16)
                    wg = wb[b][0][:, 4096:8192].rearrange("p (k n) -> p k n", k=16)
                    wview_reg[0] = wb[b][1]
                    k = 0
                    for cc in range(2):
                        ch = jg * 2 + cc
                        for (t0, n) in tilesA:
                            pa, ra = nextps()
                            ps_reg[0] = ra
                            mm16(pa, n, wa, cc * 128, t0)
                            pg, rg = nextps()
                            ps_reg[0] = rg
                            mm16(pg, n, wg, cc * 128, t0)
                            s_, rs_ = sig[k % 2]
                            g_, rg_ = gst[k % 2]
                            k += 1
                            P.op("act", lambda e, pg=pg, s_=s_, n=n: e.activation(out=s_[:, 0:n], in_=pg[:, 0:n], func=AF.Sigmoid), [rg], [rs_])
                            P.op("dve", lambda e, pa=pa, s_=s_, g_=g_, n=n: e.tensor_tensor(out=g_[:, 0:n], in0=pa[:, 0:n], in1=s_[:, 0:n], op=ALU.mult), [ra, rs_], [rg_])
                            c0 = PAD + r0 - 128 + t0
                            P.dma("sp", lambda e, g_=g_, ch=ch, c0=c0, n=n: e.dma_start(out=self.GLU[ch * 128:(ch + 1) * 128, c0:c0 + n], in_=g_[:, 0:n]), rg_, False, [self.R("GLU")])
                jobs.append((lf, cf))
            ku = [0]
            for jg in range(2):
                def lf(b, jg=jg):
                    self.wload(wb[b][0], wb[b][1], Wl, 16, C_B + jg * 512, 512, 0)

                def cf(b, jg=jg):
                    wu = wb[b][0][:].rearrange("p (k n) -> p k n", k=16)
                    wview_reg[0] = wb[b][1]
                    for cc in range(4):
                        ch = jg * 4 + cc
                        for (t0, n) in tilesR:
                            pu, ru = nextps()
                            ps_reg[0] = ru
                            mm16(pu, n, wu, cc * 128, t0)
                            q_, rq_ = qst[ku[0] % 2]
                            ku[0] += 1
                            P.op("act", lambda e, pu=pu, q_=q_, n=n: e.activation(out=q_[:, 0:n], in_=pu[:, 0:n], func=AF.Gelu), [ru], [rq_])
                            c0 = PAD + r0 - 128 + t0
                            P.dma("sp", lambda e, q_=q_, ch=ch, c0=c0, n=n: e.dma_start(out=self.HB[ch * 128:(ch + 1) * 128, c0:c0 + n], in_=q_[:, 0:n]), rq_, False)
                jobs.append((lf, cf))
            for jg in range(2):
                def lf(b, jg=jg):
                    self.wload(wb[b][0], wb[b][1], Wl, 16, C_Q + jg * 512, 512, 0)

                def cf(b, jg=jg):
                    wq = wb[b][0][:].rearrange("p (k n) -> p k n", k=16)
                    wview_reg[0] = wb[b][1]
                    k = 0
                    for cc in range(4):
                        ch = jg * 4 + cc
                        for (t0, n) in tilesR:
                            pq, rq = nextps()
                            ps_reg[0] = rq
                            mm16(pq, n, wq, cc * 128, t0)
                            q_, rq_ = qst[k % 2]
                            k += 1
                            P.op("act", lambda e, pq=pq, q_=q_, n=n: e.activation(out=q_[:, 0:n], in_=pq[:, 0:n], func=AF.Copy, scale=0.125), [rq], [rq_])
                            c0 = PAD + r0 - 128 + t0
                            P.dma("sp", lambda e, q_=q_, ch=ch, c0=c0, n=n: e.dma_start(out=self.Q[ch * 128:(ch + 1) * 128, c0:c0 + n], in_=q_[:, 0:n]), rq_, False, [self.R("Q")])
                jobs.append((lf, cf))
            def lfg(b):
                self.wload(wb[b][0], wb[b][1], Wl, 16, C_G, 48, 0)

            def cfg(b):
                wg = wb[b][0][:, 0:16 * 48].rearrange("p (k n) -> p k n", k=16)
                for i in range(T // 128):
                    pg, rg = nextps()
                    t0 = 128 + i * 128
                    for kc in range(16):
                        P.op("pe", lambda e, kc=kc, pg=pg, t0=t0: e.matmul(pg[:, 0:48], lhsT=hT[:, kc, t0:t0 + 128], rhs=wg[:, kc, :], start=(kc == 0), stop=(kc == 15)),
                             [wb[b][1]] + hregs, [rg])
                    P.op("act", lambda e, pg=pg, i=i: e.activation(out=gall[:, i, :], in_=pg[:, 0:48], func=AF.Sigmoid), [rg], [r_gall])
                dst = self.GATES[PAD + r0:PAD + r1, :].rearrange("(n p) c -> p n c", p=128)
                P.dma("sp", lambda e: e.dma_start(out=dst, in_=gall[:]), r_gall, False, [self.R("GATES")])
            jobs.append((lfg, cfg))
            self.run_jobs(jobs)
            P.barrier()
            HBv = self.HB.rearrange("(c p) t -> p c t", p=128)
            for i in range(T // 128):
                t0 = 128 + i * 128
                cu = PAD + r0 + i * 128
                P.dma("sp", lambda e, cu=cu: e.dma_start(out=ut[:], in_=HBv[:, :, cu:cu + 128]), r_ut, True)
                vg_, rvg = vg[i % 2]
                vl_, rvl = vln[i % 2]
                for half in range(2):
                    pv, rv = self.ps[4 + half], self.psr[4 + half]
                    for kc in range(16):
                        P.op("pe", lambda e, kc=kc, pv=pv, half=half, t0=t0: e.matmul(pv[:], lhsT=hT[:, kc, t0:t0 + 128], rhs=wv3[:, kc, half * 512:(half + 1) * 512],
                                                                                     start=(kc == 0), stop=(kc == 15)), [r_wvt] + hregs, [rv])
                    P.op("act", lambda e, pv=pv, half=half, vg_=vg_: e.activation(out=vg_[:, half * 512:(half + 1) * 512], in_=pv[:], func=AF.Gelu), [rv], [rvg])
                for half in range(2):
                    P.op("dve", lambda e, half=half, vg_=vg_: e.bn_stats(out=bst[:, half, :], in_=vg_[:, half * 512:(half + 1) * 512]), [rvg], [r_bst])
                P.op("dve", lambda e: e.bn_aggr(out=mv[:], in_=bst[:].rearrange("p a s -> p (a s)")), [r_bst], [r_mv])
                P.op("act", lambda e: e.activation(out=mv[:, 1:2], in_=mv[:, 1:2], func=AF.Sqrt, bias=self.epsT[:, 0:1], scale=1.0), [r_mv], [r_mv])
                P.op("dve", lambda e: e.reciprocal(out=mv[:, 1:2], in_=mv[:, 1:2]), [r_mv], [r_mv])
                P.op("dve", lambda e, vg_=vg_: e.tensor_scalar(out=vg_[:], in0=vg_[:], scalar1=mv[:, 0:1], scalar2=mv[:, 1:2], op0=ALU.subtract, op1=ALU.mult), [rvg, r_mv], [rvg])
                P.op("dve", lambda e, vg_=vg_: e.tensor_tensor(out=vg_[:], in0=vg_[:], in1=lnbt[:, 0, :], op=ALU.mult), [rvg, r_lnb], [rvg])
                P.op("dve", lambda e, vg_=vg_, vl_=vl_: e.tensor_tensor(out=vl_[:], in0=vg_[:], in1=lnbt[:, 1, :], op=ALU.add), [rvg, r_lnb], [rvl])
                for hb in range(2):
                    pf, rf = self.ps[6 + hb], self.psr[6 + hb]
                    for gg in range(4):
                        g = hb * 4 + gg
                        P.op("pe", lambda e, pf=pf, gg=gg, g=g, vl_=vl_: e.matmul(pf[:, gg * 128:(gg + 1) * 128], lhsT=vl_[:, g * 128:(g + 1) * 128], rhs=wsT[:, g, :], start=True, stop=False),
                             [rvl, r_ws], [rf])
                        P.op("pe", lambda e, pf=pf, gg=gg, g=g: e.matmul(pf[:, gg * 128:(gg + 1) * 128], lhsT=self.ones_b[0:1, :], rhs=sgbr[0:1, g * 128:(g + 1) * 128], start=False, stop=True),
                             [r_sgb], [rf])
                    P.op("dve", lambda e, pf=pf, hb=hb: e.tensor_tensor(out=ut[:, hb * 4:(hb + 1) * 4, :], in0=ut[:, hb * 4:(hb + 1) * 4, :],
                                                                      in1=pf[:].rearrange("p (g t) -> p g t", g=4), op=ALU.mult), [rf, r_ut], [r_ut])
                P.dma("sp", lambda e, cu=cu: e.dma_start(out=HBv[:, :, cu:cu + 128], in_=ut[:]), r_ut, False)
            P.end_phase()

    def phase_m2(self, l, r0, r1):
        P = self.P
        T = r1 - r0
        TA = T + 128
        with ExitStack() as st:
            gl = [self.tile(st, "gl", [128, TA], BF16) for _ in range(3)]
            dgs = [self.tile(st, "dgs", [128, 31, 128], BF16) for _ in range(2)]
            cbf, r_cbf = self.tile(st, "cbf", [128, 8, T], BF16)
            cw, r_cw = self.tile(st, "cw", [128, 8, 31], F32)
            av, r_av = self.tile(st, "av", [128, 3, 8], F32)
            sq = [self.tile(st, "sq", [128, 512], BF16) for _ in range(2)]
            mean, r_mean = self.tile(st, "mean", [128, 512], F32)
            msq, r_msq = self.tile(st, "msq", [128, 512], F32)
            rstd, r_rstd = self.tile(st, "rstd", [128, 512], F32)
            t1 = [self.tile(st, "t1", [128, 512], F32) for _ in range(2)]
            hst = [self.tile(st, "hst", [128, 8, 512], BF16) for _ in range(2)]
            P.dma("sp", lambda e: e.dma_start(out=cw[:], in_=self.convaw[:, l * 248:(l + 1) * 248].rearrange("p (c k) -> p c k", c=8)), r_cw, True)
            P.dma("sp", lambda e: e.dma_start(out=av[:], in_=self.avec[:, l * 24:(l + 1) * 24].rearrange("p (a c) -> p a c", a=3)), r_av, True)
            for c in range(8):
                g_, rg = gl[c % 3]
                d_, rd_ = dgs[c % 2]
                c0 = PAD + r0 - 128
                P.dma("sp", lambda e: e.dma_start(out=g_[:], in_=self.GLU[c * 128:(c + 1) * 128, c0:c0 + TA]), rg, True)
                for k in range(31):
                    P.op("dve", lambda e: e.tensor_scalar(out=d_[:, k, :], in0=self.ident_f[:], scalar1=cw[:, c, k:k + 1], scalar2=None, op0=ALU.mult), [r_cw], [rd_])
                for ti, (t0, n) in enumerate(self.tok_tiles(0, T)):
                    bk = 2 + (c * 8 + ti) % 4
                    ps, pr = self.ps[bk], self.psr[bk]
                    for k in range(31):
                        P.op("pe", lambda e: e.matmul(ps[:, 0:n], lhsT=d_[:, k, :], rhs=g_[:, 98 + k + t0:98 + k + t0 + n], start=(k == 0), stop=(k == 30)), [rd_, rg], [pr])
                    P.op("act", lambda e: e.activation(out=cbf[:, c, t0:t0 + n], in_=ps[:, 0:n], func=AF.Identity, bias=av[:, 0, c:c + 1], scale=1.0), [pr, r_av], [r_cbf])
            for ti, (t0, n) in enumerate(self.tok_tiles(0, T)):
                pS, rS = self.ps[0], self.psr[0]
                pQ, rQ = self.ps[1], self.psr[1]
                for c in range(8):
                    s_, rs_ = sq[c % 2]
                    P.op("act", lambda e, s_=s_, c=c, t0=t0, n=n: e.activation(out=s_[:, 0:n], in_=cbf[:, c, t0:t0 + n], func=AF.Square), [r_cbf], [rs_])
                    P.op("pe", lambda e, c=c, t0=t0, n=n: e.matmul(pS[:, 0:n], lhsT=self.ones_b[:], rhs=cbf[:, c, t0:t0 + n], start=(c == 0), stop=(c == 7)), [r_cbf], [rS])
                    P.op("pe", lambda e, s_=s_, c=c, n=n: e.matmul(pQ[:, 0:n], lhsT=self.ones_b[:], rhs=s_[:, 0:n], start=(c == 0), stop=(c == 7)), [rs_], [rQ])
                P.op("dve", lambda e, n=n: e.tensor_scalar(out=mean[:, 0:n], in0=pS[:, 0:n], scalar1=1.0 / 1024, scalar2=None, op0=ALU.mult), [rS], [r_mean])
                P.op("dve", lambda e, n=n: e.tensor_tensor(out=msq[:, 0:n], in0=mean[:, 0:n], in1=mean[:, 0:n], op=ALU.mult), [r_mean], [r_msq])
                P.op("dve", lambda e, n=n: e.scalar_tensor_tensor(out=rstd[:, 0:n], in0=pQ[:, 0:n], scalar=1.0 / 1024, in1=msq[:, 0:n], op0=ALU.mult, op1=ALU.subtract), [rQ, r_msq], [r_rstd])
                P.op("act", lambda e, n=n: e.activation(out=rstd[:, 0:n], in_=rstd[:, 0:n], func=AF.Sqrt, bias=self.epsT[:, 0:1], scale=1.0), [r_rstd], [r_rstd])
                P.op("dve", lambda e, n=n: e.reciprocal(out=rstd[:, 0:n], in_=rstd[:, 0:n]), [r_rstd], [r_rstd])
                h_, rh = hst[ti % 2]
                for c in range(8):
                    t_, rt = t1[c % 2]
                    P.op("dve", lambda e, t_=t_, c=c, t0=t0, n=n: e.tensor_tensor(out=t_[:, 0:n], in0=cbf[:, c, t0:t0 + n], in1=mean[:, 0:n], op=ALU.subtract), [r_cbf, r_mean], [rt])
                    P.op("dve", lambda e, t_=t_, n=n: e.tensor_tensor(out=t_[:, 0:n], in0=t_[:, 0:n], in1=rstd[:, 0:n], op=ALU.mult), [rt, r_rstd], [rt])
                    P.op("act", lambda e, t_=t_, h_=h_, c=c, n=n: e.activation(out=h_[:, c, 0:n], in_=t_[:, 0:n], func=AF.Silu, bias=av[:, 2, c:c + 1], scale=av[:, 1, c:c + 1]), [rt, r_av], [rh])
                dst = self.HA.rearrange("(c p) t -> p c t", p=128)[:, :, PAD + r0 + t0:PAD + r0 + t0 + n]
                P.dma("sp", lambda e, dst=dst, h_=h_, n=n: e.dma_start(out=dst, in_=h_[:, :, 0:n]), rh, False, [self.R("HA")])
            P.end_phase()

    def phase_att(self, l, r0, r1):
        P = self.P
        T = r1 - r0
        nqt = T // 128
        qt0 = r0 // 128
        BIG = 30000.0
        with ExitStack() as st:
            kin = [self.tile(st, "kin", [128, SEQ], BF16) for _ in range(4)]
            vs, r_vs = self.tile(st, "vs", [128, 32, 65], BF16)
            vw, r_vw = self.tile(st, "vw", [128, 32, 65], BF16)
            qT, r_q = self.tile(st, "qT", [128, 4, T], BF16)
            gt, r_gt = self.tile(st, "gt", [128, nqt, 48], F32)
            mtab, r_mt = self.tile(st, "mtab", [128, 3, nqt, 64], F32)
            Et, r_E = self.tile(st, "Et", [128, 32, 128], BF16)
            smap, r_sm = self.tile(st, "smap", [128, 2, 64], BF16)
            w1 = [self.tile(st, "w1", [128, 32, 256], BF16) for _ in range(2)]
            w2 = [self.tile(st, "w2", [128, 2, 64], BF16) for _ in range(2)]
            pe = [self.tile(st, "pe", [128, 32], BF16) for _ in range(2)]
            cb = [self.tile(st, "cb", [128, 2], F32) for _ in range(2)]
            hid, r_hid = self.tile(st, "hid", [128, 2, 256], BF16)
            kcT, r_kc = self.tile(st, "kcT", [128, 256], BF16)
            rv, r_rv = self.tile(st, "rv", [128, 2, 64], BF16)
            pb = [self.tile(st, "pb", [128, 4, 128], BF16) for _ in range(4)]
            cm, r_cm = self.tile(st, "cm", [128, 128], F32)
            cmb = [self.tile(st, "cmb", [128, 4, 128], BF16) for _ in range(4)]
            trib = [self.tile(st, "trib", [128, 4, 128], BF16) for _ in range(2)]
            selb = [self.tile(st, "selb", [128, 4, 128], BF16) for _ in range(2)]
            negb, r_negb = self.tile(st, "negb", [128, 1], F32)
            sm = {}
            for nm, shp in (("den", [128, 4]), ("cg", [128, 4]), ("imp", [128, 64]), ("score", [128, 64]), ("wk", [128, 64]), ("selw", [128, 128]), ("m8", [128, 8]),
                            ("oacc", [128, 4, 64]), ("otmp", [128, 4, 64]), ("den2", [128, 4]), ("cg2", [128, 4])):
                sm[nm] = self.tile(st, nm, shp, F32)
            obf = [self.tile(st, "obf", [128, 256], BF16) for _ in range(2)]
            ost = [self.tile(st, "ost", [128, 2, 128], BF16) for _ in range(2)]
            P.dma("sp", lambda e: e.dma_start(out=gt[:], in_=self.GATES[PAD + r0:PAD + r1, :].rearrange("(n p) c -> p n c", p=128)), r_gt, True)
            for a_ in range(3):
                P.dma("sp", lambda e: e.dma_start(out=mtab[:, a_, :, :], in_=self.m_sel[a_].rearrange("p (q j) -> p q j", j=64)[:, qt0:qt0 + nqt, :]), r_mt, True)
            P.op("dve", lambda e: e.memset(sm["selw"][0][:], 0.0), [], [sm["selw"][1]])
            P.op("dve", lambda e: e.memset(qT[64:128], 0.0), [], [r_q])
            for ty in (0, 1, 3):
                P.op("dve", lambda e: e.memset(kin[ty][0][64:128], 0.0), [], [kin[ty][1]])
            P.dma("pool", lambda e: e.dma_start(out=kin[2][0][64:128], in_=self.c_E), kin[2][1], True)
            for kv in range(2):
                P.op("dve", lambda e: e.memset(w1[kv][0][64:128], 0.0), [], [w1[kv][1]])
                P.op("dve", lambda e: e.memset(pe[kv][0][64:128], 0.0), [], [pe[kv][1]])


            P.dma("pool", lambda e: e.dma_start(out=smap[:], in_=self.c_smap.rearrange("p (a j) -> p a j", a=2)), r_sm, True)
            for kv in range(2):
                i_ = l * 2 + kv
                P.dma("pool", lambda e: e.dma_start(out=w1[kv][0][0:64], in_=self.cw1[i_].rearrange("d (l h) -> d l h", l=32)), w1[kv][1], True)
                P.dma("pool", lambda e: e.dma_start(out=w2[kv][0][:], in_=self.cw2[i_].rearrange("(c p) d -> p c d", p=128)), w2[kv][1], True)
                P.dma("pool", lambda e: e.dma_start(out=pe[kv][0][0:64], in_=self.cpe[i_]), pe[kv][1], True)
            P.op("dve", lambda e: e.memset(vs[:, :, 64:65], 1.0), [], [r_vs])
            P.op("dve", lambda e: e.memset(vw[:, :, 64:65], 1.0), [], [r_vw])
            P.op("dve", lambda e: e.memset(hid[:], 0.0), [], [r_hid])
            P.op("dve", lambda e: e.memset(kcT[:], 0.0), [], [r_kc])
            P.op("dve", lambda e: e.memset(negb[:], -BIG), [], [r_negb])
            for ti_, tri in enumerate((self.trile, self.trigt)):
                P.op("dve", lambda e: e.tensor_scalar(out=trib[ti_][0][:], in0=tri[:].unsqueeze(1).to_broadcast([128, 4, 128]), scalar1=-1.0, scalar2=BIG, op0=ALU.add, op1=ALU.mult), [], [trib[ti_][1]])
            for kv in range(2):
                ps, pr = self.ps[0], self.psr[0]
                for hc in range(2):
                    for li in range(32):
                        P.op("pe", lambda e: e.matmul(ps[:, hc:hc + 1], lhsT=w1[kv][0][:, li, hc * 128:(hc + 1) * 128], rhs=pe[kv][0][:, li:li + 1],
                                                      start=(li == 0), stop=(li == 31)), [w1[kv][1], pe[kv][1]], [pr])
                P.op("act", lambda e: e.activation(out=cb[kv][0][:], in_=ps[:, 0:2], func=AF.Copy), [pr], [cb[kv][1]])
            PC, PS_, PW, T32, TBF = 3, 4, 5, 6, 7
            psbf = self.ps[TBF][:].bitcast(BF16)
            den, r_den = sm["den"]; cg, r_cg = sm["cg"]; imp, r_imp = sm["imp"]; score, r_sc = sm["score"]
            wk, r_wk = sm["wk"]; selw, r_sel = sm["selw"]; sel = selw[:, 64:128]; m8, r_m8 = sm["m8"]; oacc, r_oa = sm["oacc"]; otmp, r_ot = sm["otmp"]
            den2, r_den2 = sm["den2"]; cg2, r_cg2 = sm["cg2"]
            cnt = {"s": 0, "p": 0, "cmb": 0, "q": 0}
            for g in range(4):
                for ty in range(4):
                    P.dma("sp", lambda e: e.dma_start(out=kin[ty][0][0:64], in_=self.KT[ty][g * 64:(g + 1) * 64, PAD:PAD + SEQ]), kin[ty][1], True)
                P.dma("sp", lambda e: e.dma_start(out=vs[:, :, 0:64], in_=self.VT[0][PAD:PAD + SEQ, g * 64:(g + 1) * 64].rearrange("(n p) d -> p n d", p=128)), r_vs, True)
                P.dma("sp", lambda e: e.dma_start(out=vw[:, :, 0:64], in_=self.VT[1][PAD:PAD + SEQ, g * 64:(g + 1) * 64].rearrange("(n p) d -> p n d", p=128)), r_vw, True)
                P.op("dve", lambda e: e.tensor_tensor(out=vw[:], in0=vw[:], in1=self.kval[:, 0:32].unsqueeze(2).to_broadcast([128, 32, 65]), op=ALU.mult), [r_vw], [r_vw])
                P.dma("sp", lambda e: e.dma_start(out=qT[0:64], in_=self.Q[g * 256:(g + 1) * 256, PAD + r0:PAD + r1].rearrange("(h d) t -> d h t", d=64)), r_q, True)
                for kv in range(2):
                    src, rsrc = kin[kv]
                    for hc in range(2):
                        ps, pr = self.ps[hc], self.psr[hc]
                        for li in range(32):
                            P.op("pe", lambda e: e.matmul(ps[:, 0:255], lhsT=w1[kv][0][:, li, hc * 128:(hc + 1) * 128], rhs=src[:, li:li + 16 * 254 + 1:16],
                                                          start=(li == 0), stop=(li == 31)), [w1[kv][1], rsrc], [pr])
                        P.op("act", lambda e: e.activation(out=hid[:, hc, 0:255], in_=ps[:, 0:255], func=AF.Silu, bias=cb[kv][0][:, hc:hc + 1], scale=1.0), [pr, cb[kv][1]], [r_hid])
                    ps, pr = self.ps[2], self.psr[2]
                    if kv == 0:
                        for hc in range(2):
                            P.op("pe", lambda e: e.matmul(ps[0:64, 0:255], lhsT=w2[0][0][:, hc, :], rhs=hid[:, hc, 0:255], start=(hc == 0), stop=(hc == 1)), [w2[0][1], r_hid], [pr])
                        P.op("act", lambda e: e.activation(out=kcT[0:64, 0:255], in_=ps[0:64, 0:255], func=AF.Copy), [pr], [r_kc])
                    else:
                        for nt_ in range(2):
                            for hc in range(2):
                                P.op("pe", lambda e: e.matmul(ps[:, nt_ * 64:(nt_ + 1) * 64], lhsT=hid[:, hc, nt_ * 128:(nt_ + 1) * 128], rhs=w2[1][0][:, hc, :],
                                                              start=(hc == 0), stop=(hc == 1)), [w2[1][1], r_hid], [pr])
                        P.op("act", lambda e: e.activation(out=rv[:], in_=ps[:, 0:128].rearrange("p (a d) -> p a d", a=2), func=AF.Copy), [pr], [r_rv])
                pending = [None]
                for i in range(nqt):
                    qt = qt0 + i
                    qv = qT[:, :, i * 128:(i + 1) * 128]
                    gsl = gt[:, i, g * 12:(g + 1) * 12].rearrange("p (h b) -> p h b", b=3)
                    pc, rpc = self.ps[PC], self.psr[PC]
                    pso, rpso = self.ps[PS_], self.psr[PS_]
                    pwo, rpwo = self.ps[PW], self.psr[PW]
                    psov = pso[:, 0:260].rearrange("p (h d) -> p h d", h=4)
                    pwov = pwo[:, 0:260].rearrange("p (h d) -> p h d", h=4)
                    sb_, rsb_ = selb[cnt["q"] % 2]
                    ob_, rob_ = obf[cnt["q"] % 2]
                    os_, ros_ = ost[cnt["q"] % 2]
                    cnt["q"] += 1
                    nts = [0] if qt < 16 else [0, 1]
                    steps = []
                    for nt_ in nts:
                        c4, rc4 = cmb[cnt["cmb"] % 4]
                        cnt["cmb"] += 1
                        thr = float(128 * qt - 2048 * nt_ - 31)
                        P.op("dve", lambda e: e.tensor_scalar(out=cm[:], in0=self.dtab[:], scalar1=thr, scalar2=self.nval[:, nt_:nt_ + 1], op0=ALU.is_le, op1=ALU.mult), [], [r_cm])
                        P.op("dve", lambda e: e.tensor_scalar(out=c4[:], in0=cm[:].unsqueeze(1).to_broadcast([128, 4, 128]), scalar1=-1.0, scalar2=BIG, op0=ALU.add, op1=ALU.mult), [r_cm], [rc4])

                        def pv_c(p_, rp_, nt_=nt_):
                            first = (nt_ == nts[0])
                            for h in range(4):
                                P.op("pe", lambda e: e.matmul(pc[:, h * 64:(h + 1) * 64], lhsT=p_[:, h, :], rhs=rv[:, nt_, :], start=(first and h == 0), stop=True, skip_group_check=True), [rp_, r_rv], [rpc])
                                P.op("pe", lambda e: e.matmul(pc[:, 256 + h * 64:256 + (h + 1) * 64], lhsT=p_[:, h, :], rhs=smap[:, nt_, :], start=False, stop=True, skip_group_check=True), [rp_, r_sm], [rpc])
                        steps.append(("c", kcT[:, nt_ * 128:(nt_ + 1) * 128], r_kc, [(self.ident_b[:], c4[:], [rc4])], pv_c))
                    k0 = max(0, qt - 4)
                    for kt in range(k0, qt + 1):
                        biases = []
                        if kt == qt:
                            biases.append((self.ident_b[:], trib[0][0][:], [trib[0][1]]))
                        elif kt == qt - 4:
                            biases.append((self.ident_b[:], trib[1][0][:], [trib[1][1]]))

                        def pv_w(p_, rp_, kt=kt):
                            for h in range(4):
                                P.op("pe", lambda e: e.matmul(pwov[:, h, :], lhsT=p_[:, h, :], rhs=vw[:, kt, :], start=(kt == k0 and h == 0), stop=True, skip_group_check=True), [rp_, r_vw], [rpwo])
                        steps.append(("w", kin[3][0][:, kt * 128:(kt + 1) * 128], kin[3][1], biases, pv_w))
                    n_pre = len(steps)
                    for kt in range(0, qt + 1):
                        biases = []
                        if kt == qt:
                            biases.append((self.ident_b[:], trib[0][0][:], [trib[0][1]]))

                        def pv_s(p_, rp_, kt=kt):
                            for h in range(4):
                                P.op("pe", lambda e: e.matmul(psov[:, h, :], lhsT=p_[:, h, :], rhs=vs[:, kt, :], start=(kt == 0 and h == 0), stop=True, skip_group_check=True), [rp_, r_vs], [rpso])
                        steps.append(("s", kin[2][0][:, kt * 128:(kt + 1) * 128], kin[2][1], biases, pv_s, sb_[:], rsb_))
                    N = len(steps)
                    sbank = {}

                    def emit_score(k):
                        kind, lhsT, lreg, biases = steps[k][0:4]
                        rhs_, rreg_ = (steps[k][5], steps[k][6]) if len(steps[k]) > 5 else (qv, r_q)
                        bk = cnt["s"] % 3
                        cnt["s"] += 1
                        ps, pr = self.ps[bk], self.psr[bk]
                        sbank[k] = (ps, pr)
                        P.op("pe", lambda e: e.matmul(ps[:], lhsT=lhsT, rhs=rhs_, start=True, stop=(len(biases) == 0)), [lreg, rreg_], [pr])
                        for bi, (bl, br, bregs) in enumerate(biases):
                            P.op("pe", lambda e: e.matmul(ps[:], lhsT=bl, rhs=br, start=False, stop=(bi == len(biases) - 1)), bregs, [pr])

                    def post_cmp_dve():
                        pc2 = pc[:, 256:512].rearrange("p (h j) -> p h j", h=4)
                        P.op("dve", lambda e: e.tensor_reduce(out=den[:], in_=pc2, axis=AX.X, op=ALU.add), [rpc], [r_den])
                        P.op("dve", lambda e: e.tensor_scalar(out=den[:], in0=den[:], scalar1=0.5, scalar2=1e-30, op0=ALU.mult, op1=ALU.max), [r_den], [r_den])
                        P.op("dve", lambda e: e.reciprocal(out=den[:], in_=den[:]), [r_den], [r_den])
                        P.op("dve", lambda e: e.tensor_scalar(out=imp[:], in0=pc2[:, 0, :], scalar1=den[:, 0:1], scalar2=None, op0=ALU.mult), [rpc, r_den], [r_imp])
                        for h in range(1, 4):
                            P.op("dve", lambda e: e.scalar_tensor_tensor(out=imp[:], in0=pc2[:, h, :], scalar=den[:, h:h + 1], in1=imp[:], op0=ALU.mult, op1=ALU.add), [rpc, r_den, r_imp], [r_imp])
                        P.op("dve", lambda e: e.tensor_tensor(out=score[:], in0=imp[:], in1=mtab[:, 0, i, :], op=ALU.mult), [r_imp, r_mt], [r_sc])
                        P.op("dve", lambda e: e.tensor_tensor(out=score[:], in0=score[:], in1=mtab[:, 1, i, :], op=ALU.add), [r_sc, r_mt], [r_sc])
                        P.op("dve", lambda e: e.max(out=m8[:], in_=score[:]), [r_sc], [r_m8])
                        P.op("dve", lambda e: e.match_replace(out=wk[:], in_to_replace=m8[:], in_values=score[:], imm_value=-2.0), [r_m8, r_sc], [r_wk])
                        P.op("dve", lambda e: e.max(out=m8[:], in_=wk[:]), [r_wk], [r_m8])
                        P.op("dve", lambda e: e.match_replace(out=wk[:], in_to_replace=m8[:], in_values=wk[:], imm_value=-2.0), [r_m8, r_wk], [r_wk])
                        P.op("dve", lambda e: e.tensor_tensor(out=sel, in0=score[:], in1=wk[:], op=ALU.subtract), [r_sc, r_wk], [r_sel])
                        P.op("dve", lambda e: e.scalar_tensor_tensor(out=sel, in0=sel, scalar=1.0, in1=mtab[:, 2, i, :], op0=ALU.min, op1=ALU.mult), [r_sel, r_mt], [r_sel])
                        P.op("dve", lambda e: e.tensor_tensor(out=cg[:], in0=den[:], in1=gsl[:, :, 0], op=ALU.mult), [r_den, r_gt], [r_cg])
                        P.op("dve", lambda e: e.tensor_tensor(out=oacc[:], in0=pc[:, 0:256].rearrange("p (h d) -> p h d", h=4), in1=cg[:].unsqueeze(2).to_broadcast([128, 4, 64]), op=ALU.mult), [rpc, r_cg], [r_oa])

                    def pre_slc():
                        pt, rpt = self.ps[T32], self.psr[T32]
                        P.op("pe", lambda e: e.transpose(out=pt[:, 0:128], in_=selw[:], identity=self.ident_f[:]), [r_sel], [rpt])
                        P.op("act", lambda e: e.activation(out=sb_[64:128], in_=pt[64:128, 0:128].unsqueeze(1).to_broadcast([64, 4, 128]), func=AF.Identity, bias=negb[64:128, 0:1], scale=BIG), [rpt, r_negb], [rsb_])
                        P.op("dve", lambda e: e.tensor_copy(out=sb_[0:64], in_=qT[0:64, :, i * 128:(i + 1) * 128]), [r_q], [rsb_])

                    LA = getattr(self, "lookahead", 1)
                    if LA:
                        emit_score(0)
                    for k in range(N):
                        if not LA:
                            if k == n_pre:
                                pre_slc()
                            emit_score(k)
                        elif k + 1 < N:
                            if k + 1 == n_pre:
                                pre_slc()
                            emit_score(k + 1)
                        ps, pr = sbank.pop(k)
                        p_, rp_ = pb[cnt["p"] % 4]
                        cnt["p"] += 1
                        P.op("act", lambda e: e.activation(out=p_[:], in_=ps[:].rearrange("p (h q) -> p h q", h=4), func=AF.Exp), [pr], [rp_])
                        steps[k][4](p_, rp_)
                        if k == len(nts) - 1:
                            post_cmp_dve()
                            if pending[0] is not None:
                                pending[0]()
                                pending[0] = None
                    P.op("dve", lambda e: e.tensor_scalar(out=den2[:], in0=psov[:, :, 64], scalar1=1e-30, scalar2=None, op0=ALU.max), [rpso], [r_den2])
                    P.op("dve", lambda e: e.reciprocal(out=den2[:], in_=den2[:]), [r_den2], [r_den2])
                    P.op("dve", lambda e: e.tensor_tensor(out=cg2[:], in0=den2[:], in1=gsl[:, :, 1], op=ALU.mult), [r_den2, r_gt], [r_cg2])
                    P.op("dve", lambda e: e.tensor_tensor(out=otmp[:], in0=psov[:, :, 0:64], in1=cg2[:].unsqueeze(2).to_broadcast([128, 4, 64]), op=ALU.mult), [rpso, r_cg2], [r_ot])
                    P.op("dve", lambda e: e.tensor_tensor(out=oacc[:], in0=oacc[:], in1=otmp[:], op=ALU.add), [r_oa, r_ot], [r_oa])
                    P.op("dve", lambda e: e.tensor_scalar(out=den2[:], in0=pwov[:, :, 64], scalar1=1e-30, scalar2=None, op0=ALU.max), [rpwo], [r_den2])
                    P.op("dve", lambda e: e.reciprocal(out=den2[:], in_=den2[:]), [r_den2], [r_den2])
                    P.op("dve", lambda e: e.tensor_tensor(out=cg2[:], in0=den2[:], in1=gsl[:, :, 2], op=ALU.mult), [r_den2, r_gt], [r_cg2])
                    P.op("dve", lambda e: e.tensor_tensor(out=otmp[:], in0=pwov[:, :, 0:64], in1=cg2[:].unsqueeze(2).to_broadcast([128, 4, 64]), op=ALU.mult), [rpwo, r_cg2], [r_ot])
                    P.op("dve", lambda e: e.tensor_tensor(out=ob_[:].rearrange("p (h d) -> p h d", h=4), in0=oacc[:], in1=otmp[:], op=ALU.add), [r_oa, r_ot], [rob_])

                    def post_pe(i=i, ob_=ob_, rob_=rob_, os_=os_, ros_=ros_):
                        rptb = self.psr[TBF]
                        for half in range(2):
                            P.op("pe", lambda e: e.transpose(out=psbf[:, half * 128:(half + 1) * 128], in_=ob_[:, half * 128:(half + 1) * 128], identity=self.ident_b[:]), [rob_], [rptb])
                        P.op("act", lambda e: e.activation(out=os_[:], in_=psbf[:, 0:256].rearrange("p (a q) -> p a q", a=2), func=AF.Copy), [rptb], [ros_])
                        c0 = PAD + r0 + i * 128
                        dst = self.OC[g * 256:(g + 1) * 256, c0:c0 + 128].rearrange("(a p) t -> p a t", p=128)
                        P.dma("sp", lambda e: e.dma_start(out=dst, in_=os_[:]), ros_, False)
                    pending[0] = post_pe
                if pending[0] is not None:
                    pending[0]()
                    pending[0] = None
            P.end_phase()

    def phase_merge(self, l, X, Xout, r0, r1):
        P = self.P
        Wl = self.w_in[l]
        with ExitStack() as st:
            nt = self.norm_tiles(st)
            pt = self.post_tiles(st)
            hT, r_h = self.tile(st, "hT", [128, 16, 512], BF16)
            ins = [self.tile(st, "hin", [128, 8, 512], BF16) for _ in range(3)]
            wb = [self.tile(st, "wbuf", [128, 8192], BF16) for _ in range(3)]
            mT, r_m = self.tile(st, "mT", [128, 16, 512], BF16)
            mixed, r_mx = self.tile(st, "mixed", [128, 16, 512], F32)
            sgs = [self.tile(st, "sgs", [128, 4, 512], BF16) for _ in range(2)]
            accm, r_accm = self.tile(st, "accm", [128, 4, 512], F32)
            tmp = [self.tile(st, "tmp", [128, 512], F32) for _ in range(2)]
            sq = [self.tile(st, "sq", [128, 512], BF16) for _ in range(2)]
            srcs = [self.HA, self.HB, self.OC]
            wouts = [self.w_a_out[l], self.w_b_out[l], self.w_c_out[l]]
            hregs = [self.newreg("hT"), self.newreg("hT")]
            sched = []
            pk = [0]

            def nb():
                bk = pk[0] % 6
                pk[0] += 1
                return self.ps[bk], self.psr[bk]
            for (s0, n) in self.tok_tiles(r0, r1 - r0):
                sc_ = {}
                sched.append(sc_)

                def nfn(b, s0=s0, n=n):
                    self.fill_hT(nt, X, PAD + s0, n, (l * 4 + 0) * 16, hT, hregs)
                    for b3 in range(3):
                        P.dma("sp", lambda e: e.dma_start(out=ins[b3][0][:, :, 0:n], in_=srcs[b3].rearrange("(c p) t -> p c t", p=128)[:, :, PAD + s0:PAD + s0 + n]),
                              ins[b3][1], True)
                sc_["N"] = [(None, nfn)]
                jobs = []
                for dg in range(4):
                    for b3 in range(3):
                        def lfg(b, dg=dg, b3=b3):
                            self.wload(wb[b][0], wb[b][1], Wl, 16, C_M + b3 * 2048 + dg * 512, 512, 0)

                        def cfg(b, dg=dg, b3=b3, n=n, hregs=hregs):
                            wbt, rwb = wb[b]
                            wg = wbt[:, 0:8192].rearrange("p (k n) -> p k n", k=16)
                            s_, rs_ = sgs[b3 % 2]
                            for cc in range(4):
                                pg, rg = nb()
                                for kc in range(16):
                                    P.op("pe", lambda e: e.matmul(pg[:, 0:n], lhsT=wg[:, kc, cc * 128:(cc + 1) * 128], rhs=hT[:, kc, 0:n], start=(kc == 0), stop=(kc == 15)), [rwb] + hregs, [rg])
                                P.op("act", lambda e: e.activation(out=s_[:, cc, 0:n], in_=pg[:, 0:n], func=AF.Sigmoid), [rg], [rs_])
                        jobs.append((lfg, cfg))

                        def lfy(b, dg=dg, b3=b3):
                            self.wload(wb[b][0], wb[b][1], wouts[b3], 8, dg * 512, 512, 0)

                        def cfy(b, dg=dg, b3=b3, n=n):
                            wbt, rwb = wb[b]
                            wy = wbt[:, 0:4096].rearrange("p (k n) -> p k n", k=8)
                            s_, rs_ = sgs[b3 % 2]
                            for cc in range(4):
                                py, ry = nb()
                                for kc in range(8):
                                    P.op("pe", lambda e: e.matmul(py[:, 0:n], lhsT=wy[:, kc, cc * 128:(cc + 1) * 128], rhs=ins[b3][0][:, kc, 0:n], start=(kc == 0), stop=(kc == 7)), [rwb, ins[b3][1]], [ry])
                                if b3 == 0:
                                    P.op("dve", lambda e: e.tensor_tensor(out=accm[:, cc, 0:n], in0=py[:, 0:n], in1=s_[:, cc, 0:n], op=ALU.mult), [ry, rs_], [r_accm])
                                else:
                                    t_, rt_ = tmp[cc % 2]
                                    P.op("dve", lambda e: e.tensor_tensor(out=t_[:, 0:n], in0=py[:, 0:n], in1=s_[:, cc, 0:n], op=ALU.mult), [ry, rs_], [rt_])
                                    if b3 == 1:
                                        P.op("dve", lambda e: e.tensor_tensor(out=accm[:, cc, 0:n], in0=accm[:, cc, 0:n], in1=t_[:, 0:n], op=ALU.add), [r_accm, rt_], [r_accm])
                                    else:
                                        P.op("dve", lambda e: e.tensor_tensor(out=mT[:, dg * 4 + cc, 0:n], in0=accm[:, cc, 0:n], in1=t_[:, 0:n], op=ALU.add), [r_accm, rt_], [r_m])
                        jobs.append((lfy, cfy))
                sc_["A"] = jobs
                jobs = []
                for jg in range(4):
                    def lf(b, jg=jg):
                        self.wload(wb[b][0], wb[b][1], self.w_o[l], 16, jg * 512, 512, 0)

                    def cf(b, jg=jg, n=n):
                        wo = wb[b][0][:, 0:8192].rearrange("p (k n) -> p k n", k=16)
                        for cc in range(4):
                            dch = jg * 4 + cc
                            po, ro = self.ps[dch % 4], self.psr[dch % 4]
                            for kc in range(16):
                                P.op("pe", lambda e, kc=kc, po=po, cc=cc: e.matmul(po[:, 0:n], lhsT=wo[:, kc, cc * 128:(cc + 1) * 128], rhs=mT[:, kc, 0:n], start=(kc == 0), stop=(kc == 15)), [wb[b][1], r_m], [ro])
                            s_, rs_ = sq[dch % 2]
                            P.op("act", lambda e, po=po, dch=dch: e.activation(out=mixed[:, dch, 0:n], in_=po[:, 0:n], func=AF.Copy), [ro], [r_mx])
                            P.op("act", lambda e, po=po, s_=s_: e.activation(out=s_[:, 0:n], in_=po[:, 0:n], func=AF.Square), [ro], [rs_])
                            P.op("pe", lambda e, s_=s_, dch=dch: e.matmul(self.ps[6][:, 0:n], lhsT=self.ones_b[:], rhs=s_[:, 0:n], start=(dch == 0), stop=(dch == 15)), [rs_], [self.psr[6]])
                    jobs.append((lf, cf))
                sc_["B"] = jobs
                sc_["P"] = [(None, lambda b, s0=s0, n=n: self.post_norm(st, mixed, r_mx, 6, n, (l * 4 + 1) * 16, X, Xout, PAD + s0, PAD + s0, pt))]
            nt_ = len(sched)
            seq = sched[0]["N"] + sched[0]["A"]
            for t in range(nt_):
                if t + 1 < nt_:
                    seq += sched[t + 1]["N"]
                seq += sched[t]["B"]
                if t + 1 < nt_:
                    seq += sched[t + 1]["A"][:4] + sched[t]["P"] + sched[t + 1]["A"][4:]
                else:
                    seq += sched[t]["P"]
            self.run_jobs(seq, nbuf=3)
            P.end_phase()

    def phase_ffn(self, l, X, Xout, f0, f1, out_col0):
        P = self.P
        Wu = self.w_up[l]
        Wd = self.w_down[l]
        with ExitStack() as st:
            nt = self.norm_tiles(st)
            pt = self.post_tiles(st)
            hT, _ = self.tile(st, "hT", [128, 16, 512], BF16)
            wb = [self.tile(st, "wbuf", [128, 8192], BF16) for _ in range(3)]
            act, r_act = self.tile(st, "act", [128, 44, 512], BF16)
            mixed, r_mx = self.tile(st, "mixed", [128, 16, 512], F32)
            pre = [self.tile(st, "pre", [128, 514], F32) for _ in range(2)]
            uu = [self.tile(st, "uu", [128, 512], F32) for _ in range(3)]
            sgf, r_sgf = self.tile(st, "sgf", [128, 4, 512], F32)
            sq = [self.tile(st, "sq", [128, 512], BF16) for _ in range(2)]
            carry, r_carry = self.tile(st, "carry", [128, 88, 2], F32)
            fw, r_fw = self.tile(st, "fw", [128, 88, 3], F32)
            fb, r_fb = self.tile(st, "fb", [128, 88], F32)
            P.dma("sp", lambda e: e.dma_start(out=fw[:], in_=self.ffw[:, l * 264:(l + 1) * 264].rearrange("p (c k) -> p c k", k=3)), r_fw, True)
            P.dma("sp", lambda e: e.dma_start(out=fb[:], in_=self.ffb[:, l * 88:(l + 1) * 88]), r_fb, True)
            tiles = [(f0 - 2, 2)] + self.tok_tiles(f0, f1 - f0)
            pk = [0]
            hregs = [self.newreg("hT"), self.newreg("hT")]
            sched = {}
            for tix, (s0, n) in enumerate(tiles):
                halo = (tix == 0)
                sched[tix] = {}
                sched[tix]["N"] = [(None, lambda b, s0=s0, n=n: self.fill_hT(nt, X, PAD + s0, n, (l * 4 + 2) * 16, hT, hregs))]
                jobs = []
                for grp in range(11):
                    for gv in range(2):
                        def lf(b, grp=grp, gv=gv):
                            self.wload(wb[b][0], wb[b][1], Wu, 16, gv * DFF + grp * 512, 512, 0)

                        def cf(b, grp=grp, gv=gv, n=n, halo=halo, hregs=hregs):
                            wbt, rwb = wb[b]
                            wv_ = wbt[:, 0:8192].rearrange("p (k n) -> p k n", k=16)
                            for cc in range(4):
                                jg = grp * 4 + cc
                                j = jg + 44 * gv
                                bk = pk[0] % 6
                                pk[0] += 1
                                ps, pr = self.ps[bk], self.psr[bk]
                                for kc in range(16):
                                    P.op("pe", lambda e: e.matmul(ps[:, 0:n], lhsT=wv_[:, kc, cc * 128:(cc + 1) * 128], rhs=hT[:, kc, 0:n], start=(kc == 0), stop=(kc == 15)),
                                         [rwb] + hregs, [pr])
                                if halo:
                                    P.op("act", lambda e: e.activation(out=carry[:, j, :], in_=ps[:, 0:2], func=AF.Copy), [pr], [r_carry])
                                    continue
                                p_, rp_ = pre[cc % 2]
                                u_, ru_ = uu[cc % 3]
                                P.op("act", lambda e: e.activation(out=p_[:, 2:2 + n], in_=ps[:, 0:n], func=AF.Copy), [pr], [rp_])
                                P.op("act", lambda e: e.activation(out=p_[:, 0:2], in_=carry[:, j, :], func=AF.Copy), [r_carry], [rp_])
                                P.op("act", lambda e: e.activation(out=carry[:, j, :], in_=p_[:, n:n + 2], func=AF.Copy), [rp_], [r_carry])
                                P.op("dve", lambda e: e.tensor_scalar(out=u_[:, 0:n], in0=p_[:, 2:2 + n], scalar1=fw[:, j, 2:3], scalar2=fb[:, j:j + 1], op0=ALU.mult, op1=ALU.add), [rp_, r_fw, r_fb], [ru_])
                                P.op("dve", lambda e: e.scalar_tensor_tensor(out=u_[:, 0:n], in0=p_[:, 1:1 + n], scalar=fw[:, j, 1:2], in1=u_[:, 0:n], op0=ALU.mult, op1=ALU.add), [rp_, ru_], [ru_])
                                P.op("dve", lambda e: e.scalar_tensor_tensor(out=u_[:, 0:n], in0=p_[:, 0:n], scalar=fw[:, j, 0:1], in1=u_[:, 0:n], op0=ALU.mult, op1=ALU.add), [rp_, ru_], [ru_])
                                if gv == 0:
                                    P.op("act", lambda e: e.activation(out=sgf[:, cc, 0:n], in_=u_[:, 0:n], func=AF.Silu), [ru_], [r_sgf])
                                else:
                                    P.op("dve", lambda e: e.tensor_tensor(out=act[:, jg, 0:n], in0=sgf[:, cc, 0:n], in1=u_[:, 0:n], op=ALU.mult), [r_sgf, ru_], [r_act])
                        jobs.append((lf, cf))
                sched[tix]["U"] = jobs
                jobs = []
                if not halo:
                    kranges = [(0, 16), (16, 16), (32, 12)]
                    for dg in range(4):
                        for kr, (k0_, nk) in enumerate(kranges):
                            def lf(b, dg=dg, k0_=k0_, nk=nk):
                                self.wload(wb[b][0], wb[b][1], Wd, nk, dg * 512, 512, 0, k0=k0_)

                            def cf(b, dg=dg, kr=kr, k0_=k0_, nk=nk, n=n):
                                wd = wb[b][0][:, 0:nk * 512].rearrange("p (k n) -> p k n", k=nk)
                                for cc in range(4):
                                    dch = dg * 4 + cc
                                    po, ro = self.ps[cc], self.psr[cc]
                                    for kc in range(nk):
                                        P.op("pe", lambda e: e.matmul(po[:, 0:n], lhsT=wd[:, kc, cc * 128:(cc + 1) * 128], rhs=act[:, k0_ + kc, 0:n], start=(kr == 0 and kc == 0), stop=(kr == 2 and kc == nk - 1)),
                                             [wb[b][1], r_act], [ro])
                                    if kr == 2:
                                        s_, rs_ = sq[dch % 2]
                                        P.op("act", lambda e: e.activation(out=mixed[:, dch, 0:n], in_=po[:, 0:n], func=AF.Copy), [ro], [r_mx])
                                        P.op("act", lambda e: e.activation(out=s_[:, 0:n], in_=po[:, 0:n], func=AF.Square), [ro], [rs_])
                                        P.op("pe", lambda e: e.matmul(self.ps[6][:, 0:n], lhsT=self.ones_b[:], rhs=s_[:, 0:n], start=(dch == 0), stop=(dch == 15)), [rs_], [self.psr[6]])
                            jobs.append((lf, cf))
                sched[tix]["D"] = jobs
                sched[tix]["P"] = [(None, lambda b, s0=s0, n=n: self.post_norm(st, mixed, r_mx, 6, n, (l * 4 + 3) * 16, X, Xout, PAD + s0, out_col0 + (s0 - f0), pt))]
            nt_ = len(tiles)
            seq = sched[0]["N"] + sched[0]["U"] + sched[1]["N"] + sched[1]["U"]
            for t in range(1, nt_):
                if t + 1 < nt_:
                    seq += sched[t + 1]["N"]
                seq += sched[t]["D"]
                if t + 1 < nt_:
                    seq += sched[t + 1]["U"][:4] + sched[t]["P"] + sched[t + 1]["U"][4:]
                else:
                    seq += sched[t]["P"]
            self.run_jobs(seq, nbuf=3)
            P.end_phase()

    def build(self):
        ph = []
        ph.append(lambda: self.phase_kv(0, self.xT))
        for (r0, r1) in ((0, 2048), (2048, 4096)):
            ph.append(lambda r0=r0, r1=r1: self.phase_m1(0, self.xT, r0, r1))
            ph.append(lambda r0=r0, r1=r1: self.phase_m2(0, r0, r1))
            ph.append(lambda r0=r0, r1=r1: self.phase_att(0, r0, r1))
            ph.append(lambda r0=r0, r1=r1: self.phase_merge(0, self.xT, self.XM, r0, r1))
        ph.append(lambda: self.phase_ffn(0, self.XM, self.X1, 0, 4096, PAD))
        ph.append(lambda: self.phase_kv(1, self.X1))
        ph.append(lambda: self.phase_m1(1, self.X1, 1920, 4096))
        ph.append(lambda: self.phase_m2(1, 1920, 4096))
        ph.append(lambda: self.phase_att(1, 1920, 4096))
        ph.append(lambda: self.phase_merge(1, self.X1, self.XM, 1920, 4096))
        ph.append(lambda: self.phase_ffn(1, self.XM, self.OUT, 2048, 4096, 0))
        sel = self.stop if self.stop is not None else range(len(ph))
        for i in sel:
            ph[i]()
        self.P.barrier()
        self.P.emit()
        return self.nc


def _colvec(v, nchunk):
    return np.ascontiguousarray(v.reshape(nchunk, 128).T)


def make_inputs(inp):
    L = 2
    f = lambda a: np.ascontiguousarray(np.asarray(a, dtype=np.float32))
    shared = {}
    for k in ("w_in", "w_a_out", "w_b_out", "w_c_out", "w_o", "w_up", "w_down"):
        shared[k] = f(inp[k])
    nw = np.zeros((128, L * 4 * 16), np.float32)
    for l in range(L):
        for i, k in enumerate(("norm_mix_pre", "norm_mix_post", "norm_ffn_pre", "norm_ffn_post")):
            nw[:, (l * 4 + i) * 16:(l * 4 + i + 1) * 16] = _colvec(f(inp[k])[l], 16)
    shared["normw"] = nw
    caw = np.zeros((128, L * 8 * 31), np.float32)
    av = np.zeros((128, L * 3 * 8), np.float32)
    for l in range(L):
        w = f(inp["conv_a_w"])[l]
        caw[:, l * 248:(l + 1) * 248] = w.T.reshape(8, 128, 31).transpose(1, 0, 2).reshape(128, 248)
        for i, k in enumerate(("conv_a_b", "ln_a_g", "ln_a_b")):
            av[:, (l * 3 + i) * 8:(l * 3 + i + 1) * 8] = _colvec(f(inp[k])[l], 8)
    shared["convaw"] = caw
    shared["avec"] = av
    lnb = np.zeros((128, L * 2 * 1024), np.float32)
    for l in range(L):
        lnb[:, (l * 2) * 1024:(l * 2 + 1) * 1024] = f(inp["ln_b_g"])[l][None, :]
        lnb[:, (l * 2 + 1) * 1024:(l * 2 + 2) * 1024] = f(inp["ln_b_b"])[l][None, :]
    shared["lnb"] = lnb
    shared["sgw"] = np.ascontiguousarray(f(inp["sg_w"]).transpose(0, 3, 1, 2).reshape(L, 128, 1024))
    shared["sgb"] = np.ascontiguousarray(f(inp["sg_b"]).reshape(1, L * 1024))
    cw1 = np.zeros((L * 2, 64, 32 * 256), np.float32)
    cw2 = np.zeros((L * 2, 256, 64), np.float32)
    cpe = np.zeros((L * 2, 64, 32), np.float32)
    for l in range(L):
        for kv, s in enumerate(("k", "v")):
            cw1[l * 2 + kv] = f(inp["cmp_w1_" + s])[l].transpose(1, 0, 2).reshape(64, 32 * 256)
            cw2[l * 2 + kv] = f(inp["cmp_w2_" + s])[l]
            cpe[l * 2 + kv] = f(inp["cmp_pe_" + s])[l].T
    shared["cw1"], shared["cw2"], shared["cpe"] = cw1, cw2, cpe
    ffw = np.zeros((128, L * 88 * 3), np.float32)
    ffb = np.zeros((128, L * 88), np.float32)
    for l in range(L):
        w = f(inp["ffn_conv_w"])[l]
        ffw[:, l * 264:(l + 1) * 264] = w.T.reshape(88, 128, 3).transpose(1, 0, 2).reshape(128, 264)
        ffb[:, l * 88:(l + 1) * 88] = _colvec(f(inp["ffn_conv_b"])[l], 88)
    shared["ffw"], shared["ffb"] = ffw, ffb
    p = np.arange(128)
    shared["c_ident"] = np.eye(128, dtype=np.float32)
    shared["c_trile"] = (p[:, None] <= p[None, :]).astype(np.float32)
    shared["c_trigt"] = (p[:, None] > p[None, :]).astype(np.float32)
    shared["c_dtab"] = (16.0 * p[:, None] - p[None, :]).astype(np.float32)
    k = np.arange(4096)
    shared["c_E"] = (k[None, :] // 64 == np.arange(64)[:, None]).astype(np.float32)
    n = np.arange(256)
    sm = np.zeros((256, 64), np.float32)
    for nn in range(255):
        sm[nn, nn // 4] += 1.0
        sm[nn, (nn + 1) // 4] += 1.0
    shared["c_smap"] = np.ascontiguousarray(sm.reshape(2, 128, 64).transpose(1, 0, 2).reshape(128, 128))
    x = f(inp["x"])
    maps = []
    for b in range(4):
        for s in range(2):
            m = dict(shared)
            xT = np.zeros((D, NCOL), np.float32)
            tok = np.zeros((NCOL,), np.float32)
            if s == 1:
                xT[:, PAD:] = x[b].T
                tok[PAD:] = 1.0
                j0 = 0
            else:
                xT[:, PAD + 2048:] = x[b, :2048].T
                tok[PAD + 2048:] = 1.0
                j0 = 32
            m["xT"] = xT
            m["m_tok"] = np.ascontiguousarray(np.broadcast_to(tok[None, :], (128, NCOL)))
            kval = np.ones((128, 32), np.float32)
            nval = np.ones((128, 2), np.float32)
            if s == 0:
                kval[:, :16] = 0.0
                nval[:, 0] = 0.0
            m["m_kval"], m["m_nval"] = kval, nval
            t = np.arange(4096).reshape(32, 128).T
            cur = t // 64
            j = np.arange(64)[None, None, :]
            valid = (j <= cur[:, :, None]) & (j >= j0)
            forced = ((j == j0) | (j == cur[:, :, None]) | (j == cur[:, :, None] - 1)) & valid
            M1 = (valid & ~forced).astype(np.float32)
            M2 = np.where(forced, 1e4 + j, np.where(valid, 0.0, -1.0)).astype(np.float32)
            M3 = valid.astype(np.float32)
            m["m_sel"] = np.ascontiguousarray(np.stack([M1, M2, M3]).reshape(3, 128, 32 * 64))
            maps.append(m)
    return maps


_CACHE = {}


def kernel(**inputs):
    maps = make_inputs(inputs)
    if "nc" not in _CACHE:
        _CACHE["nc"] = Builder().build()
    nc = _CACHE["nc"]
    res = run_bass_kernel_spmd(nc, maps, core_ids=list(range(8)))
    out = np.zeros((4, SEQ, D), np.float32)
    for b in range(4):
        for s in range(2):
            o = res.results[b * 2 + s]["OUT"]
            out[b, s * 2048:(s + 1) * 2048, :] = o.T
    return out
```

```python
import numpy as np
from contextlib import ExitStack
import concourse.bass as bass
import concourse.mybir as mybir
from concourse.bass_utils import run_bass_kernel_spmd

F32 = mybir.dt.float32
BF16 = mybir.dt.bfloat16
AF = mybir.ActivationFunctionType
ALU = mybir.AluOpType
AX = mybir.AxisListType

ENGS = ("pe", "act", "dve", "pool", "sp")

D = 2048
SEQ = 4096
PAD = 128
NCOL = PAD + SEQ
NIN = 12848
DFF = 5632
EPS = 1e-6
C_A, C_B, C_Q, C_KV, C_G, C_M = 0, 2048, 4096, 5120, 6656, 6704


class Reg:
    __slots__ = ("name", "w", "r", "dkey", "dcount")

    def __init__(self, name):
        self.name = name
        self.w = None
        self.r = {}
        self.dkey = None
        self.dcount = 0


class _Rec:
    def __init__(self):
        self.call = None

    def __getattr__(self, name):
        def f(*a, **k):
            self.call = (name, a, k)
            return self
        return f


def _record(fn):
    r = _Rec()
    fn(r)
    assert r.call is not None
    return r.call


class Prog:
    def __init__(self, nc):
        self.nc = nc
        self.streams = {e: [] for e in ENGS}
        self.cnt = {e: 0 for e in ENGS}
        self.seen = {e: {} for e in ENGS}
        self.dsems = {}
        self.ndsem = 0
        self.semh = {}
        self.free_dkeys = []
        self.phase_keys = []

    def sb(self, stack, name, shape, dt):
        return stack.enter_context(self.nc.sbuf_tensor(name, list(shape), dt))

    def _waits(self, eng, deps):
        out = []
        seen = self.seen[eng]
        best = {}
        for (k, v) in deps:
            if best.get(k, -1) < v:
                best[k] = v
        for k, v in best.items():
            if k == eng:
                if eng == "pe":
                    continue
                if v <= self.cnt[eng] - 2:
                    continue
            if seen.get(k, -1) >= v:
                continue
            seen[k] = v
            out.append((k, v))
        return out

    def _deps(self, reads, writes):
        deps = []
        for r in reads:
            if r.w is not None:
                deps.append(r.w)
        for w in writes:
            if w.w is not None:
                deps.append(w.w)
            deps.extend(w.r.items())
        return deps

    def op(self, eng, fn, reads=(), writes=()):
        waits = self._waits(eng, self._deps(reads, writes))
        self.cnt[eng] += 1
        me = (eng, self.cnt[eng])
        self.streams[eng].append((waits, _record(fn), (eng, 1)))
        for r in reads:
            r.r[me[0]] = me[1]
        for w in writes:
            w.w = me
            w.r = {}
        return me

    def dma(self, q, fn, sb, load, dram=()):
        if sb.dkey is None:
            if self.free_dkeys:
                sb.dkey = self.free_dkeys.pop()
                sb.dcount = self.dsems[sb.dkey]
            else:
                sb.dkey = "d%d" % self.ndsem
                self.ndsem += 1
            self.phase_keys.append(sb.dkey)
        if load:
            deps = self._deps(dram, [sb])
        else:
            deps = self._deps([sb], dram)
        waits = self._waits(q, deps)
        sb.dcount += 16
        self.dsems[sb.dkey] = sb.dcount
        me = (sb.dkey, sb.dcount)
        self.streams[q].append((waits, _record(fn), (sb.dkey, 16)))
        if load:
            sb.w = me
            sb.r = {}
            for d in dram:
                d.r[me[0]] = me[1]
        else:
            sb.r[me[0]] = me[1]
            for d in dram:
                d.w = me
                d.r = {}
        return me

    def barrier(self):
        allk = [(e, self.cnt[e]) for e in ENGS if self.cnt[e] > 0]
        allk += list(self.dsems.items())
        for e in ENGS:
            waits = self._waits(e, [kv for kv in allk if kv[0] != e])
            if waits:
                self.streams[e].append((waits, None, None))

    def end_phase(self, persistent=False):
        self.barrier()
        if not persistent:
            self.free_dkeys.extend(self.phase_keys)
        self.phase_keys = []

    def emit(self):
        nc = self.nc
        keys = list(ENGS) + list(self.dsems.keys())
        with ExitStack() as st:
            for k in keys:
                self.semh[k] = st.enter_context(nc.semaphore("s_" + k))
            block = st.enter_context(nc.Block())
            semh = self.semh

            def run(e, stream):
                for waits, fn, inc in stream:
                    for (k, v) in waits:
                        e.wait_ge(semh[k], v)
                    if fn is not None:
                        name, a, k = fn
                        ins = getattr(e, name)(*a, **k)
                        ins.then_inc(semh[inc[0]], inc[1])

            @block.tensor
            def _(e):
                run(e, self.streams["pe"])

            @block.scalar
            def _(e):
                run(e, self.streams["act"])

            @block.vector
            def _(e):
                run(e, self.streams["dve"])

            @block.gpsimd
            def _(e):
                run(e, self.streams["pool"])

            @block.sync
            def _(e):
                run(e, self.streams["sp"])


class Builder:
    def __init__(self, debug=False, nlayers=2, stop=None):
        self.debug = debug
        self.stop = stop
        nc = bass.Bass("TRN2", target_bir_lowering=False)
        self.nc = nc
        self.P = Prog(nc)
        self.I = {}
        self.regs = {}
        self.gst = ExitStack()
        self._uid = 0
        self.declare_io()
        self.alloc_consts()

    def R(self, name):
        if name not in self.regs:
            self.regs[name] = Reg(name)
        return self.regs[name]

    def newreg(self, name):
        self._uid += 1
        return Reg("%s_%d" % (name, self._uid))

    def din(self, name, shape, dt=F32):
        t = self.nc.dram_tensor(name, list(shape), dt, kind="ExternalInput").ap()
        self.I[name] = t
        return t

    def dscr(self, name, shape, dt, out=False):
        kind = "ExternalOutput" if (out or self.debug) else "Internal"
        return self.nc.dram_tensor(name, list(shape), dt, kind=kind).ap()

    def tile(self, st, name, shape, dt):
        self._uid += 1
        t = self.P.sb(st, "%s_%d" % (name, self._uid), shape, dt)
        return t, self.newreg(name)

    def dump(self, name, tile_, reg, shape, dt=F32):
        if not self.debug or True:
            return
        d = self.nc.dram_tensor("dbg_" + name, list(shape), dt, kind="ExternalOutput").ap()
        self.P.dma("sp", lambda e: e.dma_start(out=d, in_=tile_[:]), reg, False)

    def declare_io(self):
        L = 2
        self.xT = self.din("xT", [D, NCOL])
        self.w_in = self.din("w_in", [L, D, NIN])
        self.w_a_out = self.din("w_a_out", [L, 1024, D])
        self.w_b_out = self.din("w_b_out", [L, 1024, D])
        self.w_c_out = self.din("w_c_out", [L, 1024, D])
        self.w_o = self.din("w_o", [L, D, D])
        self.w_up = self.din("w_up", [L, D, 2 * DFF])
        self.w_down = self.din("w_down", [L, DFF, D])
        self.normw = self.din("normw", [128, L * 4 * 16])
        self.convaw = self.din("convaw", [128, L * 8 * 31])
        self.avec = self.din("avec", [128, L * 3 * 8])
        self.lnb = self.din("lnb", [128, L * 2 * 1024])
        self.sgw = self.din("sgw", [L, 128, 8 * 128])
        self.sgb = self.din("sgb", [1, L * 1024])
        self.cw1 = self.din("cw1", [L * 2, 64, 32 * 256])
        self.cw2 = self.din("cw2", [L * 2, 256, 64])
        self.cpe = self.din("cpe", [L * 2, 64, 32])
        self.ffw = self.din("ffw", [128, L * 88 * 3])
        self.ffb = self.din("ffb", [128, L * 88])
        self.c_ident = self.din("c_ident", [128, 128])
        self.c_trile = self.din("c_trile", [128, 128])
        self.c_trigt = self.din("c_trigt", [128, 128])
        self.c_dtab = self.din("c_dtab", [128, 128])
        self.c_E = self.din("c_E", [64, 32 * 128])
        self.c_smap = self.din("c_smap", [128, 2 * 64])
        self.m_tok = self.din("m_tok", [128, NCOL])
        self.m_kval = self.din("m_kval", [128, 32])
        self.m_nval = self.din("m_nval", [128, 2])
        self.m_sel = self.din("m_sel", [3, 128, 32 * 64])
        self.XM = self.dscr("XM", [D, NCOL], F32)
        self.X1 = self.dscr("X1", [D, NCOL], F32)
        self.OUT = self.dscr("OUT", [D, 2048], F32, out=True)
        self.GLU = self.dscr("GLU", [1024, NCOL], BF16)
        self.HA = self.dscr("HA", [1024, NCOL], BF16)
        self.HB = self.dscr("HB", [1024, NCOL], BF16)
        self.OC = self.dscr("OC", [1024, NCOL], BF16)
        self.Q = self.dscr("Q", [1024, NCOL], BF16)
        self.GATES = self.dscr("GATES", [NCOL, 48], F32)
        self.KT = [self.dscr("KT%d" % i, [256, NCOL], BF16) for i in range(4)]
        self.VT = [self.dscr("VT%d" % i, [NCOL, 256], BF16) for i in range(2)]

    def alloc_consts(self):
        P, st = self.P, self.gst
        nc = self.nc
        T = lambda n, s, d: self.tile(st, n, s, d)
        self.ident_f, r_if = T("ident_f", [128, 128], F32)
        self.ident_b, r_ib = T("ident_b", [128, 128], BF16)
        self.ones_b, r_ob = T("ones_b", [128, 128], BF16)
        self.trile, r1 = T("trile", [128, 128], F32)
        self.trigt, r2 = T("trigt", [128, 128], F32)
        self.dtab, r3 = T("dtab", [128, 128], F32)
        self.tokv, r4 = T("tokv", [128, NCOL], BF16)
        self.kval, r5 = T("kval", [128, 32], F32)
        self.nval, r6 = T("nval", [128, 2], F32)
        self.normw_s, r7 = T("normw", [128, 128], F32)
        self.epsT, r8 = T("eps", [128, 1], F32)
        self.zero_f, r9 = T("zero_f", [128, 128], F32)
        self.r_const = self.newreg("const")
        rc = self.r_const
        ld = lambda t, src: P.dma("sp", lambda e: e.dma_start(out=t[:], in_=src), rc, True)
        ld(self.ident_f, self.c_ident)
        ld(self.trile, self.c_trile)
        ld(self.trigt, self.c_trigt)
        ld(self.dtab, self.c_dtab)
        ld(self.kval, self.m_kval)
        ld(self.nval, self.m_nval)
        ld(self.normw_s, self.normw)
        P.dma("pool", lambda e: e.dma_start(out=self.ident_b[:], in_=self.c_ident), rc, True)
        P.dma("pool", lambda e: e.dma_start(out=self.tokv[:], in_=self.m_tok), rc, True)
        P.op("dve", lambda e: e.memset(self.ones_b[:], 1.0), [], [rc])
        P.op("dve", lambda e: e.memset(self.epsT[:], EPS), [], [rc])
        P.op("dve", lambda e: e.memset(self.zero_f[:], 0.0), [], [rc])
        for X in (self.XM, self.X1):
            Xv = X.rearrange("(c p) t -> p c t", p=128)
            for c in range(16):
                P.dma("sp", lambda e, c=c, Xv=Xv: e.dma_start(out=Xv[:, c, 0:128], in_=self.zero_f[:]), rc, False)
        self.ps = []
        self.psr = []
        for i in range(8):
            t = st.enter_context(nc.psum_tensor("psb%d" % i, [128, 512], F32))
            self.ps.append(t)
            self.psr.append(self.newreg("ps%d" % i))
        P.end_phase(persistent=True)

    def load_norm(self, st_tiles, X, col0, n, nw_off, dst_fn, dst_reg, bank=7):
        P = self.P
        xt, r_xt, sq, r_sq, rs, r_rs = st_tiles
        Xv = X.rearrange("(c p) t -> p c t", p=128)
        P.dma("sp", lambda e: e.dma_start(out=xt[:, :, 0:n], in_=Xv[:, :, col0:col0 + n]), r_xt, True)
        ps, pr = self.ps[bank], self.psr[bank]
        for c in range(16):
            P.op("act", lambda e, c=c: e.activation(out=sq[c % 2][:, 0:n], in_=xt[:, c, 0:n], func=AF.Square), [r_xt], [r_sq[c % 2]])
            P.op("pe", lambda e, c=c: e.matmul(ps[:, 0:n], lhsT=self.ones_b[:], rhs=sq[c % 2][:, 0:n], start=(c == 0), stop=(c == 15)), [r_sq[c % 2]], [pr])
        P.op("act", lambda e: e.activation(out=rs[:, 0:n], in_=ps[:, 0:n], func=AF.Sqrt, bias=self.epsT[:, 0:1], scale=1.0 / D), [pr], [r_rs])
        P.op("dve", lambda e: e.reciprocal(out=rs[:, 0:n], in_=rs[:, 0:n]), [r_rs], [r_rs])
        P.op("dve", lambda e: e.tensor_tensor(out=rs[:, 0:n], in0=rs[:, 0:n], in1=self.tokv[:, col0:col0 + n], op=ALU.mult), [r_rs], [r_rs])
        for c in range(16):
            P.op("dve", lambda e, c=c: e.scalar_tensor_tensor(out=dst_fn(c), in0=xt[:, c, 0:n], scalar=self.normw_s[:, nw_off + c:nw_off + c + 1],
                                                              in1=rs[:, 0:n], op0=ALU.mult, op1=ALU.mult), [r_xt, r_rs], [dst_reg])

    def norm_tiles(self, st):
        xt, r_xt = self.tile(st, "xt", [128, 16, 256], F32)
        sq0, r0 = self.tile(st, "sq0", [128, 256], BF16)
        sq1, r1 = self.tile(st, "sq1", [128, 256], BF16)
        rs, r_rs = self.tile(st, "rs", [128, 256], F32)
        return (xt, r_xt, [sq0, sq1], [r0, r1], rs, r_rs)

    def fill_hT(self, st_tiles, X, col0, ntok, nw_off, hT, hregs):
        for i, t0 in enumerate(range(0, ntok, 256)):
            n = min(256, ntok - t0)
            self.load_norm(st_tiles, X, col0 + t0, n, nw_off, lambda c, t0=t0, n=n: hT[:, c, t0:t0 + n], hregs[i])

    def wload(self, wbuf, wreg, Wsrc, kc, col0, ncols, off=0, k0=0):
        view = wbuf[:, off:off + kc * ncols].rearrange("p (k n) -> p k n", k=kc)
        src = Wsrc.rearrange("(k p) n -> p k n", p=128)[:, k0:k0 + kc, col0:col0 + ncols]
        self.P.dma("pool", lambda e: e.dma_start(out=view, in_=src), wreg, True)
        return view

    def run_jobs(self, jobs, nbuf=2):
        loads = [i for i, (lf, _) in enumerate(jobs) if lf is not None]
        for j in range(min(nbuf - 1, len(loads))):
            jobs[loads[j]][0](j % nbuf)
        li = 0
        for i, (lf, cf) in enumerate(jobs):
            if lf is None:
                cf(None)
                continue
            nx = li + nbuf - 1
            if nx < len(loads):
                jobs[loads[nx]][0](nx % nbuf)
            cf(li % nbuf)
            li += 1

    def post_norm(self, st, mixed, r_mixed, ss_bank, n, nw_off, Xin, Xout, cin0, cout0, tl):
        P = self.P
        rs2, r_rs2, xr, r_xr, ot, r_ot, tt, r_tt = tl
        ps, pr = self.ps[ss_bank], self.psr[ss_bank]
        P.op("act", lambda e: e.activation(out=rs2[:, 0:n], in_=ps[:, 0:n], func=AF.Sqrt, bias=self.epsT[:, 0:1], scale=1.0 / D), [pr], [r_rs2])
        P.op("dve", lambda e: e.reciprocal(out=rs2[:, 0:n], in_=rs2[:, 0:n]), [r_rs2], [r_rs2])
        Xi = Xin.rearrange("(c p) t -> p c t", p=128)
        Xo = Xout.rearrange("(c p) t -> p c t", p=128)
        for c in range(16):
            b = c % 2
            P.dma("sp", lambda e, c=c, b=b: e.dma_start(out=xr[b][:, 0:n], in_=Xi[:, c, cin0:cin0 + n]), r_xr[b], True)
            P.op("dve", lambda e, c=c, b=b: e.tensor_tensor(out=tt[b][:, 0:n], in0=mixed[:, c, 0:n], in1=rs2[:, 0:n], op=ALU.mult), [r_mixed, r_rs2], [r_tt[b]])
            P.op("dve", lambda e, c=c, b=b: e.scalar_tensor_tensor(out=ot[b][:, 0:n], in0=tt[b][:, 0:n], scalar=self.normw_s[:, nw_off + c:nw_off + c + 1],
                                                                   in1=xr[b][:, 0:n], op0=ALU.mult, op1=ALU.add), [r_tt[b], r_xr[b]], [r_ot[b]])
            P.dma("sp", lambda e, c=c, b=b: e.dma_start(out=Xo[:, c, cout0:cout0 + n], in_=ot[b][:, 0:n]), r_ot[b], False)

    def post_tiles(self, st):
        rs2, r_rs2 = self.tile(st, "rs2", [128, 512], F32)
        xr = []; r_xr = []; ot = []; r_ot = []; tt = []; r_tt = []
        for b in range(2):
            a, ra = self.tile(st, "xr", [128, 512], F32); xr.append(a); r_xr.append(ra)
            a, ra = self.tile(st, "ot", [128, 512], F32); ot.append(a); r_ot.append(ra)
            a, ra = self.tile(st, "tt", [128, 512], F32); tt.append(a); r_tt.append(ra)
        return (rs2, r_rs2, xr, r_xr, ot, r_ot, tt, r_tt)

    def tok_tiles(self, start, ntok, step=512):
        return [(t0, min(step, start + ntok - t0)) for t0 in range(start, start + ntok, step)]

    def phase_kv(self, l, X):
        P = self.P
        with ExitStack() as st:
            nt = self.norm_tiles(st)
            wkv, r_wkv = self.tile(st, "wkv", [128, 16 * 1536], BF16)
            wv = wkv[:].rearrange("p (k n) -> p k n", k=16)
            Wl = self.w_in[l]
            for j in range(3):
                src = Wl.rearrange("(k p) n -> p k n", p=128)[:, :, C_KV + j * 512:C_KV + (j + 1) * 512]
                P.dma("pool", lambda e, j=j, src=src: e.dma_start(out=wv[:, :, j * 512:(j + 1) * 512], in_=src), r_wkv, True)
            hts = [self.tile(st, "hTt", [128, 16, 256], BF16) for _ in range(2)]
            ksts = [self.tile(st, "kst", [128, 8, 256], BF16) for _ in range(2)]
            vsts = [self.tile(st, "vst", [128, 2, 512], BF16) for _ in range(2)]
            fm_off = [0, 256, 512, 1024]
            tm_off = [768, 1280]
            for ti, t0 in enumerate(range(0, SEQ, 256)):
                hT, r_h = hts[ti % 2]
                kst, r_k = ksts[ti % 2]
                vst, r_v = vsts[ti % 2]
                col0 = PAD + t0
                self.load_norm(nt, X, col0, 256, (l * 4 + 0) * 16, lambda c, hT=hT: hT[:, c, :], r_h)
                for ty in range(4):
                    bk = ty % 4
                    ps, pr = self.ps[bk], self.psr[bk]
                    for half in range(2):
                        off = fm_off[ty] + half * 128
                        for kc in range(16):
                            P.op("pe", lambda e, ps=ps, half=half, off=off, kc=kc, hT=hT: e.matmul(ps[:, half * 256:(half + 1) * 256], lhsT=wv[:, kc, off:off + 128],
                                                                                                  rhs=hT[:, kc, :], start=(kc == 0), stop=(kc == 15)), [r_wkv, r_h], [pr])
                    P.op("act", lambda e, ps=ps, ty=ty, kst=kst: e.activation(out=kst[:, 2 * ty:2 * ty + 2, :], in_=ps[:].rearrange("p (h t) -> p h t", h=2), func=AF.Copy), [pr], [r_k])
                for ty in range(4):
                    dst = self.KT[ty].rearrange("(h p) t -> p h t", p=128)[:, :, col0:col0 + 256]
                    P.dma("sp", lambda e, ty=ty, dst=dst, kst=kst: e.dma_start(out=dst, in_=kst[:, 2 * ty:2 * ty + 2, :]), r_k, False, [self.R("KT")])
                for sub in range(2):
                    ps, pr = self.ps[4 + sub], self.psr[4 + sub]
                    for j in range(2):
                        for kc in range(16):
                            P.op("pe", lambda e, ps=ps, sub=sub, j=j, kc=kc, hT=hT: e.matmul(ps[:, j * 256:(j + 1) * 256], lhsT=hT[:, kc, sub * 128:(sub + 1) * 128],
                                                                                             rhs=wv[:, kc, tm_off[j]:tm_off[j] + 256], start=(kc == 0), stop=(kc == 15)), [r_wkv, r_h], [pr])
                    P.op("dve", lambda e, ps=ps, sub=sub, vst=vst: e.tensor_copy(out=vst[:, sub, :], in_=ps[:]), [pr], [r_v])
                for j in range(2):
                    dst = self.VT[j][col0:col0 + 256, :].rearrange("(s p) c -> p s c", p=128)
                    P.dma("sp", lambda e, j=j, dst=dst, vst=vst: e.dma_start(out=dst, in_=vst[:, :, j * 256:(j + 1) * 256]), r_v, False, [self.R("VT")])
            P.end_phase()

    def phase_m1(self, l, X, r0, r1):
        P = self.P
        T = r1 - r0
        TA = T + 128
        Wl = self.w_in[l]
        with ExitStack() as st:
            nt = self.norm_tiles(st)
            hT, _ = self.tile(st, "hT", [128, 16, TA], BF16)
            hregs = [self.newreg("hT") for _ in range((TA + 255) // 256)]
            wb = [self.tile(st, "wbuf", [128, 8192], BF16) for _ in range(2)]
            wvt, r_wvt = self.tile(st, "wvt", [128, 16 * 1024], BF16)
            ut, r_ut = self.tile(st, "ut", [128, 8, 128], BF16)
            sig = [self.tile(st, "sig", [128, 512], F32) for _ in range(2)]
            gst = [self.tile(st, "gst", [128, 512], BF16) for _ in range(2)]
            qst = [self.tile(st, "qst", [128, 512], BF16) for _ in range(2)]
            vg = [self.tile(st, "vg", [128, 1024], F32)] * 2
            vln = [self.tile(st, "vln", [128, 1024], BF16)] * 2
            bst, r_bst = self.tile(st, "bst", [128, 2, 6], F32)
            mv, r_mv = self.tile(st, "mv", [128, 2], F32)
            wsT, r_ws = self.tile(st, "wsT", [128, 8, 128], BF16)
            wsF, r_wsF = self.tile(st, "wsF", [128, 8, 128], BF16)
            lnbt, r_lnb = self.tile(st, "lnbt", [128, 2, 1024], F32)
            sgbr, r_sgb = self.tile(st, "sgbr", [1, 1024], BF16)
            gall, r_gall = self.tile(st, "gall", [128, T // 128, 48], F32)
            P.dma("sp", lambda e: e.dma_start(out=lnbt[:], in_=self.lnb[:, l * 2048:(l + 1) * 2048].rearrange("p (a c) -> p a c", a=2)), r_lnb, True)
            P.dma("pool", lambda e: e.dma_start(out=sgbr[:], in_=self.sgb[:, l * 1024:(l + 1) * 1024]), r_sgb, True)
            P.dma("pool", lambda e: e.dma_start(out=wsF[:], in_=self.sgw[l].rearrange("s (g t) -> s g t", g=8)), r_wsF, True)
            P.op("dve", lambda e: e.tensor_tensor(out=wsT[:], in0=wsF[:], in1=self.trile[:].unsqueeze(1).to_broadcast([128, 8, 128]), op=ALU.mult), [r_wsF], [r_ws])
            for j in range(2):
                src = Wl.rearrange("(k p) n -> p k n", p=128)[:, :, C_B + 1024 + j * 512:C_B + 1024 + (j + 1) * 512]
                P.dma("pool", lambda e, j=j, src=src: e.dma_start(out=wvt[:].rearrange("p (k n) -> p k n", k=16)[:, :, j * 512:(j + 1) * 512], in_=src), r_wvt, True)
            wv3 = wvt[:].rearrange("p (k n) -> p k n", k=16)
            self.fill_hT(nt, X, PAD + r0 - 128, TA, (l * 4 + 0) * 16, hT, hregs)
            tilesA = self.tok_tiles(0, TA)
            tilesR = self.tok_tiles(128, T)
            jobs = []
            psrot = [0]

            def nextps():
                b = psrot[0] % 4
                psrot[0] += 1
                return self.ps[b], self.psr[b]

            def mm16(ps, n, wview, c0, t0):
                for kc in range(16):
                    P.op("pe", lambda e, kc=kc: e.matmul(ps[:, 0:n], lhsT=wview[:, kc, c0:c0 + 128], rhs=hT[:, kc, t0:t0 + n], start=(kc == 0), stop=(kc == 15)),
                         [wview_reg[0]] + hregs, [ps_reg[0]])

            wview_reg = [None]
            ps_reg = [None]
            for jg in range(4):
                def lf(b, jg=jg):
                    self.wload(wb[b][0], wb[b][1], Wl, 16, C_A + jg * 256, 256, 0)
                    self.wload(wb[b][0], wb[b][1], Wl, 16, C_A + 1024 + jg * 256, 256, 16 * 256)

                def cf(b, jg=jg):
                    wa = wb[b][0][:, 0:4096].rearrange("p (k n) -> p k n", k=16)
                    wg = wb[b][0][:, 4096:8192].rearrange("p (k n) -> p k n", k=16)
                    wview_reg[0] = wb[b][1]
                    k = 0
                    for cc in range(2):
                        ch = jg * 2 + cc
                        for (t0, n) in tilesA:
                            pa, ra = nextps()
                            ps_reg[0] = ra
                            mm16(pa, n, wa, cc * 128, t0)
                            pg, rg = nextps()
                            ps_reg[0] = rg
                            mm16(pg, n, wg, cc * 128, t0)
                            s_, rs_ = sig[k % 2]
                            g_, rg_ = gst[k % 2]
                            k += 1
                            P.op("act", lambda e, pg=pg, s_=s_, n=n: e.activation(out=s_[:, 0:n], in_=pg[:, 0:n], func=AF.Sigmoid), [rg], [rs_])
                            P.op("dve", lambda e, pa=pa, s_=s_, g_=g_, n=n: e.tensor_tensor(out=g_[:, 0:n], in0=pa[:, 0:n], in1=s_[:, 0:n], op=ALU.mult), [ra, rs_], [rg_])
                            c0 = PAD + r0 - 128 + t0
                            P.dma("sp", lambda e, g_=g_, ch=ch, c0=c0, n=n: e.dma_start(out=self.GLU[ch * 128:(ch + 1) * 128, c0:c0 + n], in_=g_[:, 0:n]), rg_, False, [self.R("GLU")])
                jobs.append((lf, cf))
            ku = [0]
            for jg in range(2):
                def lf(b, jg=jg):
                    self.wload(wb[b][0], wb[b][1], Wl, 16, C_B + jg * 512, 512, 0)

                def cf(b, jg=jg):
                    wu = wb[b][0][:].rearrange("p (k n) -> p k n", k=16)
                    wview_reg[0] = wb[b][1]
                    for cc in range(4):
                        ch = jg * 4 + cc
                        for (t0, n) in tilesR:
                            pu, ru = nextps()
                            ps_reg[0] = ru
                            mm16(pu, n, wu, cc * 128, t0)
                            q_, rq_ = qst[ku[0] % 2]
                            ku[0] += 1
                            P.op("act", lambda e, pu=pu, q_=q_, n=n: e.activation(out=q_[:, 0:n], in_=pu[:, 0:n], func=AF.Gelu), [ru], [rq_])
                            c0 = PAD + r0 - 128 + t0
                            P.dma("sp", lambda e, q_=q_, ch=ch, c0=c0, n=n: e.dma_start(out=self.HB[ch * 128:(ch + 1) * 128, c0:c0 + n], in_=q_[:, 0:n]), rq_, False)
                jobs.append((lf, cf))
            for jg in range(2):
                def lf(b, jg=jg):
                    self.wload(wb[b][0], wb[b][1], Wl, 16, C_Q + jg * 512, 512, 0)

                def cf(b, jg=jg):
                    wq = wb[b][0][:].rearrange("p (k n) -> p k n", k=16)
                    wview_reg[0] = wb[b][1]
                    k = 0
                    for cc in range(4):
                        ch = jg * 4 + cc
                        for (t0, n) in tilesR:
                            pq, rq = nextps()
                            ps_reg[0] = rq
                            mm16(pq, n, wq, cc * 128, t0)
                            q_, rq_ = qst[k % 2]
                            k += 1
                            P.op("act", lambda e, pq=pq, q_=q_, n=n: e.activation(out=q_[:, 0:n], in_=pq[:, 0:n], func=AF.Copy, scale=0.125), [rq], [rq_])
                            c0 = PAD + r0 - 128 + t0
                            P.dma("sp", lambda e, q_=q_, ch=ch, c0=c0, n=n: e.dma_start(out=self.Q[ch * 128:(ch + 1) * 128, c0:c0 + n], in_=q_[:, 0:n]), rq_, False, [self.R("Q")])
                jobs.append((lf, cf))
            def lfg(b):
                self.wload(wb[b][0], wb[b][1], Wl, 16, C_G, 48, 0)

            def cfg(b):
                wg = wb[b][0][:, 0:16 * 48].rearrange("p (k n) -> p k n", k=16)
                for i in range(T // 128):
                    pg, rg = nextps()
                    t0 = 128 + i * 128
                    for kc in range(16):
                        P.op("pe", lambda e, kc=kc, pg=pg, t0=t0: e.matmul(pg[:, 0:48], lhsT=hT[:, kc, t0:t0 + 128], rhs=wg[:, kc, :], start=(kc == 0), stop=(kc == 15)),
                             [wb[b][1]] + hregs, [rg])
                    P.op("act", lambda e, pg=pg, i=i: e.activation(out=gall[:, i, :], in_=pg[:, 0:48], func=AF.Sigmoid), [rg], [r_gall])
                dst = self.GATES[PAD + r0:PAD + r1, :].rearrange("(n p) c -> p n c", p=128)
                P.dma("sp", lambda e: e.dma_start(out=dst, in_=gall[:]), r_gall, False, [self.R("GATES")])
            jobs.append((lfg, cfg))
            self.run_jobs(jobs)
            P.barrier()
            HBv = self.HB.rearrange("(c p) t -> p c t", p=128)
            for i in range(T // 128):
                t0 = 128 + i * 128
                cu = PAD + r0 + i * 128
                P.dma("sp", lambda e, cu=cu: e.dma_start(out=ut[:], in_=HBv[:, :, cu:cu + 128]), r_ut, True)
                vg_, rvg = vg[i % 2]
                vl_, rvl = vln[i % 2]
                for half in range(2):
                    pv, rv = self.ps[4 + half], self.psr[4 + half]
                    for kc in range(16):
                        P.op("pe", lambda e, kc=kc, pv=pv, half=half, t0=t0: e.matmul(pv[:], lhsT=hT[:, kc, t0:t0 + 128], rhs=wv3[:, kc, half * 512:(half + 1) * 512],
                                                                                     start=(kc == 0), stop=(kc == 15)), [r_wvt] + hregs, [rv])
                    P.op("act", lambda e, pv=pv, half=half, vg_=vg_: e.activation(out=vg_[:, half * 512:(half + 1) * 512], in_=pv[:], func=AF.Gelu), [rv], [rvg])
                for half in range(2):
                    P.op("dve", lambda e, half=half, vg_=vg_: e.bn_stats(out=bst[:, half, :], in_=vg_[:, half * 512:(half + 1) * 512]), [rvg], [r_bst])
                P.op("dve", lambda e: e.bn_aggr(out=mv[:], in_=bst[:].rearrange("p a s -> p (a s)")), [r_bst], [r_mv])
                P.op("act", lambda e: e.activation(out=mv[:, 1:2], in_=mv[:, 1:2], func=AF.Sqrt, bias=self.epsT[:, 0:1], scale=1.0), [r_mv], [r_mv])
                P.op("dve", lambda e: e.reciprocal(out=mv[:, 1:2], in_=mv[:, 1:2]), [r_mv], [r_mv])
                P.op("dve", lambda e, vg_=vg_: e.tensor_scalar(out=vg_[:], in0=vg_[:], scalar1=mv[:, 0:1], scalar2=mv[:, 1:2], op0=ALU.subtract, op1=ALU.mult), [rvg, r_mv], [rvg])
                P.op("dve", lambda e, vg_=vg_: e.tensor_tensor(out=vg_[:], in0=vg_[:], in1=lnbt[:, 0, :], op=ALU.mult), [rvg, r_lnb], [rvg])
                P.op("dve", lambda e, vg_=vg_, vl_=vl_: e.tensor_tensor(out=vl_[:], in0=vg_[:], in1=lnbt[:, 1, :], op=ALU.add), [rvg, r_lnb], [rvl])
                for hb in range(2):
                    pf, rf = self.ps[6 + hb], self.psr[6 + hb]
                    for gg in range(4):
                        g = hb * 4 + gg
                        P.op("pe", lambda e, pf=pf, gg=gg, g=g, vl_=vl_: e.matmul(pf[:, gg * 128:(gg + 1) * 128], lhsT=vl_[:, g * 128:(g + 1) * 128], rhs=wsT[:, g, :], start=True, stop=False),
                             [rvl, r_ws], [rf])
                        P.op("pe", lambda e, pf=pf, gg=gg, g=g: e.matmul(pf[:, gg * 128:(gg + 1) * 128], lhsT=self.ones_b[0:1, :], rhs=sgbr[0:1, g * 128:(g + 1) * 128], start=False, stop=True),
                             [r_sgb], [rf])
                    P.op("dve", lambda e, pf=pf, hb=hb: e.tensor_tensor(out=ut[:, hb * 4:(hb + 1) * 4, :], in0=ut[:, hb * 4:(hb + 1) * 4, :],
                                                                      in1=pf[:].rearrange("p (g t) -> p g t", g=4), op=ALU.mult), [rf, r_ut], [r_ut])
                P.dma("sp", lambda e, cu=cu: e.dma_start(out=HBv[:, :, cu:cu + 128], in_=ut[:]), r_ut, False)
            P.end_phase()

    def phase_m2(self, l, r0, r1):
        P = self.P
        T = r1 - r0
        TA = T + 128
        with ExitStack() as st:
            gl = [self.tile(st, "gl", [128, TA], BF16) for _ in range(3)]
            dgs = [self.tile(st, "dgs", [128, 31, 128], BF16) for _ in range(2)]
            cbf, r_cbf = self.tile(st, "cbf", [128, 8, T], BF16)
            cw, r_cw = self.tile(st, "cw", [128, 8, 31], F32)
            av, r_av = self.tile(st, "av", [128, 3, 8], F32)
            sq = [self.tile(st, "sq", [128, 512], BF16) for _ in range(2)]
            mean, r_mean = self.tile(st, "mean", [128, 512], F32)
            msq, r_msq = self.tile(st, "msq", [128, 512], F32)
            rstd, r_rstd = self.tile(st, "rstd", [128, 512], F32)
            t1 = [self.tile(st, "t1", [128, 512], F32) for _ in range(2)]
            hst = [self.tile(st, "hst", [128, 8, 512], BF16) for _ in range(2)]
            P.dma("sp", lambda e: e.dma_start(out=cw[:], in_=self.convaw[:, l * 248:(l + 1) * 248].rearrange("p (c k) -> p c k", c=8)), r_cw, True)
            P.dma("sp", lambda e: e.dma_start(out=av[:], in_=self.avec[:, l * 24:(l + 1) * 24].rearrange("p (a c) -> p a c", a=3)), r_av, True)
            for c in range(8):
                g_, rg = gl[c % 3]
                d_, rd_ = dgs[c % 2]
                c0 = PAD + r0 - 128
                P.dma("sp", lambda e: e.dma_start(out=g_[:], in_=self.GLU[c * 128:(c + 1) * 128, c0:c0 + TA]), rg, True)
                for k in range(31):
                    P.op("dve", lambda e: e.tensor_scalar(out=d_[:, k, :], in0=self.ident_f[:], scalar1=cw[:, c, k:k + 1], scalar2=None, op0=ALU.mult), [r_cw], [rd_])
                for ti, (t0, n) in enumerate(self.tok_tiles(0, T)):
                    bk = 2 + (c * 8 + ti) % 4
                    ps, pr = self.ps[bk], self.psr[bk]
                    for k in range(31):
                        P.op("pe", lambda e: e.matmul(ps[:, 0:n], lhsT=d_[:, k, :], rhs=g_[:, 98 + k + t0:98 + k + t0 + n], start=(k == 0), stop=(k == 30)), [rd_, rg], [pr])
                    P.op("act", lambda e: e.activation(out=cbf[:, c, t0:t0 + n], in_=ps[:, 0:n], func=AF.Identity, bias=av[:, 0, c:c + 1], scale=1.0), [pr, r_av], [r_cbf])
            for ti, (t0, n) in enumerate(self.tok_tiles(0, T)):
                pS, rS = self.ps[0], self.psr[0]
                pQ, rQ = self.ps[1], self.psr[1]
                for c in range(8):
                    s_, rs_ = sq[c % 2]
                    P.op("act", lambda e, s_=s_, c=c, t0=t0, n=n: e.activation(out=s_[:, 0:n], in_=cbf[:, c, t0:t0 + n], func=AF.Square), [r_cbf], [rs_])
                    P.op("pe", lambda e, c=c, t0=t0, n=n: e.matmul(pS[:, 0:n], lhsT=self.ones_b[:], rhs=cbf[:, c, t0:t0 + n], start=(c == 0), stop=(c == 7)), [r_cbf], [rS])
                    P.op("pe", lambda e, s_=s_, c=c, n=n: e.matmul(pQ[:, 0:n], lhsT=self.ones_b[:], rhs=s_[:, 0:n], start=(c == 0), stop=(c == 7)), [rs_], [rQ])
                P.op("dve", lambda e, n=n: e.tensor_scalar(out=mean[:, 0:n], in0=pS[:, 0:n], scalar1=1.0 / 1024, scalar2=None, op0=ALU.mult), [rS], [r_mean])
                P.op("dve", lambda e, n=n: e.tensor_tensor(out=msq[:, 0:n], in0=mean[:, 0:n], in1=mean[:, 0:n], op=ALU.mult), [r_mean], [r_msq])
                P.op("dve", lambda e, n=n: e.scalar_tensor_tensor(out=rstd[:, 0:n], in0=pQ[:, 0:n], scalar=1.0 / 1024, in1=msq[:, 0:n], op0=ALU.mult, op1=ALU.subtract), [rQ, r_msq], [r_rstd])
                P.op("act", lambda e, n=n: e.activation(out=rstd[:, 0:n], in_=rstd[:, 0:n], func=AF.Sqrt, bias=self.epsT[:, 0:1], scale=1.0), [r_rstd], [r_rstd])
                P.op("dve", lambda e, n=n: e.reciprocal(out=rstd[:, 0:n], in_=rstd[:, 0:n]), [r_rstd], [r_rstd])
                h_, rh = hst[ti % 2]
                for c in range(8):
                    t_, rt = t1[c % 2]
                    P.op("dve", lambda e, t_=t_, c=c, t0=t0, n=n: e.tensor_tensor(out=t_[:, 0:n], in0=cbf[:, c, t0:t0 + n], in1=mean[:, 0:n], op=ALU.subtract), [r_cbf, r_mean], [rt])
                    P.op("dve", lambda e, t_=t_, n=n: e.tensor_tensor(out=t_[:, 0:n], in0=t_[:, 0:n], in1=rstd[:, 0:n], op=ALU.mult), [rt, r_rstd], [rt])
                    P.op("act", lambda e, t_=t_, h_=h_, c=c, n=n: e.activation(out=h_[:, c, 0:n], in_=t_[:, 0:n], func=AF.Silu, bias=av[:, 2, c:c + 1], scale=av[:, 1, c:c + 1]), [rt, r_av], [rh])
                dst = self.HA.rearrange("(c p) t -> p c t", p=128)[:, :, PAD + r0 + t0:PAD + r0 + t0 + n]
                P.dma("sp", lambda e, dst=dst, h_=h_, n=n: e.dma_start(out=dst, in_=h_[:, :, 0:n]), rh, False, [self.R("HA")])
            P.end_phase()

    def phase_att(self, l, r0, r1):
        P = self.P
        T = r1 - r0
        nqt = T // 128
        qt0 = r0 // 128
        BIG = 30000.0
        with ExitStack() as st:
            kin = [self.tile(st, "kin", [128, SEQ], BF16) for _ in range(4)]
            vs, r_vs = self.tile(st, "vs", [128, 32, 65], BF16)
            vw, r_vw = self.tile(st, "vw", [128, 32, 65], BF16)
            qT, r_q = self.tile(st, "qT", [128, 4, T], BF16)
            gt, r_gt = self.tile(st, "gt", [128, nqt, 48], F32)
            mtab, r_mt = self.tile(st, "mtab", [128, 3, nqt, 64], F32)
            Et, r_E = self.tile(st, "Et", [128, 32, 128], BF16)
            smap, r_sm = self.tile(st, "smap", [128, 2, 64], BF16)
            w1 = [self.tile(st, "w1", [128, 32, 256], BF16) for _ in range(2)]
            w2 = [self.tile(st, "w2", [128, 2, 64], BF16) for _ in range(2)]
            pe = [self.tile(st, "pe", [128, 32], BF16) for _ in range(2)]
            cb = [self.tile(st, "cb", [128, 2], F32) for _ in range(2)]
            hid, r_hid = self.tile(st, "hid", [128, 2, 256], BF16)
            kcT, r_kc = self.tile(st, "kcT", [128, 256], BF16)
            rv, r_rv = self.tile(st, "rv", [128, 2, 64], BF16)
            pb = [self.tile(st, "pb", [128, 4, 128], BF16) for _ in range(5)]
            cm, r_cm = self.tile(st, "cm", [128, 128], F32)
            cmb = [self.tile(st, "cmb", [128, 4, 128], BF16) for _ in range(4)]
            trib = [self.tile(st, "trib", [128, 4, 128], BF16) for _ in range(2)]
            selb = [self.tile(st, "selb", [128, 4, 128], BF16) for _ in range(2)]
            negb, r_negb = self.tile(st, "negb", [128, 1], F32)
            sm = {}
            for nm, shp in (("den", [128, 4]), ("cg", [128, 4]), ("imp", [128, 64]), ("score", [128, 64]), ("wk", [128, 64]), ("selw", [128, 128]), ("m8", [128, 8]),
                            ("oacc", [128, 4, 64]), ("otmp", [128, 4, 64]), ("den2", [128, 4]), ("cg2", [128, 4])):
                sm[nm] = self.tile(st, nm, shp, F32)
            obf = [self.tile(st, "obf", [128, 256], BF16) for _ in range(2)]
            ost = [self.tile(st, "ost", [128, 2, 128], BF16) for _ in range(2)]
            P.dma("sp", lambda e: e.dma_start(out=gt[:], in_=self.GATES[PAD + r0:PAD + r1, :].rearrange("(n p) c -> p n c", p=128)), r_gt, True)
            for a_ in range(3):
                P.dma("sp", lambda e: e.dma_start(out=mtab[:, a_, :, :], in_=self.m_sel[a_].rearrange("p (q j) -> p q j", j=64)[:, qt0:qt0 + nqt, :]), r_mt, True)
            P.op("dve", lambda e: e.memset(sm["selw"][0][:], 0.0), [], [sm["selw"][1]])
            P.op("dve", lambda e: e.memset(qT[64:128], 0.0), [], [r_q])
            for ty in (0, 1, 3):
                P.op("dve", lambda e: e.memset(kin[ty][0][64:128], 0.0), [], [kin[ty][1]])
            P.dma("pool", lambda e: e.dma_start(out=kin[2][0][64:128], in_=self.c_E), kin[2][1], True)
            for kv in range(2):
                P.op("dve", lambda e: e.memset(w1[kv][0][64:128], 0.0), [], [w1[kv][1]])
                P.op("dve", lambda e: e.memset(pe[kv][0][64:128], 0.0), [], [pe[kv][1]])


            P.dma("pool", lambda e: e.dma_start(out=smap[:], in_=self.c_smap.rearrange("p (a j) -> p a j", a=2)), r_sm, True)
            for kv in range(2):
                i_ = l * 2 + kv
                P.dma("pool", lambda e: e.dma_start(out=w1[kv][0][0:64], in_=self.cw1[i_].rearrange("d (l h) -> d l h", l=32)), w1[kv][1], True)
                P.dma("pool", lambda e: e.dma_start(out=w2[kv][0][:], in_=self.cw2[i_].rearrange("(c p) d -> p c d", p=128)), w2[kv][1], True)
                P.dma("pool", lambda e: e.dma_start(out=pe[kv][0][0:64], in_=self.cpe[i_]), pe[kv][1], True)
            P.op("dve", lambda e: e.memset(vs[:, :, 64:65], 1.0), [], [r_vs])
            P.op("dve", lambda e: e.memset(vw[:, :, 64:65], 1.0), [], [r_vw])
            P.op("dve", lambda e: e.memset(hid[:], 0.0), [], [r_hid])
            P.op("dve", lambda e: e.memset(kcT[:], 0.0), [], [r_kc])
            P.op("dve", lambda e: e.memset(negb[:], -BIG), [], [r_negb])
            for ti_, tri in enumerate((self.trile, self.trigt)):
                P.op("dve", lambda e: e.tensor_scalar(out=trib[ti_][0][:], in0=tri[:].unsqueeze(1).to_broadcast([128, 4, 128]), scalar1=-1.0, scalar2=BIG, op0=ALU.add, op1=ALU.mult), [], [trib[ti_][1]])
            for kv in range(2):
                ps, pr = self.ps[0], self.psr[0]
                for hc in range(2):
                    for li in range(32):
                        P.op("pe", lambda e: e.matmul(ps[:, hc:hc + 1], lhsT=w1[kv][0][:, li, hc * 128:(hc + 1) * 128], rhs=pe[kv][0][:, li:li + 1],
                                                      start=(li == 0), stop=(li == 31)), [w1[kv][1], pe[kv][1]], [pr])
                P.op("act", lambda e: e.activation(out=cb[kv][0][:], in_=ps[:, 0:2], func=AF.Copy), [pr], [cb[kv][1]])
            PC, PS_, PW, T32, TBF = 3, 4, 5, 7, 7
            SB = [0, 1, 2, 6]
            psbf = self.ps[TBF][:].bitcast(BF16)[:, 512:1024]
            den, r_den = sm["den"]; cg, r_cg = sm["cg"]; imp, r_imp = sm["imp"]; score, r_sc = sm["score"]
            wk, r_wk = sm["wk"]; selw, r_sel = sm["selw"]; sel = selw[:, 64:128]; m8, r_m8 = sm["m8"]; oacc, r_oa = sm["oacc"]; otmp, r_ot = sm["otmp"]
            den2, r_den2 = sm["den2"]; cg2, r_cg2 = sm["cg2"]
            cnt = {"s": 0, "p": 0, "cmb": 0, "q": 0}
            for g in range(4):
                for ty in range(4):
                    P.dma("sp", lambda e: e.dma_start(out=kin[ty][0][0:64], in_=self.KT[ty][g * 64:(g + 1) * 64, PAD:PAD + SEQ]), kin[ty][1], True)
                P.dma("sp", lambda e: e.dma_start(out=vs[:, :, 0:64], in_=self.VT[0][PAD:PAD + SEQ, g * 64:(g + 1) * 64].rearrange("(n p) d -> p n d", p=128)), r_vs, True)
                P.dma("sp", lambda e: e.dma_start(out=vw[:, :, 0:64], in_=self.VT[1][PAD:PAD + SEQ, g * 64:(g + 1) * 64].rearrange("(n p) d -> p n d", p=128)), r_vw, True)
                P.op("dve", lambda e: e.tensor_tensor(out=vw[:], in0=vw[:], in1=self.kval[:, 0:32].unsqueeze(2).to_broadcast([128, 32, 65]), op=ALU.mult), [r_vw], [r_vw])
                P.dma("sp", lambda e: e.dma_start(out=qT[0:64], in_=self.Q[g * 256:(g + 1) * 256, PAD + r0:PAD + r1].rearrange("(h d) t -> d h t", d=64)), r_q, True)
                for kv in range(2):
                    src, rsrc = kin[kv]
                    for hc in range(2):
                        ps, pr = self.ps[hc], self.psr[hc]
                        for li in range(32):
                            P.op("pe", lambda e: e.matmul(ps[:, 0:255], lhsT=w1[kv][0][:, li, hc * 128:(hc + 1) * 128], rhs=src[:, li:li + 16 * 254 + 1:16],
                                                          start=(li == 0), stop=(li == 31)), [w1[kv][1], rsrc], [pr])
                        P.op("act", lambda e: e.activation(out=hid[:, hc, 0:255], in_=ps[:, 0:255], func=AF.Silu, bias=cb[kv][0][:, hc:hc + 1], scale=1.0), [pr, cb[kv][1]], [r_hid])
                    ps, pr = self.ps[2], self.psr[2]
                    if kv == 0:
                        for hc in range(2):
                            P.op("pe", lambda e: e.matmul(ps[0:64, 0:255], lhsT=w2[0][0][:, hc, :], rhs=hid[:, hc, 0:255], start=(hc == 0), stop=(hc == 1)), [w2[0][1], r_hid], [pr])
                        P.op("act", lambda e: e.activation(out=kcT[0:64, 0:255], in_=ps[0:64, 0:255], func=AF.Copy), [pr], [r_kc])
                    else:
                        for nt_ in range(2):
                            for hc in range(2):
                                P.op("pe", lambda e: e.matmul(ps[:, nt_ * 64:(nt_ + 1) * 64], lhsT=hid[:, hc, nt_ * 128:(nt_ + 1) * 128], rhs=w2[1][0][:, hc, :],
                                                              start=(hc == 0), stop=(hc == 1)), [w2[1][1], r_hid], [pr])
                        P.op("act", lambda e: e.activation(out=rv[:], in_=ps[:, 0:128].rearrange("p (a d) -> p a d", a=2), func=AF.Copy), [pr], [r_rv])
                pending = [None]
                for i in range(nqt):
                    qt = qt0 + i
                    qv = qT[:, :, i * 128:(i + 1) * 128]
                    gsl = gt[:, i, g * 12:(g + 1) * 12].rearrange("p (h b) -> p h b", b=3)
                    pc, rpc = self.ps[PC], self.psr[PC]
                    pso, rpso = self.ps[PS_], self.psr[PS_]
                    pwo, rpwo = self.ps[PW], self.psr[PW]
                    psov = pso[:, 0:260].rearrange("p (h d) -> p h d", h=4)
                    pwov = pwo[:, 0:260].rearrange("p (h d) -> p h d", h=4)
                    sb_, rsb_ = selb[cnt["q"] % 2]
                    ob_, rob_ = obf[cnt["q"] % 2]
                    os_, ros_ = ost[cnt["q"] % 2]
                    cnt["q"] += 1
                    nts = [0] if qt < 16 else [0, 1]
                    steps = []
                    for nt_ in nts:
                        c4, rc4 = cmb[cnt["cmb"] % 4]
                        cnt["cmb"] += 1
                        thr = float(128 * qt - 2048 * nt_ - 31)
                        P.op("dve", lambda e: e.tensor_scalar(out=cm[:], in0=self.dtab[:], scalar1=thr, scalar2=self.nval[:, nt_:nt_ + 1], op0=ALU.is_le, op1=ALU.mult), [], [r_cm])
                        P.op("dve", lambda e: e.tensor_scalar(out=c4[:], in0=cm[:].unsqueeze(1).to_broadcast([128, 4, 128]), scalar1=-1.0, scalar2=BIG, op0=ALU.add, op1=ALU.mult), [r_cm], [rc4])

                        def pv_c(p_, rp_, nt_=nt_):
                            first = (nt_ == nts[0])
                            for h in range(4):
                                P.op("pe", lambda e: e.matmul(pc[:, h * 64:(h + 1) * 64], lhsT=p_[:, h, :], rhs=rv[:, nt_, :], start=(first and h == 0), stop=True, skip_group_check=True), [rp_, r_rv], [rpc])
                                P.op("pe", lambda e: e.matmul(pc[:, 256 + h * 64:256 + (h + 1) * 64], lhsT=p_[:, h, :], rhs=smap[:, nt_, :], start=False, stop=True, skip_group_check=True), [rp_, r_sm], [rpc])
                        steps.append(("c", kcT[:, nt_ * 128:(nt_ + 1) * 128], r_kc, [(self.ident_b[:], c4[:], [rc4])], pv_c))
                    k0 = max(0, qt - 4)
                    for kt in range(k0, qt + 1):
                        biases = []
                        if kt == qt:
                            biases.append((self.ident_b[:], trib[0][0][:], [trib[0][1]]))
                        elif kt == qt - 4:
                            biases.append((self.ident_b[:], trib[1][0][:], [trib[1][1]]))

                        def pv_w(p_, rp_, kt=kt):
                            for h in range(4):
                                P.op("pe", lambda e: e.matmul(pwov[:, h, :], lhsT=p_[:, h, :], rhs=vw[:, kt, :], start=(kt == k0 and h == 0), stop=True, skip_group_check=True), [rp_, r_vw], [rpwo])
                        steps.append(("w", kin[3][0][:, kt * 128:(kt + 1) * 128], kin[3][1], biases, pv_w))
                    n_pre = len(steps)
                    for kt in range(0, qt + 1):
                        biases = []
                        if kt == qt:
                            biases.append((self.ident_b[:], trib[0][0][:], [trib[0][1]]))

                        def pv_s(p_, rp_, kt=kt):
                            for h in range(4):
                                P.op("pe", lambda e: e.matmul(psov[:, h, :], lhsT=p_[:, h, :], rhs=vs[:, kt, :], start=(kt == 0 and h == 0), stop=True, skip_group_check=True), [rp_, r_vs], [rpso])
                        steps.append(("s", kin[2][0][:, kt * 128:(kt + 1) * 128], kin[2][1], biases, pv_s, sb_[:], rsb_))
                    N = len(steps)
                    sbank = {}

                    def emit_score(k):
                        kind, lhsT, lreg, biases = steps[k][0:4]
                        rhs_, rreg_ = (steps[k][5], steps[k][6]) if len(steps[k]) > 5 else (qv, r_q)
                        bk = SB[cnt["s"] % 4]
                        cnt["s"] += 1
                        ps, pr = self.ps[bk], self.psr[bk]
                        sbank[k] = (ps, pr)
                        P.op("pe", lambda e: e.matmul(ps[:], lhsT=lhsT, rhs=rhs_, start=True, stop=(len(biases) == 0)), [lreg, rreg_], [pr])
                        for bi, (bl, br, bregs) in enumerate(biases):
                            P.op("pe", lambda e: e.matmul(ps[:], lhsT=bl, rhs=br, start=False, stop=(bi == len(biases) - 1)), bregs, [pr])

                    def post_cmp_dve():
                        pc2 = pc[:, 256:512].rearrange("p (h j) -> p h j", h=4)
                        P.op("dve", lambda e: e.tensor_reduce(out=den[:], in_=pc2, axis=AX.X, op=ALU.add), [rpc], [r_den])
                        P.op("dve", lambda e: e.tensor_scalar(out=den[:], in0=den[:], scalar1=0.5, scalar2=1e-30, op0=ALU.mult, op1=ALU.max), [r_den], [r_den])
                        P.op("dve", lambda e: e.reciprocal(out=den[:], in_=den[:]), [r_den], [r_den])
                        P.op("dve", lambda e: e.tensor_scalar(out=imp[:], in0=pc2[:, 0, :], scalar1=den[:, 0:1], scalar2=None, op0=ALU.mult), [rpc, r_den], [r_imp])
                        for h in range(1, 4):
                            P.op("dve", lambda e: e.scalar_tensor_tensor(out=imp[:], in0=pc2[:, h, :], scalar=den[:, h:h + 1], in1=imp[:], op0=ALU.mult, op1=ALU.add), [rpc, r_den, r_imp], [r_imp])
                        P.op("dve", lambda e: e.tensor_tensor(out=score[:], in0=imp[:], in1=mtab[:, 0, i, :], op=ALU.mult), [r_imp, r_mt], [r_sc])
                        P.op("dve", lambda e: e.tensor_tensor(out=score[:], in0=score[:], in1=mtab[:, 1, i, :], op=ALU.add), [r_sc, r_mt], [r_sc])
                        P.op("dve", lambda e: e.max(out=m8[:], in_=score[:]), [r_sc], [r_m8])
                        P.op("dve", lambda e: e.match_replace(out=wk[:], in_to_replace=m8[:], in_values=score[:], imm_value=-2.0), [r_m8, r_sc], [r_wk])
                        P.op("dve", lambda e: e.max(out=m8[:], in_=wk[:]), [r_wk], [r_m8])
                        P.op("dve", lambda e: e.match_replace(out=wk[:], in_to_replace=m8[:], in_values=wk[:], imm_value=-2.0), [r_m8, r_wk], [r_wk])
                        P.op("dve", lambda e: e.tensor_tensor(out=sel, in0=score[:], in1=wk[:], op=ALU.subtract), [r_sc, r_wk], [r_sel])
                        P.op("dve", lambda e: e.scalar_tensor_tensor(out=sel, in0=sel, scalar=1.0, in1=mtab[:, 2, i, :], op0=ALU.min, op1=ALU.mult), [r_sel, r_mt], [r_sel])
                        P.op("dve", lambda e: e.tensor_tensor(out=cg[:], in0=den[:], in1=gsl[:, :, 0], op=ALU.mult), [r_den, r_gt], [r_cg])
                        P.op("dve", lambda e: e.tensor_tensor(out=oacc[:], in0=pc[:, 0:256].rearrange("p (h d) -> p h d", h=4), in1=cg[:].unsqueeze(2).to_broadcast([128, 4, 64]), op=ALU.mult), [rpc, r_cg], [r_oa])

                    def pre_slc():
                        pt, rpt = self.ps[T32], self.psr[T32]
                        P.op("pe", lambda e: e.transpose(out=pt[:, 0:128], in_=selw[:], identity=self.ident_f[:]), [r_sel], [rpt])
                        P.op("act", lambda e: e.activation(out=sb_[64:128], in_=pt[64:128, 0:128].unsqueeze(1).to_broadcast([64, 4, 128]), func=AF.Identity, bias=negb[64:128, 0:1], scale=BIG), [rpt, r_negb], [rsb_])
                        P.op("dve", lambda e: e.tensor_copy(out=sb_[0:64], in_=qT[0:64, :, i * 128:(i + 1) * 128]), [r_q], [rsb_])

                    LA = getattr(self, "lookahead", 2)
                    if LA:
                        LA = max(1, min(LA, n_pre - len(nts)))
                    for k0_ in range(min(LA, N)):
                        if k0_ == n_pre:
                            pre_slc()
                        emit_score(k0_)
                    for k in range(N):
                        if not LA:
                            if k == n_pre:
                                pre_slc()
                            emit_score(k)
                        elif k + LA < N:
                            if k + LA == n_pre:
                                pre_slc()
                            emit_score(k + LA)
                        ps, pr = sbank.pop(k)
                        p_, rp_ = pb[cnt["p"] % 5]
                        cnt["p"] += 1
                        P.op("act", lambda e: e.activation(out=p_[:], in_=ps[:].rearrange("p (h q) -> p h q", h=4), func=AF.Exp), [pr], [rp_])
                        steps[k][4](p_, rp_)
                        if k == len(nts) - 1:
                            post_cmp_dve()
                            if pending[0] is not None:
                                pending[0]()
                                pending[0] = None
                    P.op("dve", lambda e: e.tensor_scalar(out=den2[:], in0=psov[:, :, 64], scalar1=1e-30, scalar2=None, op0=ALU.max), [rpso], [r_den2])
                    P.op("dve", lambda e: e.reciprocal(out=den2[:], in_=den2[:]), [r_den2], [r_den2])
                    P.op("dve", lambda e: e.tensor_tensor(out=cg2[:], in0=den2[:], in1=gsl[:, :, 1], op=ALU.mult), [r_den2, r_gt], [r_cg2])
                    P.op("dve", lambda e: e.tensor_tensor(out=otmp[:], in0=psov[:, :, 0:64], in1=cg2[:].unsqueeze(2).to_broadcast([128, 4, 64]), op=ALU.mult), [rpso, r_cg2], [r_ot])
                    P.op("dve", lambda e: e.tensor_tensor(out=oacc[:], in0=oacc[:], in1=otmp[:], op=ALU.add), [r_oa, r_ot], [r_oa])
                    P.op("dve", lambda e: e.tensor_scalar(out=den2[:], in0=pwov[:, :, 64], scalar1=1e-30, scalar2=None, op0=ALU.max), [rpwo], [r_den2])
                    P.op("dve", lambda e: e.reciprocal(out=den2[:], in_=den2[:]), [r_den2], [r_den2])
                    P.op("dve", lambda e: e.tensor_tensor(out=cg2[:], in0=den2[:], in1=gsl[:, :, 2], op=ALU.mult), [r_den2, r_gt], [r_cg2])
                    P.op("dve", lambda e: e.tensor_tensor(out=otmp[:], in0=pwov[:, :, 0:64], in1=cg2[:].unsqueeze(2).to_broadcast([128, 4, 64]), op=ALU.mult), [rpwo, r_cg2], [r_ot])
                    P.op("dve", lambda e: e.tensor_tensor(out=ob_[:].rearrange("p (h d) -> p h d", h=4), in0=oacc[:], in1=otmp[:], op=ALU.add), [r_oa, r_ot], [rob_])

                    def post_pe(i=i, ob_=ob_, rob_=rob_, os_=os_, ros_=ros_):
                        rptb = self.psr[TBF]
                        for half in range(2):
                            P.op("pe", lambda e: e.transpose(out=psbf[:, half * 128:(half + 1) * 128], in_=ob_[:, half * 128:(half + 1) * 128], identity=self.ident_b[:]), [rob_], [rptb])
                        P.op("act", lambda e: e.activation(out=os_[:], in_=psbf[:, 0:256].rearrange("p (a q) -> p a q", a=2), func=AF.Copy), [rptb], [ros_])
                        c0 = PAD + r0 + i * 128
                        dst = self.OC[g * 256:(g + 1) * 256, c0:c0 + 128].rearrange("(a p) t -> p a t", p=128)
                        P.dma("sp", lambda e: e.dma_start(out=dst, in_=os_[:]), ros_, False)
                    pending[0] = post_pe
                if pending[0] is not None:
                    pending[0]()
                    pending[0] = None
            P.end_phase()

    def phase_merge(self, l, X, Xout, r0, r1):
        P = self.P
        Wl = self.w_in[l]
        with ExitStack() as st:
            nt = self.norm_tiles(st)
            pt = self.post_tiles(st)
            hT, r_h = self.tile(st, "hT", [128, 16, 512], BF16)
            ins = [self.tile(st, "hin", [128, 8, 512], BF16) for _ in range(3)]
            wb = [self.tile(st, "wbuf", [128, 8192], BF16) for _ in range(3)]
            mT, r_m = self.tile(st, "mT", [128, 16, 512], BF16)
            mixed, r_mx = self.tile(st, "mixed", [128, 16, 512], F32)
            sgs = [self.tile(st, "sgs", [128, 4, 512], BF16) for _ in range(2)]
            accm, r_accm = self.tile(st, "accm", [128, 4, 512], F32)
            tmp = [self.tile(st, "tmp", [128, 512], F32) for _ in range(2)]
            sq = [self.tile(st, "sq", [128, 512], BF16) for _ in range(2)]
            srcs = [self.HA, self.HB, self.OC]
            wouts = [self.w_a_out[l], self.w_b_out[l], self.w_c_out[l]]
            hregs = [self.newreg("hT"), self.newreg("hT")]
            sched = []
            pk = [0]

            def nb():
                bk = pk[0] % 6
                pk[0] += 1
                return self.ps[bk], self.psr[bk]
            for (s0, n) in self.tok_tiles(r0, r1 - r0):
                sc_ = {}
                sched.append(sc_)

                def nfn(b, s0=s0, n=n):
                    self.fill_hT(nt, X, PAD + s0, n, (l * 4 + 0) * 16, hT, hregs)
                    for b3 in range(3):
                        P.dma("sp", lambda e: e.dma_start(out=ins[b3][0][:, :, 0:n], in_=srcs[b3].rearrange("(c p) t -> p c t", p=128)[:, :, PAD + s0:PAD + s0 + n]),
                              ins[b3][1], True)
                sc_["N"] = [(None, nfn)]
                jobs = []
                for dg in range(4):
                    for b3 in range(3):
                        def lfg(b, dg=dg, b3=b3):
                            self.wload(wb[b][0], wb[b][1], Wl, 16, C_M + b3 * 2048 + dg * 512, 512, 0)

                        def cfg(b, dg=dg, b3=b3, n=n, hregs=hregs):
                            wbt, rwb = wb[b]
                            wg = wbt[:, 0:8192].rearrange("p (k n) -> p k n", k=16)
                            s_, rs_ = sgs[b3 % 2]
                            for cc in range(4):
                                pg, rg = nb()
                                for kc in range(16):
                                    P.op("pe", lambda e: e.matmul(pg[:, 0:n], lhsT=wg[:, kc, cc * 128:(cc + 1) * 128], rhs=hT[:, kc, 0:n], start=(kc == 0), stop=(kc == 15)), [rwb] + hregs, [rg])
                                P.op("act", lambda e: e.activation(out=s_[:, cc, 0:n], in_=pg[:, 0:n], func=AF.Sigmoid), [rg], [rs_])
                        jobs.append((lfg, cfg))

                        def lfy(b, dg=dg, b3=b3):
                            self.wload(wb[b][0], wb[b][1], wouts[b3], 8, dg * 512, 512, 0)

                        def cfy(b, dg=dg, b3=b3, n=n):
                            wbt, rwb = wb[b]
                            wy = wbt[:, 0:4096].rearrange("p (k n) -> p k n", k=8)
                            s_, rs_ = sgs[b3 % 2]
                            for cc in range(4):
                                py, ry = nb()
                                for kc in range(8):
                                    P.op("pe", lambda e: e.matmul(py[:, 0:n], lhsT=wy[:, kc, cc * 128:(cc + 1) * 128], rhs=ins[b3][0][:, kc, 0:n], start=(kc == 0), stop=(kc == 7)), [rwb, ins[b3][1]], [ry])
                                if b3 == 0:
                                    P.op("dve", lambda e: e.tensor_tensor(out=accm[:, cc, 0:n], in0=py[:, 0:n], in1=s_[:, cc, 0:n], op=ALU.mult), [ry, rs_], [r_accm])
                                else:
                                    t_, rt_ = tmp[cc % 2]
                                    P.op("dve", lambda e: e.tensor_tensor(out=t_[:, 0:n], in0=py[:, 0:n], in1=s_[:, cc, 0:n], op=ALU.mult), [ry, rs_], [rt_])
                                    if b3 == 1:
                                        P.op("dve", lambda e: e.tensor_tensor(out=accm[:, cc, 0:n], in0=accm[:, cc, 0:n], in1=t_[:, 0:n], op=ALU.add), [r_accm, rt_], [r_accm])
                                    else:
                                        P.op("dve", lambda e: e.tensor_tensor(out=mT[:, dg * 4 + cc, 0:n], in0=accm[:, cc, 0:n], in1=t_[:, 0:n], op=ALU.add), [r_accm, rt_], [r_m])
                        jobs.append((lfy, cfy))
                sc_["A"] = jobs
                jobs = []
                for jg in range(4):
                    def lf(b, jg=jg):
                        self.wload(wb[b][0], wb[b][1], self.w_o[l], 16, jg * 512, 512, 0)

                    def cf(b, jg=jg, n=n):
                        wo = wb[b][0][:, 0:8192].rearrange("p (k n) -> p k n", k=16)
                        for cc in range(4):
                            dch = jg * 4 + cc
                            po, ro = self.ps[dch % 4], self.psr[dch % 4]
                            for kc in range(16):
                                P.op("pe", lambda e, kc=kc, po=po, cc=cc: e.matmul(po[:, 0:n], lhsT=wo[:, kc, cc * 128:(cc + 1) * 128], rhs=mT[:, kc, 0:n], start=(kc == 0), stop=(kc == 15)), [wb[b][1], r_m], [ro])
                            s_, rs_ = sq[dch % 2]
                            P.op("act", lambda e, po=po, dch=dch: e.activation(out=mixed[:, dch, 0:n], in_=po[:, 0:n], func=AF.Copy), [ro], [r_mx])
                            P.op("act", lambda e, po=po, s_=s_: e.activation(out=s_[:, 0:n], in_=po[:, 0:n], func=AF.Square), [ro], [rs_])
                            P.op("pe", lambda e, s_=s_, dch=dch: e.matmul(self.ps[6][:, 0:n], lhsT=self.ones_b[:], rhs=s_[:, 0:n], start=(dch == 0), stop=(dch == 15)), [rs_], [self.psr[6]])
                    jobs.append((lf, cf))
                sc_["B"] = jobs
                sc_["P"] = [(None, lambda b, s0=s0, n=n: self.post_norm(st, mixed, r_mx, 6, n, (l * 4 + 1) * 16, X, Xout, PAD + s0, PAD + s0, pt))]
            nt_ = len(sched)
            seq = sched[0]["N"] + sched[0]["A"]
            for t in range(nt_):
                if t + 1 < nt_:
                    seq += sched[t + 1]["N"]
                seq += sched[t]["B"]
                if t + 1 < nt_:
                    seq += sched[t + 1]["A"][:4] + sched[t]["P"] + sched[t + 1]["A"][4:]
                else:
                    seq += sched[t]["P"]
            self.run_jobs(seq, nbuf=3)
            P.end_phase()

    def phase_ffn(self, l, X, Xout, f0, f1, out_col0):
        P = self.P
        Wu = self.w_up[l]
        Wd = self.w_down[l]
        with ExitStack() as st:
            nt = self.norm_tiles(st)
            pt = self.post_tiles(st)
            hT, _ = self.tile(st, "hT", [128, 16, 512], BF16)
            wb = [self.tile(st, "wbuf", [128, 8192], BF16) for _ in range(3)]
            act, r_act = self.tile(st, "act", [128, 44, 512], BF16)
            mixed, r_mx = self.tile(st, "mixed", [128, 16, 512], F32)
            pre = [self.tile(st, "pre", [128, 514], F32) for _ in range(2)]
            uu = [self.tile(st, "uu", [128, 512], F32) for _ in range(3)]
            sgf, r_sgf = self.tile(st, "sgf", [128, 4, 512], F32)
            sq = [self.tile(st, "sq", [128, 512], BF16) for _ in range(2)]
            carry, r_carry = self.tile(st, "carry", [128, 88, 2], F32)
            fw, r_fw = self.tile(st, "fw", [128, 88, 3], F32)
            fb, r_fb = self.tile(st, "fb", [128, 88], F32)
            P.dma("sp", lambda e: e.dma_start(out=fw[:], in_=self.ffw[:, l * 264:(l + 1) * 264].rearrange("p (c k) -> p c k", k=3)), r_fw, True)
            P.dma("sp", lambda e: e.dma_start(out=fb[:], in_=self.ffb[:, l * 88:(l + 1) * 88]), r_fb, True)
            tiles = [(f0 - 2, 2)] + self.tok_tiles(f0, f1 - f0)
            pk = [0]
            hregs = [self.newreg("hT"), self.newreg("hT")]
            sched = {}
            for tix, (s0, n) in enumerate(tiles):
                halo = (tix == 0)
                sched[tix] = {}
                sched[tix]["N"] = [(None, lambda b, s0=s0, n=n: self.fill_hT(nt, X, PAD + s0, n, (l * 4 + 2) * 16, hT, hregs))]
                jobs = []
                for grp in range(11):
                    for gv in range(2):
                        def lf(b, grp=grp, gv=gv):
                            self.wload(wb[b][0], wb[b][1], Wu, 16, gv * DFF + grp * 512, 512, 0)

                        def cf(b, grp=grp, gv=gv, n=n, halo=halo, hregs=hregs):
                            wbt, rwb = wb[b]
                            wv_ = wbt[:, 0:8192].rearrange("p (k n) -> p k n", k=16)
                            for cc in range(4):
                                jg = grp * 4 + cc
                                j = jg + 44 * gv
                                bk = pk[0] % 6
                                pk[0] += 1
                                ps, pr = self.ps[bk], self.psr[bk]
                                for kc in range(16):
                                    P.op("pe", lambda e: e.matmul(ps[:, 0:n], lhsT=wv_[:, kc, cc * 128:(cc + 1) * 128], rhs=hT[:, kc, 0:n], start=(kc == 0), stop=(kc == 15)),
                                         [rwb] + hregs, [pr])
                                if halo:
                                    P.op("act", lambda e: e.activation(out=carry[:, j, :], in_=ps[:, 0:2], func=AF.Copy), [pr], [r_carry])
                                    continue
                                p_, rp_ = pre[cc % 2]
                                u_, ru_ = uu[cc % 3]
                                P.op("act", lambda e: e.activation(out=p_[:, 2:2 + n], in_=ps[:, 0:n], func=AF.Copy), [pr], [rp_])
                                P.op("act", lambda e: e.activation(out=p_[:, 0:2], in_=carry[:, j, :], func=AF.Copy), [r_carry], [rp_])
                                P.op("act", lambda e: e.activation(out=carry[:, j, :], in_=p_[:, n:n + 2], func=AF.Copy), [rp_], [r_carry])
                                P.op("dve", lambda e: e.tensor_scalar(out=u_[:, 0:n], in0=p_[:, 2:2 + n], scalar1=fw[:, j, 2:3], scalar2=fb[:, j:j + 1], op0=ALU.mult, op1=ALU.add), [rp_, r_fw, r_fb], [ru_])
                                P.op("dve", lambda e: e.scalar_tensor_tensor(out=u_[:, 0:n], in0=p_[:, 1:1 + n], scalar=fw[:, j, 1:2], in1=u_[:, 0:n], op0=ALU.mult, op1=ALU.add), [rp_, ru_], [ru_])
                                P.op("dve", lambda e: e.scalar_tensor_tensor(out=u_[:, 0:n], in0=p_[:, 0:n], scalar=fw[:, j, 0:1], in1=u_[:, 0:n], op0=ALU.mult, op1=ALU.add), [rp_, ru_], [ru_])
                                if gv == 0:
                                    P.op("act", lambda e: e.activation(out=sgf[:, cc, 0:n], in_=u_[:, 0:n], func=AF.Silu), [ru_], [r_sgf])
                                else:
                                    P.op("dve", lambda e: e.tensor_tensor(out=act[:, jg, 0:n], in0=sgf[:, cc, 0:n], in1=u_[:, 0:n], op=ALU.mult), [r_sgf, ru_], [r_act])
                        jobs.append((lf, cf))
                sched[tix]["U"] = jobs
                jobs = []
                if not halo:
                    kranges = [(0, 16), (16, 16), (32, 12)]
                    for dg in range(4):
                        for kr, (k0_, nk) in enumerate(kranges):
                            def lf(b, dg=dg, k0_=k0_, nk=nk):
                                self.wload(wb[b][0], wb[b][1], Wd, nk, dg * 512, 512, 0, k0=k0_)

                            def cf(b, dg=dg, kr=kr, k0_=k0_, nk=nk, n=n):
                                wd = wb[b][0][:, 0:nk * 512].rearrange("p (k n) -> p k n", k=nk)
                                for cc in range(4):
                                    dch = dg * 4 + cc
                                    po, ro = self.ps[cc], self.psr[cc]
                                    for kc in range(nk):
                                        P.op("pe", lambda e: e.matmul(po[:, 0:n], lhsT=wd[:, kc, cc * 128:(cc + 1) * 128], rhs=act[:, k0_ + kc, 0:n], start=(kr == 0 and kc == 0), stop=(kr == 2 and kc == nk - 1)),
                                             [wb[b][1], r_act], [ro])
                                    if kr == 2:
                                        s_, rs_ = sq[dch % 2]
                                        P.op("act", lambda e: e.activation(out=mixed[:, dch, 0:n], in_=po[:, 0:n], func=AF.Copy), [ro], [r_mx])
                                        P.op("act", lambda e: e.activation(out=s_[:, 0:n], in_=po[:, 0:n], func=AF.Square), [ro], [rs_])
                                        P.op("pe", lambda e: e.matmul(self.ps[6][:, 0:n], lhsT=self.ones_b[:], rhs=s_[:, 0:n], start=(dch == 0), stop=(dch == 15)), [rs_], [self.psr[6]])
                            jobs.append((lf, cf))
                sched[tix]["D"] = jobs
                sched[tix]["P"] = [(None, lambda b, s0=s0, n=n: self.post_norm(st, mixed, r_mx, 6, n, (l * 4 + 3) * 16, X, Xout, PAD + s0, out_col0 + (s0 - f0), pt))]
            nt_ = len(tiles)
            seq = sched[0]["N"] + sched[0]["U"] + sched[1]["N"] + sched[1]["U"]
            for t in range(1, nt_):
                if t + 1 < nt_:
                    seq += sched[t + 1]["N"]
                seq += sched[t]["D"]
                if t + 1 < nt_:
                    seq += sched[t + 1]["U"][:4] + sched[t]["P"] + sched[t + 1]["U"][4:]
                else:
                    seq += sched[t]["P"]
            self.run_jobs(seq, nbuf=3)
            P.end_phase()

    def build(self):
        ph = []
        ph.append(lambda: self.phase_kv(0, self.xT))
        for (r0, r1) in ((0, 2048), (2048, 4096)):
            ph.append(lambda r0=r0, r1=r1: self.phase_m1(0, self.xT, r0, r1))
            ph.append(lambda r0=r0, r1=r1: self.phase_m2(0, r0, r1))
            ph.append(lambda r0=r0, r1=r1: self.phase_att(0, r0, r1))
            ph.append(lambda r0=r0, r1=r1: self.phase_merge(0, self.xT, self.XM, r0, r1))
        ph.append(lambda: self.phase_ffn(0, self.XM, self.X1, 0, 4096, PAD))
        ph.append(lambda: self.phase_kv(1, self.X1))
        ph.append(lambda: self.phase_m1(1, self.X1, 1920, 4096))
        ph.append(lambda: self.phase_m2(1, 1920, 4096))
        ph.append(lambda: self.phase_att(1, 1920, 4096))
        ph.append(lambda: self.phase_merge(1, self.X1, self.XM, 1920, 4096))
        ph.append(lambda: self.phase_ffn(1, self.XM, self.OUT, 2048, 4096, 0))
        sel = self.stop if self.stop is not None else range(len(ph))
        for i in sel:
            ph[i]()
        self.P.barrier()
        self.P.emit()
        return self.nc


def _colvec(v, nchunk):
    return np.ascontiguousarray(v.reshape(nchunk, 128).T)


def make_inputs(inp):
    L = 2
    f = lambda a: np.ascontiguousarray(np.asarray(a, dtype=np.float32))
    shared = {}
    for k in ("w_in", "w_a_out", "w_b_out", "w_c_out", "w_o", "w_up", "w_down"):
        shared[k] = f(inp[k])
    nw = np.zeros((128, L * 4 * 16), np.float32)
    for l in range(L):
        for i, k in enumerate(("norm_mix_pre", "norm_mix_post", "norm_ffn_pre", "norm_ffn_post")):
            nw[:, (l * 4 + i) * 16:(l * 4 + i + 1) * 16] = _colvec(f(inp[k])[l], 16)
    shared["normw"] = nw
    caw = np.zeros((128, L * 8 * 31), np.float32)
    av = np.zeros((128, L * 3 * 8), np.float32)
    for l in range(L):
        w = f(inp["conv_a_w"])[l]
        caw[:, l * 248:(l + 1) * 248] = w.T.reshape(8, 128, 31).transpose(1, 0, 2).reshape(128, 248)
        for i, k in enumerate(("conv_a_b", "ln_a_g", "ln_a_b")):
            av[:, (l * 3 + i) * 8:(l * 3 + i + 1) * 8] = _colvec(f(inp[k])[l], 8)
    shared["convaw"] = caw
    shared["avec"] = av
    lnb = np.zeros((128, L * 2 * 1024), np.float32)
    for l in range(L):
        lnb[:, (l * 2) * 1024:(l * 2 + 1) * 1024] = f(inp["ln_b_g"])[l][None, :]
        lnb[:, (l * 2 + 1) * 1024:(l * 2 + 2) * 1024] = f(inp["ln_b_b"])[l][None, :]
    shared["lnb"] = lnb
    shared["sgw"] = np.ascontiguousarray(f(inp["sg_w"]).transpose(0, 3, 1, 2).reshape(L, 128, 1024))
    shared["sgb"] = np.ascontiguousarray(f(inp["sg_b"]).reshape(1, L * 1024))
    cw1 = np.zeros((L * 2, 64, 32 * 256), np.float32)
    cw2 = np.zeros((L * 2, 256, 64), np.float32)
    cpe = np.zeros((L * 2, 64, 32), np.float32)
    for l in range(L):
        for kv, s in enumerate(("k", "v")):
            cw1[l * 2 + kv] = f(inp["cmp_w1_" + s])[l].transpose(1, 0, 2).reshape(64, 32 * 256)
            cw2[l * 2 + kv] = f(inp["cmp_w2_" + s])[l]
            cpe[l * 2 + kv] = f(inp["cmp_pe_" + s])[l].T
    shared["cw1"], shared["cw2"], shared["cpe"] = cw1, cw2, cpe
    ffw = np.zeros((128, L * 88 * 3), np.float32)
    ffb = np.zeros((128, L * 88), np.float32)
    for l in range(L):
        w = f(inp["ffn_conv_w"])[l]
        ffw[:, l * 264:(l + 1) * 264] = w.T.reshape(88, 128, 3).transpose(1, 0, 2).reshape(128, 264)
        ffb[:, l * 88:(l + 1) * 88] = _colvec(f(inp["ffn_conv_b"])[l], 88)
    shared["ffw"], shared["ffb"] = ffw, ffb
    p = np.arange(128)
    shared["c_ident"] = np.eye(128, dtype=np.float32)
    shared["c_trile"] = (p[:, None] <= p[None, :]).astype(np.float32)
    shared["c_trigt"] = (p[:, None] > p[None, :]).astype(np.float32)
    shared["c_dtab"] = (16.0 * p[:, None] - p[None, :]).astype(np.float32)
    k = np.arange(4096)
    shared["c_E"] = (k[None, :] // 64 == np.arange(64)[:, None]).astype(np.float32)
    n = np.arange(256)
    sm = np.zeros((256, 64), np.float32)
    for nn in range(255):
        sm[nn, nn // 4] += 1.0
        sm[nn, (nn + 1) // 4] += 1.0
    shared["c_smap"] = np.ascontiguousarray(sm.reshape(2, 128, 64).transpose(1, 0, 2).reshape(128, 128))
    x = f(inp["x"])
    maps = []
    for b in range(4):
        for s in range(2):
            m = dict(shared)
            xT = np.zeros((D, NCOL), np.float32)
            tok = np.zeros((NCOL,), np.float32)
            if s == 1:
                xT[:, PAD:] = x[b].T
                tok[PAD:] = 1.0
                j0 = 0
            else:
                xT[:, PAD + 2048:] = x[b, :2048].T
                tok[PAD + 2048:] = 1.0
                j0 = 32
            m["xT"] = xT
            m["m_tok"] = np.ascontiguousarray(np.broadcast_to(tok[None, :], (128, NCOL)))
            kval = np.ones((128, 32), np.float32)
            nval = np.ones((128, 2), np.float32)
            if s == 0:
                kval[:, :16] = 0.0
                nval[:, 0] = 0.0
            m["m_kval"], m["m_nval"] = kval, nval
            t = np.arange(4096).reshape(32, 128).T
            cur = t // 64
            j = np.arange(64)[None, None, :]
            valid = (j <= cur[:, :, None]) & (j >= j0)
            forced = ((j == j0) | (j == cur[:, :, None]) | (j == cur[:, :, None] - 1)) & valid
            M1 = (valid & ~forced).astype(np.float32)
            M2 = np.where(forced, 1e4 + j, np.where(valid, 0.0, -1.0)).astype(np.float32)
            M3 = valid.astype(np.float32)
            m["m_sel"] = np.ascontiguousarray(np.stack([M1, M2, M3]).reshape(3, 128, 32 * 64))
            maps.append(m)
    return maps


_CACHE = {}


def kernel(**inputs):
    maps = make_inputs(inputs)
    if "nc" not in _CACHE:
        _CACHE["nc"] = Builder().build()
    nc = _CACHE["nc"]
    res = run_bass_kernel_spmd(nc, maps, core_ids=list(range(8)))
    out = np.zeros((4, SEQ, D), np.float32)
    for b in range(4):
        for s in range(2):
            o = res.results[b * 2 + s]["OUT"]
            out[b, s * 2048:(s + 1) * 2048, :] = o.T
    return out
```
